# Optimizing a Trainium2 kernel written in Bass

```python
import math
import jax, jax.numpy as jnp
from jax import lax
import numpy as np

D_MODEL = 1024
BATCH = 4
SEQ = 4096
DEPTH = 1

SB_HEADS = 8
SB_HEAD_DIM = 64
SB_WIDTH = SB_HEADS * SB_HEAD_DIM
MLA_HEADS = 8
MLA_NOPE_DIM = 64
MLA_ROPE_DIM = 32
MLA_V_DIM = 64
MLA_Q_RANK = 384
MLA_KV_RANK = 256
MLA_WIDTH = MLA_HEADS * MLA_V_DIM
MLA_QK_DIM = MLA_NOPE_DIM + MLA_ROPE_DIM

Q_BLOCK = 128
ROPE_BASE = 10000.0
EPS = 1e-6

SPLITS = (SB_WIDTH, SB_WIDTH, SB_WIDTH, SB_WIDTH,
          MLA_Q_RANK, MLA_KV_RANK, MLA_ROPE_DIM, MLA_WIDTH,
          D_MODEL, D_MODEL)
IN_WIDTH = int(sum(SPLITS))
SPLIT_POINTS = tuple(int(v) for v in np.cumsum(SPLITS)[:-1])

kernel_name = "hybrid_stickbreaking_mla_adaln_block"


def rmsnorm(x, g):
    xf = x.astype(jnp.float32)
    y = xf * lax.rsqrt(jnp.mean(xf * xf, axis=-1, keepdims=True) + EPS)
    return (y * g.astype(jnp.float32)).astype(x.dtype)


def rope(x, positions):
    r = x.shape[-1]
    inv_freq = ROPE_BASE ** (-jnp.arange(0, r, 2, dtype=jnp.float32) / r)
    ang = positions.astype(jnp.float32)[:, :, None, None] * inv_freq
    cos, sin = jnp.cos(ang), jnp.sin(ang)
    xf = x.astype(jnp.float32)
    x1, x2 = xf[..., : r // 2], xf[..., r // 2:]
    return jnp.concatenate([x1 * cos - x2 * sin, x1 * sin + x2 * cos], axis=-1)


def to_blocks(t):
    b, h, s, d = t.shape
    return t.reshape(b, h, s // Q_BLOCK, Q_BLOCK, d).transpose(2, 0, 1, 3, 4)


def from_blocks(t):
    n, b, h, q, d = t.shape
    return t.transpose(1, 2, 0, 3, 4).reshape(b, h, n * q, d)


def stick_breaking_attention(q, k, v):
    s_len = k.shape[2]
    scale = 1.0 / math.sqrt(q.shape[-1])
    key_idx = jnp.arange(s_len)

    def block(args):
        qb, i = args
        z = jnp.einsum('bhqd,bhkd->bhqk', qb, k) * scale
        q_idx = i * Q_BLOCK + jnp.arange(Q_BLOCK)
        strict = key_idx[None, :] < q_idx[:, None]
        log_one_minus = jnp.where(strict, jax.nn.log_sigmoid(-z), 0.0)
        suffix = lax.cumsum(log_one_minus, axis=3, reverse=True) - log_one_minus
        w = jnp.where(strict, jnp.exp(jax.nn.log_sigmoid(z) + suffix), 0.0)
        return jnp.einsum('bhqk,bhkd->bhqd', w, v)

    n_blk = s_len // Q_BLOCK
    out = lax.map(block, (to_blocks(q), jnp.arange(n_blk)))
    return from_blocks(out)


def latent_attention(q_nope, q_pe, k_nope, k_pe, v):
    s_len = k_nope.shape[2]
    scale = 1.0 / math.sqrt(MLA_QK_DIM)
    key_idx = jnp.arange(s_len)

    def block(args):
        qn, qp, i = args
        sc = (jnp.einsum('bhqd,bhkd->bhqk', qn, k_nope)
              + jnp.einsum('bhqr,bkr->bhqk', qp, k_pe)) * scale
        q_idx = i * Q_BLOCK + jnp.arange(Q_BLOCK)
        causal = key_idx[None, :] <= q_idx[:, None]
        p = jax.nn.softmax(jnp.where(causal, sc, -jnp.inf), axis=-1)
        return jnp.einsum('bhqk,bhkd->bhqd', p, v)

    n_blk = s_len // Q_BLOCK
    out = lax.map(block, (to_blocks(q_nope), to_blocks(q_pe), jnp.arange(n_blk)))
    return from_blocks(out)


def split_heads(t, h):
    b, s, w = t.shape
    return t.reshape(b, s, h, w // h).transpose(0, 2, 1, 3)


def merge_heads(t):
    b, h, s, d = t.shape
    return t.transpose(0, 2, 1, 3).reshape(b, s, h * d)


def setup_inputs(seed: int = 0) -> dict:
    key = jax.random.key(seed)
    ks = jax.random.split(key, 16)
    f32 = jnp.float32

    def nrm(k, shape, fan_in, mult=1.0):
        return jax.random.normal(k, shape, f32) * (mult * fan_in ** -0.5)

    x = jax.random.normal(ks[0], (BATCH, SEQ, D_MODEL), f32)
    c = jax.random.normal(ks[1], (BATCH, D_MODEL), f32)
    positions = jnp.broadcast_to(jnp.arange(SEQ, dtype=jnp.int32), (BATCH, SEQ))
    w_ada = nrm(ks[2], (DEPTH, D_MODEL, 3 * D_MODEL), D_MODEL, 0.2)
    b_ada = jax.random.normal(ks[3], (DEPTH, 3 * D_MODEL), f32) * 0.02
    norm_gain = 1.0 + 0.02 * jax.random.normal(ks[4], (DEPTH, D_MODEL), f32)
    w_in = nrm(ks[5], (DEPTH, D_MODEL, IN_WIDTH), D_MODEL)
    q_norm_gain = 1.0 + 0.02 * jax.random.normal(ks[6], (DEPTH, MLA_Q_RANK), f32)
    w_uq = nrm(ks[7], (DEPTH, MLA_Q_RANK, MLA_HEADS * MLA_QK_DIM), MLA_Q_RANK)
    kv_norm_gain = 1.0 + 0.02 * jax.random.normal(ks[8], (DEPTH, MLA_KV_RANK), f32)
    w_ukv = nrm(ks[9], (DEPTH, MLA_KV_RANK, MLA_HEADS * (MLA_NOPE_DIM + MLA_V_DIM)), MLA_KV_RANK)
    w_branch_a = nrm(ks[10], (DEPTH, SB_WIDTH, D_MODEL), SB_WIDTH)
    w_branch_b = nrm(ks[11], (DEPTH, MLA_WIDTH, D_MODEL), MLA_WIDTH)
    w_out = nrm(ks[12], (DEPTH, D_MODEL, D_MODEL), D_MODEL)
    final_norm_gain = 1.0 + 0.02 * jax.random.normal(ks[13], (D_MODEL,), f32)
    return {"x": x, "c": c, "positions": positions, "w_ada": w_ada, "b_ada": b_ada,
            "norm_gain": norm_gain, "w_in": w_in, "q_norm_gain": q_norm_gain, "w_uq": w_uq,
            "kv_norm_gain": kv_norm_gain, "w_ukv": w_ukv, "w_branch_a": w_branch_a,
            "w_branch_b": w_branch_b, "w_out": w_out, "final_norm_gain": final_norm_gain}


def reference(x, c, positions, w_ada, b_ada, norm_gain, w_in, q_norm_gain, w_uq,
              kv_norm_gain, w_ukv, w_branch_a, w_branch_b, w_out, final_norm_gain):
    b, s, _ = x.shape
    f32 = jnp.float32
    for l in range(DEPTH):
        mod = c @ w_ada[l] + b_ada[l]
        shift, scale, gate = jnp.split(mod, 3, axis=-1)
        h = rmsnorm(x, norm_gain[l]) * (1.0 + scale[:, None, :]) + shift[:, None, :]

        proj = h @ w_in[l]
        (sb_q, sb_k, sb_v, sb_z, c_q, c_kv, k_rot, mla_z, g_a, g_b) = jnp.split(proj, SPLIT_POINTS, axis=-1)

        o_a = stick_breaking_attention(split_heads(sb_q, SB_HEADS).astype(f32),
                                       split_heads(sb_k, SB_HEADS).astype(f32),
                                       split_heads(sb_v, SB_HEADS).astype(f32))
        o_a = merge_heads(o_a).astype(x.dtype)
        y_a = (o_a * jax.nn.silu(sb_z)) @ w_branch_a[l]

        q = (rmsnorm(c_q, q_norm_gain[l]) @ w_uq[l]).reshape(b, s, MLA_HEADS, MLA_QK_DIM)
        q_nope = q[..., :MLA_NOPE_DIM].astype(f32)
        q_pe = rope(q[..., MLA_NOPE_DIM:], positions)
        kv = (rmsnorm(c_kv, kv_norm_gain[l]) @ w_ukv[l]).reshape(b, s, MLA_HEADS, MLA_NOPE_DIM + MLA_V_DIM)
        k_nope = kv[..., :MLA_NOPE_DIM].astype(f32)
        v_b = kv[..., MLA_NOPE_DIM:].astype(f32)
        k_pe = rope(k_rot[:, :, None, :], positions)[:, :, 0, :]
        o_b = latent_attention(q_nope.transpose(0, 2, 1, 3), q_pe.transpose(0, 2, 1, 3),
                               k_nope.transpose(0, 2, 1, 3), k_pe, v_b.transpose(0, 2, 1, 3))
        o_b = merge_heads(o_b).astype(x.dtype)
        y_b = (o_b * jax.nn.silu(mla_z)) @ w_branch_b[l]

        merged = jax.nn.sigmoid(g_a) * y_a + jax.nn.sigmoid(g_b) * y_b
        x = x + gate[:, None, :] * (merged @ w_out[l])
    return rmsnorm(x, final_norm_gain)
```

```python
import math
from contextlib import ExitStack
import numpy as np
import concourse.bass as bass
import concourse.mybir as mybir
from concourse.bass_utils import run_bass_kernel_spmd

F32 = mybir.dt.float32
BF16 = mybir.dt.bfloat16
I32 = mybir.dt.int32
AF = mybir.ActivationFunctionType
ALU = mybir.AluOpType

ENGS = ["pe", "act", "dve", "pool", "sp"]
PH = 2048
NPH = {"pe": 7, "act": 5, "dve": 4, "pool": 1, "sp": 1}
ND = 16

D = 1024
S_LEN = 4096
NQ = 2048
SBQ, SBK, SBV, SBZ, CQ, CKV, KROT, MLAZ, GA, GB = 0, 512, 1024, 1536, 2048, 2432, 2688, 2720, 3232, 4256
EPS = 1e-6
MASKV = -30000.0
SB_FILL = 4
MLA_FILL = 0
DEBUG = False


class Sched:
    def __init__(self, nc, esem, dsem):
        self.nc = nc
        self.esem = esem
        self.dsem = dsem
        self.ops = {e: [] for e in ENGS}
        self.cnt = {e: 0 for e in ENGS}
        self.waited_e = {e: {} for e in ENGS}
        self.waited_d = {e: {} for e in ENGS}
        self.last_w = {}
        self.readers = {}
        self.dma_i = 0
        self.dma_n = {"sp": 0, "pool": 0}

    def _wait(self, E, ev):
        if ev[0] == "e":
            _, e2, n = ev
            if e2 == E and E == "pe":
                return
            if self.waited_e[E].get(e2, 0) >= n:
                return
            self.waited_e[E][e2] = n
            sem = self.esem[e2][(n - 1) // PH]
            val = (n - 1) % PH + 1
        else:
            q, i = ev[1]
            k = i % ND
            val = 16 * (i // ND + 1)
            if self.waited_d[E].get((q, k), 0) >= val:
                return
            self.waited_d[E][(q, k)] = val
            sem = self.dsem[q][k]
        self.ops[E].append(lambda eng, sem=sem, val=val: eng.wait_ge(sem, val))

    def _deps(self, E, reads, writes, extra=(), dma_accum=False):
        deps = list(extra)
        for b in reads:
            for ev in self.last_w.get(b, ()):
                deps.append(ev)
        for b in writes:
            for ev in self.last_w.get(b, ()):
                if dma_accum and ev[0] == "d":
                    continue
                deps.append(ev)
            r = self.readers.get(b)
            if r:
                for e2, n in r[0].items():
                    deps.append(("e", e2, n))
                for i in r[1]:
                    deps.append(("d", i))
        for ev in deps:
            self._wait(E, ev)

    def _record(self, ev, reads, writes, dma_accum=False):
        for b in reads:
            r = self.readers.setdefault(b, ({}, []))
            if ev[0] == "e":
                r[0][ev[1]] = ev[2]
            else:
                r[1].append(ev[1])
        for b in writes:
            r = self.readers.get(b)
            had_readers = bool(r and (r[0] or r[1]))
            if dma_accum and not had_readers:
                self.last_w[b] = [e for e in self.last_w.get(b, ()) if e[0] == "d"] + [ev]
            else:
                self.last_w[b] = [ev]
            self.readers[b] = ({}, [])

    def op(self, E, fn, reads=(), writes=(), extra=()):
        self._deps(E, reads, writes, extra)
        n = self.cnt[E] + 1
        self.cnt[E] = n
        assert (n - 1) // PH < NPH[E], "too many instrs on %s" % E
        sem = self.esem[E][(n - 1) // PH]
        self.ops[E].append(lambda eng, fn=fn, sem=sem: fn(eng).then_inc(sem, 1))
        ev = ("e", E, n)
        self._record(ev, reads, writes)
        return ev

    def dma(self, Q, out, in_, reads=(), writes=(), extra=()):
        i = self.dma_n[Q]
        self.dma_n[Q] += 1
        self.dma_i += 1
        ex = list(extra)
        if i >= ND:
            ex.append(("d", (Q, i - ND)))
        self._deps(Q, reads, writes, ex, dma_accum=True)
        sem = self.dsem[Q][i % ND]
        self.ops[Q].append(
            lambda eng, out=out, in_=in_, sem=sem: eng.dma_start(out=out, in_=in_).then_inc(sem, 16))
        ev = ("d", (Q, i))
        self._record(ev, reads, writes, dma_accum=True)
        return ev

    def barrier(self):
        snap = dict(self.cnt)
        for E in ENGS:
            for e2 in ENGS:
                if e2 != E and snap[e2] > 0:
                    self._wait(E, ("e", e2, snap[e2]))

    def emit(self, block):
        ops = self.ops

        @block.tensor
        def _(eng):
            for f in ops["pe"]:
                f(eng)

        @block.scalar
        def _(eng):
            for f in ops["act"]:
                f(eng)

        @block.vector
        def _(eng):
            for f in ops["dve"]:
                f(eng)

        @block.gpsimd
        def _(eng):
            for f in ops["pool"]:
                f(eng)

        @block.sync
        def _(eng):
            for f in ops["sp"]:
                f(eng)


def build_program(debug=False, stop_after=None):
    nc = bass.Bass("TRN2", target_bir_lowering=False)

    def din(name, shape, dt=F32):
        return nc.dram_tensor(name, list(shape), dt, kind="ExternalInput").ap()

    xf = din("xf", [S_LEN, D])
    xq = din("xq", [NQ, D])
    posall = din("posall", [4, 1536], I32)
    cT_d = din("cT", [128, 8])
    w_ada = din("w_ada", [D, 3 * D])
    b_adaT = din("b_adaT", [128, 24])
    b_gate = din("b_gate", [1, D])
    ngT = din("ngT", [128, 8])
    w_in = din("w_in", [D, 5280])
    qgT = din("qgT", [128, 3])
    w_uq = din("w_uq", [384, 768])
    kvgT = din("kvgT", [128, 2])
    w_ukv = din("w_ukv", [256, 1024])
    w_a = din("w_a", [512, D])
    w_b = din("w_b", [512, D])
    w_out = din("w_out", [D, D])
    fg_row = din("fg_row", [1, D])
    cmats = din("cmats", [128, 7, 128])
    invf_d = din("invf", [128, 1])
    out_d = nc.dram_tensor("out", [NQ, D], F32, kind="ExternalOutput").ap()
    dbg_outs = {}

    with ExitStack() as es:
        esem = {e: [es.enter_context(nc.semaphore("s_%s_%d" % (e, i))) for i in range(NPH[e])] for e in ENGS}
        dsem = {q: [es.enter_context(nc.semaphore("d_%s_%d" % (q, i))) for i in range(ND)] for q in ("sp", "pool")}
        S = Sched(nc, esem, dsem)

        ARENA_KB = 204
        arena = es.enter_context(nc.sbuf_tensor("arena", [128, ARENA_KB * 512], BF16))
        arena32 = arena.bitcast(F32)
        arenai = arena.bitcast(I32)

        def view(off_bytes, shape, dt):
            n = int(np.prod(shape[1:]))
            esz = 2 if dt == BF16 else 4
            assert off_bytes % 4 == 0
            assert off_bytes + n * esz <= ARENA_KB * 1024, (off_bytes, shape)
            base = {BF16: arena, F32: arena32, I32: arenai}[dt]
            o = off_bytes // esz
            ap = base[:, o:o + n]
            if len(shape) == 3:
                ap = ap.rearrange("p (a b) -> p a b", b=shape[2])
            elif len(shape) == 4:
                ap = ap.rearrange("p (a b c) -> p a b c", b=shape[2], c=shape[3])
            return ap

        class Bump:
            def __init__(self, start_kb, end_kb):
                self.o = start_kb * 1024
                self.end = end_kb * 1024

            def __call__(self, shape, dt):
                n = int(np.prod(shape[1:])) * (2 if dt == BF16 else 4)
                n = (n + 3) // 4 * 4
                v = view(self.o, shape, dt)
                self.o += n
                assert self.o <= self.end, ("bump overflow", self.o, self.end)
                return v

        psum_all = es.enter_context(nc.psum_tensor("psall", [128, 4096], F32))
        psum_bf = psum_all.bitcast(BF16)
        bk = [psum_all[:, i * 512:(i + 1) * 512] for i in range(8)]
        bkb = [psum_bf[:, i * 1024:(i + 1) * 1024] for i in range(8)]

        def bkpair(i):
            return psum_all[:, i * 512:(i + 2) * 512].rearrange("p (c n) -> p c n", n=512)

        def MM(out, lhsT, rhs, start, stop, reads=(), writes=()):
            return S.op("pe", lambda e: e.matmul(out, lhsT, rhs, start=start, stop=stop, skip_group_check=True),
                        reads=reads, writes=writes)

        def ACT(out, in_, func, reads=(), writes=(), **kw):
            return S.op("act", lambda e: e.activation(out, in_, func, **kw), reads=reads, writes=writes)

        def TT(eng, out, in0, in1, op, reads=(), writes=()):
            return S.op(eng, lambda e: e.tensor_tensor(out, in0, in1, op), reads=reads, writes=writes)

        def TS(eng, out, in0, s1, s2, op0, op1=None, reads=(), writes=()):
            if op1 is None:
                return S.op(eng, lambda e: e.tensor_scalar(out, in0, s1, None, op0), reads=reads, writes=writes)
            return S.op(eng, lambda e: e.tensor_scalar(out, in0, s1, s2, op0, op1), reads=reads, writes=writes)

        def CP(eng, out, in_, reads=(), writes=()):
            return S.op(eng, lambda e: e.tensor_copy(out, in_), reads=reads, writes=writes)

        def dump(name, ap, shape, reads=()):
            if not debug:
                return
            t = nc.dram_tensor("dbg_" + name, list(shape), F32, kind="ExternalOutput").ap()
            dbg_outs[name] = t
            S.dma("pool", t, ap, reads=reads)

        P = Bump(0, 16)
        cm = P([128, 7, 128], BF16)
        ident = cm[:, 0, :]
        triI = cm[:, 1, :]
        omt = cm[:, 2, :]
        m_sb = [cm[:, 3, :], cm[:, 4, :]]
        m_mla = [cm[:, 5, :], cm[:, 6, :]]
        zeros_bf = P([128, 128], BF16)
        ones_bf = P([128, 128], BF16)
        mod_sb = P([128, 24], F32)
        gs = P([128, 8], F32)
        small = P([128, 64], F32)
        cT = small[:, 0:8]
        badaT = small[:, 8:32]
        ng = small[:, 32:40]
        qg = small[:, 40:43]
        kvg = small[:, 43:45]
        epsc = small[:, 45:46]
        invf = small[:, 46:47]
        tmp8 = small[:, 48:56]
        stat = P([128, 64], F32)
        tabcos = P([128, 1536], F32)
        tabsin = P([128, 1536], F32)
        shift = mod_sb[:, 0:8]

        L = Bump(16, 52)
        ckvn = L([128, 2, S_LEN], BF16)
        kpe = L([128, S_LEN], BF16)
        cqn = L([128, 3, NQ], BF16)
        SBD = Bump(52, 132)
        KT = SBD([128, 4, S_LEN], BF16)
        Vsb = SBD([128, 32, 512], BF16)
        QT = SBD([128, 4, NQ], BF16)
        OA = Bump(132, 164)
        oaT = OA([128, 4, NQ], BF16)
        obT = OA([128, 4, NQ], BF16)

        WK = view(132 * 1024, [128, 8, 1344], BF16)
        w3 = w_in.rearrange("(k p) n -> p k n", p=128)

        def load_WK():
            for k in range(8):
                S.dma("pool", WK[:, k, 0:1024], w3[:, k, SBK:SBK + 1024], writes=["WK"])
                S.dma("pool", WK[:, k, 1024:1312], w3[:, k, CKV:CKV + 288], writes=["WK"])
                S.dma("pool", WK[:, k, 1312:1328], w3[:, k, KROT + 16:KROT + 32], writes=["WK"])
                S.dma("pool", WK[:, k, 1328:1344], w3[:, k, KROT:KROT + 16], writes=["WK"])
            TS("pool", WK[:, :, 1312:1328], WK[:, :, 1312:1328], -1.0, None, ALU.mult, reads=["WK"], writes=["WK"])

        def load_WQ():
            for k in range(8):
                S.dma("pool", WK[:, k, 0:512], w3[:, k, SBQ:SBQ + 512], writes=["WK"])
                S.dma("pool", WK[:, k, 512:896], w3[:, k, CQ:CQ + 384], writes=["WK"])

        A_ = Bump(52, 132)
        wst = [A_([128, D], F32) for _ in range(8)]
        posi = A_([128, 1536], I32)
        ang = A_([128, 1536], F32)
        tt = A_([128, 1536], F32)
        ki = A_([128, 1536], I32)
        kf = A_([128, 1536], F32)
        ff = A_([128, 1536], F32)

        S.dma("pool", cm, cmats, writes=["cm"])
        load_WK()
        S.dma("sp", cT, cT_d, writes=["small"])
        S.dma("sp", badaT, b_adaT, writes=["small"])
        S.dma("sp", ng, ngT, writes=["small"])
        S.dma("sp", qg, qgT, writes=["small"])
        S.dma("sp", kvg, kvgT, writes=["small"])
        S.dma("sp", invf, invf_d, writes=["small"])
        for q in range(4):
            S.dma("sp", posi[q * 32:(q + 1) * 32, :], posall[q:q + 1, :].broadcast_to([32, 1536]), writes=["posi"])
        S.op("dve", lambda e: e.memset(zeros_bf, 0.0), writes=["zeros"])
        S.op("dve", lambda e: e.memset(ones_bf, 1.0), writes=["ones"])
        S.op("dve", lambda e: e.memset(epsc, EPS), reads=["small"], writes=["small"])

        CP("dve", ang, posi, reads=["posi"], writes=["ang"])
        TS("dve", ang, ang, invf, None, ALU.mult, reads=["ang", "small"], writes=["ang"])
        inv2pi = 1.0 / (2.0 * math.pi)
        for (tab, phase, nm) in ((tabsin, 0.0, "sin"), (tabcos, 0.25, "cos")):
            TS("dve", tt, ang, inv2pi, phase, ALU.mult, ALU.add, reads=["ang"], writes=["tt"])
            CP("dve", ki, tt, reads=["tt"], writes=["ki"])
            CP("dve", kf, ki, reads=["ki"], writes=["kf"])
            TT("dve", ff, tt, kf, ALU.subtract, reads=["tt", "kf"], writes=["ff"])
            TS("dve", kf, ff, 0.5, None, ALU.is_gt, reads=["ff"], writes=["kf"])
            TT("dve", ff, ff, kf, ALU.subtract, reads=["ff", "kf"], writes=["ff"])
            TS("dve", kf, ff, -0.5, None, ALU.is_lt, reads=["ff"], writes=["kf"])
            TT("dve", ff, ff, kf, ALU.add, reads=["ff", "kf"], writes=["ff"])
            ACT(tab, ff, AF.Sin, reads=["ff"], writes=["tab" + nm], scale=6.283185)
        for half in range(2):
            for k in range(8):
                S.dma("sp", wst[k], w_ada[k * 128:(k + 1) * 128, half * D:(half + 1) * D], writes=["wst%d" % k])
            for jj in range(8):
                j = half * 8 + jj
                for k in range(8):
                    MM(bk[0][:, j:j + 1], wst[k][:, jj * 128:(jj + 1) * 128], cT[:, k:k + 1], k == 0, k == 7,
                       reads=["wst%d" % k, "small"], writes=["bk0"])
        TT("dve", mod_sb[:, 0:16], bk[0][:, 0:16], badaT[:, 0:16], ALU.add, reads=["bk0", "small"], writes=["mod"])
        TS("dve", tmp8, mod_sb[:, 8:16], 1.0, None, ALU.add, reads=["mod"], writes=["tmp8"])
        TT("dve", gs, tmp8, ng, ALU.mult, reads=["tmp8", "small"], writes=["gs"])

        dump("gs", gs, [128, 8], reads=["gs"])
        dump("mod", mod_sb[:, 0:16], [128, 16], reads=["mod"])
        dump("tabcos", tabcos, [128, 1536], reads=["tabcos"])
        dump("tabsin", tabsin, [128, 1536], reads=["tabsin"])
        S.barrier()

        B_ = Bump(164, 204)
        B2 = Bump(154, 164)
        jit = B2([128, 2, 512], F32)
        sqj = B2([128, D], BF16)
        sq = B2([128, 3, 512], BF16)
        xt = [B_([128, D], F32) for _ in range(2)]
        xn = B_([128, 4, D], BF16)
        hTs = [B_([128, 8, 512], BF16) for _ in range(2)]
        rbc = B_([128, 512], F32)
        t1 = B_([128, 512], F32)
        t2 = B_([128, 512], F32)

        state = {"xt": 0, "st": 0, "bank": 2, "ev": 0, "fill": 0}

        def next_bank():
            b = state["bank"]
            state["bank"] = 2 + (b - 2 + 1) % 6
            return b

        def evac_engine():
            state["ev"] ^= 1
            return "act" if state["ev"] else "dve"

        def norm_stats(src, xts):
            for blk in range(4):
                slot = state["xt"]
                state["xt"] ^= 1
                xk = "xt%d" % slot
                c = state["st"]
                state["st"] = (c + 1) % 8
                sk = "stat%d" % c
                S.dma("sp", xts[slot], src[blk * 128:(blk + 1) * 128, :], writes=[xk])
                ACT(sqj, xts[slot], AF.Square, reads=[xk], writes=["sqj", sk], accum_out=stat[:, c:c + 1])
                ACT(stat[:, 8 + c:9 + c], stat[:, c:c + 1], AF.Ln, reads=[sk, "small"], writes=[sk],
                    scale=1.0 / D, bias=epsc)
                ACT(stat[:, 16 + c:17 + c], stat[:, 8 + c:9 + c], AF.Exp, reads=[sk], writes=[sk], scale=-0.5)
                TS("dve", xn[:, blk, :], xts[slot], stat[:, 16 + c:17 + c], None, ALU.mult,
                   reads=[xk, sk], writes=["xn%d" % blk])

        def norm_T(hT):
            for k in range(8):
                tb = k % 2
                for blk in range(4):
                    S.op("pe", lambda e, k=k, blk=blk, tb=tb, xn_=xn: e.transpose(
                        bkb[tb][:, blk * 128:(blk + 1) * 128], xn_[:, blk, k * 128:(k + 1) * 128], ident),
                        reads=["xn%d" % blk, "cm"], writes=["bk%d" % tb])
                if evac_engine() == "act":
                    ACT(hT[:, k, :], bkb[tb][:, 0:512], AF.Identity, reads=["bk%d" % tb, "gs", "mod"],
                        writes=[("hT", id(hT))], scale=gs[:, k:k + 1], bias=shift[:, k:k + 1])
                else:
                    TS("dve", hT[:, k, :], bkb[tb][:, 0:512], gs[:, k:k + 1], shift[:, k:k + 1], ALU.mult, ALU.add,
                       reads=["bk%d" % tb, "gs", "mod"], writes=[("hT", id(hT))])

        def rms_sq(banks, nch):
            for c in range(nch):
                ACT(sq[:, c, :], bk[banks[c]][:, :], AF.Square, reads=["bk%d" % banks[c]], writes=["sq%d" % c])

        def rms_fin(banks, nch, dim, dst, t0, gkey):
            sb_ = next_bank()
            for c in range(nch):
                MM(bk[sb_][:, :], ones_bf, sq[:, c, :], c == 0, c == nch - 1, reads=["sq%d" % c, "ones"],
                   writes=["bk%d" % sb_])
            ACT(rbc, bk[sb_][:, :], AF.Ln, reads=["bk%d" % sb_, "small"], writes=["rbc"], scale=1.0 / dim, bias=epsc)
            ACT(rbc, rbc, AF.Exp, reads=["rbc"], writes=["rbc"], scale=-0.5)
            for c in range(nch):
                TT("dve", dst[:, c, t0:t0 + 512], bk[banks[c]][:, :], rbc, ALU.mult,
                   reads=["bk%d" % banks[c], "rbc"], writes=[gkey])

        def proj_K(tc):
            t0 = tc * 512
            hT = hTs[tc % 2]
            hk = ("hT", id(hT))
            for cc in range(4):
                b = next_bank()
                for k in range(8):
                    MM(bk[b][:, :], WK[:, k, cc * 128:(cc + 1) * 128], hT[:, k, :], k == 0, k == 7,
                       reads=["WK", hk], writes=["bk%d" % b])
                if evac_engine() == "act":
                    ACT(KT[:, cc, t0:t0 + 512], bk[b][:, :], AF.Copy, reads=["bk%d" % b], writes=["KT"], scale=0.125)
                else:
                    TS("dve", KT[:, cc, t0:t0 + 512], bk[b][:, :], 0.125, None, ALU.mult, reads=["bk%d" % b], writes=["KT"])
            for blk in range(4):
                b = next_bank()
                for k in range(8):
                    MM(bk[b][:, :], hT[:, k, blk * 128:(blk + 1) * 128], WK[:, k, 512:1024], k == 0, k == 7,
                       reads=["WK", hk], writes=["bk%d" % b])
                if evac_engine() == "act":
                    ACT(Vsb[:, tc * 4 + blk, :], bk[b][:, :], AF.Copy, reads=["bk%d" % b], writes=["Vsb"])
                else:
                    CP("dve", Vsb[:, tc * 4 + blk, :], bk[b][:, :], reads=["bk%d" % b], writes=["Vsb"])
            cb = []
            for c2 in range(2):
                b = next_bank()
                cb.append(b)
                for k in range(8):
                    MM(bk[b][:, :], WK[:, k, 1024 + c2 * 128:1024 + (c2 + 1) * 128], hT[:, k, :], k == 0, k == 7,
                       reads=["WK", hk], writes=["bk%d" % b])
            rms_sq(cb, 2)
            bA = next_bank()
            for k in range(8):
                MM(bk[bA][0:96, :], WK[:, k, 1216:1312], hT[:, k, :], k == 0, k == 7, reads=["WK", hk], writes=["bk%d" % bA])
            bB = next_bank()
            for k in range(8):
                MM(bk[bB][0:96, :], WK[:, k, 1248:1344], hT[:, k, :], k == 0, k == 7, reads=["WK", hk], writes=["bk%d" % bB])
            qd, col = tc // 3, (tc % 3) * 512
            S.dma("sp", jit[64:96, 0, :], tabcos[qd * 32:(qd + 1) * 32, col:col + 512], writes=["jit"])
            S.dma("sp", jit[64:96, 1, :], tabsin[qd * 32:(qd + 1) * 32, col:col + 512], writes=["jit"])
            TT("dve", t1[64:96, :], bk[bA][64:96, :], jit[64:96, 0, :], ALU.mult, reads=["bk%d" % bA, "jit"], writes=["t1"])
            TT("dve", t2[64:96, :], bk[bB][64:96, :], jit[64:96, 1, :], ALU.mult, reads=["bk%d" % bB, "jit"], writes=["t2"])
            TT("dve", kpe[64:96, t0:t0 + 512], t1[64:96, :], t2[64:96, :], ALU.add, reads=["t1", "t2"], writes=["kpe"])
            return lambda: rms_fin(cb, 2, 256, ckvn, t0, "ckvn")
        def proj_Q(qc):
            q0 = qc * 512
            hT = hTs[qc % 2]
            hk = ("hT", id(hT))
            for cc in range(4):
                b = next_bank()
                for k in range(8):
                    MM(bk[b][:, :], WK[:, k, cc * 128:(cc + 1) * 128], hT[:, k, :], k == 0, k == 7,
                       reads=["WK", hk], writes=["bk%d" % b])
                if evac_engine() == "act":
                    ACT(QT[:, cc, q0:q0 + 512], bk[b][:, :], AF.Copy, reads=["bk%d" % b], writes=["QT"])
                else:
                    CP("dve", QT[:, cc, q0:q0 + 512], bk[b][:, :], reads=["bk%d" % b], writes=["QT"])
            cb = []
            for c3 in range(3):
                b = next_bank()
                cb.append(b)
                for k in range(8):
                    MM(bk[b][:, :], WK[:, k, 512 + c3 * 128:512 + (c3 + 1) * 128], hT[:, k, :], k == 0, k == 7,
                       reads=["WK", hk], writes=["bk%d" % b])
            rms_sq(cb, 3)
            return lambda: rms_fin(cb, 3, 384, cqn, q0, "cqn")
        chunks = [("K", i) for i in range(8)] + [("Q", i) for i in range(4)]

        def src_of(ch):
            return (xf if ch[0] == "K" else xq)[ch[1] * 512:(ch[1] + 1) * 512, :]

        norm_stats(src_of(chunks[0]), xt)
        norm_T(hTs[0])
        for i, ch in enumerate(chunks):
            if i + 1 < len(chunks):
                norm_stats(src_of(chunks[i + 1]), xt)
            if ch == ("Q", 0):
                load_WQ()
            fin = (proj_K if ch[0] == "K" else proj_Q)(ch[1])
            if i + 1 < len(chunks):
                norm_T(hTs[(i + 1) % 2])
            fin()
        dump("KT", KT.rearrange("p a b -> p (a b)"), [128, 4 * S_LEN], reads=["KT"])
        dump("QT", QT.rearrange("p a b -> p (a b)"), [128, 4 * NQ], reads=["QT"])
        dump("Vsb", Vsb.rearrange("p a b -> p (a b)"), [128, 32 * 512], reads=["Vsb"])
        dump("ckvn", ckvn.rearrange("p a b -> p (a b)"), [128, 2 * S_LEN], reads=["ckvn"])
        dump("cqn", cqn.rearrange("p a b -> p (a b)"), [128, 3 * NQ], reads=["cqn"])
        dump("kpe", kpe[64:96, :], [32, S_LEN], reads=["kpe"])
        S.barrier()

        def chain_cols(g, kb):
            c0 = max(0, kb // 2 - 4 * g)
            masks = []
            for c in range(c0, 4):
                j = 4 * g + c
                if kb == 2 * j + 1:
                    masks.append((c, 0))
                elif kb == 2 * j:
                    masks.append((c, 1))
            return c0, masks

        def run_chains(chains):
            live = list(chains)
            while live:
                nxt = []
                for gen in live:
                    try:
                        next(gen)
                        nxt.append(gen)
                    except StopIteration:
                        pass
                live = nxt

        def run_slots(slots):
            slots = [list(x) for x in slots]
            cur = [sl.pop(0) if sl else None for sl in slots]
            while any(c is not None for c in cur):
                for i in range(len(cur)):
                    while cur[i] is not None:
                        try:
                            next(cur[i])
                            break
                        except StopIteration:
                            cur[i] = slots[i].pop(0) if slots[i] else None

        if stop_after != "B":
            C1 = Bump(164, 204)
            e_sb = C1([128, 3, 2, 512], F32)
            sp_sb = C1([128, 2, 2, 512], BF16)
            g_sb = C1([128, 2, 2, 512], F32)
            w_sb = C1([128, 2, 2, 512], BF16)
            zp, Ap = bkpair(0), bkpair(2)

            def sb_pair(hp, g):
                cc = hp
                units = list(range(8 * g + 7, -1, -1))
                nU = len(units)
                for c in range(2):
                    MM(bk[2 + c][:, :], zeros_bf, QT[:, 0, 0:512], True, True, reads=["zeros"], writes=["A"])
                    MM(bk[4 + c][0:64, :], zeros_bf[:, 0:64], QT[:, 0, 0:512], True, True, reads=["zeros"], writes=["o"])

                def fill(n):
                    for _ in range(n):
                        fb = 6 + (state["fill"] % 2)
                        state["fill"] += 1
                        MM(bk[fb][:, :], ident, QT[:, 0, 0:512], True, True)

                def lo_of(i):
                    return chain_cols(g, units[i])[0] * 128

                def qk(i):
                    kb = units[i]
                    c0, masks = chain_cols(g, kb)
                    lo = c0 * 128
                    for c in range(2):
                        bp = 64 * c
                        MM(bk[c][:, lo:512], KT[bp:bp + 64, cc, kb * 128:(kb + 1) * 128],
                           QT[bp:bp + 64, cc, g * 512 + lo:(g + 1) * 512], True, len(masks) == 0, writes=["z"])
                        for mi, (cb, which) in enumerate(masks):
                            MM(bk[c][:, cb * 128:(cb + 1) * 128], ident, m_sb[which], False, mi == len(masks) - 1, writes=["z"])

                def tri(i):
                    lo = lo_of(i)
                    for c in range(2):
                        MM(bk[2 + c][:, lo:512], triI, sp_sb[:, i % 2, c, lo:512], False, True, reads=["sp%d" % (i % 2)], writes=["A"])

                def omt_(i):
                    lo = lo_of(i)
                    for c in range(2):
                        MM(bk[2 + c][:, lo:512], omt, sp_sb[:, i % 2, c, lo:512], False, True, reads=["sp%d" % (i % 2)], writes=["A"])

                def pv(i):
                    lo = lo_of(i)
                    kb = units[i]
                    for c in range(2):
                        h = 2 * hp + c
                        MM(bk[4 + c][0:64, lo:512], Vsb[:, kb, h * 64:(h + 1) * 64], w_sb[:, i % 2, c, lo:512], False, True,
                           reads=["w%d" % (i % 2)], writes=["o"])

                def act_e(i):
                    lo = lo_of(i)
                    ACT(e_sb[:, i % 3, :, lo:512], zp[:, :, lo:512], AF.Exp, reads=["z"], writes=["e%d" % (i % 3)])

                def act_sp(i):
                    lo = lo_of(i)
                    ACT(sp_sb[:, i % 2, :, lo:512], e_sb[:, i % 3, :, lo:512], AF.Ln, reads=["e%d" % (i % 3)],
                        writes=["sp%d" % (i % 2)], bias=1.0)

                def act_g(i):
                    lo = lo_of(i)
                    ACT(g_sb[:, i % 2, :, lo:512], Ap[:, :, lo:512], AF.Exp, reads=["A"], writes=["g%d" % (i % 2)], scale=-1.0)

                def dve_w(i):
                    lo = lo_of(i)
                    TT("dve", w_sb[:, i % 2, :, lo:512], e_sb[:, i % 3, :, lo:512], g_sb[:, i % 2, :, lo:512], ALU.mult,
                       reads=["e%d" % (i % 3), "g%d" % (i % 2)], writes=["w%d" % (i % 2)])

                qk(0)
                act_e(0)
                act_sp(0)
                if nU > 1:
                    qk(1)
                for i in range(nU):
                    tri(i)
                    fill(SB_FILL)
                    if i + 1 < nU:
                        act_e(i + 1)
                    act_g(i)
                    if i + 1 < nU:
                        act_sp(i + 1)
                    if i >= 1:
                        pv(i - 1)
                    if i + 2 < nU:
                        qk(i + 2)
                    if i + 1 < nU:
                        omt_(i)
                    dve_w(i)
                pv(nU - 1)
                for c in range(2):
                    CP("dve", oaT[64 * c:64 * c + 64, cc, g * 512:(g + 1) * 512], bk[4 + c][0:64, :], reads=["o"], writes=["oaT"])

            for g in range(4):
                for hp in range(4):
                    sb_pair(hp, g)
            dump("oaT", oaT.rearrange("p a b -> p (a b)"), [128, 4 * NQ], reads=["oaT"])
            S.barrier()

        if stop_after not in ("B", "C1"):
            C2 = Bump(52, 132)
            Vm = C2([128, 32, 4, 192], BF16)
            KhT0_ = C2([128, S_LEN], BF16)
            QhT = [C2([128, NQ], BF16) for _ in range(2)]
            wukv = C2([128, 2, 1024], BF16)
            wuqa = C2([128, 3, 8, 128], BF16)
            wvc = C2([128, 2, 512], BF16)
            C2b = Bump(164, 204)
            wst2 = C2b([128, 3, 1024], F32)
            KhT = [KhT0_, view(164 * 1024, [128, S_LEN], BF16)]
            KKEY = ["KhT0", "wst2"]
            jq = C2b([128, NQ], F32)
            p_sb = C2b([128, 2, 2, 512], BF16)
            lnl = C2b([128, 512], F32)
            rinv = C2b([128, 512], F32)
            tq1 = C2b([128, 512], F32)
            tq2 = C2b([128, 512], F32)

            S.dma("sp", wst2[:, 0:2, :], w_ukv.rearrange("(k p) n -> p k n", p=128), writes=["wst2"])
            for k in range(2):
                TS("dve", wukv[:, k, :], wst2[:, k, :], kvg[:, k:k + 1], None, ALU.mult, reads=["wst2", "small"], writes=["wukv"])
            S.dma("sp", wst2[:, :, 0:768], w_uq.rearrange("(k p) n -> p k n", p=128), reads=[], writes=["wst2"])
            wst2q = wst2[:, :, 0:768].rearrange("p k (h c) -> p k h c", c=96)
            for k in range(3):
                TS("dve", wuqa[:, k, :, 0:96], wst2q[:, k, :, :], qg[:, k:k + 1], None, ALU.mult, reads=["wst2", "small"], writes=["wuqa"])
                TS("dve", wuqa[:, k, :, 96:112], wst2q[:, k, :, 80:96], qg[:, k:k + 1], -1.0, ALU.mult, ALU.mult,
                   reads=["wst2", "small"], writes=["wuqa"])
                TS("dve", wuqa[:, k, :, 112:128], wst2q[:, k, :, 64:80], qg[:, k:k + 1], None, ALU.mult,
                   reads=["wst2", "small"], writes=["wuqa"])
            for qc in range(4):
                ci = 8 + qc
                qd, col = ci // 3, (ci % 3) * 512
                S.dma("sp", jq[64:96, qc * 512:(qc + 1) * 512], tabcos[qd * 32:(qd + 1) * 32, col:col + 512], writes=["jq"])
                S.dma("sp", jq[96:128, qc * 512:(qc + 1) * 512], tabsin[qd * 32:(qd + 1) * 32, col:col + 512], writes=["jq"])
            for blk in range(32):
                S.op("pool", lambda e, blk=blk: e.memset(Vm[:, blk, :, 64:128], 1.0), writes=["Vm"])
            wukv4 = wukv.rearrange("p k (h c) -> p k h c", c=128)
            for k in range(2):
                CP("dve", wvc[:, k, :].rearrange("p (h c) -> p h c", c=64), wukv4[:, k, :, 64:128], reads=["wukv"], writes=["wvc"])
            for blk in range(32):
                b = 6 + blk % 2
                for k in range(2):
                    MM(bk[b][:, :], ckvn[:, k, blk * 128:(blk + 1) * 128],
                       wvc[:, k, :], k == 0, k == 1, reads=["wvc"], writes=["bk%d" % b])
                pv = bk[b][:, :].rearrange("p (q t c) -> p q t c", t=2, c=64)
                ACT(Vm[:, blk, :, 0:64], pv[:, :, 0, :], AF.Copy, reads=["bk%d" % b], writes=["Vm"])
                CP("dve", Vm[:, blk, :, 128:192], pv[:, :, 1, :], reads=["bk%d" % b], writes=["Vm"])

            def prep_head(h):
                Kh, Qh = KhT[h % 2], QhT[h % 2]
                kk, qk_ = KKEY[h % 2], "QhT%d" % (h % 2)
                for tc in range(8):
                    b = 6 + tc % 2
                    for k in range(2):
                        MM(bk[b][0:64, :], wukv[:, k, h * 128:h * 128 + 64], ckvn[:, k, tc * 512:(tc + 1) * 512], k == 0, k == 1,
                           reads=["wukv"], writes=["bk%d" % b])
                    CP("dve", Kh[0:64, tc * 512:(tc + 1) * 512], bk[b][0:64, :], reads=["bk%d" % b], writes=[kk])
                    yield
                CP("pool", Kh[64:96, :], kpe[64:96, :], writes=[kk])
                for qc in range(4):
                    b = 6 + qc % 2
                    kb_ = "bk%d" % b
                    cs = slice(qc * 512, (qc + 1) * 512)
                    for k in range(3):
                        MM(bk[b][:, :], wuqa[:, k, h, :], cqn[:, k, cs], k == 0, k == 2, reads=["wuqa"], writes=[kb_])
                    CP("dve", Qh[0:64, cs], bk[b][0:64, :], reads=[kb_], writes=[qk_])
                    TT("dve", tq1[64:96, :], bk[b][64:96, :], jq[64:96, cs], ALU.mult, reads=[kb_, "jq"], writes=["tq1"])
                    TT("dve", tq2[64:96, :], bk[b][96:128, :], jq[96:128, cs], ALU.mult, reads=[kb_, "jq"], writes=["tq2"])
                    TT("dve", Qh[64:96, cs], tq1[64:96, :], tq2[64:96, :], ALU.add, reads=["tq1", "tq2"], writes=[qk_])
                    yield
                    yield

            sm_scale = 1.0 / math.sqrt(96.0)

            def mla_parts(h, g, ci):
                cc, bp = h // 2, (h % 2) * 64
                Kh, Qh = KhT[h % 2], QhT[h % 2]
                kk, qk_ = KKEY[h % 2], "QhT%d" % (h % 2)
                lb = 64 - bp
                vl = Vm[:, :, h // 2, bp:bp + 128]
                ob = 4 + ci % 2
                ko = "bk%d" % ob
                units = list(range(8 * g + 7, -1, -1))
                nP = len(units) // 2

                def qk(j):
                    zb = 2 * (j % 2)
                    kz = "zp%d" % (j % 2)
                    lo = chain_cols(g, units[2 * j])[0] * 128
                    for c in range(2):
                        kb = units[2 * j + c]
                        c0, masks = chain_cols(g, kb)
                        assert c0 * 128 == lo
                        MM(bk[zb + c][:, lo:512], Kh[0:96, kb * 128:(kb + 1) * 128], Qh[0:96, g * 512 + lo:(g + 1) * 512],
                           True, len(masks) == 0, reads=[kk, qk_], writes=[kz])
                        for mi, (cb, which) in enumerate(masks):
                            MM(bk[zb + c][:, cb * 128:(cb + 1) * 128], ident, m_mla[which], False, mi == len(masks) - 1, writes=[kz])
                    ACT(p_sb[:, j % 2, :, lo:512], bkpair(zb)[:, :, lo:512], AF.Exp, reads=[kz], writes=["p%d" % (j % 2)],
                        scale=sm_scale)

                def prologue():
                    MM(bk[ob][:, :], zeros_bf, ckvn[:, 0, 0:512], True, True, reads=["zeros"], writes=[ko])
                    qk(0)
                    if nP > 1:
                        qk(1)

                def rounds(stepper):
                    for j in range(nP):
                        lo = chain_cols(g, units[2 * j])[0] * 128
                        for c in range(2):
                            kb = units[2 * j + c]
                            MM(bk[ob][:, lo:512], vl[:, kb, :], p_sb[:, j % 2, c, lo:512], False, True,
                               reads=["p%d" % (j % 2), "Vm"], writes=[ko])
                        if j + 2 < nP:
                            for fi in range(MLA_FILL):
                                MM(bk[2 * (j % 2) + fi % 2][:, :], ident, ckvn[:, 0, 0:512], True, True, writes=["zp%d" % (j % 2)])
                            qk(j + 2)
                        stepper()

                def tail():
                    ACT(lnl[bp:bp + 64, :], bk[ob][lb:lb + 64, :], AF.Ln, reads=[ko], writes=["lnl"])
                    ACT(rinv[bp:bp + 64, :], lnl[bp:bp + 64, :], AF.Exp, reads=["lnl"], writes=["rinv"], scale=-1.0)
                    TT("dve", obT[bp:bp + 64, cc, g * 512:(g + 1) * 512], bk[ob][bp:bp + 64, :], rinv[bp:bp + 64, :], ALU.mult,
                       reads=[ko, "rinv"], writes=["obT"])

                return prologue, rounds, tail

            for _ in prep_head(0):
                pass
            if debug:
                dump("KhT0", KhT[0][0:96, :], [96, S_LEN], reads=["KhT0"])
                dump("QhT0", QhT[0][0:96, :], [96, NQ], reads=["QhT0"])
            prev_tail = None
            ci = 0
            for h in range(8):
                prep = prep_head(h + 1) if h + 1 < 8 else iter(())

                def stepper(prep=prep):
                    next(prep, None)

                for g in (3, 2, 1, 0):
                    prologue, rounds, tail = mla_parts(h, g, ci)
                    ci += 1
                    prologue()
                    if prev_tail is not None:
                        prev_tail()
                    rounds(stepper)
                    prev_tail = tail
                for _ in prep:
                    pass
            prev_tail()
            dump("obT", obT.rearrange("p a b -> p (a b)"), [128, 4 * NQ], reads=["obT"])
            S.barrier()

        if stop_after is None:
            Dw = Bump(16, 132)
            WG = Dw([128, 8, 3072], BF16)
            WA = Dw([128, 4, D], BF16)
            WB = Dw([128, 4, D], BF16)
            WO = Dw([128, 8, D], BF16)
            gate_bc = Dw([128, D], F32)
            fg_bc = Dw([128, D], F32)
            hTd = Dw([128, 8, 512], BF16)
            mergedT = Dw([128, 8, 512], BF16)
            xnew = Dw([128, D], F32)
            outt = Dw([128, D], F32)
            sqjD = Dw([128, D], BF16)
            Dt = Bump(164, 204)
            xtD = [Dt([128, D], F32) for _ in range(2)]
            xres = [Dt([128, D], F32) for _ in range(2)]
            xnD = Dt([128, 4, D], BF16)
            og_off = Dt.o
            ogA = Dt([128, 4, 512], BF16)
            ogB = Dt([128, 4, 512], BF16)
            xnew2 = view(og_off, [128, D], F32)
            outt2 = view(og_off + 4096, [128, D], F32)
            sg = [Dt([128, 512], F32) for _ in range(2)]
            tm = [Dt([128, 512], F32) for _ in range(2)]
            Di = Bump(164, 180)
            bgate_bc = Di([128, D], F32)
            cbc = Di([128, 8, 128], F32)
            wstD = [Di([128, D], F32) for _ in range(2)]

            for k in range(8):
                S.dma("pool", WG[:, k, 0:512], w3[:, k, SBZ:SBZ + 512], writes=["WGz"])
                S.dma("pool", WG[:, k, 512:1024], w3[:, k, MLAZ:MLAZ + 512], writes=["WGz"])
            for k in range(8):
                S.dma("pool", WG[:, k, 1024:3072], w3[:, k, GA:GA + 2048], writes=["WGg"])
            for k in range(4):
                S.dma("pool", WA[:, k, :], w_a[k * 128:(k + 1) * 128, :], writes=["WA"])
                S.dma("pool", WB[:, k, :], w_b[k * 128:(k + 1) * 128, :], writes=["WB"])
            for k in range(8):
                S.dma("pool", WO[:, k, :], w_out[k * 128:(k + 1) * 128, :], writes=["WO"])

            S.dma("sp", fg_bc, fg_row.broadcast_to([128, D]), writes=["fg_bc"])
            S.dma("sp", bgate_bc, b_gate.broadcast_to([128, D]), writes=["bgate"])
            S.op("dve", lambda e: e.memset(cbc, 1.0), writes=["cbc"])
            for k in range(8):
                TS("dve", cbc[:, k, :], cbc[:, k, :], cT[:, k:k + 1], None, ALU.mult, reads=["cbc", "small"], writes=["cbc"])
            for k in range(8):
                S.dma("sp", wstD[k % 2], w_ada[k * 128:(k + 1) * 128, 2 * D:3 * D], writes=["wstD%d" % (k % 2)])
                for half in range(2):
                    MM(bk[half][:, :], cbc[:, k, :], wstD[k % 2][:, half * 512:(half + 1) * 512], k == 0, k == 7,
                       reads=["cbc", "wstD%d" % (k % 2)], writes=["bk%d" % half])
            for half in range(2):
                TT("dve", gate_bc[:, half * 512:(half + 1) * 512], bk[half][:, :], bgate_bc[:, half * 512:(half + 1) * 512],
                   ALU.add, reads=["bk%d" % half, "bgate"], writes=["gate_bc"])
            S.barrier()
            xn = xnD
            sqj = sqjD
            hkd = ("hT", id(hTd))
            norm_stats(xq[0:512, :], xtD)
            norm_T(hTd)
            for qc in range(4):
                q0 = qc * 512
                if qc + 1 < 4:
                    norm_stats(xq[q0 + 512:q0 + 1024, :], xtD)
                for (off, oT, og, nm) in ((0, oaT, ogA, "ogA"), (512, obT, ogB, "ogB")):
                    for cc in range(4):
                        b = next_bank()
                        for k in range(8):
                            MM(bk[b][:, :], WG[:, k, off + cc * 128:off + (cc + 1) * 128], hTd[:, k, :], k == 0, k == 7,
                               reads=["WGz", hkd], writes=["bk%d" % b])
                        si = cc % 2
                        ACT(sg[si], bk[b][:, :], AF.Sigmoid, reads=["bk%d" % b], writes=["sg%d" % si])
                        TT("dve", tm[si], bk[b][:, :], sg[si], ALU.mult, reads=["bk%d" % b, "sg%d" % si], writes=["tm%d" % si])
                        TT("dve", og[:, cc, :], tm[si], oT[:, cc, q0:q0 + 512], ALU.mult, reads=["tm%d" % si], writes=[nm])
                for n in range(8):
                    bga, bgb, bya, byb = next_bank(), next_bank(), next_bank(), next_bank()
                    for k in range(8):
                        MM(bk[bga][:, :], WG[:, k, 1024 + n * 128:1024 + (n + 1) * 128], hTd[:, k, :], k == 0, k == 7,
                           reads=["WGg", hkd], writes=["bk%d" % bga])
                    for k in range(8):
                        MM(bk[bgb][:, :], WG[:, k, 2048 + n * 128:2048 + (n + 1) * 128], hTd[:, k, :], k == 0, k == 7,
                           reads=["WGg", hkd], writes=["bk%d" % bgb])
                    for k in range(4):
                        MM(bk[bya][:, :], WA[:, k, n * 128:(n + 1) * 128], ogA[:, k, :], k == 0, k == 3,
                           reads=["WA", "ogA"], writes=["bk%d" % bya])
                    for k in range(4):
                        MM(bk[byb][:, :], WB[:, k, n * 128:(n + 1) * 128], ogB[:, k, :], k == 0, k == 3,
                           reads=["WB", "ogB"], writes=["bk%d" % byb])
                    ACT(sg[0], bk[bga][:, :], AF.Sigmoid, reads=["bk%d" % bga], writes=["sg0"])
                    ACT(sg[1], bk[bgb][:, :], AF.Sigmoid, reads=["bk%d" % bgb], writes=["sg1"])
                    TT("dve", tm[0], bk[bya][:, :], sg[0], ALU.mult, reads=["bk%d" % bya, "sg0"], writes=["tm0"])
                    TT("dve", tm[1], bk[byb][:, :], sg[1], ALU.mult, reads=["bk%d" % byb, "sg1"], writes=["tm1"])
                    TT("dve", mergedT[:, n, :], tm[0], tm[1], ALU.add, reads=["tm0", "tm1"], writes=["merged"])
                if qc + 1 < 4:
                    norm_T(hTd)
                for blk in range(4):
                    rs = blk % 2
                    xk = "xres%d" % rs
                    xnw, xnk = (xnew, "xnew") if blk % 2 == 0 else (xnew2, "ogA")
                    ott, otk = (outt, "outt") if blk % 2 == 0 else (outt2, "ogB")
                    S.dma("sp", xres[rs], xq[q0 + blk * 128:q0 + (blk + 1) * 128, :], writes=[xk])
                    for half in range(2):
                        b = next_bank()
                        for k in range(8):
                            MM(bk[b][:, :], mergedT[:, k, blk * 128:(blk + 1) * 128], WO[:, k, half * 512:(half + 1) * 512],
                               k == 0, k == 7, reads=["merged", "WO"], writes=["bk%d" % b])
                        hs = slice(half * 512, (half + 1) * 512)
                        TT("dve", xnw[:, hs], bk[b][:, :], gate_bc[:, hs], ALU.mult, reads=["bk%d" % b, "gate_bc"], writes=[xnk])
                    TT("dve", xnw, xnw, xres[rs], ALU.add, reads=[xnk, xk], writes=[xnk])
                    c = state["st"]
                    state["st"] = (c + 1) % 8
                    sk = "stat%d" % c
                    ACT(sqjD, xnw, AF.Square, reads=[xnk], writes=["sqjD", sk], accum_out=stat[:, c:c + 1])
                    ACT(stat[:, 8 + c:9 + c], stat[:, c:c + 1], AF.Ln, reads=[sk, "small"], writes=[sk], scale=1.0 / D, bias=epsc)
                    ACT(stat[:, 16 + c:17 + c], stat[:, 8 + c:9 + c], AF.Exp, reads=[sk], writes=[sk], scale=-0.5)
                    S.op("dve", lambda e, c=c, ott=ott, xnw=xnw: e.scalar_tensor_tensor(ott, xnw, stat[:, 16 + c:17 + c], fg_bc, ALU.mult, ALU.mult),
                         reads=[xnk, sk, "fg_bc"], writes=[otk])
                    S.dma("pool", out_d[q0 + blk * 128:q0 + (blk + 1) * 128, :], ott, reads=[otk])

        for q in ("sp", "pool"):
            for i in range(max(0, S.dma_n[q] - ND), S.dma_n[q]):
                S._wait("pool", ("d", (q, i)))
        with nc.Block() as block:
            S.emit(block)
    return nc, dbg_outs


def host_inputs(inputs):
    x = np.asarray(inputs["x"], np.float32)
    c = np.asarray(inputs["c"], np.float32)
    pos = np.asarray(inputs["positions"], np.int32)
    f = lambda k: np.ascontiguousarray(np.asarray(inputs[k], np.float32))
    w_ada = f("w_ada")[0]
    b_ada = f("b_ada")[0]
    tri = np.tril(np.ones((128, 128), np.float32))
    ident = np.eye(128, dtype=np.float32)
    omt = 1.0 - tri
    ss, tt = np.meshgrid(np.arange(128), np.arange(128), indexing="ij")
    strict = np.where(ss < tt, 0.0, MASKV).astype(np.float32)
    causal = np.where(ss <= tt, 0.0, MASKV).astype(np.float32)
    allm = np.full((128, 128), MASKV, np.float32)
    nom = np.zeros((128, 128), np.float32)
    inv_freq = (10000.0 ** (-np.arange(0, 32, 2, dtype=np.float32) / np.float32(32))).astype(np.float32)
    invf = np.tile(np.concatenate([inv_freq, inv_freq]), 4).reshape(128, 1).astype(np.float32)
    common = {
        "w_ada": w_ada,
        "b_adaT": np.ascontiguousarray(b_ada.reshape(24, 128).T),
        "b_gate": np.ascontiguousarray(b_ada[2 * D:3 * D].reshape(1, D)),
        "ngT": np.ascontiguousarray(f("norm_gain")[0].reshape(8, 128).T),
        "w_in": f("w_in")[0],
        "qgT": np.ascontiguousarray(f("q_norm_gain")[0].reshape(3, 128).T),
        "w_uq": f("w_uq")[0],
        "kvgT": np.ascontiguousarray(f("kv_norm_gain")[0].reshape(2, 128).T),
        "w_ukv": f("w_ukv")[0],
        "w_a": f("w_branch_a")[0],
        "w_b": f("w_branch_b")[0],
        "w_out": f("w_out")[0],
        "fg_row": np.ascontiguousarray(f("final_norm_gain").reshape(1, D)),
        "invf": invf,
    }
    maps = []
    for core in range(8):
        b, p = core // 2, core % 2
        blocks = [2 * j + p for j in range(16)]
        xb = x[b].reshape(32, 128, D)
        xq = np.ascontiguousarray(xb[blocks].reshape(NQ, D))
        pq = pos[b].reshape(32, 128)[blocks].reshape(NQ)
        posall = np.ascontiguousarray(np.concatenate([pos[b], pq]).reshape(4, 1536).astype(np.int32))
        if p == 0:
            mats = [ident, tri, omt, allm, strict, allm, causal]
        else:
            mats = [ident, tri, omt, strict, nom, causal, nom]
        cm = np.ascontiguousarray(np.stack(mats, axis=1).astype(np.float32))
        m = dict(common)
        m.update({"xf": np.ascontiguousarray(x[b]), "xq": xq, "posall": posall,
                  "cT": np.ascontiguousarray(c[b].reshape(8, 128).T), "cmats": cm})
        maps.append(m)
    return maps


_CACHE = {}


def kernel(**inputs):
    maps = host_inputs(inputs)
    if "nc" not in _CACHE:
        _CACHE["nc"] = build_program()[0]
    nc = _CACHE["nc"]
    res = run_bass_kernel_spmd(nc, maps, core_ids=list(range(8)))
    out = np.zeros((4, 32, 128, D), np.float32)
    for core in range(8):
        b, p = core // 2, core % 2
        o = np.asarray(res.results[core]["out"], np.float32).reshape(16, 128, D)
        out[b, p::2] = o
    return out.reshape(4, S_LEN, D)
```

```python
import math
from contextlib import ExitStack
import numpy as np
import concourse.bass as bass
import concourse.mybir as mybir
from concourse.bass_utils import run_bass_kernel_spmd

F32 = mybir.dt.float32
BF16 = mybir.dt.bfloat16
I32 = mybir.dt.int32
AF = mybir.ActivationFunctionType
ALU = mybir.AluOpType

ENGS = ["pe", "act", "dve", "pool", "sp"]
PH = 2048
NPH = {"pe": 7, "act": 5, "dve": 4, "pool": 1, "sp": 1}
ND = 16

D = 1024
S_LEN = 4096
NQ = 2048
SBQ, SBK, SBV, SBZ, CQ, CKV, KROT, MLAZ, GA, GB = 0, 512, 1024, 1536, 2048, 2432, 2688, 2720, 3232, 4256
EPS = 1e-6
MASKV = -30000.0
SB_FILL = 4
MLA_FILL = 0
DEBUG = False


class Sched:
    def __init__(self, nc, esem, dsem):
        self.nc = nc
        self.esem = esem
        self.dsem = dsem
        self.ops = {e: [] for e in ENGS}
        self.cnt = {e: 0 for e in ENGS}
        self.waited_e = {e: {} for e in ENGS}
        self.waited_d = {e: {} for e in ENGS}
        self.last_w = {}
        self.readers = {}
        self.dma_i = 0
        self.dma_n = {"sp": 0, "pool": 0}

    def _wait(self, E, ev):
        if ev[0] == "e":
            _, e2, n = ev
            if e2 == E and E == "pe":
                return
            if self.waited_e[E].get(e2, 0) >= n:
                return
            self.waited_e[E][e2] = n
            sem = self.esem[e2][(n - 1) // PH]
            val = (n - 1) % PH + 1
        else:
            q, i = ev[1]
            k = i % ND
            val = 16 * (i // ND + 1)
            if self.waited_d[E].get((q, k), 0) >= val:
                return
            self.waited_d[E][(q, k)] = val
            sem = self.dsem[q][k]
        self.ops[E].append(lambda eng, sem=sem, val=val: eng.wait_ge(sem, val))

    def _deps(self, E, reads, writes, extra=(), dma_accum=False):
        deps = list(extra)
        for b in reads:
            for ev in self.last_w.get(b, ()):
                deps.append(ev)
        for b in writes:
            for ev in self.last_w.get(b, ()):
                if dma_accum and ev[0] == "d":
                    continue
                deps.append(ev)
            r = self.readers.get(b)
            if r:
                for e2, n in r[0].items():
                    deps.append(("e", e2, n))
                for i in r[1]:
                    deps.append(("d", i))
        for ev in deps:
            self._wait(E, ev)

    def _record(self, ev, reads, writes, dma_accum=False):
        for b in reads:
            r = self.readers.setdefault(b, ({}, []))
            if ev[0] == "e":
                r[0][ev[1]] = ev[2]
            else:
                r[1].append(ev[1])
        for b in writes:
            r = self.readers.get(b)
            had_readers = bool(r and (r[0] or r[1]))
            if dma_accum and not had_readers:
                self.last_w[b] = [e for e in self.last_w.get(b, ()) if e[0] == "d"] + [ev]
            else:
                self.last_w[b] = [ev]
            self.readers[b] = ({}, [])

    def op(self, E, fn, reads=(), writes=(), extra=()):
        self._deps(E, reads, writes, extra)
        n = self.cnt[E] + 1
        self.cnt[E] = n
        assert (n - 1) // PH < NPH[E], "too many instrs on %s" % E
        sem = self.esem[E][(n - 1) // PH]
        self.ops[E].append(lambda eng, fn=fn, sem=sem: fn(eng).then_inc(sem, 1))
        ev = ("e", E, n)
        self._record(ev, reads, writes)
        return ev

    def dma(self, Q, out, in_, reads=(), writes=(), extra=()):
        i = self.dma_n[Q]
        self.dma_n[Q] += 1
        self.dma_i += 1
        ex = list(extra)
        if i >= ND:
            ex.append(("d", (Q, i - ND)))
        self._deps(Q, reads, writes, ex, dma_accum=True)
        sem = self.dsem[Q][i % ND]
        self.ops[Q].append(
            lambda eng, out=out, in_=in_, sem=sem: eng.dma_start(out=out, in_=in_).then_inc(sem, 16))
        ev = ("d", (Q, i))
        self._record(ev, reads, writes, dma_accum=True)
        return ev

    def barrier(self):
        snap = dict(self.cnt)
        for E in ENGS:
            for e2 in ENGS:
                if e2 != E and snap[e2] > 0:
                    self._wait(E, ("e", e2, snap[e2]))

    def emit(self, block):
        ops = self.ops

        @block.tensor
        def _(eng):
            for f in ops["pe"]:
                f(eng)

        @block.scalar
        def _(eng):
            for f in ops["act"]:
                f(eng)

        @block.vector
        def _(eng):
            for f in ops["dve"]:
                f(eng)

        @block.gpsimd
        def _(eng):
            for f in ops["pool"]:
                f(eng)

        @block.sync
        def _(eng):
            for f in ops["sp"]:
                f(eng)


def build_program(debug=False, stop_after=None):
    nc = bass.Bass("TRN2", target_bir_lowering=False)

    def din(name, shape, dt=F32):
        return nc.dram_tensor(name, list(shape), dt, kind="ExternalInput").ap()

    xf = din("xf", [S_LEN, D])
    xq = din("xq", [NQ, D])
    posall = din("posall", [4, 1536], I32)
    cT_d = din("cT", [128, 8])
    w_ada = din("w_ada", [D, 3 * D])
    b_adaT = din("b_adaT", [128, 24])
    b_gate = din("b_gate", [1, D])
    ngT = din("ngT", [128, 8])
    w_in = din("w_in", [D, 5280])
    qgT = din("qgT", [128, 3])
    w_uq = din("w_uq", [384, 768])
    kvgT = din("kvgT", [128, 2])
    w_ukv = din("w_ukv", [256, 1024])
    w_a = din("w_a", [512, D])
    w_b = din("w_b", [512, D])
    w_out = din("w_out", [D, D])
    fg_row = din("fg_row", [1, D])
    cmats = din("cmats", [128, 7, 128])
    invf_d = din("invf", [128, 1])
    out_d = nc.dram_tensor("out", [NQ, D], F32, kind="ExternalOutput").ap()
    dbg_outs = {}

    with ExitStack() as es:
        esem = {e: [es.enter_context(nc.semaphore("s_%s_%d" % (e, i))) for i in range(NPH[e])] for e in ENGS}
        dsem = {q: [es.enter_context(nc.semaphore("d_%s_%d" % (q, i))) for i in range(ND)] for q in ("sp", "pool")}
        S = Sched(nc, esem, dsem)

        ARENA_KB = 204
        arena = es.enter_context(nc.sbuf_tensor("arena", [128, ARENA_KB * 512], BF16))
        arena32 = arena.bitcast(F32)
        arenai = arena.bitcast(I32)

        def view(off_bytes, shape, dt):
            n = int(np.prod(shape[1:]))
            esz = 2 if dt == BF16 else 4
            assert off_bytes % 4 == 0
            assert off_bytes + n * esz <= ARENA_KB * 1024, (off_bytes, shape)
            base = {BF16: arena, F32: arena32, I32: arenai}[dt]
            o = off_bytes // esz
            ap = base[:, o:o + n]
            if len(shape) == 3:
                ap = ap.rearrange("p (a b) -> p a b", b=shape[2])
            elif len(shape) == 4:
                ap = ap.rearrange("p (a b c) -> p a b c", b=shape[2], c=shape[3])
            return ap

        class Bump:
            def __init__(self, start_kb, end_kb):
                self.o = start_kb * 1024
                self.end = end_kb * 1024

            def __call__(self, shape, dt):
                n = int(np.prod(shape[1:])) * (2 if dt == BF16 else 4)
                n = (n + 3) // 4 * 4
                v = view(self.o, shape, dt)
                self.o += n
                assert self.o <= self.end, ("bump overflow", self.o, self.end)
                return v

        psum_all = es.enter_context(nc.psum_tensor("psall", [128, 4096], F32))
        psum_bf = psum_all.bitcast(BF16)
        bk = [psum_all[:, i * 512:(i + 1) * 512] for i in range(8)]
        bkb = [psum_bf[:, i * 1024:(i + 1) * 1024] for i in range(8)]

        def bkpair(i):
            return psum_all[:, i * 512:(i + 2) * 512].rearrange("p (c n) -> p c n", n=512)

        def MM(out, lhsT, rhs, start, stop, reads=(), writes=()):
            return S.op("pe", lambda e: e.matmul(out, lhsT, rhs, start=start, stop=stop, skip_group_check=True),
                        reads=reads, writes=writes)

        def ACT(out, in_, func, reads=(), writes=(), **kw):
            return S.op("act", lambda e: e.activation(out, in_, func, **kw), reads=reads, writes=writes)

        def TT(eng, out, in0, in1, op, reads=(), writes=()):
            return S.op(eng, lambda e: e.tensor_tensor(out, in0, in1, op), reads=reads, writes=writes)

        def TS(eng, out, in0, s1, s2, op0, op1=None, reads=(), writes=()):
            if op1 is None:
                return S.op(eng, lambda e: e.tensor_scalar(out, in0, s1, None, op0), reads=reads, writes=writes)
            return S.op(eng, lambda e: e.tensor_scalar(out, in0, s1, s2, op0, op1), reads=reads, writes=writes)

        def CP(eng, out, in_, reads=(), writes=()):
            return S.op(eng, lambda e: e.tensor_copy(out, in_), reads=reads, writes=writes)

        def dump(name, ap, shape, reads=()):
            if not debug:
                return
            t = nc.dram_tensor("dbg_" + name, list(shape), F32, kind="ExternalOutput").ap()
            dbg_outs[name] = t
            S.dma("pool", t, ap, reads=reads)

        P = Bump(0, 16)
        cm = P([128, 7, 128], BF16)
        ident = cm[:, 0, :]
        triI = cm[:, 1, :]
        omt = cm[:, 2, :]
        m_sb = [cm[:, 3, :], cm[:, 4, :]]
        m_mla = [cm[:, 5, :], cm[:, 6, :]]
        zeros_bf = P([128, 128], BF16)
        ones_bf = P([128, 128], BF16)
        mod_sb = P([128, 24], F32)
        gs = P([128, 8], F32)
        small = P([128, 64], F32)
        cT = small[:, 0:8]
        badaT = small[:, 8:32]
        ng = small[:, 32:40]
        qg = small[:, 40:43]
        kvg = small[:, 43:45]
        epsc = small[:, 45:46]
        invf = small[:, 46:47]
        tmp8 = small[:, 48:56]
        stat = P([128, 64], F32)
        tabcos = P([128, 1536], F32)
        tabsin = P([128, 1536], F32)
        shift = mod_sb[:, 0:8]

        L = Bump(16, 52)
        ckvn = L([128, 2, S_LEN], BF16)
        kpe = L([128, S_LEN], BF16)
        cqn = L([128, 3, NQ], BF16)
        SBD = Bump(52, 132)
        KT = SBD([128, 4, S_LEN], BF16)
        Vsb = SBD([128, 32, 512], BF16)
        QT = SBD([128, 4, NQ], BF16)
        OA = Bump(132, 164)
        oaT = OA([128, 4, NQ], BF16)
        obT = OA([128, 4, NQ], BF16)

        WK = view(132 * 1024, [128, 8, 1344], BF16)
        w3 = w_in.rearrange("(k p) n -> p k n", p=128)

        def load_WK():
            for k in range(8):
                S.dma("pool", WK[:, k, 0:1024], w3[:, k, SBK:SBK + 1024], writes=["WK"])
                S.dma("pool", WK[:, k, 1024:1312], w3[:, k, CKV:CKV + 288], writes=["WK"])
                S.dma("pool", WK[:, k, 1312:1328], w3[:, k, KROT + 16:KROT + 32], writes=["WK"])
                S.dma("pool", WK[:, k, 1328:1344], w3[:, k, KROT:KROT + 16], writes=["WK"])
            TS("pool", WK[:, :, 1312:1328], WK[:, :, 1312:1328], -1.0, None, ALU.mult, reads=["WK"], writes=["WK"])

        def load_WQ():
            for k in range(8):
                S.dma("pool", WK[:, k, 0:512], w3[:, k, SBQ:SBQ + 512], writes=["WK"])
                S.dma("pool", WK[:, k, 512:896], w3[:, k, CQ:CQ + 384], writes=["WK"])

        A_ = Bump(52, 132)
        wst = [A_([128, D], F32) for _ in range(8)]
        posi = A_([128, 1536], I32)
        ang = A_([128, 1536], F32)
        tt = A_([128, 1536], F32)
        ki = A_([128, 1536], I32)
        kf = A_([128, 1536], F32)
        ff = A_([128, 1536], F32)

        S.dma("pool", cm, cmats, writes=["cm"])
        load_WK()
        S.dma("sp", cT, cT_d, writes=["small"])
        S.dma("sp", badaT, b_adaT, writes=["small"])
        S.dma("sp", ng, ngT, writes=["small"])
        S.dma("sp", qg, qgT, writes=["small"])
        S.dma("sp", kvg, kvgT, writes=["small"])
        S.dma("sp", invf, invf_d, writes=["small"])
        for q in range(4):
            S.dma("sp", posi[q * 32:(q + 1) * 32, :], posall[q:q + 1, :].broadcast_to([32, 1536]), writes=["posi"])
        S.op("dve", lambda e: e.memset(zeros_bf, 0.0), writes=["zeros"])
        S.op("dve", lambda e: e.memset(ones_bf, 1.0), writes=["ones"])
        S.op("dve", lambda e: e.memset(epsc, EPS), reads=["small"], writes=["small"])

        CP("dve", ang, posi, reads=["posi"], writes=["ang"])
        TS("dve", ang, ang, invf, None, ALU.mult, reads=["ang", "small"], writes=["ang"])
        inv2pi = 1.0 / (2.0 * math.pi)
        for (tab, phase, nm) in ((tabsin, 0.0, "sin"), (tabcos, 0.25, "cos")):
            TS("dve", tt, ang, inv2pi, phase, ALU.mult, ALU.add, reads=["ang"], writes=["tt"])
            CP("dve", ki, tt, reads=["tt"], writes=["ki"])
            CP("dve", kf, ki, reads=["ki"], writes=["kf"])
            TT("dve", ff, tt, kf, ALU.subtract, reads=["tt", "kf"], writes=["ff"])
            TS("dve", kf, ff, 0.5, None, ALU.is_gt, reads=["ff"], writes=["kf"])
            TT("dve", ff, ff, kf, ALU.subtract, reads=["ff", "kf"], writes=["ff"])
            TS("dve", kf, ff, -0.5, None, ALU.is_lt, reads=["ff"], writes=["kf"])
            TT("dve", ff, ff, kf, ALU.add, reads=["ff", "kf"], writes=["ff"])
            ACT(tab, ff, AF.Sin, reads=["ff"], writes=["tab" + nm], scale=6.283185)
        for half in range(2):
            for k in range(8):
                S.dma("sp", wst[k], w_ada[k * 128:(k + 1) * 128, half * D:(half + 1) * D], writes=["wst%d" % k])
            for jj in range(8):
                j = half * 8 + jj
                for k in range(8):
                    MM(bk[0][:, j:j + 1], wst[k][:, jj * 128:(jj + 1) * 128], cT[:, k:k + 1], k == 0, k == 7,
                       reads=["wst%d" % k, "small"], writes=["bk0"])
        TT("dve", mod_sb[:, 0:16], bk[0][:, 0:16], badaT[:, 0:16], ALU.add, reads=["bk0", "small"], writes=["mod"])
        TS("dve", tmp8, mod_sb[:, 8:16], 1.0, None, ALU.add, reads=["mod"], writes=["tmp8"])
        TT("dve", gs, tmp8, ng, ALU.mult, reads=["tmp8", "small"], writes=["gs"])

        dump("gs", gs, [128, 8], reads=["gs"])
        dump("mod", mod_sb[:, 0:16], [128, 16], reads=["mod"])
        dump("tabcos", tabcos, [128, 1536], reads=["tabcos"])
        dump("tabsin", tabsin, [128, 1536], reads=["tabsin"])
        S.barrier()

        B_ = Bump(164, 204)
        B2 = Bump(154, 164)
        jit = B2([128, 2, 512], F32)
        sqj = B2([128, D], BF16)
        sq = B2([128, 3, 512], BF16)
        xt = [B_([128, D], F32) for _ in range(2)]
        xn = B_([128, 4, D], BF16)
        hTs = [B_([128, 8, 512], BF16) for _ in range(2)]
        rbc = B_([128, 512], F32)
        t1 = B_([128, 512], F32)
        t2 = B_([128, 512], F32)

        state = {"xt": 0, "st": 0, "bank": 2, "ev": 0, "fill": 0, "sbu": 0}

        def next_bank():
            b = state["bank"]
            state["bank"] = 2 + (b - 2 + 1) % 6
            return b

        def evac_engine():
            state["ev"] ^= 1
            return "act" if state["ev"] else "dve"

        def norm_stats(src, xts):
            for blk in range(4):
                slot = state["xt"]
                state["xt"] ^= 1
                xk = "xt%d" % slot
                c = state["st"]
                state["st"] = (c + 1) % 8
                sk = "stat%d" % c
                S.dma("sp", xts[slot], src[blk * 128:(blk + 1) * 128, :], writes=[xk])
                ACT(sqj, xts[slot], AF.Square, reads=[xk], writes=["sqj", sk], accum_out=stat[:, c:c + 1])
                ACT(stat[:, 8 + c:9 + c], stat[:, c:c + 1], AF.Ln, reads=[sk, "small"], writes=[sk],
                    scale=1.0 / D, bias=epsc)
                ACT(stat[:, 16 + c:17 + c], stat[:, 8 + c:9 + c], AF.Exp, reads=[sk], writes=[sk], scale=-0.5)
                TS("dve", xn[:, blk, :], xts[slot], stat[:, 16 + c:17 + c], None, ALU.mult,
                   reads=[xk, sk], writes=["xn%d" % blk])

        def norm_T(hT):
            for k in range(8):
                tb = k % 2
                for blk in range(4):
                    S.op("pe", lambda e, k=k, blk=blk, tb=tb, xn_=xn: e.transpose(
                        bkb[tb][:, blk * 128:(blk + 1) * 128], xn_[:, blk, k * 128:(k + 1) * 128], ident),
                        reads=["xn%d" % blk, "cm"], writes=["bk%d" % tb])
                if evac_engine() == "act":
                    ACT(hT[:, k, :], bkb[tb][:, 0:512], AF.Identity, reads=["bk%d" % tb, "gs", "mod"],
                        writes=[("hT", id(hT))], scale=gs[:, k:k + 1], bias=shift[:, k:k + 1])
                else:
                    TS("dve", hT[:, k, :], bkb[tb][:, 0:512], gs[:, k:k + 1], shift[:, k:k + 1], ALU.mult, ALU.add,
                       reads=["bk%d" % tb, "gs", "mod"], writes=[("hT", id(hT))])

        def rms_sq(banks, nch):
            for c in range(nch):
                ACT(sq[:, c, :], bk[banks[c]][:, :], AF.Square, reads=["bk%d" % banks[c]], writes=["sq%d" % c])

        def rms_fin(banks, nch, dim, dst, t0, gkey):
            sb_ = next_bank()
            for c in range(nch):
                MM(bk[sb_][:, :], ones_bf, sq[:, c, :], c == 0, c == nch - 1, reads=["sq%d" % c, "ones"],
                   writes=["bk%d" % sb_])
            ACT(rbc, bk[sb_][:, :], AF.Ln, reads=["bk%d" % sb_, "small"], writes=["rbc"], scale=1.0 / dim, bias=epsc)
            ACT(rbc, rbc, AF.Exp, reads=["rbc"], writes=["rbc"], scale=-0.5)
            for c in range(nch):
                TT("dve", dst[:, c, t0:t0 + 512], bk[banks[c]][:, :], rbc, ALU.mult,
                   reads=["bk%d" % banks[c], "rbc"], writes=[gkey])

        def proj_K(tc):
            t0 = tc * 512
            hT = hTs[tc % 2]
            hk = ("hT", id(hT))
            for cc in range(4):
                b = next_bank()
                for k in range(8):
                    MM(bk[b][:, :], WK[:, k, cc * 128:(cc + 1) * 128], hT[:, k, :], k == 0, k == 7,
                       reads=["WK", hk], writes=["bk%d" % b])
                if evac_engine() == "act":
                    ACT(KT[:, cc, t0:t0 + 512], bk[b][:, :], AF.Copy, reads=["bk%d" % b], writes=["KT"], scale=0.125)
                else:
                    TS("dve", KT[:, cc, t0:t0 + 512], bk[b][:, :], 0.125, None, ALU.mult, reads=["bk%d" % b], writes=["KT"])
            for blk in range(4):
                b = next_bank()
                for k in range(8):
                    MM(bk[b][:, :], hT[:, k, blk * 128:(blk + 1) * 128], WK[:, k, 512:1024], k == 0, k == 7,
                       reads=["WK", hk], writes=["bk%d" % b])
                if evac_engine() == "act":
                    ACT(Vsb[:, tc * 4 + blk, :], bk[b][:, :], AF.Copy, reads=["bk%d" % b], writes=["Vsb"])
                else:
                    CP("dve", Vsb[:, tc * 4 + blk, :], bk[b][:, :], reads=["bk%d" % b], writes=["Vsb"])
            cb = []
            for c2 in range(2):
                b = next_bank()
                cb.append(b)
                for k in range(8):
                    MM(bk[b][:, :], WK[:, k, 1024 + c2 * 128:1024 + (c2 + 1) * 128], hT[:, k, :], k == 0, k == 7,
                       reads=["WK", hk], writes=["bk%d" % b])
            rms_sq(cb, 2)
            bA = next_bank()
            for k in range(8):
                MM(bk[bA][0:96, :], WK[:, k, 1216:1312], hT[:, k, :], k == 0, k == 7, reads=["WK", hk], writes=["bk%d" % bA])
            bB = next_bank()
            for k in range(8):
                MM(bk[bB][0:96, :], WK[:, k, 1248:1344], hT[:, k, :], k == 0, k == 7, reads=["WK", hk], writes=["bk%d" % bB])
            qd, col = tc // 3, (tc % 3) * 512
            S.dma("sp", jit[64:96, 0, :], tabcos[qd * 32:(qd + 1) * 32, col:col + 512], writes=["jit"])
            S.dma("sp", jit[64:96, 1, :], tabsin[qd * 32:(qd + 1) * 32, col:col + 512], writes=["jit"])
            TT("dve", t1[64:96, :], bk[bA][64:96, :], jit[64:96, 0, :], ALU.mult, reads=["bk%d" % bA, "jit"], writes=["t1"])
            TT("dve", t2[64:96, :], bk[bB][64:96, :], jit[64:96, 1, :], ALU.mult, reads=["bk%d" % bB, "jit"], writes=["t2"])
            TT("dve", kpe[64:96, t0:t0 + 512], t1[64:96, :], t2[64:96, :], ALU.add, reads=["t1", "t2"], writes=["kpe"])
            return lambda: rms_fin(cb, 2, 256, ckvn, t0, "ckvn")
        def proj_Q(qc):
            q0 = qc * 512
            hT = hTs[qc % 2]
            hk = ("hT", id(hT))
            for cc in range(4):
                b = next_bank()
                for k in range(8):
                    MM(bk[b][:, :], WK[:, k, cc * 128:(cc + 1) * 128], hT[:, k, :], k == 0, k == 7,
                       reads=["WK", hk], writes=["bk%d" % b])
                if evac_engine() == "act":
                    ACT(QT[:, cc, q0:q0 + 512], bk[b][:, :], AF.Copy, reads=["bk%d" % b], writes=["QT"])
                else:
                    CP("dve", QT[:, cc, q0:q0 + 512], bk[b][:, :], reads=["bk%d" % b], writes=["QT"])
            cb = []
            for c3 in range(3):
                b = next_bank()
                cb.append(b)
                for k in range(8):
                    MM(bk[b][:, :], WK[:, k, 512 + c3 * 128:512 + (c3 + 1) * 128], hT[:, k, :], k == 0, k == 7,
                       reads=["WK", hk], writes=["bk%d" % b])
            rms_sq(cb, 3)
            return lambda: rms_fin(cb, 3, 384, cqn, q0, "cqn")
        chunks = [("K", i) for i in range(8)] + [("Q", i) for i in range(4)]

        def src_of(ch):
            return (xf if ch[0] == "K" else xq)[ch[1] * 512:(ch[1] + 1) * 512, :]

        norm_stats(src_of(chunks[0]), xt)
        norm_T(hTs[0])
        for i, ch in enumerate(chunks):
            if i + 1 < len(chunks):
                norm_stats(src_of(chunks[i + 1]), xt)
            if ch == ("Q", 0):
                load_WQ()
            fin = (proj_K if ch[0] == "K" else proj_Q)(ch[1])
            if i + 1 < len(chunks):
                norm_T(hTs[(i + 1) % 2])
            fin()
        dump("KT", KT.rearrange("p a b -> p (a b)"), [128, 4 * S_LEN], reads=["KT"])
        dump("QT", QT.rearrange("p a b -> p (a b)"), [128, 4 * NQ], reads=["QT"])
        dump("Vsb", Vsb.rearrange("p a b -> p (a b)"), [128, 32 * 512], reads=["Vsb"])
        dump("ckvn", ckvn.rearrange("p a b -> p (a b)"), [128, 2 * S_LEN], reads=["ckvn"])
        dump("cqn", cqn.rearrange("p a b -> p (a b)"), [128, 3 * NQ], reads=["cqn"])
        dump("kpe", kpe[64:96, :], [32, S_LEN], reads=["kpe"])
        S.barrier()

        def chain_cols(g, kb):
            c0 = max(0, kb // 2 - 4 * g)
            masks = []
            for c in range(c0, 4):
                j = 4 * g + c
                if kb == 2 * j + 1:
                    masks.append((c, 0))
                elif kb == 2 * j:
                    masks.append((c, 1))
            return c0, masks

        def run_chains(chains):
            live = list(chains)
            while live:
                nxt = []
                for gen in live:
                    try:
                        next(gen)
                        nxt.append(gen)
                    except StopIteration:
                        pass
                live = nxt

        def run_slots(slots):
            slots = [list(x) for x in slots]
            cur = [sl.pop(0) if sl else None for sl in slots]
            while any(c is not None for c in cur):
                for i in range(len(cur)):
                    while cur[i] is not None:
                        try:
                            next(cur[i])
                            break
                        except StopIteration:
                            cur[i] = slots[i].pop(0) if slots[i] else None

        if stop_after != "B":
            C1 = Bump(164, 204)
            e_sb = C1([128, 3, 2, 512], F32)
            sp_sb = C1([128, 2, 2, 512], BF16)
            g_sb = C1([128, 2, 2, 512], F32)
            w_sb = C1([128, 2, 2, 512], BF16)
            zp, Ap = bkpair(0), bkpair(2)

            def sb_pair(hp, g):
                cc = hp
                units = list(range(8 * g + 7, -1, -1))
                nU = len(units)
                ub = state["sbu"]
                state["sbu"] += nU
                def fill(n):
                    for _ in range(n):
                        fb = 6 + (state["fill"] % 2)
                        state["fill"] += 1
                        MM(bk[fb][:, :], ident, QT[:, 0, 0:512], True, True)

                def lo_of(i):
                    return chain_cols(g, units[i])[0] * 128

                def qk(i):
                    kb = units[i]
                    c0, masks = chain_cols(g, kb)
                    lo = c0 * 128
                    for c in range(2):
                        bp = 64 * c
                        MM(bk[c][:, lo:512], KT[bp:bp + 64, cc, kb * 128:(kb + 1) * 128],
                           QT[bp:bp + 64, cc, g * 512 + lo:(g + 1) * 512], True, len(masks) == 0, writes=["z"])
                        for mi, (cb, which) in enumerate(masks):
                            MM(bk[c][:, cb * 128:(cb + 1) * 128], ident, m_sb[which], False, mi == len(masks) - 1, writes=["z"])

                def tri(i):
                    lo = lo_of(i)
                    for c in range(2):
                        MM(bk[2 + c][:, lo:512], triI, sp_sb[:, (ub + i) % 2, c, lo:512], False, True, reads=["sp%d" % ((ub + i) % 2)], writes=["A"])

                def omt_(i):
                    lo = lo_of(i)
                    for c in range(2):
                        MM(bk[2 + c][:, lo:512], omt, sp_sb[:, (ub + i) % 2, c, lo:512], False, True, reads=["sp%d" % ((ub + i) % 2)], writes=["A"])

                def pv(i):
                    lo = lo_of(i)
                    kb = units[i]
                    for c in range(2):
                        h = 2 * hp + c
                        MM(bk[4 + c][0:64, lo:512], Vsb[:, kb, h * 64:(h + 1) * 64], w_sb[:, (ub + i) % 2, c, lo:512], False, True,
                           reads=["w%d" % ((ub + i) % 2)], writes=["o"])

                def act_e(i):
                    lo = lo_of(i)
                    ACT(e_sb[:, (ub + i) % 3, :, lo:512], zp[:, :, lo:512], AF.Exp, reads=["z"], writes=["e%d" % ((ub + i) % 3)])

                def act_sp(i):
                    lo = lo_of(i)
                    ACT(sp_sb[:, (ub + i) % 2, :, lo:512], e_sb[:, (ub + i) % 3, :, lo:512], AF.Ln, reads=["e%d" % ((ub + i) % 3)],
                        writes=["sp%d" % ((ub + i) % 2)], bias=1.0)

                def act_g(i):
                    lo = lo_of(i)
                    ACT(g_sb[:, (ub + i) % 2, :, lo:512], Ap[:, :, lo:512], AF.Exp, reads=["A"], writes=["g%d" % ((ub + i) % 2)], scale=-1.0)

                def dve_w(i):
                    lo = lo_of(i)
                    TT("dve", w_sb[:, (ub + i) % 2, :, lo:512], e_sb[:, (ub + i) % 3, :, lo:512], g_sb[:, (ub + i) % 2, :, lo:512], ALU.mult,
                       reads=["e%d" % ((ub + i) % 3), "g%d" % ((ub + i) % 2)], writes=["w%d" % ((ub + i) % 2)])

                def prologue():
                    qk(0)
                    act_e(0)
                    act_sp(0)
                    if nU > 1:
                        qk(1)

                def main(next_prologue):
                    for c in range(2):
                        MM(bk[2 + c][:, :], zeros_bf, QT[:, 0, 0:512], True, True, reads=["zeros"], writes=["A"])
                        MM(bk[4 + c][0:64, :], zeros_bf[:, 0:64], QT[:, 0, 0:512], True, True, reads=["zeros"], writes=["o"])
                    for i in range(nU):
                        if i == nU - 1 and next_prologue is not None:
                            next_prologue()
                        tri(i)
                        fill(SB_FILL)
                        if i + 1 < nU:
                            act_e(i + 1)
                        act_g(i)
                        if i + 1 < nU:
                            act_sp(i + 1)
                        if i >= 1:
                            pv(i - 1)
                        if i + 2 < nU:
                            qk(i + 2)
                        if i + 1 < nU:
                            omt_(i)
                        dve_w(i)
                    pv(nU - 1)
                    for c in range(2):
                        CP("dve", oaT[64 * c:64 * c + 64, cc, g * 512:(g + 1) * 512], bk[4 + c][0:64, :], reads=["o"], writes=["oaT"])

                return prologue, main

            sb_chains = [sb_pair(hp, g) for g in range(4) for hp in range(4)]
            sb_chains[0][0]()
            for ci_, (pro, main) in enumerate(sb_chains):
                main(sb_chains[ci_ + 1][0] if ci_ + 1 < len(sb_chains) else None)
            dump("oaT", oaT.rearrange("p a b -> p (a b)"), [128, 4 * NQ], reads=["oaT"])
            S.barrier()

        if stop_after not in ("B", "C1"):
            C2 = Bump(52, 132)
            Vm = C2([128, 32, 4, 192], BF16)
            KhT0_ = C2([128, S_LEN], BF16)
            QhT = [C2([128, NQ], BF16) for _ in range(2)]
            wukv = C2([128, 2, 1024], BF16)
            wuqa = C2([128, 3, 8, 128], BF16)
            wvc = C2([128, 2, 512], BF16)
            C2b = Bump(164, 204)
            wst2 = C2b([128, 3, 1024], F32)
            KhT = [KhT0_, view(164 * 1024, [128, S_LEN], BF16)]
            KKEY = ["KhT0", "wst2"]
            jq = C2b([128, NQ], F32)
            p_sb = C2b([128, 2, 2, 512], BF16)
            lnl = C2b([128, 512], F32)
            rinv = C2b([128, 512], F32)
            tq1 = C2b([128, 512], F32)
            tq2 = C2b([128, 512], F32)

            S.dma("sp", wst2[:, 0:2, :], w_ukv.rearrange("(k p) n -> p k n", p=128), writes=["wst2"])
            for k in range(2):
                TS("dve", wukv[:, k, :], wst2[:, k, :], kvg[:, k:k + 1], None, ALU.mult, reads=["wst2", "small"], writes=["wukv"])
            S.dma("sp", wst2[:, :, 0:768], w_uq.rearrange("(k p) n -> p k n", p=128), reads=[], writes=["wst2"])
            wst2q = wst2[:, :, 0:768].rearrange("p k (h c) -> p k h c", c=96)
            for k in range(3):
                TS("dve", wuqa[:, k, :, 0:96], wst2q[:, k, :, :], qg[:, k:k + 1], None, ALU.mult, reads=["wst2", "small"], writes=["wuqa"])
                TS("dve", wuqa[:, k, :, 96:112], wst2q[:, k, :, 80:96], qg[:, k:k + 1], -1.0, ALU.mult, ALU.mult,
                   reads=["wst2", "small"], writes=["wuqa"])
                TS("dve", wuqa[:, k, :, 112:128], wst2q[:, k, :, 64:80], qg[:, k:k + 1], None, ALU.mult,
                   reads=["wst2", "small"], writes=["wuqa"])
            for qc in range(4):
                ci = 8 + qc
                qd, col = ci // 3, (ci % 3) * 512
                S.dma("sp", jq[64:96, qc * 512:(qc + 1) * 512], tabcos[qd * 32:(qd + 1) * 32, col:col + 512], writes=["jq"])
                S.dma("sp", jq[96:128, qc * 512:(qc + 1) * 512], tabsin[qd * 32:(qd + 1) * 32, col:col + 512], writes=["jq"])
            for blk in range(32):
                S.op("pool", lambda e, blk=blk: e.memset(Vm[:, blk, :, 64:128], 1.0), writes=["Vm"])
            wukv4 = wukv.rearrange("p k (h c) -> p k h c", c=128)
            for k in range(2):
                CP("dve", wvc[:, k, :].rearrange("p (h c) -> p h c", c=64), wukv4[:, k, :, 64:128], reads=["wukv"], writes=["wvc"])
            for blk in range(32):
                b = 6 + blk % 2
                for k in range(2):
                    MM(bk[b][:, :], ckvn[:, k, blk * 128:(blk + 1) * 128],
                       wvc[:, k, :], k == 0, k == 1, reads=["wvc"], writes=["bk%d" % b])
                pv = bk[b][:, :].rearrange("p (q t c) -> p q t c", t=2, c=64)
                ACT(Vm[:, blk, :, 0:64], pv[:, :, 0, :], AF.Copy, reads=["bk%d" % b], writes=["Vm"])
                CP("dve", Vm[:, blk, :, 128:192], pv[:, :, 1, :], reads=["bk%d" % b], writes=["Vm"])

            def prep_head(h):
                Kh, Qh = KhT[h % 2], QhT[h % 2]
                kk, qk_ = KKEY[h % 2], "QhT%d" % (h % 2)
                for tc in range(8):
                    b = 6 + tc % 2
                    for k in range(2):
                        MM(bk[b][0:64, :], wukv[:, k, h * 128:h * 128 + 64], ckvn[:, k, tc * 512:(tc + 1) * 512], k == 0, k == 1,
                           reads=["wukv"], writes=["bk%d" % b])
                    CP("dve", Kh[0:64, tc * 512:(tc + 1) * 512], bk[b][0:64, :], reads=["bk%d" % b], writes=[kk])
                    yield
                CP("pool", Kh[64:96, :], kpe[64:96, :], writes=[kk])
                for qc in range(4):
                    b = 6 + qc % 2
                    kb_ = "bk%d" % b
                    cs = slice(qc * 512, (qc + 1) * 512)
                    for k in range(3):
                        MM(bk[b][:, :], wuqa[:, k, h, :], cqn[:, k, cs], k == 0, k == 2, reads=["wuqa"], writes=[kb_])
                    CP("dve", Qh[0:64, cs], bk[b][0:64, :], reads=[kb_], writes=[qk_])
                    TT("dve", tq1[64:96, :], bk[b][64:96, :], jq[64:96, cs], ALU.mult, reads=[kb_, "jq"], writes=["tq1"])
                    TT("dve", tq2[64:96, :], bk[b][96:128, :], jq[96:128, cs], ALU.mult, reads=[kb_, "jq"], writes=["tq2"])
                    TT("dve", Qh[64:96, cs], tq1[64:96, :], tq2[64:96, :], ALU.add, reads=["tq1", "tq2"], writes=[qk_])
                    yield
                    yield

            sm_scale = 1.0 / math.sqrt(96.0)

            def mla_parts(h, g, ci):
                cc, bp = h // 2, (h % 2) * 64
                Kh, Qh = KhT[h % 2], QhT[h % 2]
                kk, qk_ = KKEY[h % 2], "QhT%d" % (h % 2)
                lb = 64 - bp
                vl = Vm[:, :, h // 2, bp:bp + 128]
                ob = 4 + ci % 2
                ko = "bk%d" % ob
                units = list(range(8 * g + 7, -1, -1))
                nP = len(units) // 2

                def qk(j):
                    zb = 2 * (j % 2)
                    kz = "zp%d" % (j % 2)
                    lo = chain_cols(g, units[2 * j])[0] * 128
                    for c in range(2):
                        kb = units[2 * j + c]
                        c0, masks = chain_cols(g, kb)
                        assert c0 * 128 == lo
                        MM(bk[zb + c][:, lo:512], Kh[0:96, kb * 128:(kb + 1) * 128], Qh[0:96, g * 512 + lo:(g + 1) * 512],
                           True, len(masks) == 0, reads=[kk, qk_], writes=[kz])
                        for mi, (cb, which) in enumerate(masks):
                            MM(bk[zb + c][:, cb * 128:(cb + 1) * 128], ident, m_mla[which], False, mi == len(masks) - 1, writes=[kz])
                    ACT(p_sb[:, j % 2, :, lo:512], bkpair(zb)[:, :, lo:512], AF.Exp, reads=[kz], writes=["p%d" % (j % 2)],
                        scale=sm_scale)

                def prologue():
                    MM(bk[ob][:, :], zeros_bf, ckvn[:, 0, 0:512], True, True, reads=["zeros"], writes=[ko])
                    qk(0)
                    if nP > 1:
                        qk(1)

                def rounds(stepper, next_prologue=None):
                    for j in range(nP):
                        lo = chain_cols(g, units[2 * j])[0] * 128
                        for c in range(2):
                            kb = units[2 * j + c]
                            MM(bk[ob][:, lo:512], vl[:, kb, :], p_sb[:, j % 2, c, lo:512], False, True,
                               reads=["p%d" % (j % 2), "Vm"], writes=[ko])
                        if j == nP - 1 and next_prologue is not None:
                            next_prologue()
                        if j + 2 < nP:
                            for fi in range(MLA_FILL):
                                MM(bk[2 * (j % 2) + fi % 2][:, :], ident, ckvn[:, 0, 0:512], True, True, writes=["zp%d" % (j % 2)])
                            qk(j + 2)
                        stepper()

                def tail():
                    ACT(lnl[bp:bp + 64, :], bk[ob][lb:lb + 64, :], AF.Ln, reads=[ko], writes=["lnl"])
                    ACT(rinv[bp:bp + 64, :], lnl[bp:bp + 64, :], AF.Exp, reads=["lnl"], writes=["rinv"], scale=-1.0)
                    TT("dve", obT[bp:bp + 64, cc, g * 512:(g + 1) * 512], bk[ob][bp:bp + 64, :], rinv[bp:bp + 64, :], ALU.mult,
                       reads=[ko, "rinv"], writes=["obT"])

                return prologue, rounds, tail

            for _ in prep_head(0):
                pass
            if debug:
                dump("KhT0", KhT[0][0:96, :], [96, S_LEN], reads=["KhT0"])
                dump("QhT0", QhT[0][0:96, :], [96, NQ], reads=["QhT0"])
            mparts = [(h, g, mla_parts(h, g, 4 * h + gi)) for h in range(8) for gi, g in enumerate((3, 2, 1, 0))]
            mparts[0][2][0]()
            prev_tail = None
            prep = None
            for idx, (h, g, (prologue, rounds, tail)) in enumerate(mparts):
                if g == 3:
                    prep = prep_head(h + 1) if h + 1 < 8 else iter(())

                def stepper(prep=prep):
                    next(prep, None)

                nxt = None
                if idx + 1 < len(mparts):
                    nh = mparts[idx + 1][0]
                    npro = mparts[idx + 1][2][0]

                    def nxt(nh=nh, h=h, npro=npro, prep=prep):
                        if nh != h:
                            for _ in prep:
                                pass
                        npro()
                if prev_tail is not None:
                    prev_tail()
                rounds(stepper, nxt)
                prev_tail = tail
            prev_tail()
            dump("obT", obT.rearrange("p a b -> p (a b)"), [128, 4 * NQ], reads=["obT"])
            S.barrier()

        if stop_after is None:
            Dw = Bump(16, 132)
            WG = Dw([128, 8, 3072], BF16)
            WA = Dw([128, 4, D], BF16)
            WB = Dw([128, 4, D], BF16)
            WO = Dw([128, 8, D], BF16)
            gate_bc = Dw([128, D], F32)
            fg_bc = Dw([128, D], F32)
            hTd = Dw([128, 8, 512], BF16)
            mergedT = Dw([128, 8, 512], BF16)
            xnew = Dw([128, D], F32)
            outt = Dw([128, D], F32)
            sqjD = Dw([128, D], BF16)
            Dt = Bump(164, 204)
            xtD = [Dt([128, D], F32) for _ in range(2)]
            xres = [Dt([128, D], F32) for _ in range(2)]
            xnD = Dt([128, 4, D], BF16)
            og_off = Dt.o
            ogA = Dt([128, 4, 512], BF16)
            ogB = Dt([128, 4, 512], BF16)
            xnew2 = view(og_off, [128, D], F32)
            outt2 = view(og_off + 4096, [128, D], F32)
            sg = [Dt([128, 512], F32) for _ in range(2)]
            tm = [Dt([128, 512], F32) for _ in range(2)]
            Di = Bump(164, 180)
            bgate_bc = Di([128, D], F32)
            cbc = Di([128, 8, 128], F32)
            wstD = [Di([128, D], F32) for _ in range(2)]

            for k in range(8):
                S.dma("pool", WG[:, k, 0:512], w3[:, k, SBZ:SBZ + 512], writes=["WGz"])
                S.dma("pool", WG[:, k, 512:1024], w3[:, k, MLAZ:MLAZ + 512], writes=["WGz"])
            for k in range(8):
                S.dma("pool", WG[:, k, 1024:3072], w3[:, k, GA:GA + 2048], writes=["WGg"])
            for k in range(4):
                S.dma("pool", WA[:, k, :], w_a[k * 128:(k + 1) * 128, :], writes=["WA"])
                S.dma("pool", WB[:, k, :], w_b[k * 128:(k + 1) * 128, :], writes=["WB"])
            for k in range(8):
                S.dma("pool", WO[:, k, :], w_out[k * 128:(k + 1) * 128, :], writes=["WO"])

            S.dma("sp", fg_bc, fg_row.broadcast_to([128, D]), writes=["fg_bc"])
            S.dma("sp", bgate_bc, b_gate.broadcast_to([128, D]), writes=["bgate"])
            S.op("dve", lambda e: e.memset(cbc, 1.0), writes=["cbc"])
            for k in range(8):
                TS("dve", cbc[:, k, :], cbc[:, k, :], cT[:, k:k + 1], None, ALU.mult, reads=["cbc", "small"], writes=["cbc"])
            for k in range(8):
                S.dma("sp", wstD[k % 2], w_ada[k * 128:(k + 1) * 128, 2 * D:3 * D], writes=["wstD%d" % (k % 2)])
                for half in range(2):
                    MM(bk[half][:, :], cbc[:, k, :], wstD[k % 2][:, half * 512:(half + 1) * 512], k == 0, k == 7,
                       reads=["cbc", "wstD%d" % (k % 2)], writes=["bk%d" % half])
            for half in range(2):
                TT("dve", gate_bc[:, half * 512:(half + 1) * 512], bk[half][:, :], bgate_bc[:, half * 512:(half + 1) * 512],
                   ALU.add, reads=["bk%d" % half, "bgate"], writes=["gate_bc"])
            S.barrier()
            xn = xnD
            sqj = sqjD
            hkd = ("hT", id(hTd))
            norm_stats(xq[0:512, :], xtD)
            norm_T(hTd)
            for qc in range(4):
                q0 = qc * 512
                if qc + 1 < 4:
                    norm_stats(xq[q0 + 512:q0 + 1024, :], xtD)
                for (off, oT, og, nm) in ((0, oaT, ogA, "ogA"), (512, obT, ogB, "ogB")):
                    for cc in range(4):
                        b = next_bank()
                        for k in range(8):
                            MM(bk[b][:, :], WG[:, k, off + cc * 128:off + (cc + 1) * 128], hTd[:, k, :], k == 0, k == 7,
                               reads=["WGz", hkd], writes=["bk%d" % b])
                        si = cc % 2
                        ACT(sg[si], bk[b][:, :], AF.Sigmoid, reads=["bk%d" % b], writes=["sg%d" % si])
                        TT("dve", tm[si], bk[b][:, :], sg[si], ALU.mult, reads=["bk%d" % b, "sg%d" % si], writes=["tm%d" % si])
                        TT("dve", og[:, cc, :], tm[si], oT[:, cc, q0:q0 + 512], ALU.mult, reads=["tm%d" % si], writes=[nm])
                for n in range(8):
                    bga, bgb, bya, byb = next_bank(), next_bank(), next_bank(), next_bank()
                    for k in range(8):
                        MM(bk[bga][:, :], WG[:, k, 1024 + n * 128:1024 + (n + 1) * 128], hTd[:, k, :], k == 0, k == 7,
                           reads=["WGg", hkd], writes=["bk%d" % bga])
                    for k in range(8):
                        MM(bk[bgb][:, :], WG[:, k, 2048 + n * 128:2048 + (n + 1) * 128], hTd[:, k, :], k == 0, k == 7,
                           reads=["WGg", hkd], writes=["bk%d" % bgb])
                    for k in range(4):
                        MM(bk[bya][:, :], WA[:, k, n * 128:(n + 1) * 128], ogA[:, k, :], k == 0, k == 3,
                           reads=["WA", "ogA"], writes=["bk%d" % bya])
                    for k in range(4):
                        MM(bk[byb][:, :], WB[:, k, n * 128:(n + 1) * 128], ogB[:, k, :], k == 0, k == 3,
                           reads=["WB", "ogB"], writes=["bk%d" % byb])
                    ACT(sg[0], bk[bga][:, :], AF.Sigmoid, reads=["bk%d" % bga], writes=["sg0"])
                    ACT(sg[1], bk[bgb][:, :], AF.Sigmoid, reads=["bk%d" % bgb], writes=["sg1"])
                    TT("dve", tm[0], bk[bya][:, :], sg[0], ALU.mult, reads=["bk%d" % bya, "sg0"], writes=["tm0"])
                    TT("dve", tm[1], bk[byb][:, :], sg[1], ALU.mult, reads=["bk%d" % byb, "sg1"], writes=["tm1"])
                    TT("dve", mergedT[:, n, :], tm[0], tm[1], ALU.add, reads=["tm0", "tm1"], writes=["merged"])
                if qc + 1 < 4:
                    norm_T(hTd)
                for blk in range(4):
                    rs = blk % 2
                    xk = "xres%d" % rs
                    xnw, xnk = (xnew, "xnew") if blk % 2 == 0 else (xnew2, "ogA")
                    ott, otk = (outt, "outt") if blk % 2 == 0 else (outt2, "ogB")
                    S.dma("sp", xres[rs], xq[q0 + blk * 128:q0 + (blk + 1) * 128, :], writes=[xk])
                    for half in range(2):
                        b = next_bank()
                        for k in range(8):
                            MM(bk[b][:, :], mergedT[:, k, blk * 128:(blk + 1) * 128], WO[:, k, half * 512:(half + 1) * 512],
                               k == 0, k == 7, reads=["merged", "WO"], writes=["bk%d" % b])
                        hs = slice(half * 512, (half + 1) * 512)
                        TT("dve", xnw[:, hs], bk[b][:, :], gate_bc[:, hs], ALU.mult, reads=["bk%d" % b, "gate_bc"], writes=[xnk])
                    TT("dve", xnw, xnw, xres[rs], ALU.add, reads=[xnk, xk], writes=[xnk])
                    c = state["st"]
                    state["st"] = (c + 1) % 8
                    sk = "stat%d" % c
                    ACT(sqjD, xnw, AF.Square, reads=[xnk], writes=["sqjD", sk], accum_out=stat[:, c:c + 1])
                    ACT(stat[:, 8 + c:9 + c], stat[:, c:c + 1], AF.Ln, reads=[sk, "small"], writes=[sk], scale=1.0 / D, bias=epsc)
                    ACT(stat[:, 16 + c:17 + c], stat[:, 8 + c:9 + c], AF.Exp, reads=[sk], writes=[sk], scale=-0.5)
                    S.op("dve", lambda e, c=c, ott=ott, xnw=xnw: e.scalar_tensor_tensor(ott, xnw, stat[:, 16 + c:17 + c], fg_bc, ALU.mult, ALU.mult),
                         reads=[xnk, sk, "fg_bc"], writes=[otk])
                    S.dma("pool", out_d[q0 + blk * 128:q0 + (blk + 1) * 128, :], ott, reads=[otk])

        for q in ("sp", "pool"):
            for i in range(max(0, S.dma_n[q] - ND), S.dma_n[q]):
                S._wait("pool", ("d", (q, i)))
        with nc.Block() as block:
            S.emit(block)
    return nc, dbg_outs


def host_inputs(inputs):
    x = np.asarray(inputs["x"], np.float32)
    c = np.asarray(inputs["c"], np.float32)
    pos = np.asarray(inputs["positions"], np.int32)
    f = lambda k: np.ascontiguousarray(np.asarray(inputs[k], np.float32))
    w_ada = f("w_ada")[0]
    b_ada = f("b_ada")[0]
    tri = np.tril(np.ones((128, 128), np.float32))
    ident = np.eye(128, dtype=np.float32)
    omt = 1.0 - tri
    ss, tt = np.meshgrid(np.arange(128), np.arange(128), indexing="ij")
    strict = np.where(ss < tt, 0.0, MASKV).astype(np.float32)
    causal = np.where(ss <= tt, 0.0, MASKV).astype(np.float32)
    allm = np.full((128, 128), MASKV, np.float32)
    nom = np.zeros((128, 128), np.float32)
    inv_freq = (10000.0 ** (-np.arange(0, 32, 2, dtype=np.float32) / np.float32(32))).astype(np.float32)
    invf = np.tile(np.concatenate([inv_freq, inv_freq]), 4).reshape(128, 1).astype(np.float32)
    common = {
        "w_ada": w_ada,
        "b_adaT": np.ascontiguousarray(b_ada.reshape(24, 128).T),
        "b_gate": np.ascontiguousarray(b_ada[2 * D:3 * D].reshape(1, D)),
        "ngT": np.ascontiguousarray(f("norm_gain")[0].reshape(8, 128).T),
        "w_in": f("w_in")[0],
        "qgT": np.ascontiguousarray(f("q_norm_gain")[0].reshape(3, 128).T),
        "w_uq": f("w_uq")[0],
        "kvgT": np.ascontiguousarray(f("kv_norm_gain")[0].reshape(2, 128).T),
        "w_ukv": f("w_ukv")[0],
        "w_a": f("w_branch_a")[0],
        "w_b": f("w_branch_b")[0],
        "w_out": f("w_out")[0],
        "fg_row": np.ascontiguousarray(f("final_norm_gain").reshape(1, D)),
        "invf": invf,
    }
    maps = []
    for core in range(8):
        b, p = core // 2, core % 2
        blocks = [2 * j + p for j in range(16)]
        xb = x[b].reshape(32, 128, D)
        xq = np.ascontiguousarray(xb[blocks].reshape(NQ, D))
        pq = pos[b].reshape(32, 128)[blocks].reshape(NQ)
        posall = np.ascontiguousarray(np.concatenate([pos[b], pq]).reshape(4, 1536).astype(np.int32))
        if p == 0:
            mats = [ident, tri, omt, allm, strict, allm, causal]
        else:
            mats = [ident, tri, omt, strict, nom, causal, nom]
        cm = np.ascontiguousarray(np.stack(mats, axis=1).astype(np.float32))
        m = dict(common)
        m.update({"xf": np.ascontiguousarray(x[b]), "xq": xq, "posall": posall,
                  "cT": np.ascontiguousarray(c[b].reshape(8, 128).T), "cmats": cm})
        maps.append(m)
    return maps


_CACHE = {}


def kernel(**inputs):
    maps = host_inputs(inputs)
    if "nc" not in _CACHE:
        _CACHE["nc"] = build_program()[0]
    nc = _CACHE["nc"]
    res = run_bass_kernel_spmd(nc, maps, core_ids=list(range(8)))
    out = np.zeros((4, 32, 128, D), np.float32)
    for core in range(8):
        b, p = core // 2, core % 2
        o = np.asarray(res.results[core]["out"], np.float32).reshape(16, 128, D)
        out[b, p::2] = o
    return out.reshape(4, S_LEN, D)
```

```python
import math
from contextlib import ExitStack
import numpy as np
import concourse.bass as bass
import concourse.mybir as mybir
from concourse.bass_utils import run_bass_kernel_spmd

F32 = mybir.dt.float32
BF16 = mybir.dt.bfloat16
I32 = mybir.dt.int32
AF = mybir.ActivationFunctionType
ALU = mybir.AluOpType

ENGS = ["pe", "act", "dve", "pool", "sp"]
PH = 2048
NPH = {"pe": 7, "act": 5, "dve": 4, "pool": 1, "sp": 1}
ND = 16

D = 1024
S_LEN = 4096
NQ = 2048
SBQ, SBK, SBV, SBZ, CQ, CKV, KROT, MLAZ, GA, GB = 0, 512, 1024, 1536, 2048, 2432, 2688, 2720, 3232, 4256
EPS = 1e-6
MASKV = -30000.0
SB_FILL = 4
MLA_FILL = 0
DEBUG = False


class Sched:
    def __init__(self, nc, esem, dsem):
        self.nc = nc
        self.esem = esem
        self.dsem = dsem
        self.ops = {e: [] for e in ENGS}
        self.cnt = {e: 0 for e in ENGS}
        self.waited_e = {e: {} for e in ENGS}
        self.waited_d = {e: {} for e in ENGS}
        self.last_w = {}
        self.readers = {}
        self.dma_i = 0
        self.dma_n = {"sp": 0, "pool": 0}

    def _wait(self, E, ev):
        if ev[0] == "e":
            _, e2, n = ev
            if e2 == E and E == "pe":
                return
            if self.waited_e[E].get(e2, 0) >= n:
                return
            self.waited_e[E][e2] = n
            sem = self.esem[e2][(n - 1) // PH]
            val = (n - 1) % PH + 1
        else:
            q, i = ev[1]
            k = i % ND
            val = 16 * (i // ND + 1)
            if self.waited_d[E].get((q, k), 0) >= val:
                return
            self.waited_d[E][(q, k)] = val
            sem = self.dsem[q][k]
        self.ops[E].append(lambda eng, sem=sem, val=val: eng.wait_ge(sem, val))

    def _deps(self, E, reads, writes, extra=(), dma_accum=False):
        deps = list(extra)
        for b in reads:
            for ev in self.last_w.get(b, ()):
                deps.append(ev)
        for b in writes:
            for ev in self.last_w.get(b, ()):
                if dma_accum and ev[0] == "d":
                    continue
                deps.append(ev)
            r = self.readers.get(b)
            if r:
                for e2, n in r[0].items():
                    deps.append(("e", e2, n))
                for i in r[1]:
                    deps.append(("d", i))
        for ev in deps:
            self._wait(E, ev)

    def _record(self, ev, reads, writes, dma_accum=False):
        for b in reads:
            r = self.readers.setdefault(b, ({}, []))
            if ev[0] == "e":
                r[0][ev[1]] = ev[2]
            else:
                r[1].append(ev[1])
        for b in writes:
            r = self.readers.get(b)
            had_readers = bool(r and (r[0] or r[1]))
            if dma_accum and not had_readers:
                self.last_w[b] = [e for e in self.last_w.get(b, ()) if e[0] == "d"] + [ev]
            else:
                self.last_w[b] = [ev]
            self.readers[b] = ({}, [])

    def op(self, E, fn, reads=(), writes=(), extra=()):
        self._deps(E, reads, writes, extra)
        n = self.cnt[E] + 1
        self.cnt[E] = n
        assert (n - 1) // PH < NPH[E], "too many instrs on %s" % E
        sem = self.esem[E][(n - 1) // PH]
        self.ops[E].append(lambda eng, fn=fn, sem=sem: fn(eng).then_inc(sem, 1))
        ev = ("e", E, n)
        self._record(ev, reads, writes)
        return ev

    def dma(self, Q, out, in_, reads=(), writes=(), extra=()):
        i = self.dma_n[Q]
        self.dma_n[Q] += 1
        self.dma_i += 1
        ex = list(extra)
        if i >= ND:
            ex.append(("d", (Q, i - ND)))
        self._deps(Q, reads, writes, ex, dma_accum=True)
        sem = self.dsem[Q][i % ND]
        self.ops[Q].append(
            lambda eng, out=out, in_=in_, sem=sem: eng.dma_start(out=out, in_=in_).then_inc(sem, 16))
        ev = ("d", (Q, i))
        self._record(ev, reads, writes, dma_accum=True)
        return ev

    def barrier(self):
        snap = dict(self.cnt)
        for E in ENGS:
            for e2 in ENGS:
                if e2 != E and snap[e2] > 0:
                    self._wait(E, ("e", e2, snap[e2]))

    def emit(self, block):
        ops = self.ops

        @block.tensor
        def _(eng):
            for f in ops["pe"]:
                f(eng)

        @block.scalar
        def _(eng):
            for f in ops["act"]:
                f(eng)

        @block.vector
        def _(eng):
            for f in ops["dve"]:
                f(eng)

        @block.gpsimd
        def _(eng):
            for f in ops["pool"]:
                f(eng)

        @block.sync
        def _(eng):
            for f in ops["sp"]:
                f(eng)


def build_program(debug=False, stop_after=None):
    nc = bass.Bass("TRN2", target_bir_lowering=False)

    def din(name, shape, dt=F32):
        return nc.dram_tensor(name, list(shape), dt, kind="ExternalInput").ap()

    xf = din("xf", [S_LEN, D])
    xq = din("xq", [NQ, D])
    posall = din("posall", [4, 1536], I32)
    cT_d = din("cT", [128, 8])
    w_ada = din("w_ada", [D, 3 * D])
    b_adaT = din("b_adaT", [128, 24])
    b_gate = din("b_gate", [1, D])
    ngT = din("ngT", [128, 8])
    w_in = din("w_in", [D, 5280])
    qgT = din("qgT", [128, 3])
    w_uq = din("w_uq", [384, 768])
    kvgT = din("kvgT", [128, 2])
    w_ukv = din("w_ukv", [256, 1024])
    w_a = din("w_a", [512, D])
    w_b = din("w_b", [512, D])
    w_out = din("w_out", [D, D])
    fg_row = din("fg_row", [1, D])
    cmats = din("cmats", [128, 7, 128])
    invf_d = din("invf", [128, 1])
    out_d = nc.dram_tensor("out", [NQ, D], F32, kind="ExternalOutput").ap()
    dbg_outs = {}

    with ExitStack() as es:
        esem = {e: [es.enter_context(nc.semaphore("s_%s_%d" % (e, i))) for i in range(NPH[e])] for e in ENGS}
        dsem = {q: [es.enter_context(nc.semaphore("d_%s_%d" % (q, i))) for i in range(ND)] for q in ("sp", "pool")}
        S = Sched(nc, esem, dsem)

        ARENA_KB = 204
        arena = es.enter_context(nc.sbuf_tensor("arena", [128, ARENA_KB * 512], BF16))
        arena32 = arena.bitcast(F32)
        arenai = arena.bitcast(I32)

        def view(off_bytes, shape, dt):
            n = int(np.prod(shape[1:]))
            esz = 2 if dt == BF16 else 4
            assert off_bytes % 4 == 0
            assert off_bytes + n * esz <= ARENA_KB * 1024, (off_bytes, shape)
            base = {BF16: arena, F32: arena32, I32: arenai}[dt]
            o = off_bytes // esz
            ap = base[:, o:o + n]
            if len(shape) == 3:
                ap = ap.rearrange("p (a b) -> p a b", b=shape[2])
            elif len(shape) == 4:
                ap = ap.rearrange("p (a b c) -> p a b c", b=shape[2], c=shape[3])
            return ap

        class Bump:
            def __init__(self, start_kb, end_kb):
                self.o = start_kb * 1024
                self.end = end_kb * 1024

            def __call__(self, shape, dt):
                n = int(np.prod(shape[1:])) * (2 if dt == BF16 else 4)
                n = (n + 3) // 4 * 4
                v = view(self.o, shape, dt)
                self.o += n
                assert self.o <= self.end, ("bump overflow", self.o, self.end)
                return v

        psum_all = es.enter_context(nc.psum_tensor("psall", [128, 4096], F32))
        psum_bf = psum_all.bitcast(BF16)
        bk = [psum_all[:, i * 512:(i + 1) * 512] for i in range(8)]
        bkb = [psum_bf[:, i * 1024:(i + 1) * 1024] for i in range(8)]

        def bkpair(i):
            return psum_all[:, i * 512:(i + 2) * 512].rearrange("p (c n) -> p c n", n=512)

        def MM(out, lhsT, rhs, start, stop, reads=(), writes=()):
            return S.op("pe", lambda e: e.matmul(out, lhsT, rhs, start=start, stop=stop, skip_group_check=True),
                        reads=reads, writes=writes)

        def ACT(out, in_, func, reads=(), writes=(), **kw):
            return S.op("act", lambda e: e.activation(out, in_, func, **kw), reads=reads, writes=writes)

        def TT(eng, out, in0, in1, op, reads=(), writes=()):
            return S.op(eng, lambda e: e.tensor_tensor(out, in0, in1, op), reads=reads, writes=writes)

        def TS(eng, out, in0, s1, s2, op0, op1=None, reads=(), writes=()):
            if op1 is None:
                return S.op(eng, lambda e: e.tensor_scalar(out, in0, s1, None, op0), reads=reads, writes=writes)
            return S.op(eng, lambda e: e.tensor_scalar(out, in0, s1, s2, op0, op1), reads=reads, writes=writes)

        def CP(eng, out, in_, reads=(), writes=()):
            return S.op(eng, lambda e: e.tensor_copy(out, in_), reads=reads, writes=writes)

        def dump(name, ap, shape, reads=()):
            if not debug:
                return
            t = nc.dram_tensor("dbg_" + name, list(shape), F32, kind="ExternalOutput").ap()
            dbg_outs[name] = t
            S.dma("pool", t, ap, reads=reads)

        P = Bump(0, 16)
        cm = P([128, 7, 128], BF16)
        ident = cm[:, 0, :]
        triI = cm[:, 1, :]
        omt = cm[:, 2, :]
        m_sb = [cm[:, 3, :], cm[:, 4, :]]
        m_mla = [cm[:, 5, :], cm[:, 6, :]]
        zeros_bf = P([128, 128], BF16)
        ones_bf = P([128, 128], BF16)
        mod_sb = P([128, 24], F32)
        gs = P([128, 8], F32)
        small = P([128, 64], F32)
        cT = small[:, 0:8]
        badaT = small[:, 8:32]
        ng = small[:, 32:40]
        qg = small[:, 40:43]
        kvg = small[:, 43:45]
        epsc = small[:, 45:46]
        invf = small[:, 46:47]
        tmp8 = small[:, 48:56]
        stat = P([128, 64], F32)
        tabcos = P([128, 1536], F32)
        tabsin = P([128, 1536], F32)
        shift = mod_sb[:, 0:8]

        L = Bump(16, 52)
        ckvn = L([128, 2, S_LEN], BF16)
        kpe = L([128, S_LEN], BF16)
        cqn = L([128, 3, NQ], BF16)
        SBD = Bump(52, 132)
        KT = SBD([128, 4, S_LEN], BF16)
        Vsb = SBD([128, 32, 512], BF16)
        QT = SBD([128, 4, NQ], BF16)
        OA = Bump(132, 164)
        oaT = OA([128, 4, NQ], BF16)
        obT = OA([128, 4, NQ], BF16)

        WK = view(132 * 1024, [128, 8, 1344], BF16)
        w3 = w_in.rearrange("(k p) n -> p k n", p=128)

        def load_WK():
            for k in range(8):
                S.dma("pool", WK[:, k, 0:1024], w3[:, k, SBK:SBK + 1024], writes=["WK"])
                S.dma("pool", WK[:, k, 1024:1312], w3[:, k, CKV:CKV + 288], writes=["WK"])
                S.dma("pool", WK[:, k, 1312:1328], w3[:, k, KROT + 16:KROT + 32], writes=["WK"])
                S.dma("pool", WK[:, k, 1328:1344], w3[:, k, KROT:KROT + 16], writes=["WK"])
            TS("pool", WK[:, :, 1312:1328], WK[:, :, 1312:1328], -1.0, None, ALU.mult, reads=["WK"], writes=["WK"])

        def load_WQ():
            for k in range(8):
                S.dma("pool", WK[:, k, 0:512], w3[:, k, SBQ:SBQ + 512], writes=["WK"])
                S.dma("pool", WK[:, k, 512:896], w3[:, k, CQ:CQ + 384], writes=["WK"])

        A_ = Bump(52, 132)
        wst = [A_([128, D], F32) for _ in range(8)]
        posi = A_([128, 1536], I32)
        ang = A_([128, 1536], F32)
        tt = A_([128, 1536], F32)
        ki = A_([128, 1536], I32)
        kf = A_([128, 1536], F32)
        ff = A_([128, 1536], F32)

        S.dma("pool", cm, cmats, writes=["cm"])
        load_WK()
        S.dma("sp", cT, cT_d, writes=["small"])
        S.dma("sp", badaT, b_adaT, writes=["small"])
        S.dma("sp", ng, ngT, writes=["small"])
        S.dma("sp", qg, qgT, writes=["small"])
        S.dma("sp", kvg, kvgT, writes=["small"])
        S.dma("sp", invf, invf_d, writes=["small"])
        for q in range(4):
            S.dma("sp", posi[q * 32:(q + 1) * 32, :], posall[q:q + 1, :].broadcast_to([32, 1536]), writes=["posi"])
        S.op("dve", lambda e: e.memset(zeros_bf, 0.0), writes=["zeros"])
        S.op("dve", lambda e: e.memset(ones_bf, 1.0), writes=["ones"])
        S.op("dve", lambda e: e.memset(epsc, EPS), reads=["small"], writes=["small"])

        CP("dve", ang, posi, reads=["posi"], writes=["ang"])
        TS("dve", ang, ang, invf, None, ALU.mult, reads=["ang", "small"], writes=["ang"])
        inv2pi = 1.0 / (2.0 * math.pi)
        for (tab, phase, nm) in ((tabsin, 0.0, "sin"), (tabcos, 0.25, "cos")):
            TS("dve", tt, ang, inv2pi, phase, ALU.mult, ALU.add, reads=["ang"], writes=["tt"])
            CP("dve", ki, tt, reads=["tt"], writes=["ki"])
            CP("dve", kf, ki, reads=["ki"], writes=["kf"])
            TT("dve", ff, tt, kf, ALU.subtract, reads=["tt", "kf"], writes=["ff"])
            TS("dve", kf, ff, 0.5, None, ALU.is_gt, reads=["ff"], writes=["kf"])
            TT("dve", ff, ff, kf, ALU.subtract, reads=["ff", "kf"], writes=["ff"])
            TS("dve", kf, ff, -0.5, None, ALU.is_lt, reads=["ff"], writes=["kf"])
            TT("dve", ff, ff, kf, ALU.add, reads=["ff", "kf"], writes=["ff"])
            ACT(tab, ff, AF.Sin, reads=["ff"], writes=["tab" + nm], scale=6.283185)
        for half in range(2):
            for k in range(8):
                S.dma("sp", wst[k], w_ada[k * 128:(k + 1) * 128, half * D:(half + 1) * D], writes=["wst%d" % k])
            for jj in range(8):
                j = half * 8 + jj
                for k in range(8):
                    MM(bk[0][:, j:j + 1], wst[k][:, jj * 128:(jj + 1) * 128], cT[:, k:k + 1], k == 0, k == 7,
                       reads=["wst%d" % k, "small"], writes=["bk0"])
        TT("dve", mod_sb[:, 0:16], bk[0][:, 0:16], badaT[:, 0:16], ALU.add, reads=["bk0", "small"], writes=["mod"])
        TS("dve", tmp8, mod_sb[:, 8:16], 1.0, None, ALU.add, reads=["mod"], writes=["tmp8"])
        TT("dve", gs, tmp8, ng, ALU.mult, reads=["tmp8", "small"], writes=["gs"])

        dump("gs", gs, [128, 8], reads=["gs"])
        dump("mod", mod_sb[:, 0:16], [128, 16], reads=["mod"])
        dump("tabcos", tabcos, [128, 1536], reads=["tabcos"])
        dump("tabsin", tabsin, [128, 1536], reads=["tabsin"])
        S.barrier()

        B_ = Bump(164, 204)
        B2 = Bump(154, 164)
        jit = B2([128, 2, 512], F32)
        sqj = B2([128, D], BF16)
        sq = B2([128, 3, 512], BF16)
        xt = [B_([128, D], F32) for _ in range(2)]
        xn = B_([128, 4, D], BF16)
        hTs = [B_([128, 8, 512], BF16) for _ in range(2)]
        rbc = B_([128, 512], F32)
        t1 = B_([128, 512], F32)
        t2 = B_([128, 512], F32)

        state = {"xt": 0, "st": 0, "bank": 2, "ev": 0, "fill": 0, "sbu": 0}

        def next_bank():
            b = state["bank"]
            state["bank"] = 2 + (b - 2 + 1) % 6
            return b

        def evac_engine():
            state["ev"] ^= 1
            return "act" if state["ev"] else "dve"

        def norm_stats(src, xts):
            for blk in range(4):
                slot = state["xt"]
                state["xt"] ^= 1
                xk = "xt%d" % slot
                c = state["st"]
                state["st"] = (c + 1) % 8
                sk = "stat%d" % c
                S.dma("sp", xts[slot], src[blk * 128:(blk + 1) * 128, :], writes=[xk])
                ACT(sqj, xts[slot], AF.Square, reads=[xk], writes=["sqj", sk], accum_out=stat[:, c:c + 1])
                ACT(stat[:, 8 + c:9 + c], stat[:, c:c + 1], AF.Ln, reads=[sk, "small"], writes=[sk],
                    scale=1.0 / D, bias=epsc)
                ACT(stat[:, 16 + c:17 + c], stat[:, 8 + c:9 + c], AF.Exp, reads=[sk], writes=[sk], scale=-0.5)
                TS("dve", xn[:, blk, :], xts[slot], stat[:, 16 + c:17 + c], None, ALU.mult,
                   reads=[xk, sk], writes=["xn%d" % blk])

        def norm_T(hT):
            for k in range(8):
                tb = k % 2
                for blk in range(4):
                    S.op("pe", lambda e, k=k, blk=blk, tb=tb, xn_=xn: e.transpose(
                        bkb[tb][:, blk * 128:(blk + 1) * 128], xn_[:, blk, k * 128:(k + 1) * 128], ident),
                        reads=["xn%d" % blk, "cm"], writes=["bk%d" % tb])
                if evac_engine() == "act":
                    ACT(hT[:, k, :], bkb[tb][:, 0:512], AF.Identity, reads=["bk%d" % tb, "gs", "mod"],
                        writes=[("hT", id(hT))], scale=gs[:, k:k + 1], bias=shift[:, k:k + 1])
                else:
                    TS("dve", hT[:, k, :], bkb[tb][:, 0:512], gs[:, k:k + 1], shift[:, k:k + 1], ALU.mult, ALU.add,
                       reads=["bk%d" % tb, "gs", "mod"], writes=[("hT", id(hT))])

        def rms_sq(banks, nch):
            for c in range(nch):
                ACT(sq[:, c, :], bk[banks[c]][:, :], AF.Square, reads=["bk%d" % banks[c]], writes=["sq%d" % c])

        def rms_fin(banks, nch, dim, dst, t0, gkey):
            sb_ = next_bank()
            for c in range(nch):
                MM(bk[sb_][:, :], ones_bf, sq[:, c, :], c == 0, c == nch - 1, reads=["sq%d" % c, "ones"],
                   writes=["bk%d" % sb_])
            ACT(rbc, bk[sb_][:, :], AF.Ln, reads=["bk%d" % sb_, "small"], writes=["rbc"], scale=1.0 / dim, bias=epsc)
            ACT(rbc, rbc, AF.Exp, reads=["rbc"], writes=["rbc"], scale=-0.5)
            for c in range(nch):
                TT("dve", dst[:, c, t0:t0 + 512], bk[banks[c]][:, :], rbc, ALU.mult,
                   reads=["bk%d" % banks[c], "rbc"], writes=[gkey])

        def proj_K(tc):
            t0 = tc * 512
            hT = hTs[tc % 2]
            hk = ("hT", id(hT))
            for cc in range(4):
                b = next_bank()
                for k in range(8):
                    MM(bk[b][:, :], WK[:, k, cc * 128:(cc + 1) * 128], hT[:, k, :], k == 0, k == 7,
                       reads=["WK", hk], writes=["bk%d" % b])
                if evac_engine() == "act":
                    ACT(KT[:, cc, t0:t0 + 512], bk[b][:, :], AF.Copy, reads=["bk%d" % b], writes=["KT"], scale=0.125)
                else:
                    TS("dve", KT[:, cc, t0:t0 + 512], bk[b][:, :], 0.125, None, ALU.mult, reads=["bk%d" % b], writes=["KT"])
            for blk in range(4):
                b = next_bank()
                for k in range(8):
                    MM(bk[b][:, :], hT[:, k, blk * 128:(blk + 1) * 128], WK[:, k, 512:1024], k == 0, k == 7,
                       reads=["WK", hk], writes=["bk%d" % b])
                if evac_engine() == "act":
                    ACT(Vsb[:, tc * 4 + blk, :], bk[b][:, :], AF.Copy, reads=["bk%d" % b], writes=["Vsb"])
                else:
                    CP("dve", Vsb[:, tc * 4 + blk, :], bk[b][:, :], reads=["bk%d" % b], writes=["Vsb"])
            cb = []
            for c2 in range(2):
                b = next_bank()
                cb.append(b)
                for k in range(8):
                    MM(bk[b][:, :], WK[:, k, 1024 + c2 * 128:1024 + (c2 + 1) * 128], hT[:, k, :], k == 0, k == 7,
                       reads=["WK", hk], writes=["bk%d" % b])
            rms_sq(cb, 2)
            bA = next_bank()
            for k in range(8):
                MM(bk[bA][0:96, :], WK[:, k, 1216:1312], hT[:, k, :], k == 0, k == 7, reads=["WK", hk], writes=["bk%d" % bA])
            bB = next_bank()
            for k in range(8):
                MM(bk[bB][0:96, :], WK[:, k, 1248:1344], hT[:, k, :], k == 0, k == 7, reads=["WK", hk], writes=["bk%d" % bB])
            qd, col = tc // 3, (tc % 3) * 512
            S.dma("sp", jit[64:96, 0, :], tabcos[qd * 32:(qd + 1) * 32, col:col + 512], writes=["jit"])
            S.dma("sp", jit[64:96, 1, :], tabsin[qd * 32:(qd + 1) * 32, col:col + 512], writes=["jit"])
            TT("dve", t1[64:96, :], bk[bA][64:96, :], jit[64:96, 0, :], ALU.mult, reads=["bk%d" % bA, "jit"], writes=["t1"])
            TT("dve", t2[64:96, :], bk[bB][64:96, :], jit[64:96, 1, :], ALU.mult, reads=["bk%d" % bB, "jit"], writes=["t2"])
            TT("dve", kpe[64:96, t0:t0 + 512], t1[64:96, :], t2[64:96, :], ALU.add, reads=["t1", "t2"], writes=["kpe"])
            return lambda: rms_fin(cb, 2, 256, ckvn, t0, "ckvn")
        def proj_Q(qc):
            q0 = qc * 512
            hT = hTs[qc % 2]
            hk = ("hT", id(hT))
            for cc in range(4):
                b = next_bank()
                for k in range(8):
                    MM(bk[b][:, :], WK[:, k, cc * 128:(cc + 1) * 128], hT[:, k, :], k == 0, k == 7,
                       reads=["WK", hk], writes=["bk%d" % b])
                if evac_engine() == "act":
                    ACT(QT[:, cc, q0:q0 + 512], bk[b][:, :], AF.Copy, reads=["bk%d" % b], writes=["QT"])
                else:
                    CP("dve", QT[:, cc, q0:q0 + 512], bk[b][:, :], reads=["bk%d" % b], writes=["QT"])
            cb = []
            for c3 in range(3):
                b = next_bank()
                cb.append(b)
                for k in range(8):
                    MM(bk[b][:, :], WK[:, k, 512 + c3 * 128:512 + (c3 + 1) * 128], hT[:, k, :], k == 0, k == 7,
                       reads=["WK", hk], writes=["bk%d" % b])
            rms_sq(cb, 3)
            return lambda: rms_fin(cb, 3, 384, cqn, q0, "cqn")
        chunks = [("K", i) for i in range(8)] + [("Q", i) for i in range(4)]

        def src_of(ch):
            return (xf if ch[0] == "K" else xq)[ch[1] * 512:(ch[1] + 1) * 512, :]

        norm_stats(src_of(chunks[0]), xt)
        norm_T(hTs[0])
        for i, ch in enumerate(chunks):
            if i + 1 < len(chunks):
                norm_stats(src_of(chunks[i + 1]), xt)
            if ch == ("Q", 0):
                load_WQ()
            fin = (proj_K if ch[0] == "K" else proj_Q)(ch[1])
            if i + 1 < len(chunks):
                norm_T(hTs[(i + 1) % 2])
            fin()
        dump("KT", KT.rearrange("p a b -> p (a b)"), [128, 4 * S_LEN], reads=["KT"])
        dump("QT", QT.rearrange("p a b -> p (a b)"), [128, 4 * NQ], reads=["QT"])
        dump("Vsb", Vsb.rearrange("p a b -> p (a b)"), [128, 32 * 512], reads=["Vsb"])
        dump("ckvn", ckvn.rearrange("p a b -> p (a b)"), [128, 2 * S_LEN], reads=["ckvn"])
        dump("cqn", cqn.rearrange("p a b -> p (a b)"), [128, 3 * NQ], reads=["cqn"])
        dump("kpe", kpe[64:96, :], [32, S_LEN], reads=["kpe"])
        S.barrier()

        def chain_cols(g, kb):
            c0 = max(0, kb // 2 - 4 * g)
            masks = []
            for c in range(c0, 4):
                j = 4 * g + c
                if kb == 2 * j + 1:
                    masks.append((c, 0))
                elif kb == 2 * j:
                    masks.append((c, 1))
            return c0, masks

        def run_chains(chains):
            live = list(chains)
            while live:
                nxt = []
                for gen in live:
                    try:
                        next(gen)
                        nxt.append(gen)
                    except StopIteration:
                        pass
                live = nxt

        def run_slots(slots):
            slots = [list(x) for x in slots]
            cur = [sl.pop(0) if sl else None for sl in slots]
            while any(c is not None for c in cur):
                for i in range(len(cur)):
                    while cur[i] is not None:
                        try:
                            next(cur[i])
                            break
                        except StopIteration:
                            cur[i] = slots[i].pop(0) if slots[i] else None

        if stop_after != "B":
            C1 = Bump(164, 204)
            e_sb = C1([128, 3, 2, 512], F32)
            sp_sb = C1([128, 2, 2, 512], BF16)
            g_sb = C1([128, 2, 2, 512], F32)
            w_sb = C1([128, 2, 2, 512], BF16)
            zp, Ap = bkpair(0), bkpair(2)

            def sb_pair(hp, g):
                cc = hp
                units = list(range(8 * g + 7, -1, -1))
                nU = len(units)
                ub = state["sbu"]
                state["sbu"] += nU
                def fill(n, lo=0):
                    for _ in range(n):
                        fb = 6 + (state["fill"] % 2)
                        state["fill"] += 1
                        MM(bk[fb][:, lo:512], ident, QT[:, 0, lo:512], True, True)

                def lo_of(i):
                    return chain_cols(g, units[i])[0] * 128

                def qk(i):
                    kb = units[i]
                    c0, masks = chain_cols(g, kb)
                    lo = c0 * 128
                    for c in range(2):
                        bp = 64 * c
                        MM(bk[c][:, lo:512], KT[bp:bp + 64, cc, kb * 128:(kb + 1) * 128],
                           QT[bp:bp + 64, cc, g * 512 + lo:(g + 1) * 512], True, len(masks) == 0, writes=["z"])
                        for mi, (cb, which) in enumerate(masks):
                            MM(bk[c][:, cb * 128:(cb + 1) * 128], ident, m_sb[which], False, mi == len(masks) - 1, writes=["z"])

                def tri(i):
                    lo = lo_of(i)
                    for c in range(2):
                        MM(bk[2 + c][:, lo:512], triI, sp_sb[:, (ub + i) % 2, c, lo:512], False, True, reads=["sp%d" % ((ub + i) % 2)], writes=["A"])

                def omt_(i):
                    lo = lo_of(i)
                    for c in range(2):
                        MM(bk[2 + c][:, lo:512], omt, sp_sb[:, (ub + i) % 2, c, lo:512], False, True, reads=["sp%d" % ((ub + i) % 2)], writes=["A"])

                def pv(i):
                    lo = lo_of(i)
                    kb = units[i]
                    for c in range(2):
                        h = 2 * hp + c
                        MM(bk[4 + c][0:64, lo:512], Vsb[:, kb, h * 64:(h + 1) * 64], w_sb[:, (ub + i) % 2, c, lo:512], False, True,
                           reads=["w%d" % ((ub + i) % 2)], writes=["o"])

                def act_e(i):
                    lo = lo_of(i)
                    ACT(e_sb[:, (ub + i) % 3, :, lo:512], zp[:, :, lo:512], AF.Exp, reads=["z"], writes=["e%d" % ((ub + i) % 3)])

                def act_sp(i):
                    lo = lo_of(i)
                    ACT(sp_sb[:, (ub + i) % 2, :, lo:512], e_sb[:, (ub + i) % 3, :, lo:512], AF.Ln, reads=["e%d" % ((ub + i) % 3)],
                        writes=["sp%d" % ((ub + i) % 2)], bias=1.0)

                def act_g(i):
                    lo = lo_of(i)
                    ACT(g_sb[:, (ub + i) % 2, :, lo:512], Ap[:, :, lo:512], AF.Exp, reads=["A"], writes=["g%d" % ((ub + i) % 2)], scale=-1.0)

                def dve_w(i):
                    lo = lo_of(i)
                    TT("dve", w_sb[:, (ub + i) % 2, :, lo:512], e_sb[:, (ub + i) % 3, :, lo:512], g_sb[:, (ub + i) % 2, :, lo:512], ALU.mult,
                       reads=["e%d" % ((ub + i) % 3), "g%d" % ((ub + i) % 2)], writes=["w%d" % ((ub + i) % 2)])

                def prologue():
                    qk(0)
                    act_e(0)
                    act_sp(0)
                    if nU > 1:
                        qk(1)

                def main(next_prologue):
                    for c in range(2):
                        MM(bk[2 + c][:, :], zeros_bf, QT[:, 0, 0:512], True, True, reads=["zeros"], writes=["A"])
                        MM(bk[4 + c][0:64, :], zeros_bf[:, 0:64], QT[:, 0, 0:512], True, True, reads=["zeros"], writes=["o"])
                    for i in range(nU):
                        if i == nU - 1 and next_prologue is not None:
                            next_prologue()
                        tri(i)
                        fill(SB_FILL, lo_of(i))
                        if i + 1 < nU:
                            act_e(i + 1)
                        act_g(i)
                        if i + 1 < nU:
                            act_sp(i + 1)
                        if i >= 1:
                            pv(i - 1)
                        if i + 2 < nU:
                            qk(i + 2)
                        if i + 1 < nU:
                            omt_(i)
                        dve_w(i)
                    pv(nU - 1)
                    for c in range(2):
                        CP("dve", oaT[64 * c:64 * c + 64, cc, g * 512:(g + 1) * 512], bk[4 + c][0:64, :], reads=["o"], writes=["oaT"])

                return prologue, main

            sb_chains = [sb_pair(hp, g) for g in range(4) for hp in range(4)]
            sb_chains[0][0]()
            for ci_, (pro, main) in enumerate(sb_chains):
                main(sb_chains[ci_ + 1][0] if ci_ + 1 < len(sb_chains) else None)
            dump("oaT", oaT.rearrange("p a b -> p (a b)"), [128, 4 * NQ], reads=["oaT"])
            S.barrier()

        if stop_after not in ("B", "C1"):
            C2 = Bump(52, 132)
            Vm = C2([128, 32, 4, 192], BF16)
            KhT0_ = C2([128, S_LEN], BF16)
            QhT = [C2([128, NQ], BF16) for _ in range(2)]
            wukv = C2([128, 2, 1024], BF16)
            wuqa = C2([128, 3, 8, 128], BF16)
            wvc = C2([128, 2, 512], BF16)
            C2b = Bump(164, 204)
            wst2 = C2b([128, 3, 1024], F32)
            KhT = [KhT0_, view(164 * 1024, [128, S_LEN], BF16)]
            KKEY = ["KhT0", "wst2"]
            jq = C2b([128, NQ], F32)
            p_sb = C2b([128, 2, 2, 512], BF16)
            lnl = C2b([128, 512], F32)
            rinv = C2b([128, 512], F32)
            tq1 = C2b([128, 512], F32)
            tq2 = C2b([128, 512], F32)

            S.dma("sp", wst2[:, 0:2, :], w_ukv.rearrange("(k p) n -> p k n", p=128), writes=["wst2"])
            for k in range(2):
                TS("dve", wukv[:, k, :], wst2[:, k, :], kvg[:, k:k + 1], None, ALU.mult, reads=["wst2", "small"], writes=["wukv"])
            S.dma("sp", wst2[:, :, 0:768], w_uq.rearrange("(k p) n -> p k n", p=128), reads=[], writes=["wst2"])
            wst2q = wst2[:, :, 0:768].rearrange("p k (h c) -> p k h c", c=96)
            for k in range(3):
                TS("dve", wuqa[:, k, :, 0:96], wst2q[:, k, :, :], qg[:, k:k + 1], None, ALU.mult, reads=["wst2", "small"], writes=["wuqa"])
                TS("dve", wuqa[:, k, :, 96:112], wst2q[:, k, :, 80:96], qg[:, k:k + 1], -1.0, ALU.mult, ALU.mult,
                   reads=["wst2", "small"], writes=["wuqa"])
                TS("dve", wuqa[:, k, :, 112:128], wst2q[:, k, :, 64:80], qg[:, k:k + 1], None, ALU.mult,
                   reads=["wst2", "small"], writes=["wuqa"])
            for qc in range(4):
                ci = 8 + qc
                qd, col = ci // 3, (ci % 3) * 512
                S.dma("sp", jq[64:96, qc * 512:(qc + 1) * 512], tabcos[qd * 32:(qd + 1) * 32, col:col + 512], writes=["jq"])
                S.dma("sp", jq[96:128, qc * 512:(qc + 1) * 512], tabsin[qd * 32:(qd + 1) * 32, col:col + 512], writes=["jq"])
            for blk in range(32):
                S.op("pool", lambda e, blk=blk: e.memset(Vm[:, blk, :, 64:128], 1.0), writes=["Vm"])
            wukv4 = wukv.rearrange("p k (h c) -> p k h c", c=128)
            for k in range(2):
                CP("dve", wvc[:, k, :].rearrange("p (h c) -> p h c", c=64), wukv4[:, k, :, 64:128], reads=["wukv"], writes=["wvc"])
            for blk in range(32):
                b = 6 + blk % 2
                for k in range(2):
                    MM(bk[b][:, :], ckvn[:, k, blk * 128:(blk + 1) * 128],
                       wvc[:, k, :], k == 0, k == 1, reads=["wvc"], writes=["bk%d" % b])
                pv = bk[b][:, :].rearrange("p (q t c) -> p q t c", t=2, c=64)
                ACT(Vm[:, blk, :, 0:64], pv[:, :, 0, :], AF.Copy, reads=["bk%d" % b], writes=["Vm"])
                CP("dve", Vm[:, blk, :, 128:192], pv[:, :, 1, :], reads=["bk%d" % b], writes=["Vm"])

            def prep_head(h):
                Kh, Qh = KhT[h % 2], QhT[h % 2]
                kk, qk_ = KKEY[h % 2], "QhT%d" % (h % 2)
                for tc in range(8):
                    b = 6 + tc % 2
                    for k in range(2):
                        MM(bk[b][0:64, :], wukv[:, k, h * 128:h * 128 + 64], ckvn[:, k, tc * 512:(tc + 1) * 512], k == 0, k == 1,
                           reads=["wukv"], writes=["bk%d" % b])
                    CP("dve", Kh[0:64, tc * 512:(tc + 1) * 512], bk[b][0:64, :], reads=["bk%d" % b], writes=[kk])
                    yield
                for qc in range(4):
                    b = 6 + qc % 2
                    kb_ = "bk%d" % b
                    cs = slice(qc * 512, (qc + 1) * 512)
                    for k in range(3):
                        MM(bk[b][:, :], wuqa[:, k, h, :], cqn[:, k, cs], k == 0, k == 2, reads=["wuqa"], writes=[kb_])
                    CP("dve", Qh[0:64, cs], bk[b][0:64, :], reads=[kb_], writes=[qk_])
                    TT("dve", tq1[64:96, :], bk[b][64:96, :], jq[64:96, cs], ALU.mult, reads=[kb_, "jq"], writes=["tq1"])
                    TT("dve", tq2[64:96, :], bk[b][96:128, :], jq[96:128, cs], ALU.mult, reads=[kb_, "jq"], writes=["tq2"])
                    TT("dve", Qh[64:96, cs], tq1[64:96, :], tq2[64:96, :], ALU.add, reads=["tq1", "tq2"], writes=[qk_])
                    yield
                    yield

            sm_scale = 1.0 / math.sqrt(96.0)

            def mla_parts(h, g, ci):
                cc, bp = h // 2, (h % 2) * 64
                Kh, Qh = KhT[h % 2], QhT[h % 2]
                kk, qk_ = KKEY[h % 2], "QhT%d" % (h % 2)
                lb = 64 - bp
                vl = Vm[:, :, h // 2, bp:bp + 128]
                ob = 4 + ci % 2
                ko = "bk%d" % ob
                units = list(range(8 * g + 7, -1, -1))
                nP = len(units) // 2

                def qk(j):
                    zb = 2 * (j % 2)
                    kz = "zp%d" % (j % 2)
                    lo = chain_cols(g, units[2 * j])[0] * 128
                    for c in range(2):
                        kb = units[2 * j + c]
                        c0, masks = chain_cols(g, kb)
                        assert c0 * 128 == lo
                        MM(bk[zb + c][:, lo:512], Kh[0:96, kb * 128:(kb + 1) * 128], Qh[0:96, g * 512 + lo:(g + 1) * 512],
                           True, len(masks) == 0, reads=[kk, qk_], writes=[kz])
                        for mi, (cb, which) in enumerate(masks):
                            MM(bk[zb + c][:, cb * 128:(cb + 1) * 128], ident, m_mla[which], False, mi == len(masks) - 1, writes=[kz])
                    ACT(p_sb[:, j % 2, :, lo:512], bkpair(zb)[:, :, lo:512], AF.Exp, reads=[kz], writes=["p%d" % (j % 2)],
                        scale=sm_scale)

                def prologue():
                    MM(bk[ob][:, :], zeros_bf, ckvn[:, 0, 0:512], True, True, reads=["zeros"], writes=[ko])
                    qk(0)
                    if nP > 1:
                        qk(1)

                def rounds(stepper, next_prologue=None):
                    for j in range(nP):
                        lo = chain_cols(g, units[2 * j])[0] * 128
                        for c in range(2):
                            kb = units[2 * j + c]
                            MM(bk[ob][:, lo:512], vl[:, kb, :], p_sb[:, j % 2, c, lo:512], False, True,
                               reads=["p%d" % (j % 2), "Vm"], writes=[ko])
                        if j == nP - 1 and next_prologue is not None:
                            next_prologue()
                        if j + 2 < nP:
                            for fi in range(MLA_FILL):
                                MM(bk[2 * (j % 2) + fi % 2][:, :], ident, ckvn[:, 0, 0:512], True, True, writes=["zp%d" % (j % 2)])
                            qk(j + 2)
                        stepper()

                def tail():
                    ACT(lnl[bp:bp + 64, :], bk[ob][lb:lb + 64, :], AF.Ln, reads=[ko], writes=["lnl"])
                    ACT(rinv[bp:bp + 64, :], lnl[bp:bp + 64, :], AF.Exp, reads=["lnl"], writes=["rinv"], scale=-1.0)
                    TT("dve", obT[bp:bp + 64, cc, g * 512:(g + 1) * 512], bk[ob][bp:bp + 64, :], rinv[bp:bp + 64, :], ALU.mult,
                       reads=[ko, "rinv"], writes=["obT"])

                return prologue, rounds, tail

            for bi in range(2):
                for half in range(2):
                    hs_ = slice(half * 2048, (half + 1) * 2048)
                    CP("dve", KhT[bi][64:96, hs_], kpe[64:96, hs_], reads=["wuqa", "wukv"], writes=[KKEY[bi]])
            for _ in prep_head(0):
                pass
            if debug:
                dump("KhT0", KhT[0][0:96, :], [96, S_LEN], reads=["KhT0"])
                dump("QhT0", QhT[0][0:96, :], [96, NQ], reads=["QhT0"])
            mparts = [(h, g, mla_parts(h, g, 4 * h + gi)) for h in range(8) for gi, g in enumerate((3, 2, 1, 0))]
            mparts[0][2][0]()
            prev_tail = None
            prep = None
            for idx, (h, g, (prologue, rounds, tail)) in enumerate(mparts):
                if g == 3:
                    prep = prep_head(h + 1) if h + 1 < 8 else iter(())

                def stepper(prep=prep):
                    next(prep, None)

                nxt = None
                if idx + 1 < len(mparts):
                    nh = mparts[idx + 1][0]
                    npro = mparts[idx + 1][2][0]

                    def nxt(nh=nh, h=h, npro=npro, prep=prep):
                        if nh != h:
                            for _ in prep:
                                pass
                        npro()
                if prev_tail is not None:
                    prev_tail()
                rounds(stepper, nxt)
                prev_tail = tail
            prev_tail()
            dump("obT", obT.rearrange("p a b -> p (a b)"), [128, 4 * NQ], reads=["obT"])
            S.barrier()

        if stop_after is None:
            Dw = Bump(16, 132)
            WG = Dw([128, 8, 3072], BF16)
            WA = Dw([128, 4, D], BF16)
            WB = Dw([128, 4, D], BF16)
            WO = Dw([128, 8, D], BF16)
            gate_bc = Dw([128, D], F32)
            fg_bc = Dw([128, D], F32)
            hTd = Dw([128, 8, 512], BF16)
            mergedT = Dw([128, 8, 512], BF16)
            xnew = Dw([128, D], F32)
            outt = Dw([128, D], F32)
            sqjD = Dw([128, D], BF16)
            Dt = Bump(164, 204)
            xtD = [Dt([128, D], F32) for _ in range(2)]
            xres = [Dt([128, D], F32) for _ in range(2)]
            xnD = Dt([128, 4, D], BF16)
            og_off = Dt.o
            ogA = Dt([128, 4, 512], BF16)
            ogB = Dt([128, 4, 512], BF16)
            xnew2 = view(og_off, [128, D], F32)
            outt2 = view(og_off + 4096, [128, D], F32)
            sg = [Dt([128, 512], F32) for _ in range(2)]
            tm = [Dt([128, 512], F32) for _ in range(2)]
            Di = Bump(164, 180)
            bgate_bc = Di([128, D], F32)
            cbc = Di([128, 8, 128], F32)
            wstD = [Di([128, D], F32) for _ in range(2)]

            for k in range(8):
                S.dma("pool", WG[:, k, 0:512], w3[:, k, SBZ:SBZ + 512], writes=["WGz"])
                S.dma("pool", WG[:, k, 512:1024], w3[:, k, MLAZ:MLAZ + 512], writes=["WGz"])
            for k in range(8):
                S.dma("pool", WG[:, k, 1024:3072], w3[:, k, GA:GA + 2048], writes=["WGg"])
            for k in range(4):
                S.dma("pool", WA[:, k, :], w_a[k * 128:(k + 1) * 128, :], writes=["WA"])
                S.dma("pool", WB[:, k, :], w_b[k * 128:(k + 1) * 128, :], writes=["WB"])
            for k in range(8):
                S.dma("pool", WO[:, k, :], w_out[k * 128:(k + 1) * 128, :], writes=["WO"])

            S.dma("sp", fg_bc, fg_row.broadcast_to([128, D]), writes=["fg_bc"])
            S.dma("sp", bgate_bc, b_gate.broadcast_to([128, D]), writes=["bgate"])
            S.op("dve", lambda e: e.memset(cbc, 1.0), writes=["cbc"])
            for k in range(8):
                TS("dve", cbc[:, k, :], cbc[:, k, :], cT[:, k:k + 1], None, ALU.mult, reads=["cbc", "small"], writes=["cbc"])
            for k in range(8):
                S.dma("sp", wstD[k % 2], w_ada[k * 128:(k + 1) * 128, 2 * D:3 * D], writes=["wstD%d" % (k % 2)])
                for half in range(2):
                    MM(bk[half][:, :], cbc[:, k, :], wstD[k % 2][:, half * 512:(half + 1) * 512], k == 0, k == 7,
                       reads=["cbc", "wstD%d" % (k % 2)], writes=["bk%d" % half])
            for half in range(2):
                TT("dve", gate_bc[:, half * 512:(half + 1) * 512], bk[half][:, :], bgate_bc[:, half * 512:(half + 1) * 512],
                   ALU.add, reads=["bk%d" % half, "bgate"], writes=["gate_bc"])
            S.barrier()
            xn = xnD
            sqj = sqjD
            hkd = ("hT", id(hTd))
            norm_stats(xq[0:512, :], xtD)
            norm_T(hTd)
            for qc in range(4):
                q0 = qc * 512
                if qc + 1 < 4:
                    norm_stats(xq[q0 + 512:q0 + 1024, :], xtD)
                for (off, oT, og, nm) in ((0, oaT, ogA, "ogA"), (512, obT, ogB, "ogB")):
                    for cc in range(4):
                        b = next_bank()
                        for k in range(8):
                            MM(bk[b][:, :], WG[:, k, off + cc * 128:off + (cc + 1) * 128], hTd[:, k, :], k == 0, k == 7,
                               reads=["WGz", hkd], writes=["bk%d" % b])
                        si = cc % 2
                        ACT(sg[si], bk[b][:, :], AF.Sigmoid, reads=["bk%d" % b], writes=["sg%d" % si])
                        TT("dve", tm[si], bk[b][:, :], sg[si], ALU.mult, reads=["bk%d" % b, "sg%d" % si], writes=["tm%d" % si])
                        TT("dve", og[:, cc, :], tm[si], oT[:, cc, q0:q0 + 512], ALU.mult, reads=["tm%d" % si], writes=[nm])
                for n in range(8):
                    bga, bgb, bya, byb = next_bank(), next_bank(), next_bank(), next_bank()
                    for k in range(8):
                        MM(bk[bga][:, :], WG[:, k, 1024 + n * 128:1024 + (n + 1) * 128], hTd[:, k, :], k == 0, k == 7,
                           reads=["WGg", hkd], writes=["bk%d" % bga])
                    for k in range(8):
                        MM(bk[bgb][:, :], WG[:, k, 2048 + n * 128:2048 + (n + 1) * 128], hTd[:, k, :], k == 0, k == 7,
                           reads=["WGg", hkd], writes=["bk%d" % bgb])
                    for k in range(4):
                        MM(bk[bya][:, :], WA[:, k, n * 128:(n + 1) * 128], ogA[:, k, :], k == 0, k == 3,
                           reads=["WA", "ogA"], writes=["bk%d" % bya])
                    for k in range(4):
                        MM(bk[byb][:, :], WB[:, k, n * 128:(n + 1) * 128], ogB[:, k, :], k == 0, k == 3,
                           reads=["WB", "ogB"], writes=["bk%d" % byb])
                    ACT(sg[0], bk[bga][:, :], AF.Sigmoid, reads=["bk%d" % bga], writes=["sg0"])
                    ACT(sg[1], bk[bgb][:, :], AF.Sigmoid, reads=["bk%d" % bgb], writes=["sg1"])
                    TT("dve", tm[0], bk[bya][:, :], sg[0], ALU.mult, reads=["bk%d" % bya, "sg0"], writes=["tm0"])
                    TT("dve", tm[1], bk[byb][:, :], sg[1], ALU.mult, reads=["bk%d" % byb, "sg1"], writes=["tm1"])
                    TT("dve", mergedT[:, n, :], tm[0], tm[1], ALU.add, reads=["tm0", "tm1"], writes=["merged"])
                if qc + 1 < 4:
                    norm_T(hTd)
                for blk in range(4):
                    rs = blk % 2
                    xk = "xres%d" % rs
                    xnw, xnk = (xnew, "xnew") if blk % 2 == 0 else (xnew2, "ogA")
                    ott, otk = (outt, "outt") if blk % 2 == 0 else (outt2, "ogB")
                    S.dma("sp", xres[rs], xq[q0 + blk * 128:q0 + (blk + 1) * 128, :], writes=[xk])
                    for half in range(2):
                        b = next_bank()
                        for k in range(8):
                            MM(bk[b][:, :], mergedT[:, k, blk * 128:(blk + 1) * 128], WO[:, k, half * 512:(half + 1) * 512],
                               k == 0, k == 7, reads=["merged", "WO"], writes=["bk%d" % b])
                        hs = slice(half * 512, (half + 1) * 512)
                        TT("dve", xnw[:, hs], bk[b][:, :], gate_bc[:, hs], ALU.mult, reads=["bk%d" % b, "gate_bc"], writes=[xnk])
                    TT("dve", xnw, xnw, xres[rs], ALU.add, reads=[xnk, xk], writes=[xnk])
                    c = state["st"]
                    state["st"] = (c + 1) % 8
                    sk = "stat%d" % c
                    ACT(sqjD, xnw, AF.Square, reads=[xnk], writes=["sqjD", sk], accum_out=stat[:, c:c + 1])
                    ACT(stat[:, 8 + c:9 + c], stat[:, c:c + 1], AF.Ln, reads=[sk, "small"], writes=[sk], scale=1.0 / D, bias=epsc)
                    ACT(stat[:, 16 + c:17 + c], stat[:, 8 + c:9 + c], AF.Exp, reads=[sk], writes=[sk], scale=-0.5)
                    S.op("dve", lambda e, c=c, ott=ott, xnw=xnw: e.scalar_tensor_tensor(ott, xnw, stat[:, 16 + c:17 + c], fg_bc, ALU.mult, ALU.mult),
                         reads=[xnk, sk, "fg_bc"], writes=[otk])
                    S.dma("pool", out_d[q0 + blk * 128:q0 + (blk + 1) * 128, :], ott, reads=[otk])

        for q in ("sp", "pool"):
            for i in range(max(0, S.dma_n[q] - ND), S.dma_n[q]):
                S._wait("pool", ("d", (q, i)))
        with nc.Block() as block:
            S.emit(block)
    return nc, dbg_outs


def host_inputs(inputs):
    x = np.asarray(inputs["x"], np.float32)
    c = np.asarray(inputs["c"], np.float32)
    pos = np.asarray(inputs["positions"], np.int32)
    f = lambda k: np.ascontiguousarray(np.asarray(inputs[k], np.float32))
    w_ada = f("w_ada")[0]
    b_ada = f("b_ada")[0]
    tri = np.tril(np.ones((128, 128), np.float32))
    ident = np.eye(128, dtype=np.float32)
    omt = 1.0 - tri
    ss, tt = np.meshgrid(np.arange(128), np.arange(128), indexing="ij")
    strict = np.where(ss < tt, 0.0, MASKV).astype(np.float32)
    causal = np.where(ss <= tt, 0.0, MASKV).astype(np.float32)
    allm = np.full((128, 128), MASKV, np.float32)
    nom = np.zeros((128, 128), np.float32)
    inv_freq = (10000.0 ** (-np.arange(0, 32, 2, dtype=np.float32) / np.float32(32))).astype(np.float32)
    invf = np.tile(np.concatenate([inv_freq, inv_freq]), 4).reshape(128, 1).astype(np.float32)
    common = {
        "w_ada": w_ada,
        "b_adaT": np.ascontiguousarray(b_ada.reshape(24, 128).T),
        "b_gate": np.ascontiguousarray(b_ada[2 * D:3 * D].reshape(1, D)),
        "ngT": np.ascontiguousarray(f("norm_gain")[0].reshape(8, 128).T),
        "w_in": f("w_in")[0],
        "qgT": np.ascontiguousarray(f("q_norm_gain")[0].reshape(3, 128).T),
        "w_uq": f("w_uq")[0],
        "kvgT": np.ascontiguousarray(f("kv_norm_gain")[0].reshape(2, 128).T),
        "w_ukv": f("w_ukv")[0],
        "w_a": f("w_branch_a")[0],
        "w_b": f("w_branch_b")[0],
        "w_out": f("w_out")[0],
        "fg_row": np.ascontiguousarray(f("final_norm_gain").reshape(1, D)),
        "invf": invf,
    }
    maps = []
    for core in range(8):
        b, p = core // 2, core % 2
        blocks = [2 * j + p for j in range(16)]
        xb = x[b].reshape(32, 128, D)
        xq = np.ascontiguousarray(xb[blocks].reshape(NQ, D))
        pq = pos[b].reshape(32, 128)[blocks].reshape(NQ)
        posall = np.ascontiguousarray(np.concatenate([pos[b], pq]).reshape(4, 1536).astype(np.int32))
        if p == 0:
            mats = [ident, tri, omt, allm, strict, allm, causal]
        else:
            mats = [ident, tri, omt, strict, nom, causal, nom]
        cm = np.ascontiguousarray(np.stack(mats, axis=1).astype(np.float32))
        m = dict(common)
        m.update({"xf": np.ascontiguousarray(x[b]), "xq": xq, "posall": posall,
                  "cT": np.ascontiguousarray(c[b].reshape(8, 128).T), "cmats": cm})
        maps.append(m)
    return maps


_CACHE = {}


def kernel(**inputs):
    maps = host_inputs(inputs)
    if "nc" not in _CACHE:
        _CACHE["nc"] = build_program()[0]
    nc = _CACHE["nc"]
    res = run_bass_kernel_spmd(nc, maps, core_ids=list(range(8)))
    out = np.zeros((4, 32, 128, D), np.float32)
    for core in range(8):
        b, p = core // 2, core % 2
        o = np.asarray(res.results[core]["out"], np.float32).reshape(16, 128, D)
        out[b, p::2] = o
    return out.reshape(4, S_LEN, D)
```

```python
import math
from contextlib import ExitStack
import numpy as np
import concourse.bass as bass
import concourse.mybir as mybir
from concourse.bass_utils import run_bass_kernel_spmd

F32 = mybir.dt.float32
BF16 = mybir.dt.bfloat16
I32 = mybir.dt.int32
AF = mybir.ActivationFunctionType
ALU = mybir.AluOpType

ENGS = ["pe", "act", "dve", "pool", "sp"]
PH = 2048
NPH = {"pe": 7, "act": 5, "dve": 4, "pool": 1, "sp": 1}
ND = 16

D = 1024
S_LEN = 4096
NQ = 2048
SBQ, SBK, SBV, SBZ, CQ, CKV, KROT, MLAZ, GA, GB = 0, 512, 1024, 1536, 2048, 2432, 2688, 2720, 3232, 4256
EPS = 1e-6
MASKV = -30000.0
SB_FILL = 4
MLA_FILL = 0
DEBUG = False


class Sched:
    def __init__(self, nc, esem, dsem):
        self.nc = nc
        self.esem = esem
        self.dsem = dsem
        self.ops = {e: [] for e in ENGS}
        self.cnt = {e: 0 for e in ENGS}
        self.waited_e = {e: {} for e in ENGS}
        self.waited_d = {e: {} for e in ENGS}
        self.last_w = {}
        self.readers = {}
        self.dma_i = 0
        self.dma_n = {"sp": 0, "pool": 0}

    def _wait(self, E, ev):
        if ev[0] == "e":
            _, e2, n = ev
            if e2 == E and E == "pe":
                return
            if self.waited_e[E].get(e2, 0) >= n:
                return
            self.waited_e[E][e2] = n
            sem = self.esem[e2][(n - 1) // PH]
            val = (n - 1) % PH + 1
        else:
            q, i = ev[1]
            k = i % ND
            val = 16 * (i // ND + 1)
            if self.waited_d[E].get((q, k), 0) >= val:
                return
            self.waited_d[E][(q, k)] = val
            sem = self.dsem[q][k]
        self.ops[E].append(lambda eng, sem=sem, val=val: eng.wait_ge(sem, val))

    def _deps(self, E, reads, writes, extra=(), dma_accum=False):
        deps = list(extra)
        for b in reads:
            for ev in self.last_w.get(b, ()):
                deps.append(ev)
        for b in writes:
            for ev in self.last_w.get(b, ()):
                if dma_accum and ev[0] == "d":
                    continue
                deps.append(ev)
            r = self.readers.get(b)
            if r:
                for e2, n in r[0].items():
                    deps.append(("e", e2, n))
                for i in r[1]:
                    deps.append(("d", i))
        for ev in deps:
            self._wait(E, ev)

    def _record(self, ev, reads, writes, dma_accum=False):
        for b in reads:
            r = self.readers.setdefault(b, ({}, []))
            if ev[0] == "e":
                r[0][ev[1]] = ev[2]
            else:
                r[1].append(ev[1])
        for b in writes:
            r = self.readers.get(b)
            had_readers = bool(r and (r[0] or r[1]))
            if dma_accum and not had_readers:
                self.last_w[b] = [e for e in self.last_w.get(b, ()) if e[0] == "d"] + [ev]
            else:
                self.last_w[b] = [ev]
            self.readers[b] = ({}, [])

    def op(self, E, fn, reads=(), writes=(), extra=()):
        self._deps(E, reads, writes, extra)
        n = self.cnt[E] + 1
        self.cnt[E] = n
        assert (n - 1) // PH < NPH[E], "too many instrs on %s" % E
        sem = self.esem[E][(n - 1) // PH]
        self.ops[E].append(lambda eng, fn=fn, sem=sem: fn(eng).then_inc(sem, 1))
        ev = ("e", E, n)
        self._record(ev, reads, writes)
        return ev

    def dma(self, Q, out, in_, reads=(), writes=(), extra=()):
        i = self.dma_n[Q]
        self.dma_n[Q] += 1
        self.dma_i += 1
        ex = list(extra)
        if i >= ND:
            ex.append(("d", (Q, i - ND)))
        self._deps(Q, reads, writes, ex, dma_accum=True)
        sem = self.dsem[Q][i % ND]
        self.ops[Q].append(
            lambda eng, out=out, in_=in_, sem=sem: eng.dma_start(out=out, in_=in_).then_inc(sem, 16))
        ev = ("d", (Q, i))
        self._record(ev, reads, writes, dma_accum=True)
        return ev

    def barrier(self):
        snap = dict(self.cnt)
        for E in ENGS:
            for e2 in ENGS:
                if e2 != E and snap[e2] > 0:
                    self._wait(E, ("e", e2, snap[e2]))

    def emit(self, block):
        ops = self.ops

        @block.tensor
        def _(eng):
            for f in ops["pe"]:
                f(eng)

        @block.scalar
        def _(eng):
            for f in ops["act"]:
                f(eng)

        @block.vector
        def _(eng):
            for f in ops["dve"]:
                f(eng)

        @block.gpsimd
        def _(eng):
            for f in ops["pool"]:
                f(eng)

        @block.sync
        def _(eng):
            for f in ops["sp"]:
                f(eng)


def build_program(debug=False, stop_after=None):
    nc = bass.Bass("TRN2", target_bir_lowering=False)

    def din(name, shape, dt=F32):
        return nc.dram_tensor(name, list(shape), dt, kind="ExternalInput").ap()

    xf = din("xf", [S_LEN, D])
    xq = din("xq", [NQ, D])
    posall = din("posall", [4, 1536], I32)
    cT_d = din("cT", [128, 8])
    w_ada = din("w_ada", [D, 3 * D])
    b_adaT = din("b_adaT", [128, 24])
    b_gate = din("b_gate", [1, D])
    ngT = din("ngT", [128, 8])
    w_in = din("w_in", [D, 5280])
    qgT = din("qgT", [128, 3])
    w_uq = din("w_uq", [384, 768])
    kvgT = din("kvgT", [128, 2])
    w_ukv = din("w_ukv", [256, 1024])
    w_a = din("w_a", [512, D])
    w_b = din("w_b", [512, D])
    w_out = din("w_out", [D, D])
    fg_row = din("fg_row", [1, D])
    cmats = din("cmats", [128, 7, 128])
    invf_d = din("invf", [128, 1])
    out_d = nc.dram_tensor("out", [NQ, D], F32, kind="ExternalOutput").ap()
    dbg_outs = {}

    with ExitStack() as es:
        esem = {e: [es.enter_context(nc.semaphore("s_%s_%d" % (e, i))) for i in range(NPH[e])] for e in ENGS}
        dsem = {q: [es.enter_context(nc.semaphore("d_%s_%d" % (q, i))) for i in range(ND)] for q in ("sp", "pool")}
        S = Sched(nc, esem, dsem)

        ARENA_KB = 204
        arena = es.enter_context(nc.sbuf_tensor("arena", [128, ARENA_KB * 512], BF16))
        arena32 = arena.bitcast(F32)
        arenai = arena.bitcast(I32)

        def view(off_bytes, shape, dt):
            n = int(np.prod(shape[1:]))
            esz = 2 if dt == BF16 else 4
            assert off_bytes % 4 == 0
            assert off_bytes + n * esz <= ARENA_KB * 1024, (off_bytes, shape)
            base = {BF16: arena, F32: arena32, I32: arenai}[dt]
            o = off_bytes // esz
            ap = base[:, o:o + n]
            if len(shape) == 3:
                ap = ap.rearrange("p (a b) -> p a b", b=shape[2])
            elif len(shape) == 4:
                ap = ap.rearrange("p (a b c) -> p a b c", b=shape[2], c=shape[3])
            return ap

        class Bump:
            def __init__(self, start_kb, end_kb):
                self.o = start_kb * 1024
                self.end = end_kb * 1024

            def __call__(self, shape, dt):
                n = int(np.prod(shape[1:])) * (2 if dt == BF16 else 4)
                n = (n + 3) // 4 * 4
                v = view(self.o, shape, dt)
                self.o += n
                assert self.o <= self.end, ("bump overflow", self.o, self.end)
                return v

        psum_all = es.enter_context(nc.psum_tensor("psall", [128, 4096], F32))
        psum_bf = psum_all.bitcast(BF16)
        bk = [psum_all[:, i * 512:(i + 1) * 512] for i in range(8)]
        bkb = [psum_bf[:, i * 1024:(i + 1) * 1024] for i in range(8)]

        def bkpair(i):
            return psum_all[:, i * 512:(i + 2) * 512].rearrange("p (c n) -> p c n", n=512)

        def MM(out, lhsT, rhs, start, stop, reads=(), writes=()):
            return S.op("pe", lambda e: e.matmul(out, lhsT, rhs, start=start, stop=stop, skip_group_check=True),
                        reads=reads, writes=writes)

        def ACT(out, in_, func, reads=(), writes=(), **kw):
            return S.op("act", lambda e: e.activation(out, in_, func, **kw), reads=reads, writes=writes)

        def TT(eng, out, in0, in1, op, reads=(), writes=()):
            return S.op(eng, lambda e: e.tensor_tensor(out, in0, in1, op), reads=reads, writes=writes)

        def TS(eng, out, in0, s1, s2, op0, op1=None, reads=(), writes=()):
            if op1 is None:
                return S.op(eng, lambda e: e.tensor_scalar(out, in0, s1, None, op0), reads=reads, writes=writes)
            return S.op(eng, lambda e: e.tensor_scalar(out, in0, s1, s2, op0, op1), reads=reads, writes=writes)

        def CP(eng, out, in_, reads=(), writes=()):
            return S.op(eng, lambda e: e.tensor_copy(out, in_), reads=reads, writes=writes)

        def dump(name, ap, shape, reads=()):
            if not debug:
                return
            t = nc.dram_tensor("dbg_" + name, list(shape), F32, kind="ExternalOutput").ap()
            dbg_outs[name] = t
            S.dma("pool", t, ap, reads=reads)

        P = Bump(0, 16)
        cm = P([128, 7, 128], BF16)
        ident = cm[:, 0, :]
        triI = cm[:, 1, :]
        omt = cm[:, 2, :]
        m_sb = [cm[:, 3, :], cm[:, 4, :]]
        m_mla = [cm[:, 5, :], cm[:, 6, :]]
        zeros_bf = P([128, 128], BF16)
        ones_bf = P([128, 128], BF16)
        mod_sb = P([128, 24], F32)
        gs = P([128, 8], F32)
        small = P([128, 64], F32)
        cT = small[:, 0:8]
        badaT = small[:, 8:32]
        ng = small[:, 32:40]
        qg = small[:, 40:43]
        kvg = small[:, 43:45]
        epsc = small[:, 45:46]
        invf = small[:, 46:47]
        tmp8 = small[:, 48:56]
        stat = P([128, 64], F32)
        tabcos = P([128, 1536], F32)
        tabsin = P([128, 1536], F32)
        shift = mod_sb[:, 0:8]

        L = Bump(16, 52)
        ckvn = L([128, 2, S_LEN], BF16)
        kpe = L([128, S_LEN], BF16)
        cqn = L([128, 3, NQ], BF16)
        SBD = Bump(52, 132)
        KT = SBD([128, 4, S_LEN], BF16)
        Vsb = SBD([128, 32, 512], BF16)
        QT = SBD([128, 4, NQ], BF16)
        OA = Bump(132, 164)
        oaT = OA([128, 4, NQ], BF16)
        obT = OA([128, 4, NQ], BF16)

        WK = view(132 * 1024, [128, 8, 1344], BF16)
        w3 = w_in.rearrange("(k p) n -> p k n", p=128)

        def load_WK():
            for k in range(8):
                S.dma("pool", WK[:, k, 0:1024], w3[:, k, SBK:SBK + 1024], writes=["WK"])
                S.dma("pool", WK[:, k, 1024:1312], w3[:, k, CKV:CKV + 288], writes=["WK"])
                S.dma("pool", WK[:, k, 1312:1328], w3[:, k, KROT + 16:KROT + 32], writes=["WK"])
                S.dma("pool", WK[:, k, 1328:1344], w3[:, k, KROT:KROT + 16], writes=["WK"])
            TS("pool", WK[:, :, 1312:1328], WK[:, :, 1312:1328], -1.0, None, ALU.mult, reads=["WK"], writes=["WK"])

        def load_WQ():
            for k in range(8):
                S.dma("pool", WK[:, k, 0:512], w3[:, k, SBQ:SBQ + 512], writes=["WK"])
                S.dma("pool", WK[:, k, 512:896], w3[:, k, CQ:CQ + 384], writes=["WK"])

        A_ = Bump(52, 132)
        wst = [A_([128, D], F32) for _ in range(8)]
        posi = A_([128, 1536], I32)
        ang = A_([128, 1536], F32)
        tt = A_([128, 1536], F32)
        ki = A_([128, 1536], I32)
        kf = A_([128, 1536], F32)
        ff = A_([128, 1536], F32)

        S.dma("pool", cm, cmats, writes=["cm"])
        load_WK()
        S.dma("sp", cT, cT_d, writes=["small"])
        S.dma("sp", badaT, b_adaT, writes=["small"])
        S.dma("sp", ng, ngT, writes=["small"])
        S.dma("sp", qg, qgT, writes=["small"])
        S.dma("sp", kvg, kvgT, writes=["small"])
        S.dma("sp", invf, invf_d, writes=["small"])
        for q in range(4):
            S.dma("sp", posi[q * 32:(q + 1) * 32, :], posall[q:q + 1, :].broadcast_to([32, 1536]), writes=["posi"])
        S.op("dve", lambda e: e.memset(zeros_bf, 0.0), writes=["zeros"])
        S.op("dve", lambda e: e.memset(ones_bf, 1.0), writes=["ones"])
        S.op("dve", lambda e: e.memset(epsc, EPS), reads=["small"], writes=["small"])

        CP("dve", ang, posi, reads=["posi"], writes=["ang"])
        TS("dve", ang, ang, invf, None, ALU.mult, reads=["ang", "small"], writes=["ang"])
        inv2pi = 1.0 / (2.0 * math.pi)
        for (tab, phase, nm) in ((tabsin, 0.0, "sin"), (tabcos, 0.25, "cos")):
            TS("dve", tt, ang, inv2pi, phase, ALU.mult, ALU.add, reads=["ang"], writes=["tt"])
            CP("dve", ki, tt, reads=["tt"], writes=["ki"])
            CP("dve", kf, ki, reads=["ki"], writes=["kf"])
            TT("dve", ff, tt, kf, ALU.subtract, reads=["tt", "kf"], writes=["ff"])
            TS("dve", kf, ff, 0.5, None, ALU.is_gt, reads=["ff"], writes=["kf"])
            TT("dve", ff, ff, kf, ALU.subtract, reads=["ff", "kf"], writes=["ff"])
            TS("dve", kf, ff, -0.5, None, ALU.is_lt, reads=["ff"], writes=["kf"])
            TT("dve", ff, ff, kf, ALU.add, reads=["ff", "kf"], writes=["ff"])
            ACT(tab, ff, AF.Sin, reads=["ff"], writes=["tab" + nm], scale=6.283185)
        for half in range(2):
            for k in range(8):
                S.dma("sp", wst[k], w_ada[k * 128:(k + 1) * 128, half * D:(half + 1) * D], writes=["wst%d" % k])
            for jj in range(8):
                j = half * 8 + jj
                for k in range(8):
                    MM(bk[0][:, j:j + 1], wst[k][:, jj * 128:(jj + 1) * 128], cT[:, k:k + 1], k == 0, k == 7,
                       reads=["wst%d" % k, "small"], writes=["bk0"])
        TT("dve", mod_sb[:, 0:16], bk[0][:, 0:16], badaT[:, 0:16], ALU.add, reads=["bk0", "small"], writes=["mod"])
        TS("dve", tmp8, mod_sb[:, 8:16], 1.0, None, ALU.add, reads=["mod"], writes=["tmp8"])
        TT("dve", gs, tmp8, ng, ALU.mult, reads=["tmp8", "small"], writes=["gs"])

        dump("gs", gs, [128, 8], reads=["gs"])
        dump("mod", mod_sb[:, 0:16], [128, 16], reads=["mod"])
        dump("tabcos", tabcos, [128, 1536], reads=["tabcos"])
        dump("tabsin", tabsin, [128, 1536], reads=["tabsin"])
        S.barrier()

        B_ = Bump(164, 204)
        B2 = Bump(154, 164)
        jit = B2([128, 2, 512], F32)
        sqj = B2([128, D], BF16)
        sq = B2([128, 3, 512], BF16)
        xt = [B_([128, D], F32) for _ in range(2)]
        xn = B_([128, 4, D], BF16)
        hTs = [B_([128, 8, 512], BF16) for _ in range(2)]
        rbc = B_([128, 512], F32)
        t1 = B_([128, 512], F32)
        t2 = B_([128, 512], F32)

        state = {"xt": 0, "st": 0, "bank": 2, "ev": 0, "fill": 0, "sbu": 0}

        def next_bank():
            b = state["bank"]
            state["bank"] = 2 + (b - 2 + 1) % 6
            return b

        def evac_engine():
            state["ev"] ^= 1
            return "act" if state["ev"] else "dve"

        def norm_stats(src, xts):
            for blk in range(4):
                slot = state["xt"]
                state["xt"] ^= 1
                xk = "xt%d" % slot
                c = state["st"]
                state["st"] = (c + 1) % 8
                sk = "stat%d" % c
                S.dma("sp", xts[slot], src[blk * 128:(blk + 1) * 128, :], writes=[xk])
                ACT(sqj, xts[slot], AF.Square, reads=[xk], writes=["sqj", sk], accum_out=stat[:, c:c + 1])
                ACT(stat[:, 8 + c:9 + c], stat[:, c:c + 1], AF.Ln, reads=[sk, "small"], writes=[sk],
                    scale=1.0 / D, bias=epsc)
                ACT(stat[:, 16 + c:17 + c], stat[:, 8 + c:9 + c], AF.Exp, reads=[sk], writes=[sk], scale=-0.5)
                TS("dve", xn[:, blk, :], xts[slot], stat[:, 16 + c:17 + c], None, ALU.mult,
                   reads=[xk, sk], writes=["xn%d" % blk])

        def norm_T(hT):
            for k in range(8):
                tb = k % 2
                for blk in range(4):
                    S.op("pe", lambda e, k=k, blk=blk, tb=tb, xn_=xn: e.transpose(
                        bkb[tb][:, blk * 128:(blk + 1) * 128], xn_[:, blk, k * 128:(k + 1) * 128], ident),
                        reads=["xn%d" % blk, "cm"], writes=["bk%d" % tb])
                if evac_engine() == "act":
                    ACT(hT[:, k, :], bkb[tb][:, 0:512], AF.Identity, reads=["bk%d" % tb, "gs", "mod"],
                        writes=[("hT", id(hT))], scale=gs[:, k:k + 1], bias=shift[:, k:k + 1])
                else:
                    TS("dve", hT[:, k, :], bkb[tb][:, 0:512], gs[:, k:k + 1], shift[:, k:k + 1], ALU.mult, ALU.add,
                       reads=["bk%d" % tb, "gs", "mod"], writes=[("hT", id(hT))])

        def rms_sq(banks, nch):
            for c in range(nch):
                ACT(sq[:, c, :], bk[banks[c]][:, :], AF.Square, reads=["bk%d" % banks[c]], writes=["sq%d" % c])

        def rms_fin(banks, nch, dim, dst, t0, gkey):
            sb_ = next_bank()
            for c in range(nch):
                MM(bk[sb_][:, :], ones_bf, sq[:, c, :], c == 0, c == nch - 1, reads=["sq%d" % c, "ones"],
                   writes=["bk%d" % sb_])
            ACT(rbc, bk[sb_][:, :], AF.Ln, reads=["bk%d" % sb_, "small"], writes=["rbc"], scale=1.0 / dim, bias=epsc)
            ACT(rbc, rbc, AF.Exp, reads=["rbc"], writes=["rbc"], scale=-0.5)
            for c in range(nch):
                TT("dve", dst[:, c, t0:t0 + 512], bk[banks[c]][:, :], rbc, ALU.mult,
                   reads=["bk%d" % banks[c], "rbc"], writes=[gkey])

        def proj_K(tc):
            t0 = tc * 512
            hT = hTs[tc % 2]
            hk = ("hT", id(hT))
            for cc in range(4):
                b = next_bank()
                for k in range(8):
                    MM(bk[b][:, :], WK[:, k, cc * 128:(cc + 1) * 128], hT[:, k, :], k == 0, k == 7,
                       reads=["WK", hk], writes=["bk%d" % b])
                if evac_engine() == "act":
                    ACT(KT[:, cc, t0:t0 + 512], bk[b][:, :], AF.Copy, reads=["bk%d" % b], writes=["KT"], scale=0.125)
                else:
                    TS("dve", KT[:, cc, t0:t0 + 512], bk[b][:, :], 0.125, None, ALU.mult, reads=["bk%d" % b], writes=["KT"])
            for blk in range(4):
                b = next_bank()
                for k in range(8):
                    MM(bk[b][:, :], hT[:, k, blk * 128:(blk + 1) * 128], WK[:, k, 512:1024], k == 0, k == 7,
                       reads=["WK", hk], writes=["bk%d" % b])
                if evac_engine() == "act":
                    ACT(Vsb[:, tc * 4 + blk, :], bk[b][:, :], AF.Copy, reads=["bk%d" % b], writes=["Vsb"])
                else:
                    CP("dve", Vsb[:, tc * 4 + blk, :], bk[b][:, :], reads=["bk%d" % b], writes=["Vsb"])
            cb = []
            for c2 in range(2):
                b = next_bank()
                cb.append(b)
                for k in range(8):
                    MM(bk[b][:, :], WK[:, k, 1024 + c2 * 128:1024 + (c2 + 1) * 128], hT[:, k, :], k == 0, k == 7,
                       reads=["WK", hk], writes=["bk%d" % b])
            rms_sq(cb, 2)
            bA = next_bank()
            for k in range(8):
                MM(bk[bA][0:96, :], WK[:, k, 1216:1312], hT[:, k, :], k == 0, k == 7, reads=["WK", hk], writes=["bk%d" % bA])
            bB = next_bank()
            for k in range(8):
                MM(bk[bB][0:96, :], WK[:, k, 1248:1344], hT[:, k, :], k == 0, k == 7, reads=["WK", hk], writes=["bk%d" % bB])
            qd, col = tc // 3, (tc % 3) * 512
            S.dma("sp", jit[64:96, 0, :], tabcos[qd * 32:(qd + 1) * 32, col:col + 512], writes=["jit"])
            S.dma("sp", jit[64:96, 1, :], tabsin[qd * 32:(qd + 1) * 32, col:col + 512], writes=["jit"])
            TT("dve", t1[64:96, :], bk[bA][64:96, :], jit[64:96, 0, :], ALU.mult, reads=["bk%d" % bA, "jit"], writes=["t1"])
            TT("dve", t2[64:96, :], bk[bB][64:96, :], jit[64:96, 1, :], ALU.mult, reads=["bk%d" % bB, "jit"], writes=["t2"])
            TT("dve", kpe[64:96, t0:t0 + 512], t1[64:96, :], t2[64:96, :], ALU.add, reads=["t1", "t2"], writes=["kpe"])
            return lambda: rms_fin(cb, 2, 256, ckvn, t0, "ckvn")
        def proj_Q(qc):
            q0 = qc * 512
            hT = hTs[qc % 2]
            hk = ("hT", id(hT))
            for cc in range(4):
                b = next_bank()
                for k in range(8):
                    MM(bk[b][:, :], WK[:, k, cc * 128:(cc + 1) * 128], hT[:, k, :], k == 0, k == 7,
                       reads=["WK", hk], writes=["bk%d" % b])
                if evac_engine() == "act":
                    ACT(QT[:, cc, q0:q0 + 512], bk[b][:, :], AF.Copy, reads=["bk%d" % b], writes=["QT"])
                else:
                    CP("dve", QT[:, cc, q0:q0 + 512], bk[b][:, :], reads=["bk%d" % b], writes=["QT"])
            cb = []
            for c3 in range(3):
                b = next_bank()
                cb.append(b)
                for k in range(8):
                    MM(bk[b][:, :], WK[:, k, 512 + c3 * 128:512 + (c3 + 1) * 128], hT[:, k, :], k == 0, k == 7,
                       reads=["WK", hk], writes=["bk%d" % b])
            rms_sq(cb, 3)
            return lambda: rms_fin(cb, 3, 384, cqn, q0, "cqn")
        chunks = [("K", i) for i in range(8)] + [("Q", i) for i in range(4)]

        def src_of(ch):
            return (xf if ch[0] == "K" else xq)[ch[1] * 512:(ch[1] + 1) * 512, :]

        norm_stats(src_of(chunks[0]), xt)
        norm_T(hTs[0])
        for i, ch in enumerate(chunks):
            if i + 1 < len(chunks):
                norm_stats(src_of(chunks[i + 1]), xt)
            if ch == ("Q", 0):
                load_WQ()
            fin = (proj_K if ch[0] == "K" else proj_Q)(ch[1])
            if i + 1 < len(chunks):
                norm_T(hTs[(i + 1) % 2])
            fin()
        dump("KT", KT.rearrange("p a b -> p (a b)"), [128, 4 * S_LEN], reads=["KT"])
        dump("QT", QT.rearrange("p a b -> p (a b)"), [128, 4 * NQ], reads=["QT"])
        dump("Vsb", Vsb.rearrange("p a b -> p (a b)"), [128, 32 * 512], reads=["Vsb"])
        dump("ckvn", ckvn.rearrange("p a b -> p (a b)"), [128, 2 * S_LEN], reads=["ckvn"])
        dump("cqn", cqn.rearrange("p a b -> p (a b)"), [128, 3 * NQ], reads=["cqn"])
        dump("kpe", kpe[64:96, :], [32, S_LEN], reads=["kpe"])
        S.barrier()

        def chain_cols(g, kb):
            c0 = max(0, kb // 2 - 4 * g)
            masks = []
            for c in range(c0, 4):
                j = 4 * g + c
                if kb == 2 * j + 1:
                    masks.append((c, 0))
                elif kb == 2 * j:
                    masks.append((c, 1))
            return c0, masks

        def run_chains(chains):
            live = list(chains)
            while live:
                nxt = []
                for gen in live:
                    try:
                        next(gen)
                        nxt.append(gen)
                    except StopIteration:
                        pass
                live = nxt

        def run_slots(slots):
            slots = [list(x) for x in slots]
            cur = [sl.pop(0) if sl else None for sl in slots]
            while any(c is not None for c in cur):
                for i in range(len(cur)):
                    while cur[i] is not None:
                        try:
                            next(cur[i])
                            break
                        except StopIteration:
                            cur[i] = slots[i].pop(0) if slots[i] else None

        if stop_after != "B":
            C1 = Bump(164, 204)
            e_sb = C1([128, 3, 2, 512], F32)
            sp_sb = C1([128, 2, 2, 512], BF16)
            g_sb = C1([128, 2, 2, 512], F32)
            w_sb = C1([128, 2, 2, 512], BF16)
            zp, Ap = bkpair(0), bkpair(2)

            def sb_pair(hp, g):
                cc = hp
                units = list(range(8 * g + 7, -1, -1))
                nU = len(units)
                ub = state["sbu"]
                state["sbu"] += nU
                def fill(n, lo=0):
                    for _ in range(n):
                        fb = 6 + (state["fill"] % 2)
                        state["fill"] += 1
                        MM(bk[fb][:, lo:512], ident, QT[:, 0, lo:512], True, True)

                def lo_of(i):
                    return chain_cols(g, units[i])[0] * 128

                def qk(i):
                    kb = units[i]
                    c0, masks = chain_cols(g, kb)
                    lo = c0 * 128
                    for c in range(2):
                        bp = 64 * c
                        MM(bk[c][:, lo:512], KT[bp:bp + 64, cc, kb * 128:(kb + 1) * 128],
                           QT[bp:bp + 64, cc, g * 512 + lo:(g + 1) * 512], True, len(masks) == 0, writes=["z"])
                        for mi, (cb, which) in enumerate(masks):
                            MM(bk[c][:, cb * 128:(cb + 1) * 128], ident, m_sb[which], False, mi == len(masks) - 1, writes=["z"])

                def tri(i):
                    lo = lo_of(i)
                    for c in range(2):
                        MM(bk[2 + c][:, lo:512], triI, sp_sb[:, (ub + i) % 2, c, lo:512], False, True, reads=["sp%d" % ((ub + i) % 2)], writes=["A"])

                def omt_(i):
                    lo = lo_of(i)
                    for c in range(2):
                        MM(bk[2 + c][:, lo:512], omt, sp_sb[:, (ub + i) % 2, c, lo:512], False, True, reads=["sp%d" % ((ub + i) % 2)], writes=["A"])

                def pv(i):
                    lo = lo_of(i)
                    kb = units[i]
                    for c in range(2):
                        h = 2 * hp + c
                        MM(bk[4 + c][0:64, lo:512], Vsb[:, kb, h * 64:(h + 1) * 64], w_sb[:, (ub + i) % 2, c, lo:512], False, True,
                           reads=["w%d" % ((ub + i) % 2)], writes=["o"])

                def act_e(i):
                    lo = lo_of(i)
                    ACT(e_sb[:, (ub + i) % 3, :, lo:512], zp[:, :, lo:512], AF.Exp, reads=["z"], writes=["e%d" % ((ub + i) % 3)])

                def act_sp(i):
                    lo = lo_of(i)
                    ACT(sp_sb[:, (ub + i) % 2, :, lo:512], e_sb[:, (ub + i) % 3, :, lo:512], AF.Ln, reads=["e%d" % ((ub + i) % 3)],
                        writes=["sp%d" % ((ub + i) % 2)], bias=1.0)

                def act_g(i):
                    lo = lo_of(i)
                    ACT(g_sb[:, (ub + i) % 2, :, lo:512], Ap[:, :, lo:512], AF.Exp, reads=["A"], writes=["g%d" % ((ub + i) % 2)], scale=-1.0)

                def dve_w(i):
                    lo = lo_of(i)
                    TT("dve", w_sb[:, (ub + i) % 2, :, lo:512], e_sb[:, (ub + i) % 3, :, lo:512], g_sb[:, (ub + i) % 2, :, lo:512], ALU.mult,
                       reads=["e%d" % ((ub + i) % 3), "g%d" % ((ub + i) % 2)], writes=["w%d" % ((ub + i) % 2)])

                def prologue():
                    qk(0)
                    act_e(0)
                    act_sp(0)
                    if nU > 1:
                        qk(1)

                def main(next_prologue):
                    for c in range(2):
                        MM(bk[2 + c][:, :], zeros_bf, QT[:, 0, 0:512], True, True, reads=["zeros"], writes=["A"])
                        MM(bk[4 + c][0:64, :], zeros_bf[:, 0:64], QT[:, 0, 0:512], True, True, reads=["zeros"], writes=["o"])
                    for i in range(nU):
                        if i == nU - 1 and next_prologue is not None:
                            next_prologue()
                        tri(i)
                        fill(SB_FILL, lo_of(i))
                        if i + 1 < nU:
                            act_e(i + 1)
                        act_g(i)
                        if i + 1 < nU:
                            act_sp(i + 1)
                        if i >= 1:
                            pv(i - 1)
                        if i + 2 < nU:
                            qk(i + 2)
                        if i + 1 < nU:
                            omt_(i)
                        dve_w(i)
                    pv(nU - 1)
                    for c in range(2):
                        CP("dve", oaT[64 * c:64 * c + 64, cc, g * 512:(g + 1) * 512], bk[4 + c][0:64, :], reads=["o"], writes=["oaT"])

                return prologue, main

            sb_chains = [sb_pair(hp, g) for g in range(4) for hp in range(4)]
            sb_chains[0][0]()
            for ci_, (pro, main) in enumerate(sb_chains):
                main(sb_chains[ci_ + 1][0] if ci_ + 1 < len(sb_chains) else None)
            dump("oaT", oaT.rearrange("p a b -> p (a b)"), [128, 4 * NQ], reads=["oaT"])
            S.barrier()

        if stop_after not in ("B", "C1"):
            C2 = Bump(52, 132)
            Vm = C2([128, 32, 4, 192], BF16)
            KhT0_ = C2([128, S_LEN], BF16)
            QhT = [C2([128, NQ], BF16) for _ in range(2)]
            wukv = C2([128, 2, 1024], BF16)
            wuqa = C2([128, 3, 8, 128], BF16)
            wvc = C2([128, 2, 512], BF16)
            C2b = Bump(164, 204)
            wst2 = C2b([128, 3, 1024], F32)
            KhT = [KhT0_, view(164 * 1024, [128, S_LEN], BF16)]
            KKEY = ["KhT0", "wst2"]
            jq = C2b([128, NQ], F32)
            p_sb = C2b([128, 2, 2, 512], BF16)
            lnl = C2b([128, 512], F32)
            rinv = C2b([128, 512], F32)
            tq1 = C2b([128, 512], F32)
            tq2 = C2b([128, 512], F32)

            S.dma("sp", wst2[:, 0:2, :], w_ukv.rearrange("(k p) n -> p k n", p=128), writes=["wst2"])
            for k in range(2):
                TS("dve", wukv[:, k, :], wst2[:, k, :], kvg[:, k:k + 1], None, ALU.mult, reads=["wst2", "small"], writes=["wukv"])
            S.dma("sp", wst2[:, :, 0:768], w_uq.rearrange("(k p) n -> p k n", p=128), reads=[], writes=["wst2"])
            wst2q = wst2[:, :, 0:768].rearrange("p k (h c) -> p k h c", c=96)
            for k in range(3):
                TS("dve", wuqa[:, k, :, 0:96], wst2q[:, k, :, :], qg[:, k:k + 1], None, ALU.mult, reads=["wst2", "small"], writes=["wuqa"])
                TS("dve", wuqa[:, k, :, 96:112], wst2q[:, k, :, 80:96], qg[:, k:k + 1], -1.0, ALU.mult, ALU.mult,
                   reads=["wst2", "small"], writes=["wuqa"])
                TS("dve", wuqa[:, k, :, 112:128], wst2q[:, k, :, 64:80], qg[:, k:k + 1], None, ALU.mult,
                   reads=["wst2", "small"], writes=["wuqa"])
            for qc in range(4):
                ci = 8 + qc
                qd, col = ci // 3, (ci % 3) * 512
                S.dma("sp", jq[64:96, qc * 512:(qc + 1) * 512], tabcos[qd * 32:(qd + 1) * 32, col:col + 512], writes=["jq"])
                S.dma("sp", jq[96:128, qc * 512:(qc + 1) * 512], tabsin[qd * 32:(qd + 1) * 32, col:col + 512], writes=["jq"])
            for blk in range(32):
                S.op("pool", lambda e, blk=blk: e.memset(Vm[:, blk, :, 64:128], 1.0), writes=["Vm"])
            wukv4 = wukv.rearrange("p k (h c) -> p k h c", c=128)
            for k in range(2):
                CP("dve", wvc[:, k, :].rearrange("p (h c) -> p h c", c=64), wukv4[:, k, :, 64:128], reads=["wukv"], writes=["wvc"])
            for blk in range(32):
                b = 6 + blk % 2
                for k in range(2):
                    MM(bk[b][:, :], ckvn[:, k, blk * 128:(blk + 1) * 128],
                       wvc[:, k, :], k == 0, k == 1, reads=["wvc"], writes=["bk%d" % b])
                pv = bk[b][:, :].rearrange("p (q t c) -> p q t c", t=2, c=64)
                ACT(Vm[:, blk, :, 0:64], pv[:, :, 0, :], AF.Copy, reads=["bk%d" % b], writes=["Vm"])
                CP("dve", Vm[:, blk, :, 128:192], pv[:, :, 1, :], reads=["bk%d" % b], writes=["Vm"])

            def prep_head(h):
                Kh, Qh = KhT[h % 2], QhT[h % 2]
                kk, qk_ = KKEY[h % 2], "QhT%d" % (h % 2)
                for tc in range(8):
                    b = 6 + tc % 2
                    for k in range(2):
                        MM(bk[b][0:64, :], wukv[:, k, h * 128:h * 128 + 64], ckvn[:, k, tc * 512:(tc + 1) * 512], k == 0, k == 1,
                           reads=["wukv"], writes=["bk%d" % b])
                    CP("dve", Kh[0:64, tc * 512:(tc + 1) * 512], bk[b][0:64, :], reads=["bk%d" % b], writes=[kk])
                    yield
                for qc in range(4):
                    b = 6 + qc % 2
                    kb_ = "bk%d" % b
                    cs = slice(qc * 512, (qc + 1) * 512)
                    for k in range(3):
                        MM(bk[b][:, :], wuqa[:, k, h, :], cqn[:, k, cs], k == 0, k == 2, reads=["wuqa"], writes=[kb_])
                    CP("dve", Qh[0:64, cs], bk[b][0:64, :], reads=[kb_], writes=[qk_])
                    TT("dve", tq1[64:96, :], bk[b][64:96, :], jq[64:96, cs], ALU.mult, reads=[kb_, "jq"], writes=["tq1"])
                    TT("dve", tq2[64:96, :], bk[b][96:128, :], jq[96:128, cs], ALU.mult, reads=[kb_, "jq"], writes=["tq2"])
                    TT("dve", Qh[64:96, cs], tq1[64:96, :], tq2[64:96, :], ALU.add, reads=["tq1", "tq2"], writes=[qk_])
                    yield
                    yield

            sm_scale = 1.0 / math.sqrt(96.0)

            def mla_parts(h, g, ci):
                cc, bp = h // 2, (h % 2) * 64
                Kh, Qh = KhT[h % 2], QhT[h % 2]
                kk, qk_ = KKEY[h % 2], "QhT%d" % (h % 2)
                lb = 64 - bp
                vl = Vm[:, :, h // 2, bp:bp + 128]
                ob = 4 + ci % 2
                ko = "bk%d" % ob
                units = list(range(8 * g + 7, -1, -1))
                nP = len(units) // 2

                def qk(j):
                    zb = 2 * (j % 2)
                    kz = "zp%d" % (j % 2)
                    lo = chain_cols(g, units[2 * j])[0] * 128
                    for c in range(2):
                        kb = units[2 * j + c]
                        c0, masks = chain_cols(g, kb)
                        assert c0 * 128 == lo
                        MM(bk[zb + c][:, lo:512], Kh[0:96, kb * 128:(kb + 1) * 128], Qh[0:96, g * 512 + lo:(g + 1) * 512],
                           True, len(masks) == 0, reads=[kk, qk_], writes=[kz])
                        for mi, (cb, which) in enumerate(masks):
                            MM(bk[zb + c][:, cb * 128:(cb + 1) * 128], ident, m_mla[which], False, mi == len(masks) - 1, writes=[kz])
                    ACT(p_sb[:, j % 2, :, lo:512], bkpair(zb)[:, :, lo:512], AF.Exp, reads=[kz], writes=["p%d" % (j % 2)],
                        scale=sm_scale)

                def prologue():
                    MM(bk[ob][:, :], zeros_bf, ckvn[:, 0, 0:512], True, True, reads=["zeros"], writes=[ko])
                    qk(0)
                    if nP > 1:
                        qk(1)

                def rounds(stepper, next_prologue=None):
                    for j in range(nP):
                        lo = chain_cols(g, units[2 * j])[0] * 128
                        for c in range(2):
                            kb = units[2 * j + c]
                            MM(bk[ob][:, lo:512], vl[:, kb, :], p_sb[:, j % 2, c, lo:512], False, True,
                               reads=["p%d" % (j % 2), "Vm"], writes=[ko])
                        if j == nP - 1 and next_prologue is not None:
                            next_prologue()
                        if j + 2 < nP:
                            for fi in range(MLA_FILL):
                                MM(bk[2 * (j % 2) + fi % 2][:, :], ident, ckvn[:, 0, 0:512], True, True, writes=["zp%d" % (j % 2)])
                            qk(j + 2)
                        stepper()

                def tail():
                    ACT(lnl[bp:bp + 64, :], bk[ob][lb:lb + 64, :], AF.Ln, reads=[ko], writes=["lnl"])
                    ACT(rinv[bp:bp + 64, :], lnl[bp:bp + 64, :], AF.Exp, reads=["lnl"], writes=["rinv"], scale=-1.0)
                    TT("dve", obT[bp:bp + 64, cc, g * 512:(g + 1) * 512], bk[ob][bp:bp + 64, :], rinv[bp:bp + 64, :], ALU.mult,
                       reads=[ko, "rinv"], writes=["obT"])

                return prologue, rounds, tail

            for bi in range(2):
                for half in range(2):
                    hs_ = slice(half * 2048, (half + 1) * 2048)
                    CP("dve", KhT[bi][64:96, hs_], kpe[64:96, hs_], reads=["wuqa", "wukv"], writes=[KKEY[bi]])
            for _ in prep_head(0):
                pass
            if debug:
                dump("KhT0", KhT[0][0:96, :], [96, S_LEN], reads=["KhT0"])
                dump("QhT0", QhT[0][0:96, :], [96, NQ], reads=["QhT0"])
            mparts = [(h, g, mla_parts(h, g, 4 * h + gi)) for h in range(8) for gi, g in enumerate((3, 2, 1, 0))]
            mparts[0][2][0]()
            prev_tail = None
            prep = None
            for idx, (h, g, (prologue, rounds, tail)) in enumerate(mparts):
                if g == 3:
                    prep = prep_head(h + 1) if h + 1 < 8 else iter(())

                def stepper(prep=prep):
                    next(prep, None)

                nxt = None
                if idx + 1 < len(mparts):
                    nh = mparts[idx + 1][0]
                    npro = mparts[idx + 1][2][0]

                    def nxt(nh=nh, h=h, npro=npro, prep=prep):
                        if nh != h:
                            for _ in prep:
                                pass
                        npro()
                if prev_tail is not None:
                    prev_tail()
                rounds(stepper, nxt)
                prev_tail = tail
            prev_tail()
            dump("obT", obT.rearrange("p a b -> p (a b)"), [128, 4 * NQ], reads=["obT"])
            S.barrier()

        if stop_after is None:
            Dw = Bump(16, 132)
            WG = Dw([128, 8, 3072], BF16)
            WA = Dw([128, 4, D], BF16)
            WB = Dw([128, 4, D], BF16)
            WO = Dw([128, 8, D], BF16)
            gate_bc = Dw([128, D], F32)
            fg_bc = Dw([128, D], F32)
            hTd = Dw([128, 8, 512], BF16)
            merged_off = Dw.o
            mergedT = Dw([128, 8, 512], BF16)
            xnew_off = Dw.o
            xnew = Dw([128, D], F32)
            outt_off = Dw.o
            outt = Dw([128, D], F32)
            sqjD = Dw([128, D], BF16)
            Dt = Bump(164, 204)
            xtD = [Dt([128, D], F32) for _ in range(2)]
            xres = [Dt([128, D], F32) for _ in range(2)]
            xnD = Dt([128, 4, D], BF16)
            og_off = Dt.o
            ogA = Dt([128, 4, 512], BF16)
            ogB = Dt([128, 4, 512], BF16)
            xnew2 = view(og_off, [128, D], F32)
            outt2 = view(og_off + 4096, [128, D], F32)
            sg = [Dt([128, 512], F32) for _ in range(2)]
            tm = [Dt([128, 512], F32) for _ in range(2)]
            bgate_bc = view(merged_off, [128, D], F32)
            cbc = view(merged_off + 4096, [128, 8, 128], F32)
            wstD = [view(xnew_off, [128, D], F32), view(outt_off, [128, D], F32)]
            WSTK = ["xnew", "outt"]

            for k in range(8):
                S.dma("pool", WG[:, k, 0:512], w3[:, k, SBZ:SBZ + 512], writes=["WGz"])
                S.dma("pool", WG[:, k, 512:1024], w3[:, k, MLAZ:MLAZ + 512], writes=["WGz"])
            for k in range(8):
                S.dma("pool", WG[:, k, 1024:3072], w3[:, k, GA:GA + 2048], writes=["WGg"])
            for k in range(4):
                S.dma("pool", WA[:, k, :], w_a[k * 128:(k + 1) * 128, :], writes=["WA"])
                S.dma("pool", WB[:, k, :], w_b[k * 128:(k + 1) * 128, :], writes=["WB"])
            for k in range(8):
                S.dma("pool", WO[:, k, :], w_out[k * 128:(k + 1) * 128, :], writes=["WO"])

            xn = xnD
            sqj = sqjD
            norm_stats(xq[0:512, :], xtD)
            S.dma("sp", fg_bc, fg_row.broadcast_to([128, D]), writes=["fg_bc"])
            S.dma("sp", bgate_bc, b_gate.broadcast_to([128, D]), writes=["merged"])
            S.op("dve", lambda e: e.memset(cbc, 1.0), writes=["merged"])
            for k in range(8):
                TS("dve", cbc[:, k, :], cbc[:, k, :], cT[:, k:k + 1], None, ALU.mult, reads=["merged", "small"], writes=["merged"])
            for k in range(8):
                S.dma("sp", wstD[k % 2], w_ada[k * 128:(k + 1) * 128, 2 * D:3 * D], writes=[WSTK[k % 2]])
                for half in range(2):
                    MM(bk[half][:, :], cbc[:, k, :], wstD[k % 2][:, half * 512:(half + 1) * 512], k == 0, k == 7,
                       reads=["merged", WSTK[k % 2]], writes=["bk%d" % half])
            for half in range(2):
                TT("dve", gate_bc[:, half * 512:(half + 1) * 512], bk[half][:, :], bgate_bc[:, half * 512:(half + 1) * 512],
                   ALU.add, reads=["bk%d" % half, "merged"], writes=["gate_bc"])
            hkd = ("hT", id(hTd))
            norm_T(hTd)
            for qc in range(4):
                q0 = qc * 512
                if qc + 1 < 4:
                    norm_stats(xq[q0 + 512:q0 + 1024, :], xtD)
                for (off, oT, og, nm) in ((0, oaT, ogA, "ogA"), (512, obT, ogB, "ogB")):
                    for cc in range(4):
                        b = next_bank()
                        for k in range(8):
                            MM(bk[b][:, :], WG[:, k, off + cc * 128:off + (cc + 1) * 128], hTd[:, k, :], k == 0, k == 7,
                               reads=["WGz", hkd], writes=["bk%d" % b])
                        si = cc % 2
                        ACT(sg[si], bk[b][:, :], AF.Sigmoid, reads=["bk%d" % b], writes=["sg%d" % si])
                        TT("dve", tm[si], bk[b][:, :], sg[si], ALU.mult, reads=["bk%d" % b, "sg%d" % si], writes=["tm%d" % si])
                        TT("dve", og[:, cc, :], tm[si], oT[:, cc, q0:q0 + 512], ALU.mult, reads=["tm%d" % si], writes=[nm])
                for n in range(8):
                    bga, bgb, bya, byb = next_bank(), next_bank(), next_bank(), next_bank()
                    for k in range(8):
                        MM(bk[bga][:, :], WG[:, k, 1024 + n * 128:1024 + (n + 1) * 128], hTd[:, k, :], k == 0, k == 7,
                           reads=["WGg", hkd], writes=["bk%d" % bga])
                    for k in range(8):
                        MM(bk[bgb][:, :], WG[:, k, 2048 + n * 128:2048 + (n + 1) * 128], hTd[:, k, :], k == 0, k == 7,
                           reads=["WGg", hkd], writes=["bk%d" % bgb])
                    for k in range(4):
                        MM(bk[bya][:, :], WA[:, k, n * 128:(n + 1) * 128], ogA[:, k, :], k == 0, k == 3,
                           reads=["WA", "ogA"], writes=["bk%d" % bya])
                    for k in range(4):
                        MM(bk[byb][:, :], WB[:, k, n * 128:(n + 1) * 128], ogB[:, k, :], k == 0, k == 3,
                           reads=["WB", "ogB"], writes=["bk%d" % byb])
                    ACT(sg[0], bk[bga][:, :], AF.Sigmoid, reads=["bk%d" % bga], writes=["sg0"])
                    ACT(sg[1], bk[bgb][:, :], AF.Sigmoid, reads=["bk%d" % bgb], writes=["sg1"])
                    TT("dve", tm[0], bk[bya][:, :], sg[0], ALU.mult, reads=["bk%d" % bya, "sg0"], writes=["tm0"])
                    TT("dve", tm[1], bk[byb][:, :], sg[1], ALU.mult, reads=["bk%d" % byb, "sg1"], writes=["tm1"])
                    TT("dve", mergedT[:, n, :], tm[0], tm[1], ALU.add, reads=["tm0", "tm1"], writes=["merged"])
                if qc + 1 < 4:
                    norm_T(hTd)
                for blk in range(4):
                    rs = blk % 2
                    xk = "xres%d" % rs
                    xnw, xnk = (xnew, "xnew") if blk % 2 == 0 else (xnew2, "ogA")
                    ott, otk = (outt, "outt") if blk % 2 == 0 else (outt2, "ogB")
                    S.dma("sp", xres[rs], xq[q0 + blk * 128:q0 + (blk + 1) * 128, :], writes=[xk])
                    for half in range(2):
                        b = next_bank()
                        for k in range(8):
                            MM(bk[b][:, :], mergedT[:, k, blk * 128:(blk + 1) * 128], WO[:, k, half * 512:(half + 1) * 512],
                               k == 0, k == 7, reads=["merged", "WO"], writes=["bk%d" % b])
                        hs = slice(half * 512, (half + 1) * 512)
                        TT("dve", xnw[:, hs], bk[b][:, :], gate_bc[:, hs], ALU.mult, reads=["bk%d" % b, "gate_bc"], writes=[xnk])
                    TT("dve", xnw, xnw, xres[rs], ALU.add, reads=[xnk, xk], writes=[xnk])
                    c = state["st"]
                    state["st"] = (c + 1) % 8
                    sk = "stat%d" % c
                    ACT(sqjD, xnw, AF.Square, reads=[xnk], writes=["sqjD", sk], accum_out=stat[:, c:c + 1])
                    ACT(stat[:, 8 + c:9 + c], stat[:, c:c + 1], AF.Ln, reads=[sk, "small"], writes=[sk], scale=1.0 / D, bias=epsc)
                    ACT(stat[:, 16 + c:17 + c], stat[:, 8 + c:9 + c], AF.Exp, reads=[sk], writes=[sk], scale=-0.5)
                    S.op("dve", lambda e, c=c, ott=ott, xnw=xnw: e.scalar_tensor_tensor(ott, xnw, stat[:, 16 + c:17 + c], fg_bc, ALU.mult, ALU.mult),
                         reads=[xnk, sk, "fg_bc"], writes=[otk])
                    S.dma("pool", out_d[q0 + blk * 128:q0 + (blk + 1) * 128, :], ott, reads=[otk])

        for q in ("sp", "pool"):
            for i in range(max(0, S.dma_n[q] - ND), S.dma_n[q]):
                S._wait("pool", ("d", (q, i)))
        with nc.Block() as block:
            S.emit(block)
    return nc, dbg_outs


def host_inputs(inputs):
    x = np.asarray(inputs["x"], np.float32)
    c = np.asarray(inputs["c"], np.float32)
    pos = np.asarray(inputs["positions"], np.int32)
    f = lambda k: np.ascontiguousarray(np.asarray(inputs[k], np.float32))
    w_ada = f("w_ada")[0]
    b_ada = f("b_ada")[0]
    tri = np.tril(np.ones((128, 128), np.float32))
    ident = np.eye(128, dtype=np.float32)
    omt = 1.0 - tri
    ss, tt = np.meshgrid(np.arange(128), np.arange(128), indexing="ij")
    strict = np.where(ss < tt, 0.0, MASKV).astype(np.float32)
    causal = np.where(ss <= tt, 0.0, MASKV).astype(np.float32)
    allm = np.full((128, 128), MASKV, np.float32)
    nom = np.zeros((128, 128), np.float32)
    inv_freq = (10000.0 ** (-np.arange(0, 32, 2, dtype=np.float32) / np.float32(32))).astype(np.float32)
    invf = np.tile(np.concatenate([inv_freq, inv_freq]), 4).reshape(128, 1).astype(np.float32)
    common = {
        "w_ada": w_ada,
        "b_adaT": np.ascontiguousarray(b_ada.reshape(24, 128).T),
        "b_gate": np.ascontiguousarray(b_ada[2 * D:3 * D].reshape(1, D)),
        "ngT": np.ascontiguousarray(f("norm_gain")[0].reshape(8, 128).T),
        "w_in": f("w_in")[0],
        "qgT": np.ascontiguousarray(f("q_norm_gain")[0].reshape(3, 128).T),
        "w_uq": f("w_uq")[0],
        "kvgT": np.ascontiguousarray(f("kv_norm_gain")[0].reshape(2, 128).T),
        "w_ukv": f("w_ukv")[0],
        "w_a": f("w_branch_a")[0],
        "w_b": f("w_branch_b")[0],
        "w_out": f("w_out")[0],
        "fg_row": np.ascontiguousarray(f("final_norm_gain").reshape(1, D)),
        "invf": invf,
    }
    maps = []
    for core in range(8):
        b, p = core // 2, core % 2
        blocks = [2 * j + p for j in range(16)]
        xb = x[b].reshape(32, 128, D)
        xq = np.ascontiguousarray(xb[blocks].reshape(NQ, D))
        pq = pos[b].reshape(32, 128)[blocks].reshape(NQ)
        posall = np.ascontiguousarray(np.concatenate([pos[b], pq]).reshape(4, 1536).astype(np.int32))
        if p == 0:
            mats = [ident, tri, omt, allm, strict, allm, causal]
        else:
            mats = [ident, tri, omt, strict, nom, causal, nom]
        cm = np.ascontiguousarray(np.stack(mats, axis=1).astype(np.float32))
        m = dict(common)
        m.update({"xf": np.ascontiguousarray(x[b]), "xq": xq, "posall": posall,
                  "cT": np.ascontiguousarray(c[b].reshape(8, 128).T), "cmats": cm})
        maps.append(m)
    return maps


_CACHE = {}


def kernel(**inputs):
    maps = host_inputs(inputs)
    if "nc" not in _CACHE:
        _CACHE["nc"] = build_program()[0]
    nc = _CACHE["nc"]
    res = run_bass_kernel_spmd(nc, maps, core_ids=list(range(8)))
    out = np.zeros((4, 32, 128, D), np.float32)
    for core in range(8):
        b, p = core // 2, core % 2
        o = np.asarray(res.results[core]["out"], np.float32).reshape(16, 128, D)
        out[b, p::2] = o
    return out.reshape(4, S_LEN, D)
```

```python
import math
from contextlib import ExitStack
import numpy as np
import concourse.bass as bass
import concourse.mybir as mybir
from concourse.bass_utils import run_bass_kernel_spmd

F32 = mybir.dt.float32
BF16 = mybir.dt.bfloat16
I32 = mybir.dt.int32
AF = mybir.ActivationFunctionType
ALU = mybir.AluOpType

ENGS = ["pe", "act", "dve", "pool", "sp"]
PH = 2048
NPH = {"pe": 7, "act": 5, "dve": 4, "pool": 1, "sp": 1}
ND = 16

D = 1024
S_LEN = 4096
NQ = 2048
SBQ, SBK, SBV, SBZ, CQ, CKV, KROT, MLAZ, GA, GB = 0, 512, 1024, 1536, 2048, 2432, 2688, 2720, 3232, 4256
EPS = 1e-6
MASKV = -30000.0
SB_FILL = 4
MLA_FILL = 0
DEBUG = False


class Sched:
    def __init__(self, nc, esem, dsem):
        self.nc = nc
        self.esem = esem
        self.dsem = dsem
        self.ops = {e: [] for e in ENGS}
        self.cnt = {e: 0 for e in ENGS}
        self.waited_e = {e: {} for e in ENGS}
        self.waited_d = {e: {} for e in ENGS}
        self.last_w = {}
        self.readers = {}
        self.dma_i = 0
        self.dma_n = {"sp": 0, "pool": 0}

    def _wait(self, E, ev):
        if ev[0] == "e":
            _, e2, n = ev
            if e2 == E and E == "pe":
                return
            if self.waited_e[E].get(e2, 0) >= n:
                return
            self.waited_e[E][e2] = n
            sem = self.esem[e2][(n - 1) // PH]
            val = (n - 1) % PH + 1
        else:
            q, i = ev[1]
            k = i % ND
            val = 16 * (i // ND + 1)
            if self.waited_d[E].get((q, k), 0) >= val:
                return
            self.waited_d[E][(q, k)] = val
            sem = self.dsem[q][k]
        self.ops[E].append(lambda eng, sem=sem, val=val: eng.wait_ge(sem, val))

    def _deps(self, E, reads, writes, extra=(), dma_accum=False):
        deps = list(extra)
        for b in reads:
            for ev in self.last_w.get(b, ()):
                deps.append(ev)
        for b in writes:
            for ev in self.last_w.get(b, ()):
                if dma_accum and ev[0] == "d":
                    continue
                deps.append(ev)
            r = self.readers.get(b)
            if r:
                for e2, n in r[0].items():
                    deps.append(("e", e2, n))
                for i in r[1]:
                    deps.append(("d", i))
        for ev in deps:
            self._wait(E, ev)

    def _record(self, ev, reads, writes, dma_accum=False):
        for b in reads:
            r = self.readers.setdefault(b, ({}, []))
            if ev[0] == "e":
                r[0][ev[1]] = ev[2]
            else:
                r[1].append(ev[1])
        for b in writes:
            r = self.readers.get(b)
            had_readers = bool(r and (r[0] or r[1]))
            if dma_accum and not had_readers:
                self.last_w[b] = [e for e in self.last_w.get(b, ()) if e[0] == "d"] + [ev]
            else:
                self.last_w[b] = [ev]
            self.readers[b] = ({}, [])

    def op(self, E, fn, reads=(), writes=(), extra=()):
        self._deps(E, reads, writes, extra)
        n = self.cnt[E] + 1
        self.cnt[E] = n
        assert (n - 1) // PH < NPH[E], "too many instrs on %s" % E
        sem = self.esem[E][(n - 1) // PH]
        self.ops[E].append(lambda eng, fn=fn, sem=sem: fn(eng).then_inc(sem, 1))
        ev = ("e", E, n)
        self._record(ev, reads, writes)
        return ev

    def dma(self, Q, out, in_, reads=(), writes=(), extra=()):
        i = self.dma_n[Q]
        self.dma_n[Q] += 1
        self.dma_i += 1
        ex = list(extra)
        if i >= ND:
            ex.append(("d", (Q, i - ND)))
        self._deps(Q, reads, writes, ex, dma_accum=True)
        sem = self.dsem[Q][i % ND]
        self.ops[Q].append(
            lambda eng, out=out, in_=in_, sem=sem: eng.dma_start(out=out, in_=in_).then_inc(sem, 16))
        ev = ("d", (Q, i))
        self._record(ev, reads, writes, dma_accum=True)
        return ev

    def barrier(self):
        snap = dict(self.cnt)
        for E in ENGS:
            for e2 in ENGS:
                if e2 != E and snap[e2] > 0:
                    self._wait(E, ("e", e2, snap[e2]))

    def emit(self, block):
        ops = self.ops

        @block.tensor
        def _(eng):
            for f in ops["pe"]:
                f(eng)

        @block.scalar
        def _(eng):
            for f in ops["act"]:
                f(eng)

        @block.vector
        def _(eng):
            for f in ops["dve"]:
                f(eng)

        @block.gpsimd
        def _(eng):
            for f in ops["pool"]:
                f(eng)

        @block.sync
        def _(eng):
            for f in ops["sp"]:
                f(eng)


def build_program(debug=False, stop_after=None):
    nc = bass.Bass("TRN2", target_bir_lowering=False)

    def din(name, shape, dt=F32):
        return nc.dram_tensor(name, list(shape), dt, kind="ExternalInput").ap()

    xf = din("xf", [S_LEN, D])
    xq = din("xq", [NQ, D])
    posall = din("posall", [4, 1536], I32)
    cT_d = din("cT", [128, 8])
    w_ada = din("w_ada", [D, 3 * D])
    b_adaT = din("b_adaT", [128, 24])
    b_gate = din("b_gate", [1, D])
    ngT = din("ngT", [128, 8])
    w_in = din("w_in", [D, 5280])
    qgT = din("qgT", [128, 3])
    w_uq = din("w_uq", [384, 768])
    kvgT = din("kvgT", [128, 2])
    w_ukv = din("w_ukv", [256, 1024])
    w_a = din("w_a", [512, D])
    w_b = din("w_b", [512, D])
    w_out = din("w_out", [D, D])
    fg_row = din("fg_row", [1, D])
    cmats = din("cmats", [128, 7, 128])
    invf_d = din("invf", [128, 1])
    out_d = nc.dram_tensor("out", [NQ, D], F32, kind="ExternalOutput").ap()
    dbg_outs = {}

    with ExitStack() as es:
        esem = {e: [es.enter_context(nc.semaphore("s_%s_%d" % (e, i))) for i in range(NPH[e])] for e in ENGS}
        dsem = {q: [es.enter_context(nc.semaphore("d_%s_%d" % (q, i))) for i in range(ND)] for q in ("sp", "pool")}
        S = Sched(nc, esem, dsem)

        ARENA_KB = 204
        arena = es.enter_context(nc.sbuf_tensor("arena", [128, ARENA_KB * 512], BF16))
        arena32 = arena.bitcast(F32)
        arenai = arena.bitcast(I32)

        def view(off_bytes, shape, dt):
            n = int(np.prod(shape[1:]))
            esz = 2 if dt == BF16 else 4
            assert off_bytes % 4 == 0
            assert off_bytes + n * esz <= ARENA_KB * 1024, (off_bytes, shape)
            base = {BF16: arena, F32: arena32, I32: arenai}[dt]
            o = off_bytes // esz
            ap = base[:, o:o + n]
            if len(shape) == 3:
                ap = ap.rearrange("p (a b) -> p a b", b=shape[2])
            elif len(shape) == 4:
                ap = ap.rearrange("p (a b c) -> p a b c", b=shape[2], c=shape[3])
            return ap

        class Bump:
            def __init__(self, start_kb, end_kb):
                self.o = start_kb * 1024
                self.end = end_kb * 1024

            def __call__(self, shape, dt):
                n = int(np.prod(shape[1:])) * (2 if dt == BF16 else 4)
                n = (n + 3) // 4 * 4
                v = view(self.o, shape, dt)
                self.o += n
                assert self.o <= self.end, ("bump overflow", self.o, self.end)
                return v

        psum_all = es.enter_context(nc.psum_tensor("psall", [128, 4096], F32))
        psum_bf = psum_all.bitcast(BF16)
        bk = [psum_all[:, i * 512:(i + 1) * 512] for i in range(8)]
        bkb = [psum_bf[:, i * 1024:(i + 1) * 1024] for i in range(8)]

        def bkpair(i):
            return psum_all[:, i * 512:(i + 2) * 512].rearrange("p (c n) -> p c n", n=512)

        def MM(out, lhsT, rhs, start, stop, reads=(), writes=()):
            return S.op("pe", lambda e: e.matmul(out, lhsT, rhs, start=start, stop=stop, skip_group_check=True),
                        reads=reads, writes=writes)

        def ACT(out, in_, func, reads=(), writes=(), **kw):
            return S.op("act", lambda e: e.activation(out, in_, func, **kw), reads=reads, writes=writes)

        def TT(eng, out, in0, in1, op, reads=(), writes=()):
            return S.op(eng, lambda e: e.tensor_tensor(out, in0, in1, op), reads=reads, writes=writes)

        def TS(eng, out, in0, s1, s2, op0, op1=None, reads=(), writes=()):
            if op1 is None:
                return S.op(eng, lambda e: e.tensor_scalar(out, in0, s1, None, op0), reads=reads, writes=writes)
            return S.op(eng, lambda e: e.tensor_scalar(out, in0, s1, s2, op0, op1), reads=reads, writes=writes)

        def CP(eng, out, in_, reads=(), writes=()):
            return S.op(eng, lambda e: e.tensor_copy(out, in_), reads=reads, writes=writes)

        def dump(name, ap, shape, reads=()):
            if not debug:
                return
            t = nc.dram_tensor("dbg_" + name, list(shape), F32, kind="ExternalOutput").ap()
            dbg_outs[name] = t
            S.dma("pool", t, ap, reads=reads)

        P = Bump(0, 16)
        cm = P([128, 7, 128], BF16)
        ident = cm[:, 0, :]
        triI = cm[:, 1, :]
        omt = cm[:, 2, :]
        m_sb = [cm[:, 3, :], cm[:, 4, :]]
        m_mla = [cm[:, 5, :], cm[:, 6, :]]
        zeros_bf = P([128, 128], BF16)
        ones_bf = P([128, 128], BF16)
        mod_sb = P([128, 24], F32)
        gs = P([128, 8], F32)
        small = P([128, 64], F32)
        cT = small[:, 0:8]
        badaT = small[:, 8:32]
        ng = small[:, 32:40]
        qg = small[:, 40:43]
        kvg = small[:, 43:45]
        epsc = small[:, 45:46]
        invf = small[:, 46:47]
        tmp8 = small[:, 48:56]
        stat = P([128, 64], F32)
        tabcos = P([128, 1536], F32)
        tabsin = P([128, 1536], F32)
        shift = mod_sb[:, 0:8]

        L = Bump(16, 52)
        ckvn = L([128, 2, S_LEN], BF16)
        kpe = L([128, S_LEN], BF16)
        cqn = L([128, 3, NQ], BF16)
        SBD = Bump(52, 132)
        KT = SBD([128, 4, S_LEN], BF16)
        Vsb = SBD([128, 32, 512], BF16)
        QT = SBD([128, 4, NQ], BF16)
        OA = Bump(132, 164)
        oaT = OA([128, 4, NQ], BF16)
        obT = OA([128, 4, NQ], BF16)

        WK = view(132 * 1024, [128, 8, 1344], BF16)
        w3 = w_in.rearrange("(k p) n -> p k n", p=128)

        def load_WK():
            for k in range(8):
                S.dma("pool", WK[:, k, 0:1024], w3[:, k, SBK:SBK + 1024], writes=["WK"])
                S.dma("pool", WK[:, k, 1024:1312], w3[:, k, CKV:CKV + 288], writes=["WK"])
                S.dma("pool", WK[:, k, 1312:1328], w3[:, k, KROT + 16:KROT + 32], writes=["WK"])
                S.dma("pool", WK[:, k, 1328:1344], w3[:, k, KROT:KROT + 16], writes=["WK"])
            TS("pool", WK[:, :, 1312:1328], WK[:, :, 1312:1328], -1.0, None, ALU.mult, reads=["WK"], writes=["WK"])

        def load_WQ():
            for k in range(8):
                S.dma("pool", WK[:, k, 0:512], w3[:, k, SBQ:SBQ + 512], writes=["WK"])
                S.dma("pool", WK[:, k, 512:896], w3[:, k, CQ:CQ + 384], writes=["WK"])

        A_ = Bump(52, 132)
        wst = [A_([128, D], F32) for _ in range(8)]
        posi = A_([128, 1536], I32)
        ang = A_([128, 1536], F32)
        tt = A_([128, 1536], F32)
        ki = A_([128, 1536], I32)
        kf = A_([128, 1536], F32)
        ff = A_([128, 1536], F32)

        S.dma("pool", cm, cmats, writes=["cm"])
        load_WK()
        S.dma("sp", cT, cT_d, writes=["small"])
        S.dma("sp", badaT, b_adaT, writes=["small"])
        S.dma("sp", ng, ngT, writes=["small"])
        S.dma("sp", qg, qgT, writes=["small"])
        S.dma("sp", kvg, kvgT, writes=["small"])
        S.dma("sp", invf, invf_d, writes=["small"])
        for q in range(4):
            S.dma("sp", posi[q * 32:(q + 1) * 32, :], posall[q:q + 1, :].broadcast_to([32, 1536]), writes=["posi"])
        S.op("dve", lambda e: e.memset(zeros_bf, 0.0), writes=["zeros"])
        S.op("dve", lambda e: e.memset(ones_bf, 1.0), writes=["ones"])
        S.op("dve", lambda e: e.memset(epsc, EPS), reads=["small"], writes=["small"])

        CP("dve", ang, posi, reads=["posi"], writes=["ang"])
        TS("dve", ang, ang, invf, None, ALU.mult, reads=["ang", "small"], writes=["ang"])
        inv2pi = 1.0 / (2.0 * math.pi)
        for (tab, phase, nm) in ((tabsin, 0.0, "sin"), (tabcos, 0.25, "cos")):
            TS("dve", tt, ang, inv2pi, phase, ALU.mult, ALU.add, reads=["ang"], writes=["tt"])
            CP("dve", ki, tt, reads=["tt"], writes=["ki"])
            CP("dve", kf, ki, reads=["ki"], writes=["kf"])
            TT("dve", ff, tt, kf, ALU.subtract, reads=["tt", "kf"], writes=["ff"])
            TS("dve", kf, ff, 0.5, None, ALU.is_gt, reads=["ff"], writes=["kf"])
            TT("dve", ff, ff, kf, ALU.subtract, reads=["ff", "kf"], writes=["ff"])
            TS("dve", kf, ff, -0.5, None, ALU.is_lt, reads=["ff"], writes=["kf"])
            TT("dve", ff, ff, kf, ALU.add, reads=["ff", "kf"], writes=["ff"])
            ACT(tab, ff, AF.Sin, reads=["ff"], writes=["tab" + nm], scale=6.283185)
        for half in range(2):
            for k in range(8):
                S.dma("sp", wst[k], w_ada[k * 128:(k + 1) * 128, half * D:(half + 1) * D], writes=["wst%d" % k])
            for jj in range(8):
                j = half * 8 + jj
                for k in range(8):
                    MM(bk[0][:, j:j + 1], wst[k][:, jj * 128:(jj + 1) * 128], cT[:, k:k + 1], k == 0, k == 7,
                       reads=["wst%d" % k, "small"], writes=["bk0"])
        TT("dve", mod_sb[:, 0:16], bk[0][:, 0:16], badaT[:, 0:16], ALU.add, reads=["bk0", "small"], writes=["mod"])
        TS("dve", tmp8, mod_sb[:, 8:16], 1.0, None, ALU.add, reads=["mod"], writes=["tmp8"])
        TT("dve", gs, tmp8, ng, ALU.mult, reads=["tmp8", "small"], writes=["gs"])

        dump("gs", gs, [128, 8], reads=["gs"])
        dump("mod", mod_sb[:, 0:16], [128, 16], reads=["mod"])
        dump("tabcos", tabcos, [128, 1536], reads=["tabcos"])
        dump("tabsin", tabsin, [128, 1536], reads=["tabsin"])
        S.barrier()

        B_ = Bump(164, 204)
        B2 = Bump(154, 164)
        jit = B2([128, 2, 512], F32)
        sqj = B2([128, D], BF16)
        sq = B2([128, 3, 512], BF16)
        xt = [B_([128, D], F32) for _ in range(2)]
        xn = B_([128, 4, D], BF16)
        hTs = [B_([128, 8, 512], BF16) for _ in range(2)]
        rbc = B_([128, 512], F32)
        t1 = B_([128, 512], F32)
        t2 = B_([128, 512], F32)

        state = {"xt": 0, "st": 0, "bank": 2, "ev": 0, "fill": 0, "sbu": 0, "pend": []}

        def next_bank():
            b = state["bank"]
            state["bank"] = 2 + (b - 2 + 1) % 6
            return b

        def evac_engine():
            state["ev"] ^= 1
            return "act" if state["ev"] else "dve"

        def norm_stats_blk(src, xts, blk):
            slot = state["xt"]
            state["xt"] ^= 1
            xk = "xt%d" % slot
            c = state["st"]
            state["st"] = (c + 1) % 8
            sk = "stat%d" % c
            S.dma("sp", xts[slot], src[blk * 128:(blk + 1) * 128, :], writes=[xk])
            ACT(sqj, xts[slot], AF.Square, reads=[xk], writes=["sqj", sk], accum_out=stat[:, c:c + 1])
            ACT(stat[:, 8 + c:9 + c], stat[:, c:c + 1], AF.Ln, reads=[sk, "small"], writes=[sk],
                scale=1.0 / D, bias=epsc)
            ACT(stat[:, 16 + c:17 + c], stat[:, 8 + c:9 + c], AF.Exp, reads=[sk], writes=[sk], scale=-0.5)
            TS("dve", xn[:, blk, :], xts[slot], stat[:, 16 + c:17 + c], None, ALU.mult,
               reads=[xk, sk], writes=["xn%d" % blk])

        def norm_stats(src, xts):
            for blk in range(4):
                norm_stats_blk(src, xts, blk)

        def queue_stats(src, xts):
            state["pend"] = [(src, xts, blk) for blk in range(4)]

        def drain_stats(n=4):
            for _ in range(n):
                if state["pend"]:
                    a_, b_, c_ = state["pend"].pop(0)
                    norm_stats_blk(a_, b_, c_)

        def norm_T(hT):
            for k in range(8):
                tb = k % 2
                for blk in range(4):
                    S.op("pe", lambda e, k=k, blk=blk, tb=tb, xn_=xn: e.transpose(
                        bkb[tb][:, blk * 128:(blk + 1) * 128], xn_[:, blk, k * 128:(k + 1) * 128], ident),
                        reads=["xn%d" % blk, "cm"], writes=["bk%d" % tb])
                if evac_engine() == "act":
                    ACT(hT[:, k, :], bkb[tb][:, 0:512], AF.Identity, reads=["bk%d" % tb, "gs", "mod"],
                        writes=[("hT", id(hT), k)], scale=gs[:, k:k + 1], bias=shift[:, k:k + 1])
                else:
                    TS("dve", hT[:, k, :], bkb[tb][:, 0:512], gs[:, k:k + 1], shift[:, k:k + 1], ALU.mult, ALU.add,
                       reads=["bk%d" % tb, "gs", "mod"], writes=[("hT", id(hT), k)])

        def rms_sq(banks, nch):
            for c in range(nch):
                ACT(sq[:, c, :], bk[banks[c]][:, :], AF.Square, reads=["bk%d" % banks[c]], writes=["sq%d" % c])

        def rms_fin(banks, nch, dim, dst, t0, gkey):
            sb_ = next_bank()
            for c in range(nch):
                MM(bk[sb_][:, :], ones_bf, sq[:, c, :], c == 0, c == nch - 1, reads=["sq%d" % c, "ones"],
                   writes=["bk%d" % sb_])
            ACT(rbc, bk[sb_][:, :], AF.Ln, reads=["bk%d" % sb_, "small"], writes=["rbc"], scale=1.0 / dim, bias=epsc)
            ACT(rbc, rbc, AF.Exp, reads=["rbc"], writes=["rbc"], scale=-0.5)
            for c in range(nch):
                TT("dve", dst[:, c, t0:t0 + 512], bk[banks[c]][:, :], rbc, ALU.mult,
                   reads=["bk%d" % banks[c], "rbc"], writes=[gkey])

        def proj_K(tc):
            t0 = tc * 512
            hT = hTs[tc % 2]
            hk = ("hT", id(hT))
            for cc in range(4):
                b = next_bank()
                for k in range(8):
                    MM(bk[b][:, :], WK[:, k, cc * 128:(cc + 1) * 128], hT[:, k, :], k == 0, k == 7,
                       reads=["WK", hk + (k,)], writes=["bk%d" % b])
                if evac_engine() == "act":
                    ACT(KT[:, cc, t0:t0 + 512], bk[b][:, :], AF.Copy, reads=["bk%d" % b], writes=["KT"], scale=0.125)
                else:
                    TS("dve", KT[:, cc, t0:t0 + 512], bk[b][:, :], 0.125, None, ALU.mult, reads=["bk%d" % b], writes=["KT"])
                if cc % 2 == 1:
                    drain_stats(1)
            for blk in range(4):
                b = next_bank()
                for k in range(8):
                    MM(bk[b][:, :], hT[:, k, blk * 128:(blk + 1) * 128], WK[:, k, 512:1024], k == 0, k == 7,
                       reads=["WK", hk + (k,)], writes=["bk%d" % b])
                if evac_engine() == "act":
                    ACT(Vsb[:, tc * 4 + blk, :], bk[b][:, :], AF.Copy, reads=["bk%d" % b], writes=["Vsb"])
                else:
                    CP("dve", Vsb[:, tc * 4 + blk, :], bk[b][:, :], reads=["bk%d" % b], writes=["Vsb"])
                if blk % 2 == 1:
                    drain_stats(1)
            cb = []
            for c2 in range(2):
                b = next_bank()
                cb.append(b)
                for k in range(8):
                    MM(bk[b][:, :], WK[:, k, 1024 + c2 * 128:1024 + (c2 + 1) * 128], hT[:, k, :], k == 0, k == 7,
                       reads=["WK", hk + (k,)], writes=["bk%d" % b])
            rms_sq(cb, 2)
            bA = next_bank()
            for k in range(8):
                MM(bk[bA][0:96, :], WK[:, k, 1216:1312], hT[:, k, :], k == 0, k == 7, reads=["WK", hk + (k,)], writes=["bk%d" % bA])
            bB = next_bank()
            for k in range(8):
                MM(bk[bB][0:96, :], WK[:, k, 1248:1344], hT[:, k, :], k == 0, k == 7, reads=["WK", hk + (k,)], writes=["bk%d" % bB])
            qd, col = tc // 3, (tc % 3) * 512
            S.dma("sp", jit[64:96, 0, :], tabcos[qd * 32:(qd + 1) * 32, col:col + 512], writes=["jit"])
            S.dma("sp", jit[64:96, 1, :], tabsin[qd * 32:(qd + 1) * 32, col:col + 512], writes=["jit"])
            TT("dve", t1[64:96, :], bk[bA][64:96, :], jit[64:96, 0, :], ALU.mult, reads=["bk%d" % bA, "jit"], writes=["t1"])
            TT("dve", t2[64:96, :], bk[bB][64:96, :], jit[64:96, 1, :], ALU.mult, reads=["bk%d" % bB, "jit"], writes=["t2"])
            TT("dve", kpe[64:96, t0:t0 + 512], t1[64:96, :], t2[64:96, :], ALU.add, reads=["t1", "t2"], writes=["kpe"])
            return lambda: rms_fin(cb, 2, 256, ckvn, t0, "ckvn")
        def proj_Q(qc):
            q0 = qc * 512
            hT = hTs[qc % 2]
            hk = ("hT", id(hT))
            for cc in range(4):
                b = next_bank()
                for k in range(8):
                    MM(bk[b][:, :], WK[:, k, cc * 128:(cc + 1) * 128], hT[:, k, :], k == 0, k == 7,
                       reads=["WK", hk + (k,)], writes=["bk%d" % b])
                if evac_engine() == "act":
                    ACT(QT[:, cc, q0:q0 + 512], bk[b][:, :], AF.Copy, reads=["bk%d" % b], writes=["QT"])
                else:
                    CP("dve", QT[:, cc, q0:q0 + 512], bk[b][:, :], reads=["bk%d" % b], writes=["QT"])
                drain_stats(1)
            cb = []
            for c3 in range(3):
                b = next_bank()
                cb.append(b)
                for k in range(8):
                    MM(bk[b][:, :], WK[:, k, 512 + c3 * 128:512 + (c3 + 1) * 128], hT[:, k, :], k == 0, k == 7,
                       reads=["WK", hk + (k,)], writes=["bk%d" % b])
            rms_sq(cb, 3)
            return lambda: rms_fin(cb, 3, 384, cqn, q0, "cqn")
        chunks = [("K", i) for i in range(8)] + [("Q", i) for i in range(4)]

        def src_of(ch):
            return (xf if ch[0] == "K" else xq)[ch[1] * 512:(ch[1] + 1) * 512, :]

        norm_stats(src_of(chunks[0]), xt)
        norm_T(hTs[0])
        for i, ch in enumerate(chunks):
            if i + 1 < len(chunks):
                queue_stats(src_of(chunks[i + 1]), xt)
            if ch == ("Q", 0):
                load_WQ()
            fin = (proj_K if ch[0] == "K" else proj_Q)(ch[1])
            drain_stats(4)
            if i + 1 < len(chunks):
                norm_T(hTs[(i + 1) % 2])
            fin()
        dump("KT", KT.rearrange("p a b -> p (a b)"), [128, 4 * S_LEN], reads=["KT"])
        dump("QT", QT.rearrange("p a b -> p (a b)"), [128, 4 * NQ], reads=["QT"])
        dump("Vsb", Vsb.rearrange("p a b -> p (a b)"), [128, 32 * 512], reads=["Vsb"])
        dump("ckvn", ckvn.rearrange("p a b -> p (a b)"), [128, 2 * S_LEN], reads=["ckvn"])
        dump("cqn", cqn.rearrange("p a b -> p (a b)"), [128, 3 * NQ], reads=["cqn"])
        dump("kpe", kpe[64:96, :], [32, S_LEN], reads=["kpe"])
        S.barrier()

        def chain_cols(g, kb):
            c0 = max(0, kb // 2 - 4 * g)
            masks = []
            for c in range(c0, 4):
                j = 4 * g + c
                if kb == 2 * j + 1:
                    masks.append((c, 0))
                elif kb == 2 * j:
                    masks.append((c, 1))
            return c0, masks

        def run_chains(chains):
            live = list(chains)
            while live:
                nxt = []
                for gen in live:
                    try:
                        next(gen)
                        nxt.append(gen)
                    except StopIteration:
                        pass
                live = nxt

        def run_slots(slots):
            slots = [list(x) for x in slots]
            cur = [sl.pop(0) if sl else None for sl in slots]
            while any(c is not None for c in cur):
                for i in range(len(cur)):
                    while cur[i] is not None:
                        try:
                            next(cur[i])
                            break
                        except StopIteration:
                            cur[i] = slots[i].pop(0) if slots[i] else None

        if stop_after != "B":
            C1 = Bump(164, 204)
            e_sb = C1([128, 3, 2, 512], F32)
            sp_sb = C1([128, 2, 2, 512], BF16)
            g_sb = C1([128, 2, 2, 512], F32)
            w_sb = C1([128, 2, 2, 512], BF16)
            zp, Ap = bkpair(0), bkpair(2)

            def sb_pair(hp, g):
                cc = hp
                units = list(range(8 * g + 7, -1, -1))
                nU = len(units)
                ub = state["sbu"]
                state["sbu"] += nU
                def fill(n, lo=0):
                    for _ in range(n):
                        fb = 6 + (state["fill"] % 2)
                        state["fill"] += 1
                        MM(bk[fb][:, lo:512], ident, QT[:, 0, lo:512], True, True)

                def lo_of(i):
                    return chain_cols(g, units[i])[0] * 128

                def qk(i):
                    kb = units[i]
                    c0, masks = chain_cols(g, kb)
                    lo = c0 * 128
                    for c in range(2):
                        bp = 64 * c
                        MM(bk[c][:, lo:512], KT[bp:bp + 64, cc, kb * 128:(kb + 1) * 128],
                           QT[bp:bp + 64, cc, g * 512 + lo:(g + 1) * 512], True, len(masks) == 0, writes=["z"])
                        for mi, (cb, which) in enumerate(masks):
                            MM(bk[c][:, cb * 128:(cb + 1) * 128], ident, m_sb[which], False, mi == len(masks) - 1, writes=["z"])

                def tri(i):
                    lo = lo_of(i)
                    for c in range(2):
                        MM(bk[2 + c][:, lo:512], triI, sp_sb[:, (ub + i) % 2, c, lo:512], False, True, reads=["sp%d" % ((ub + i) % 2)], writes=["A"])

                def omt_(i):
                    lo = lo_of(i)
                    for c in range(2):
                        MM(bk[2 + c][:, lo:512], omt, sp_sb[:, (ub + i) % 2, c, lo:512], False, True, reads=["sp%d" % ((ub + i) % 2)], writes=["A"])

                def pv(i):
                    lo = lo_of(i)
                    kb = units[i]
                    for c in range(2):
                        h = 2 * hp + c
                        MM(bk[4 + c][0:64, lo:512], Vsb[:, kb, h * 64:(h + 1) * 64], w_sb[:, (ub + i) % 2, c, lo:512], False, True,
                           reads=["w%d" % ((ub + i) % 2)], writes=["o"])

                def act_e(i):
                    lo = lo_of(i)
                    ACT(e_sb[:, (ub + i) % 3, :, lo:512], zp[:, :, lo:512], AF.Exp, reads=["z"], writes=["e%d" % ((ub + i) % 3)])

                def act_sp(i):
                    lo = lo_of(i)
                    ACT(sp_sb[:, (ub + i) % 2, :, lo:512], e_sb[:, (ub + i) % 3, :, lo:512], AF.Ln, reads=["e%d" % ((ub + i) % 3)],
                        writes=["sp%d" % ((ub + i) % 2)], bias=1.0)

                def act_g(i):
                    lo = lo_of(i)
                    ACT(g_sb[:, (ub + i) % 2, :, lo:512], Ap[:, :, lo:512], AF.Exp, reads=["A"], writes=["g%d" % ((ub + i) % 2)], scale=-1.0)

                def dve_w(i):
                    lo = lo_of(i)
                    TT("dve", w_sb[:, (ub + i) % 2, :, lo:512], e_sb[:, (ub + i) % 3, :, lo:512], g_sb[:, (ub + i) % 2, :, lo:512], ALU.mult,
                       reads=["e%d" % ((ub + i) % 3), "g%d" % ((ub + i) % 2)], writes=["w%d" % ((ub + i) % 2)])

                def prologue():
                    qk(0)
                    act_e(0)
                    act_sp(0)
                    if nU > 1:
                        qk(1)

                def main(next_prologue):
                    for c in range(2):
                        MM(bk[2 + c][:, :], zeros_bf, QT[:, 0, 0:512], True, True, reads=["zeros"], writes=["A"])
                        MM(bk[4 + c][0:64, :], zeros_bf[:, 0:64], QT[:, 0, 0:512], True, True, reads=["zeros"], writes=["o"])
                    for i in range(nU):
                        if i == nU - 1 and next_prologue is not None:
                            next_prologue()
                        tri(i)
                        fill(SB_FILL, lo_of(i))
                        if i + 1 < nU:
                            act_e(i + 1)
                        act_g(i)
                        if i + 1 < nU:
                            act_sp(i + 1)
                        if i >= 1:
                            pv(i - 1)
                        if i + 2 < nU:
                            qk(i + 2)
                        if i + 1 < nU:
                            omt_(i)
                        dve_w(i)
                    pv(nU - 1)
                    for c in range(2):
                        CP("dve", oaT[64 * c:64 * c + 64, cc, g * 512:(g + 1) * 512], bk[4 + c][0:64, :], reads=["o"], writes=["oaT"])

                return prologue, main

            sb_chains = [sb_pair(hp, g) for g in range(4) for hp in range(4)]
            sb_chains[0][0]()
            for ci_, (pro, main) in enumerate(sb_chains):
                main(sb_chains[ci_ + 1][0] if ci_ + 1 < len(sb_chains) else None)
            dump("oaT", oaT.rearrange("p a b -> p (a b)"), [128, 4 * NQ], reads=["oaT"])
            S.barrier()

        if stop_after not in ("B", "C1"):
            C2 = Bump(52, 132)
            Vm = C2([128, 32, 4, 192], BF16)
            KhT0_ = C2([128, S_LEN], BF16)
            QhT = [C2([128, NQ], BF16) for _ in range(2)]
            wukv = C2([128, 2, 1024], BF16)
            wuqa = C2([128, 3, 8, 128], BF16)
            wvc = C2([128, 2, 512], BF16)
            C2b = Bump(164, 204)
            wst2 = C2b([128, 3, 1024], F32)
            KhT = [KhT0_, view(164 * 1024, [128, S_LEN], BF16)]
            KKEY = ["KhT0", "wst2"]
            jq = C2b([128, NQ], F32)
            p_sb = C2b([128, 2, 2, 512], BF16)
            lnl = C2b([128, 512], F32)
            rinv = C2b([128, 512], F32)
            tq1 = C2b([128, 512], F32)
            tq2 = C2b([128, 512], F32)

            S.dma("sp", wst2[:, 0:2, :], w_ukv.rearrange("(k p) n -> p k n", p=128), writes=["wst2"])
            for k in range(2):
                TS("dve", wukv[:, k, :], wst2[:, k, :], kvg[:, k:k + 1], None, ALU.mult, reads=["wst2", "small"], writes=["wukv"])
            S.dma("sp", wst2[:, :, 0:768], w_uq.rearrange("(k p) n -> p k n", p=128), reads=[], writes=["wst2"])
            wst2q = wst2[:, :, 0:768].rearrange("p k (h c) -> p k h c", c=96)
            for k in range(3):
                TS("dve", wuqa[:, k, :, 0:96], wst2q[:, k, :, :], qg[:, k:k + 1], None, ALU.mult, reads=["wst2", "small"], writes=["wuqa"])
                TS("dve", wuqa[:, k, :, 96:112], wst2q[:, k, :, 80:96], qg[:, k:k + 1], -1.0, ALU.mult, ALU.mult,
                   reads=["wst2", "small"], writes=["wuqa"])
                TS("dve", wuqa[:, k, :, 112:128], wst2q[:, k, :, 64:80], qg[:, k:k + 1], None, ALU.mult,
                   reads=["wst2", "small"], writes=["wuqa"])
            for qc in range(4):
                ci = 8 + qc
                qd, col = ci // 3, (ci % 3) * 512
                S.dma("sp", jq[64:96, qc * 512:(qc + 1) * 512], tabcos[qd * 32:(qd + 1) * 32, col:col + 512], writes=["jq"])
                S.dma("sp", jq[96:128, qc * 512:(qc + 1) * 512], tabsin[qd * 32:(qd + 1) * 32, col:col + 512], writes=["jq"])
            for blk in range(32):
                S.op("pool", lambda e, blk=blk: e.memset(Vm[:, blk, :, 64:128], 1.0), writes=["Vm"])
            wukv4 = wukv.rearrange("p k (h c) -> p k h c", c=128)
            for k in range(2):
                CP("dve", wvc[:, k, :].rearrange("p (h c) -> p h c", c=64), wukv4[:, k, :, 64:128], reads=["wukv"], writes=["wvc"])
            for blk in range(32):
                b = 6 + blk % 2
                for k in range(2):
                    MM(bk[b][:, :], ckvn[:, k, blk * 128:(blk + 1) * 128],
                       wvc[:, k, :], k == 0, k == 1, reads=["wvc"], writes=["bk%d" % b])
                pv = bk[b][:, :].rearrange("p (q t c) -> p q t c", t=2, c=64)
                ACT(Vm[:, blk, :, 0:64], pv[:, :, 0, :], AF.Copy, reads=["bk%d" % b], writes=["Vm"])
                CP("dve", Vm[:, blk, :, 128:192], pv[:, :, 1, :], reads=["bk%d" % b], writes=["Vm"])

            def prep_head(h):
                Kh, Qh = KhT[h % 2], QhT[h % 2]
                kk, qk_ = KKEY[h % 2], "QhT%d" % (h % 2)
                for tc in range(8):
                    b = 6 + tc % 2
                    for k in range(2):
                        MM(bk[b][0:64, :], wukv[:, k, h * 128:h * 128 + 64], ckvn[:, k, tc * 512:(tc + 1) * 512], k == 0, k == 1,
                           reads=["wukv"], writes=["bk%d" % b])
                    CP("dve", Kh[0:64, tc * 512:(tc + 1) * 512], bk[b][0:64, :], reads=["bk%d" % b], writes=[kk])
                    yield
                for qc in range(4):
                    b = 6 + qc % 2
                    kb_ = "bk%d" % b
                    cs = slice(qc * 512, (qc + 1) * 512)
                    for k in range(3):
                        MM(bk[b][:, :], wuqa[:, k, h, :], cqn[:, k, cs], k == 0, k == 2, reads=["wuqa"], writes=[kb_])
                    CP("dve", Qh[0:64, cs], bk[b][0:64, :], reads=[kb_], writes=[qk_])
                    TT("dve", tq1[64:96, :], bk[b][64:96, :], jq[64:96, cs], ALU.mult, reads=[kb_, "jq"], writes=["tq1"])
                    TT("dve", tq2[64:96, :], bk[b][96:128, :], jq[96:128, cs], ALU.mult, reads=[kb_, "jq"], writes=["tq2"])
                    TT("dve", Qh[64:96, cs], tq1[64:96, :], tq2[64:96, :], ALU.add, reads=["tq1", "tq2"], writes=[qk_])
                    yield
                    yield

            sm_scale = 1.0 / math.sqrt(96.0)

            def mla_parts(h, g, ci):
                cc, bp = h // 2, (h % 2) * 64
                Kh, Qh = KhT[h % 2], QhT[h % 2]
                kk, qk_ = KKEY[h % 2], "QhT%d" % (h % 2)
                lb = 64 - bp
                vl = Vm[:, :, h // 2, bp:bp + 128]
                ob = 4 + ci % 2
                ko = "bk%d" % ob
                units = list(range(8 * g + 7, -1, -1))
                nP = len(units) // 2

                def qk(j):
                    zb = 2 * (j % 2)
                    kz = "zp%d" % (j % 2)
                    lo = chain_cols(g, units[2 * j])[0] * 128
                    for c in range(2):
                        kb = units[2 * j + c]
                        c0, masks = chain_cols(g, kb)
                        assert c0 * 128 == lo
                        MM(bk[zb + c][:, lo:512], Kh[0:96, kb * 128:(kb + 1) * 128], Qh[0:96, g * 512 + lo:(g + 1) * 512],
                           True, len(masks) == 0, reads=[kk, qk_], writes=[kz])
                        for mi, (cb, which) in enumerate(masks):
                            MM(bk[zb + c][:, cb * 128:(cb + 1) * 128], ident, m_mla[which], False, mi == len(masks) - 1, writes=[kz])
                    ACT(p_sb[:, j % 2, :, lo:512], bkpair(zb)[:, :, lo:512], AF.Exp, reads=[kz], writes=["p%d" % (j % 2)],
                        scale=sm_scale)

                def prologue():
                    MM(bk[ob][:, :], zeros_bf, ckvn[:, 0, 0:512], True, True, reads=["zeros"], writes=[ko])
                    qk(0)
                    if nP > 1:
                        qk(1)

                def rounds(stepper, next_prologue=None):
                    for j in range(nP):
                        lo = chain_cols(g, units[2 * j])[0] * 128
                        for c in range(2):
                            kb = units[2 * j + c]
                            MM(bk[ob][:, lo:512], vl[:, kb, :], p_sb[:, j % 2, c, lo:512], False, True,
                               reads=["p%d" % (j % 2), "Vm"], writes=[ko])
                        if j == nP - 1 and next_prologue is not None:
                            next_prologue()
                        if j + 2 < nP:
                            for fi in range(MLA_FILL):
                                MM(bk[2 * (j % 2) + fi % 2][:, :], ident, ckvn[:, 0, 0:512], True, True, writes=["zp%d" % (j % 2)])
                            qk(j + 2)
                        stepper()

                def tail():
                    ACT(lnl[bp:bp + 64, :], bk[ob][lb:lb + 64, :], AF.Ln, reads=[ko], writes=["lnl"])
                    ACT(rinv[bp:bp + 64, :], lnl[bp:bp + 64, :], AF.Exp, reads=["lnl"], writes=["rinv"], scale=-1.0)
                    TT("dve", obT[bp:bp + 64, cc, g * 512:(g + 1) * 512], bk[ob][bp:bp + 64, :], rinv[bp:bp + 64, :], ALU.mult,
                       reads=[ko, "rinv"], writes=["obT"])

                return prologue, rounds, tail

            for bi in range(2):
                for half in range(2):
                    hs_ = slice(half * 2048, (half + 1) * 2048)
                    CP("dve", KhT[bi][64:96, hs_], kpe[64:96, hs_], reads=["wuqa", "wukv"], writes=[KKEY[bi]])
            for _ in prep_head(0):
                pass
            if debug:
                dump("KhT0", KhT[0][0:96, :], [96, S_LEN], reads=["KhT0"])
                dump("QhT0", QhT[0][0:96, :], [96, NQ], reads=["QhT0"])
            mparts = [(h, g, mla_parts(h, g, 4 * h + gi)) for h in range(8) for gi, g in enumerate((3, 2, 1, 0))]
            mparts[0][2][0]()
            prev_tail = None
            prep = None
            for idx, (h, g, (prologue, rounds, tail)) in enumerate(mparts):
                if g == 3:
                    prep = prep_head(h + 1) if h + 1 < 8 else iter(())

                def stepper(prep=prep):
                    next(prep, None)

                nxt = None
                if idx + 1 < len(mparts):
                    nh = mparts[idx + 1][0]
                    npro = mparts[idx + 1][2][0]

                    def nxt(nh=nh, h=h, npro=npro, prep=prep):
                        if nh != h:
                            for _ in prep:
                                pass
                        npro()
                if prev_tail is not None:
                    prev_tail()
                rounds(stepper, nxt)
                prev_tail = tail
            prev_tail()
            dump("obT", obT.rearrange("p a b -> p (a b)"), [128, 4 * NQ], reads=["obT"])
            S.barrier()

        if stop_after is None:
            Dw = Bump(16, 132)
            WG = Dw([128, 8, 3072], BF16)
            WA = Dw([128, 4, D], BF16)
            WB = Dw([128, 4, D], BF16)
            WO = Dw([128, 8, D], BF16)
            gate_bc = Dw([128, D], F32)
            fg_bc = Dw([128, D], F32)
            hTd = Dw([128, 8, 512], BF16)
            merged_off = Dw.o
            mergedT = Dw([128, 8, 512], BF16)
            xnew_off = Dw.o
            xnew = Dw([128, D], F32)
            outt_off = Dw.o
            outt = Dw([128, D], F32)
            sqjD = Dw([128, D], BF16)
            Dt = Bump(164, 204)
            xtD = [Dt([128, D], F32) for _ in range(2)]
            xres = [Dt([128, D], F32) for _ in range(2)]
            xnD = Dt([128, 4, D], BF16)
            og_off = Dt.o
            ogA = Dt([128, 4, 512], BF16)
            ogB = Dt([128, 4, 512], BF16)
            xnew2 = view(og_off, [128, D], F32)
            outt2 = view(og_off + 4096, [128, D], F32)
            sg = [Dt([128, 512], F32) for _ in range(2)]
            tm = [Dt([128, 512], F32) for _ in range(2)]
            bgate_bc = view(merged_off, [128, D], F32)
            cbc = view(merged_off + 4096, [128, 8, 128], F32)
            wstD = [view(xnew_off, [128, D], F32), view(outt_off, [128, D], F32)]
            WSTK = ["xnew", "outt"]

            for k in range(8):
                S.dma("pool", WG[:, k, 0:512], w3[:, k, SBZ:SBZ + 512], writes=["WGz"])
                S.dma("pool", WG[:, k, 512:1024], w3[:, k, MLAZ:MLAZ + 512], writes=["WGz"])
            for k in range(8):
                S.dma("pool", WG[:, k, 1024:3072], w3[:, k, GA:GA + 2048], writes=["WGg"])
            for k in range(4):
                S.dma("pool", WA[:, k, :], w_a[k * 128:(k + 1) * 128, :], writes=["WA"])
                S.dma("pool", WB[:, k, :], w_b[k * 128:(k + 1) * 128, :], writes=["WB"])
            for k in range(8):
                S.dma("pool", WO[:, k, :], w_out[k * 128:(k + 1) * 128, :], writes=["WO"])

            xn = xnD
            sqj = sqjD
            norm_stats(xq[0:512, :], xtD)
            S.dma("sp", fg_bc, fg_row.broadcast_to([128, D]), writes=["fg_bc"])
            S.dma("sp", bgate_bc, b_gate.broadcast_to([128, D]), writes=["merged"])
            S.op("dve", lambda e: e.memset(cbc, 1.0), writes=["merged"])
            for k in range(8):
                TS("dve", cbc[:, k, :], cbc[:, k, :], cT[:, k:k + 1], None, ALU.mult, reads=["merged", "small"], writes=["merged"])
            for k in range(8):
                S.dma("sp", wstD[k % 2], w_ada[k * 128:(k + 1) * 128, 2 * D:3 * D], writes=[WSTK[k % 2]])
                for half in range(2):
                    MM(bk[half][:, :], cbc[:, k, :], wstD[k % 2][:, half * 512:(half + 1) * 512], k == 0, k == 7,
                       reads=["merged", WSTK[k % 2]], writes=["bk%d" % half])
            for half in range(2):
                TT("dve", gate_bc[:, half * 512:(half + 1) * 512], bk[half][:, :], bgate_bc[:, half * 512:(half + 1) * 512],
                   ALU.add, reads=["bk%d" % half, "merged"], writes=["gate_bc"])
            hkd = ("hT", id(hTd))
            norm_T(hTd)
            for qc in range(4):
                q0 = qc * 512
                if qc + 1 < 4:
                    queue_stats(xq[q0 + 512:q0 + 1024, :], xtD)
                for (off, oT, og, nm) in ((0, oaT, ogA, "ogA"), (512, obT, ogB, "ogB")):
                    for cc in range(4):
                        b = next_bank()
                        for k in range(8):
                            MM(bk[b][:, :], WG[:, k, off + cc * 128:off + (cc + 1) * 128], hTd[:, k, :], k == 0, k == 7,
                               reads=["WGz", hkd + (k,)], writes=["bk%d" % b])
                        si = cc % 2
                        ACT(sg[si], bk[b][:, :], AF.Sigmoid, reads=["bk%d" % b], writes=["sg%d" % si])
                        TT("dve", tm[si], bk[b][:, :], sg[si], ALU.mult, reads=["bk%d" % b, "sg%d" % si], writes=["tm%d" % si])
                        TT("dve", og[:, cc, :], tm[si], oT[:, cc, q0:q0 + 512], ALU.mult, reads=["tm%d" % si], writes=[nm])
                for n in range(8):
                    bga, bgb, bya, byb = next_bank(), next_bank(), next_bank(), next_bank()
                    for k in range(8):
                        MM(bk[bga][:, :], WG[:, k, 1024 + n * 128:1024 + (n + 1) * 128], hTd[:, k, :], k == 0, k == 7,
                           reads=["WGg", hkd + (k,)], writes=["bk%d" % bga])
                    for k in range(8):
                        MM(bk[bgb][:, :], WG[:, k, 2048 + n * 128:2048 + (n + 1) * 128], hTd[:, k, :], k == 0, k == 7,
                           reads=["WGg", hkd + (k,)], writes=["bk%d" % bgb])
                    for k in range(4):
                        MM(bk[bya][:, :], WA[:, k, n * 128:(n + 1) * 128], ogA[:, k, :], k == 0, k == 3,
                           reads=["WA", "ogA"], writes=["bk%d" % bya])
                    for k in range(4):
                        MM(bk[byb][:, :], WB[:, k, n * 128:(n + 1) * 128], ogB[:, k, :], k == 0, k == 3,
                           reads=["WB", "ogB"], writes=["bk%d" % byb])
                    ACT(sg[0], bk[bga][:, :], AF.Sigmoid, reads=["bk%d" % bga], writes=["sg0"])
                    ACT(sg[1], bk[bgb][:, :], AF.Sigmoid, reads=["bk%d" % bgb], writes=["sg1"])
                    TT("dve", tm[0], bk[bya][:, :], sg[0], ALU.mult, reads=["bk%d" % bya, "sg0"], writes=["tm0"])
                    TT("dve", tm[1], bk[byb][:, :], sg[1], ALU.mult, reads=["bk%d" % byb, "sg1"], writes=["tm1"])
                    TT("dve", mergedT[:, n, :], tm[0], tm[1], ALU.add, reads=["tm0", "tm1"], writes=["merged"])
                    if n % 2 == 1:
                        drain_stats(1)
                drain_stats(4)
                if qc + 1 < 4:
                    norm_T(hTd)
                for blk in range(4):
                    rs = blk % 2
                    xk = "xres%d" % rs
                    xnw, xnk = (xnew, "xnew") if blk % 2 == 0 else (xnew2, "ogA")
                    ott, otk = (outt, "outt") if blk % 2 == 0 else (outt2, "ogB")
                    S.dma("sp", xres[rs], xq[q0 + blk * 128:q0 + (blk + 1) * 128, :], writes=[xk])
                    for half in range(2):
                        b = next_bank()
                        for k in range(8):
                            MM(bk[b][:, :], mergedT[:, k, blk * 128:(blk + 1) * 128], WO[:, k, half * 512:(half + 1) * 512],
                               k == 0, k == 7, reads=["merged", "WO"], writes=["bk%d" % b])
                        hs = slice(half * 512, (half + 1) * 512)
                        TT("dve", xnw[:, hs], bk[b][:, :], gate_bc[:, hs], ALU.mult, reads=["bk%d" % b, "gate_bc"], writes=[xnk])
                    TT("dve", xnw, xnw, xres[rs], ALU.add, reads=[xnk, xk], writes=[xnk])
                    c = state["st"]
                    state["st"] = (c + 1) % 8
                    sk = "stat%d" % c
                    ACT(sqjD, xnw, AF.Square, reads=[xnk], writes=["sqjD", sk], accum_out=stat[:, c:c + 1])
                    ACT(stat[:, 8 + c:9 + c], stat[:, c:c + 1], AF.Ln, reads=[sk, "small"], writes=[sk], scale=1.0 / D, bias=epsc)
                    ACT(stat[:, 16 + c:17 + c], stat[:, 8 + c:9 + c], AF.Exp, reads=[sk], writes=[sk], scale=-0.5)
                    S.op("dve", lambda e, c=c, ott=ott, xnw=xnw: e.scalar_tensor_tensor(ott, xnw, stat[:, 16 + c:17 + c], fg_bc, ALU.mult, ALU.mult),
                         reads=[xnk, sk, "fg_bc"], writes=[otk])
                    S.dma("pool", out_d[q0 + blk * 128:q0 + (blk + 1) * 128, :], ott, reads=[otk])

        for q in ("sp", "pool"):
            for i in range(max(0, S.dma_n[q] - ND), S.dma_n[q]):
                S._wait("pool", ("d", (q, i)))
        with nc.Block() as block:
            S.emit(block)
    return nc, dbg_outs


def host_inputs(inputs):
    x = np.asarray(inputs["x"], np.float32)
    c = np.asarray(inputs["c"], np.float32)
    pos = np.asarray(inputs["positions"], np.int32)
    f = lambda k: np.ascontiguousarray(np.asarray(inputs[k], np.float32))
    w_ada = f("w_ada")[0]
    b_ada = f("b_ada")[0]
    tri = np.tril(np.ones((128, 128), np.float32))
    ident = np.eye(128, dtype=np.float32)
    omt = 1.0 - tri
    ss, tt = np.meshgrid(np.arange(128), np.arange(128), indexing="ij")
    strict = np.where(ss < tt, 0.0, MASKV).astype(np.float32)
    causal = np.where(ss <= tt, 0.0, MASKV).astype(np.float32)
    allm = np.full((128, 128), MASKV, np.float32)
    nom = np.zeros((128, 128), np.float32)
    inv_freq = (10000.0 ** (-np.arange(0, 32, 2, dtype=np.float32) / np.float32(32))).astype(np.float32)
    invf = np.tile(np.concatenate([inv_freq, inv_freq]), 4).reshape(128, 1).astype(np.float32)
    common = {
        "w_ada": w_ada,
        "b_adaT": np.ascontiguousarray(b_ada.reshape(24, 128).T),
        "b_gate": np.ascontiguousarray(b_ada[2 * D:3 * D].reshape(1, D)),
        "ngT": np.ascontiguousarray(f("norm_gain")[0].reshape(8, 128).T),
        "w_in": f("w_in")[0],
        "qgT": np.ascontiguousarray(f("q_norm_gain")[0].reshape(3, 128).T),
        "w_uq": f("w_uq")[0],
        "kvgT": np.ascontiguousarray(f("kv_norm_gain")[0].reshape(2, 128).T),
        "w_ukv": f("w_ukv")[0],
        "w_a": f("w_branch_a")[0],
        "w_b": f("w_branch_b")[0],
        "w_out": f("w_out")[0],
        "fg_row": np.ascontiguousarray(f("final_norm_gain").reshape(1, D)),
        "invf": invf,
    }
    maps = []
    for core in range(8):
        b, p = core // 2, core % 2
        blocks = [2 * j + p for j in range(16)]
        xb = x[b].reshape(32, 128, D)
        xq = np.ascontiguousarray(xb[blocks].reshape(NQ, D))
        pq = pos[b].reshape(32, 128)[blocks].reshape(NQ)
        posall = np.ascontiguousarray(np.concatenate([pos[b], pq]).reshape(4, 1536).astype(np.int32))
        if p == 0:
            mats = [ident, tri, omt, allm, strict, allm, causal]
        else:
            mats = [ident, tri, omt, strict, nom, causal, nom]
        cm = np.ascontiguousarray(np.stack(mats, axis=1).astype(np.float32))
        m = dict(common)
        m.update({"xf": np.ascontiguousarray(x[b]), "xq": xq, "posall": posall,
                  "cT": np.ascontiguousarray(c[b].reshape(8, 128).T), "cmats": cm})
        maps.append(m)
    return maps


_CACHE = {}


def kernel(**inputs):
    maps = host_inputs(inputs)
    if "nc" not in _CACHE:
        _CACHE["nc"] = build_program()[0]
    nc = _CACHE["nc"]
    res = run_bass_kernel_spmd(nc, maps, core_ids=list(range(8)))
    out = np.zeros((4, 32, 128, D), np.float32)
    for core in range(8):
        b, p = core // 2, core % 2
        o = np.asarray(res.results[core]["out"], np.float32).reshape(16, 128, D)
        out[b, p::2] = o
    return out.reshape(4, S_LEN, D)
```

```python
import math
from contextlib import ExitStack
import numpy as np
import concourse.bass as bass
import concourse.mybir as mybir
from concourse.bass_utils import run_bass_kernel_spmd

F32 = mybir.dt.float32
BF16 = mybir.dt.bfloat16
I32 = mybir.dt.int32
AF = mybir.ActivationFunctionType
ALU = mybir.AluOpType

ENGS = ["pe", "act", "dve", "pool", "sp"]
PH = 2048
NPH = {"pe": 7, "act": 5, "dve": 4, "pool": 1, "sp": 1}
ND = 16

D = 1024
S_LEN = 4096
NQ = 2048
SBQ, SBK, SBV, SBZ, CQ, CKV, KROT, MLAZ, GA, GB = 0, 512, 1024, 1536, 2048, 2432, 2688, 2720, 3232, 4256
EPS = 1e-6
MASKV = -30000.0
SB_FILL = 4
MLA_FILL = 0
DEBUG = False


class Sched:
    def __init__(self, nc, esem, dsem):
        self.nc = nc
        self.esem = esem
        self.dsem = dsem
        self.ops = {e: [] for e in ENGS}
        self.cnt = {e: 0 for e in ENGS}
        self.waited_e = {e: {} for e in ENGS}
        self.waited_d = {e: {} for e in ENGS}
        self.last_w = {}
        self.readers = {}
        self.dma_i = 0
        self.dma_n = {"sp": 0, "pool": 0}

    def _wait(self, E, ev):
        if ev[0] == "e":
            _, e2, n = ev
            if e2 == E and E == "pe":
                return
            if self.waited_e[E].get(e2, 0) >= n:
                return
            self.waited_e[E][e2] = n
            sem = self.esem[e2][(n - 1) // PH]
            val = (n - 1) % PH + 1
        else:
            q, i = ev[1]
            k = i % ND
            val = 16 * (i // ND + 1)
            if self.waited_d[E].get((q, k), 0) >= val:
                return
            self.waited_d[E][(q, k)] = val
            sem = self.dsem[q][k]
        self.ops[E].append(lambda eng, sem=sem, val=val: eng.wait_ge(sem, val))

    def _deps(self, E, reads, writes, extra=(), dma_accum=False):
        deps = list(extra)
        for b in reads:
            for ev in self.last_w.get(b, ()):
                deps.append(ev)
        for b in writes:
            for ev in self.last_w.get(b, ()):
                if dma_accum and ev[0] == "d":
                    continue
                deps.append(ev)
            r = self.readers.get(b)
            if r:
                for e2, n in r[0].items():
                    deps.append(("e", e2, n))
                for i in r[1]:
                    deps.append(("d", i))
        for ev in deps:
            self._wait(E, ev)

    def _record(self, ev, reads, writes, dma_accum=False):
        for b in reads:
            r = self.readers.setdefault(b, ({}, []))
            if ev[0] == "e":
                r[0][ev[1]] = ev[2]
            else:
                r[1].append(ev[1])
        for b in writes:
            r = self.readers.get(b)
            had_readers = bool(r and (r[0] or r[1]))
            if dma_accum and not had_readers:
                self.last_w[b] = [e for e in self.last_w.get(b, ()) if e[0] == "d"] + [ev]
            else:
                self.last_w[b] = [ev]
            self.readers[b] = ({}, [])

    def op(self, E, fn, reads=(), writes=(), extra=()):
        self._deps(E, reads, writes, extra)
        n = self.cnt[E] + 1
        self.cnt[E] = n
        assert (n - 1) // PH < NPH[E], "too many instrs on %s" % E
        sem = self.esem[E][(n - 1) // PH]
        self.ops[E].append(lambda eng, fn=fn, sem=sem: fn(eng).then_inc(sem, 1))
        ev = ("e", E, n)
        self._record(ev, reads, writes)
        return ev

    def dma(self, Q, out, in_, reads=(), writes=(), extra=()):
        i = self.dma_n[Q]
        self.dma_n[Q] += 1
        self.dma_i += 1
        ex = list(extra)
        if i >= ND:
            ex.append(("d", (Q, i - ND)))
        self._deps(Q, reads, writes, ex, dma_accum=True)
        sem = self.dsem[Q][i % ND]
        self.ops[Q].append(
            lambda eng, out=out, in_=in_, sem=sem: eng.dma_start(out=out, in_=in_).then_inc(sem, 16))
        ev = ("d", (Q, i))
        self._record(ev, reads, writes, dma_accum=True)
        return ev

    def barrier(self):
        snap = dict(self.cnt)
        for E in ENGS:
            for e2 in ENGS:
                if e2 != E and snap[e2] > 0:
                    self._wait(E, ("e", e2, snap[e2]))

    def emit(self, block):
        ops = self.ops

        @block.tensor
        def _(eng):
            for f in ops["pe"]:
                f(eng)

        @block.scalar
        def _(eng):
            for f in ops["act"]:
                f(eng)

        @block.vector
        def _(eng):
            for f in ops["dve"]:
                f(eng)

        @block.gpsimd
        def _(eng):
            for f in ops["pool"]:
                f(eng)

        @block.sync
        def _(eng):
            for f in ops["sp"]:
                f(eng)


def build_program(debug=False, stop_after=None):
    nc = bass.Bass("TRN2", target_bir_lowering=False)

    def din(name, shape, dt=F32):
        return nc.dram_tensor(name, list(shape), dt, kind="ExternalInput").ap()

    xf = din("xf", [S_LEN, D])
    xq = din("xq", [NQ, D])
    posall = din("posall", [4, 1536], I32)
    cT_d = din("cT", [128, 8])
    w_ada = din("w_ada", [D, 3 * D])
    b_adaT = din("b_adaT", [128, 24])
    b_gate = din("b_gate", [1, D])
    ngT = din("ngT", [128, 8])
    w_in = din("w_in", [D, 5280])
    qgT = din("qgT", [128, 3])
    w_uq = din("w_uq", [384, 768])
    kvgT = din("kvgT", [128, 2])
    w_ukv = din("w_ukv", [256, 1024])
    w_a = din("w_a", [512, D])
    w_b = din("w_b", [512, D])
    w_out = din("w_out", [D, D])
    fg_row = din("fg_row", [1, D])
    cmats = din("cmats", [128, 7, 128])
    invf_d = din("invf", [128, 1])
    out_d = nc.dram_tensor("out", [NQ, D], F32, kind="ExternalOutput").ap()
    dbg_outs = {}

    with ExitStack() as es:
        esem = {e: [es.enter_context(nc.semaphore("s_%s_%d" % (e, i))) for i in range(NPH[e])] for e in ENGS}
        dsem = {q: [es.enter_context(nc.semaphore("d_%s_%d" % (q, i))) for i in range(ND)] for q in ("sp", "pool")}
        S = Sched(nc, esem, dsem)

        ARENA_KB = 204
        arena = es.enter_context(nc.sbuf_tensor("arena", [128, ARENA_KB * 512], BF16))
        arena32 = arena.bitcast(F32)
        arenai = arena.bitcast(I32)

        def view(off_bytes, shape, dt):
            n = int(np.prod(shape[1:]))
            esz = 2 if dt == BF16 else 4
            assert off_bytes % 4 == 0
            assert off_bytes + n * esz <= ARENA_KB * 1024, (off_bytes, shape)
            base = {BF16: arena, F32: arena32, I32: arenai}[dt]
            o = off_bytes // esz
            ap = base[:, o:o + n]
            if len(shape) == 3:
                ap = ap.rearrange("p (a b) -> p a b", b=shape[2])
            elif len(shape) == 4:
                ap = ap.rearrange("p (a b c) -> p a b c", b=shape[2], c=shape[3])
            return ap

        class Bump:
            def __init__(self, start_kb, end_kb):
                self.o = start_kb * 1024
                self.end = end_kb * 1024

            def __call__(self, shape, dt):
                n = int(np.prod(shape[1:])) * (2 if dt == BF16 else 4)
                n = (n + 3) // 4 * 4
                v = view(self.o, shape, dt)
                self.o += n
                assert self.o <= self.end, ("bump overflow", self.o, self.end)
                return v

        psum_all = es.enter_context(nc.psum_tensor("psall", [128, 4096], F32))
        psum_bf = psum_all.bitcast(BF16)
        bk = [psum_all[:, i * 512:(i + 1) * 512] for i in range(8)]
        bkb = [psum_bf[:, i * 1024:(i + 1) * 1024] for i in range(8)]

        def bkpair(i):
            return psum_all[:, i * 512:(i + 2) * 512].rearrange("p (c n) -> p c n", n=512)

        def MM(out, lhsT, rhs, start, stop, reads=(), writes=()):
            return S.op("pe", lambda e: e.matmul(out, lhsT, rhs, start=start, stop=stop, skip_group_check=True),
                        reads=reads, writes=writes)

        def ACT(out, in_, func, reads=(), writes=(), **kw):
            return S.op("act", lambda e: e.activation(out, in_, func, **kw), reads=reads, writes=writes)

        def TT(eng, out, in0, in1, op, reads=(), writes=()):
            return S.op(eng, lambda e: e.tensor_tensor(out, in0, in1, op), reads=reads, writes=writes)

        def TS(eng, out, in0, s1, s2, op0, op1=None, reads=(), writes=()):
            if op1 is None:
                return S.op(eng, lambda e: e.tensor_scalar(out, in0, s1, None, op0), reads=reads, writes=writes)
            return S.op(eng, lambda e: e.tensor_scalar(out, in0, s1, s2, op0, op1), reads=reads, writes=writes)

        def CP(eng, out, in_, reads=(), writes=()):
            return S.op(eng, lambda e: e.tensor_copy(out, in_), reads=reads, writes=writes)

        def dump(name, ap, shape, reads=()):
            if not debug:
                return
            t = nc.dram_tensor("dbg_" + name, list(shape), F32, kind="ExternalOutput").ap()
            dbg_outs[name] = t
            S.dma("pool", t, ap, reads=reads)

        P = Bump(0, 16)
        cm = P([128, 7, 128], BF16)
        ident = cm[:, 0, :]
        triI = cm[:, 1, :]
        omt = cm[:, 2, :]
        m_sb = [cm[:, 3, :], cm[:, 4, :]]
        m_mla = [cm[:, 5, :], cm[:, 6, :]]
        zeros_bf = P([128, 128], BF16)
        ones_bf = P([128, 128], BF16)
        mod_sb = P([128, 24], F32)
        gs = P([128, 8], F32)
        small = P([128, 64], F32)
        cT = small[:, 0:8]
        badaT = small[:, 8:32]
        ng = small[:, 32:40]
        qg = small[:, 40:43]
        kvg = small[:, 43:45]
        epsc = small[:, 45:46]
        invf = small[:, 46:47]
        tmp8 = small[:, 48:56]
        stat = P([128, 64], F32)
        tabcos = P([128, 1536], F32)
        tabsin = P([128, 1536], F32)
        shift = mod_sb[:, 0:8]

        L = Bump(16, 52)
        ckvn = L([128, 2, S_LEN], BF16)
        kpe = L([128, S_LEN], BF16)
        cqn = L([128, 3, NQ], BF16)
        SBD = Bump(52, 132)
        KT = SBD([128, 4, S_LEN], BF16)
        Vsb = SBD([128, 32, 512], BF16)
        QT = SBD([128, 4, NQ], BF16)
        OA = Bump(132, 164)
        oaT = OA([128, 4, NQ], BF16)
        obT = OA([128, 4, NQ], BF16)

        WK = view(132 * 1024, [128, 8, 1344], BF16)
        w3 = w_in.rearrange("(k p) n -> p k n", p=128)

        def load_WK():
            for k in range(8):
                S.dma("pool", WK[:, k, 0:1024], w3[:, k, SBK:SBK + 1024], writes=["WK"])
                S.dma("pool", WK[:, k, 1024:1312], w3[:, k, CKV:CKV + 288], writes=["WK"])
                S.dma("pool", WK[:, k, 1312:1328], w3[:, k, KROT + 16:KROT + 32], writes=["WK"])
                S.dma("pool", WK[:, k, 1328:1344], w3[:, k, KROT:KROT + 16], writes=["WK"])
            TS("pool", WK[:, :, 1312:1328], WK[:, :, 1312:1328], -1.0, None, ALU.mult, reads=["WK"], writes=["WK"])

        def load_WQ():
            for k in range(8):
                S.dma("pool", WK[:, k, 0:512], w3[:, k, SBQ:SBQ + 512], writes=["WK"])
                S.dma("pool", WK[:, k, 512:896], w3[:, k, CQ:CQ + 384], writes=["WK"])

        A_ = Bump(52, 132)
        A2_ = Bump(164, 204)
        wst = [A_([128, D], F32) for _ in range(8)] + [A2_([128, D], F32) for _ in range(8)]
        posi = A_([128, 1536], I32)
        ang = A_([128, 1536], F32)
        tt = A_([128, 1536], F32)
        ki = A_([128, 1536], I32)
        kf = A_([128, 1536], F32)
        ff = A_([128, 1536], F32)

        S.dma("pool", cm, cmats, writes=["cm"])
        load_WK()
        S.dma("sp", cT, cT_d, writes=["small"])
        S.dma("sp", badaT, b_adaT, writes=["small"])
        S.dma("sp", ng, ngT, writes=["small"])
        S.dma("sp", qg, qgT, writes=["small"])
        S.dma("sp", kvg, kvgT, writes=["small"])
        S.dma("sp", invf, invf_d, writes=["small"])
        for q in range(4):
            S.dma("sp", posi[q * 32:(q + 1) * 32, :], posall[q:q + 1, :].broadcast_to([32, 1536]), writes=["posi"])
        S.op("dve", lambda e: e.memset(zeros_bf, 0.0), writes=["zeros"])
        S.op("dve", lambda e: e.memset(ones_bf, 1.0), writes=["ones"])
        S.op("dve", lambda e: e.memset(epsc, EPS), reads=["small"], writes=["small"])

        CP("dve", ang, posi, reads=["posi"], writes=["ang"])
        TS("dve", ang, ang, invf, None, ALU.mult, reads=["ang", "small"], writes=["ang"])
        inv2pi = 1.0 / (2.0 * math.pi)
        for (tab, phase, nm) in ((tabsin, 0.0, "sin"), (tabcos, 0.25, "cos")):
            TS("dve", tt, ang, inv2pi, phase, ALU.mult, ALU.add, reads=["ang"], writes=["tt"])
            CP("dve", ki, tt, reads=["tt"], writes=["ki"])
            CP("dve", kf, ki, reads=["ki"], writes=["kf"])
            TT("dve", ff, tt, kf, ALU.subtract, reads=["tt", "kf"], writes=["ff"])
            TS("dve", kf, ff, 0.5, None, ALU.is_gt, reads=["ff"], writes=["kf"])
            TT("dve", ff, ff, kf, ALU.subtract, reads=["ff", "kf"], writes=["ff"])
            TS("dve", kf, ff, -0.5, None, ALU.is_lt, reads=["ff"], writes=["kf"])
            TT("dve", ff, ff, kf, ALU.add, reads=["ff", "kf"], writes=["ff"])
            ACT(tab, ff, AF.Sin, reads=["ff"], writes=["tab" + nm], scale=6.283185)
        for half in range(2):
            for k in range(8):
                S.dma("sp", wst[half * 8 + k], w_ada[k * 128:(k + 1) * 128, half * D:(half + 1) * D], writes=["wst%d" % (half * 8 + k)])
        for half in range(2):
            for jj in range(8):
                j = half * 8 + jj
                for k in range(8):
                    MM(bk[0][:, j:j + 1], wst[half * 8 + k][:, jj * 128:(jj + 1) * 128], cT[:, k:k + 1], k == 0, k == 7,
                       reads=["wst%d" % (half * 8 + k), "small"], writes=["bk0"])
        TT("dve", mod_sb[:, 0:16], bk[0][:, 0:16], badaT[:, 0:16], ALU.add, reads=["bk0", "small"], writes=["mod"])
        TS("dve", tmp8, mod_sb[:, 8:16], 1.0, None, ALU.add, reads=["mod"], writes=["tmp8"])
        TT("dve", gs, tmp8, ng, ALU.mult, reads=["tmp8", "small"], writes=["gs"])

        dump("gs", gs, [128, 8], reads=["gs"])
        dump("mod", mod_sb[:, 0:16], [128, 16], reads=["mod"])
        dump("tabcos", tabcos, [128, 1536], reads=["tabcos"])
        dump("tabsin", tabsin, [128, 1536], reads=["tabsin"])
        S.barrier()

        B_ = Bump(164, 204)
        B2 = Bump(154, 164)
        jit = B2([128, 2, 512], F32)
        sqj = B2([128, D], BF16)
        sq = B2([128, 3, 512], BF16)
        xt = [B_([128, D], F32) for _ in range(2)]
        xn = B_([128, 4, D], BF16)
        hTs = [B_([128, 8, 512], BF16) for _ in range(2)]
        rbc = B_([128, 512], F32)
        t1 = B_([128, 512], F32)
        t2 = B_([128, 512], F32)

        state = {"xt": 0, "st": 0, "bank": 2, "ev": 0, "fill": 0, "sbu": 0, "pend": []}

        def next_bank():
            b = state["bank"]
            state["bank"] = 2 + (b - 2 + 1) % 6
            return b

        def evac_engine():
            state["ev"] ^= 1
            return "act" if state["ev"] else "dve"

        def norm_stats_blk(src, xts, blk):
            slot = state["xt"]
            state["xt"] ^= 1
            xk = "xt%d" % slot
            c = state["st"]
            state["st"] = (c + 1) % 8
            sk = "stat%d" % c
            S.dma("sp", xts[slot], src[blk * 128:(blk + 1) * 128, :], writes=[xk])
            ACT(sqj, xts[slot], AF.Square, reads=[xk], writes=["sqj", sk], accum_out=stat[:, c:c + 1])
            ACT(stat[:, 8 + c:9 + c], stat[:, c:c + 1], AF.Ln, reads=[sk, "small"], writes=[sk],
                scale=1.0 / D, bias=epsc)
            ACT(stat[:, 16 + c:17 + c], stat[:, 8 + c:9 + c], AF.Exp, reads=[sk], writes=[sk], scale=-0.5)
            TS("dve", xn[:, blk, :], xts[slot], stat[:, 16 + c:17 + c], None, ALU.mult,
               reads=[xk, sk], writes=["xn%d" % blk])

        def norm_stats(src, xts):
            for blk in range(4):
                norm_stats_blk(src, xts, blk)

        def queue_stats(src, xts):
            state["pend"] = [(src, xts, blk) for blk in range(4)]

        def drain_stats(n=4):
            for _ in range(n):
                if state["pend"]:
                    a_, b_, c_ = state["pend"].pop(0)
                    norm_stats_blk(a_, b_, c_)

        def norm_T(hT):
            for k in range(8):
                tb = k % 2
                for blk in range(4):
                    S.op("pe", lambda e, k=k, blk=blk, tb=tb, xn_=xn: e.transpose(
                        bkb[tb][:, blk * 128:(blk + 1) * 128], xn_[:, blk, k * 128:(k + 1) * 128], ident),
                        reads=["xn%d" % blk, "cm"], writes=["bk%d" % tb])
                if evac_engine() == "act":
                    ACT(hT[:, k, :], bkb[tb][:, 0:512], AF.Identity, reads=["bk%d" % tb, "gs", "mod"],
                        writes=[("hT", id(hT), k)], scale=gs[:, k:k + 1], bias=shift[:, k:k + 1])
                else:
                    TS("dve", hT[:, k, :], bkb[tb][:, 0:512], gs[:, k:k + 1], shift[:, k:k + 1], ALU.mult, ALU.add,
                       reads=["bk%d" % tb, "gs", "mod"], writes=[("hT", id(hT), k)])

        def rms_sq(banks, nch):
            for c in range(nch):
                ACT(sq[:, c, :], bk[banks[c]][:, :], AF.Square, reads=["bk%d" % banks[c]], writes=["sq%d" % c])

        def rms_fin(banks, nch, dim, dst, t0, gkey):
            sb_ = next_bank()
            for c in range(nch):
                MM(bk[sb_][:, :], ones_bf, sq[:, c, :], c == 0, c == nch - 1, reads=["sq%d" % c, "ones"],
                   writes=["bk%d" % sb_])
            ACT(rbc, bk[sb_][:, :], AF.Ln, reads=["bk%d" % sb_, "small"], writes=["rbc"], scale=1.0 / dim, bias=epsc)
            ACT(rbc, rbc, AF.Exp, reads=["rbc"], writes=["rbc"], scale=-0.5)
            for c in range(nch):
                TT("dve", dst[:, c, t0:t0 + 512], bk[banks[c]][:, :], rbc, ALU.mult,
                   reads=["bk%d" % banks[c], "rbc"], writes=[gkey])

        def proj_K(tc):
            t0 = tc * 512
            hT = hTs[tc % 2]
            hk = ("hT", id(hT))
            for cc in range(4):
                b = next_bank()
                for k in range(8):
                    MM(bk[b][:, :], WK[:, k, cc * 128:(cc + 1) * 128], hT[:, k, :], k == 0, k == 7,
                       reads=["WK", hk + (k,)], writes=["bk%d" % b])
                if evac_engine() == "act":
                    ACT(KT[:, cc, t0:t0 + 512], bk[b][:, :], AF.Copy, reads=["bk%d" % b], writes=["KT"], scale=0.125)
                else:
                    TS("dve", KT[:, cc, t0:t0 + 512], bk[b][:, :], 0.125, None, ALU.mult, reads=["bk%d" % b], writes=["KT"])
                if cc % 2 == 1:
                    drain_stats(1)
            for blk in range(4):
                b = next_bank()
                for k in range(8):
                    MM(bk[b][:, :], hT[:, k, blk * 128:(blk + 1) * 128], WK[:, k, 512:1024], k == 0, k == 7,
                       reads=["WK", hk + (k,)], writes=["bk%d" % b])
                if evac_engine() == "act":
                    ACT(Vsb[:, tc * 4 + blk, :], bk[b][:, :], AF.Copy, reads=["bk%d" % b], writes=["Vsb"])
                else:
                    CP("dve", Vsb[:, tc * 4 + blk, :], bk[b][:, :], reads=["bk%d" % b], writes=["Vsb"])
                if blk % 2 == 1:
                    drain_stats(1)
            cb = []
            for c2 in range(2):
                b = next_bank()
                cb.append(b)
                for k in range(8):
                    MM(bk[b][:, :], WK[:, k, 1024 + c2 * 128:1024 + (c2 + 1) * 128], hT[:, k, :], k == 0, k == 7,
                       reads=["WK", hk + (k,)], writes=["bk%d" % b])
            rms_sq(cb, 2)
            bA = next_bank()
            for k in range(8):
                MM(bk[bA][0:96, :], WK[:, k, 1216:1312], hT[:, k, :], k == 0, k == 7, reads=["WK", hk + (k,)], writes=["bk%d" % bA])
            bB = next_bank()
            for k in range(8):
                MM(bk[bB][0:96, :], WK[:, k, 1248:1344], hT[:, k, :], k == 0, k == 7, reads=["WK", hk + (k,)], writes=["bk%d" % bB])
            qd, col = tc // 3, (tc % 3) * 512
            S.dma("sp", jit[64:96, 0, :], tabcos[qd * 32:(qd + 1) * 32, col:col + 512], writes=["jit"])
            S.dma("sp", jit[64:96, 1, :], tabsin[qd * 32:(qd + 1) * 32, col:col + 512], writes=["jit"])
            TT("dve", t1[64:96, :], bk[bA][64:96, :], jit[64:96, 0, :], ALU.mult, reads=["bk%d" % bA, "jit"], writes=["t1"])
            TT("dve", t2[64:96, :], bk[bB][64:96, :], jit[64:96, 1, :], ALU.mult, reads=["bk%d" % bB, "jit"], writes=["t2"])
            TT("dve", kpe[64:96, t0:t0 + 512], t1[64:96, :], t2[64:96, :], ALU.add, reads=["t1", "t2"], writes=["kpe"])
            return lambda: rms_fin(cb, 2, 256, ckvn, t0, "ckvn")
        def proj_Q(qc):
            q0 = qc * 512
            hT = hTs[qc % 2]
            hk = ("hT", id(hT))
            for cc in range(4):
                b = next_bank()
                for k in range(8):
                    MM(bk[b][:, :], WK[:, k, cc * 128:(cc + 1) * 128], hT[:, k, :], k == 0, k == 7,
                       reads=["WK", hk + (k,)], writes=["bk%d" % b])
                if evac_engine() == "act":
                    ACT(QT[:, cc, q0:q0 + 512], bk[b][:, :], AF.Copy, reads=["bk%d" % b], writes=["QT"])
                else:
                    CP("dve", QT[:, cc, q0:q0 + 512], bk[b][:, :], reads=["bk%d" % b], writes=["QT"])
                drain_stats(1)
            cb = []
            for c3 in range(3):
                b = next_bank()
                cb.append(b)
                for k in range(8):
                    MM(bk[b][:, :], WK[:, k, 512 + c3 * 128:512 + (c3 + 1) * 128], hT[:, k, :], k == 0, k == 7,
                       reads=["WK", hk + (k,)], writes=["bk%d" % b])
            rms_sq(cb, 3)
            return lambda: rms_fin(cb, 3, 384, cqn, q0, "cqn")
        chunks = [("K", i) for i in range(8)] + [("Q", i) for i in range(4)]

        def src_of(ch):
            return (xf if ch[0] == "K" else xq)[ch[1] * 512:(ch[1] + 1) * 512, :]

        norm_stats(src_of(chunks[0]), xt)
        norm_T(hTs[0])
        for i, ch in enumerate(chunks):
            if i + 1 < len(chunks):
                queue_stats(src_of(chunks[i + 1]), xt)
            if ch == ("Q", 0):
                load_WQ()
            fin = (proj_K if ch[0] == "K" else proj_Q)(ch[1])
            drain_stats(4)
            if i + 1 < len(chunks):
                norm_T(hTs[(i + 1) % 2])
            fin()
        dump("KT", KT.rearrange("p a b -> p (a b)"), [128, 4 * S_LEN], reads=["KT"])
        dump("QT", QT.rearrange("p a b -> p (a b)"), [128, 4 * NQ], reads=["QT"])
        dump("Vsb", Vsb.rearrange("p a b -> p (a b)"), [128, 32 * 512], reads=["Vsb"])
        dump("ckvn", ckvn.rearrange("p a b -> p (a b)"), [128, 2 * S_LEN], reads=["ckvn"])
        dump("cqn", cqn.rearrange("p a b -> p (a b)"), [128, 3 * NQ], reads=["cqn"])
        dump("kpe", kpe[64:96, :], [32, S_LEN], reads=["kpe"])
        S.barrier()

        def chain_cols(g, kb):
            c0 = max(0, kb // 2 - 4 * g)
            masks = []
            for c in range(c0, 4):
                j = 4 * g + c
                if kb == 2 * j + 1:
                    masks.append((c, 0))
                elif kb == 2 * j:
                    masks.append((c, 1))
            return c0, masks

        def run_chains(chains):
            live = list(chains)
            while live:
                nxt = []
                for gen in live:
                    try:
                        next(gen)
                        nxt.append(gen)
                    except StopIteration:
                        pass
                live = nxt

        def run_slots(slots):
            slots = [list(x) for x in slots]
            cur = [sl.pop(0) if sl else None for sl in slots]
            while any(c is not None for c in cur):
                for i in range(len(cur)):
                    while cur[i] is not None:
                        try:
                            next(cur[i])
                            break
                        except StopIteration:
                            cur[i] = slots[i].pop(0) if slots[i] else None

        if stop_after != "B":
            C1 = Bump(164, 204)
            e_sb = C1([128, 3, 2, 512], F32)
            sp_sb = C1([128, 2, 2, 512], BF16)
            g_sb = C1([128, 2, 2, 512], F32)
            w_sb = C1([128, 2, 2, 512], BF16)
            zp, Ap = bkpair(0), bkpair(2)

            def sb_pair(hp, g):
                cc = hp
                units = list(range(8 * g + 7, -1, -1))
                nU = len(units)
                ub = state["sbu"]
                state["sbu"] += nU
                def fill(n, lo=0):
                    for _ in range(n):
                        fb = 6 + (state["fill"] % 2)
                        state["fill"] += 1
                        MM(bk[fb][:, lo:512], ident, QT[:, 0, lo:512], True, True)

                def lo_of(i):
                    return chain_cols(g, units[i])[0] * 128

                def qk(i):
                    kb = units[i]
                    c0, masks = chain_cols(g, kb)
                    lo = c0 * 128
                    for c in range(2):
                        bp = 64 * c
                        MM(bk[c][:, lo:512], KT[bp:bp + 64, cc, kb * 128:(kb + 1) * 128],
                           QT[bp:bp + 64, cc, g * 512 + lo:(g + 1) * 512], True, len(masks) == 0, writes=["z"])
                        for mi, (cb, which) in enumerate(masks):
                            MM(bk[c][:, cb * 128:(cb + 1) * 128], ident, m_sb[which], False, mi == len(masks) - 1, writes=["z"])

                def tri(i):
                    lo = lo_of(i)
                    for c in range(2):
                        MM(bk[2 + c][:, lo:512], triI, sp_sb[:, (ub + i) % 2, c, lo:512], False, True, reads=["sp%d" % ((ub + i) % 2)], writes=["A"])

                def omt_(i):
                    lo = lo_of(i)
                    for c in range(2):
                        MM(bk[2 + c][:, lo:512], omt, sp_sb[:, (ub + i) % 2, c, lo:512], False, True, reads=["sp%d" % ((ub + i) % 2)], writes=["A"])

                def pv(i):
                    lo = lo_of(i)
                    kb = units[i]
                    for c in range(2):
                        h = 2 * hp + c
                        MM(bk[4 + c][0:64, lo:512], Vsb[:, kb, h * 64:(h + 1) * 64], w_sb[:, (ub + i) % 2, c, lo:512], False, True,
                           reads=["w%d" % ((ub + i) % 2)], writes=["o"])

                def act_e(i):
                    lo = lo_of(i)
                    ACT(e_sb[:, (ub + i) % 3, :, lo:512], zp[:, :, lo:512], AF.Exp, reads=["z"], writes=["e%d" % ((ub + i) % 3)])

                def act_sp(i):
                    lo = lo_of(i)
                    ACT(sp_sb[:, (ub + i) % 2, :, lo:512], e_sb[:, (ub + i) % 3, :, lo:512], AF.Ln, reads=["e%d" % ((ub + i) % 3)],
                        writes=["sp%d" % ((ub + i) % 2)], bias=1.0)

                def act_g(i):
                    lo = lo_of(i)
                    ACT(g_sb[:, (ub + i) % 2, :, lo:512], Ap[:, :, lo:512], AF.Exp, reads=["A"], writes=["g%d" % ((ub + i) % 2)], scale=-1.0)

                def dve_w(i):
                    lo = lo_of(i)
                    TT("dve", w_sb[:, (ub + i) % 2, :, lo:512], e_sb[:, (ub + i) % 3, :, lo:512], g_sb[:, (ub + i) % 2, :, lo:512], ALU.mult,
                       reads=["e%d" % ((ub + i) % 3), "g%d" % ((ub + i) % 2)], writes=["w%d" % ((ub + i) % 2)])

                def prologue():
                    qk(0)
                    act_e(0)
                    act_sp(0)
                    if nU > 1:
                        qk(1)

                def main(next_prologue):
                    for c in range(2):
                        MM(bk[2 + c][:, :], zeros_bf, QT[:, 0, 0:512], True, True, reads=["zeros"], writes=["A"])
                        MM(bk[4 + c][0:64, :], zeros_bf[:, 0:64], QT[:, 0, 0:512], True, True, reads=["zeros"], writes=["o"])
                    for i in range(nU):
                        if i == nU - 1 and next_prologue is not None:
                            next_prologue()
                        tri(i)
                        fill(SB_FILL, lo_of(i))
                        if i + 1 < nU:
                            act_e(i + 1)
                        act_g(i)
                        if i + 1 < nU:
                            act_sp(i + 1)
                        if i >= 1:
                            pv(i - 1)
                        if i + 2 < nU:
                            qk(i + 2)
                        if i + 1 < nU:
                            omt_(i)
                        dve_w(i)
                    pv(nU - 1)
                    for c in range(2):
                        CP("dve", oaT[64 * c:64 * c + 64, cc, g * 512:(g + 1) * 512], bk[4 + c][0:64, :], reads=["o"], writes=["oaT"])

                return prologue, main

            sb_chains = [sb_pair(hp, g) for g in range(4) for hp in range(4)]
            sb_chains[0][0]()
            for ci_, (pro, main) in enumerate(sb_chains):
                main(sb_chains[ci_ + 1][0] if ci_ + 1 < len(sb_chains) else None)
            dump("oaT", oaT.rearrange("p a b -> p (a b)"), [128, 4 * NQ], reads=["oaT"])
            S.barrier()

        if stop_after not in ("B", "C1"):
            C2 = Bump(52, 132)
            Vm = C2([128, 32, 4, 192], BF16)
            KhT0_ = C2([128, S_LEN], BF16)
            QhT = [C2([128, NQ], BF16) for _ in range(2)]
            wukv = C2([128, 2, 1024], BF16)
            wuqa = C2([128, 3, 8, 128], BF16)
            wvc = C2([128, 2, 512], BF16)
            C2b = Bump(164, 204)
            wst2 = C2b([128, 3, 1024], F32)
            KhT = [KhT0_, view(164 * 1024, [128, S_LEN], BF16)]
            KKEY = ["KhT0", "wst2"]
            jq = C2b([128, NQ], F32)
            p_sb = C2b([128, 2, 2, 512], BF16)
            lnl = C2b([128, 512], F32)
            rinv = C2b([128, 512], F32)
            tq1 = C2b([128, 512], F32)
            tq2 = C2b([128, 512], F32)

            S.dma("sp", wst2[:, 0:2, :], w_ukv.rearrange("(k p) n -> p k n", p=128), writes=["wst2"])
            for k in range(2):
                TS("dve", wukv[:, k, :], wst2[:, k, :], kvg[:, k:k + 1], None, ALU.mult, reads=["wst2", "small"], writes=["wukv"])
            S.dma("sp", wst2[:, :, 0:768], w_uq.rearrange("(k p) n -> p k n", p=128), reads=[], writes=["wst2"])
            wst2q = wst2[:, :, 0:768].rearrange("p k (h c) -> p k h c", c=96)
            for k in range(3):
                TS("dve", wuqa[:, k, :, 0:96], wst2q[:, k, :, :], qg[:, k:k + 1], None, ALU.mult, reads=["wst2", "small"], writes=["wuqa"])
                TS("dve", wuqa[:, k, :, 96:112], wst2q[:, k, :, 80:96], qg[:, k:k + 1], -1.0, ALU.mult, ALU.mult,
                   reads=["wst2", "small"], writes=["wuqa"])
                TS("dve", wuqa[:, k, :, 112:128], wst2q[:, k, :, 64:80], qg[:, k:k + 1], None, ALU.mult,
                   reads=["wst2", "small"], writes=["wuqa"])
            for qc in range(4):
                ci = 8 + qc
                qd, col = ci // 3, (ci % 3) * 512
                S.dma("sp", jq[64:96, qc * 512:(qc + 1) * 512], tabcos[qd * 32:(qd + 1) * 32, col:col + 512], writes=["jq"])
                S.dma("sp", jq[96:128, qc * 512:(qc + 1) * 512], tabsin[qd * 32:(qd + 1) * 32, col:col + 512], writes=["jq"])
            for blk in range(32):
                S.op("pool", lambda e, blk=blk: e.memset(Vm[:, blk, :, 64:128], 1.0), writes=["Vm"])
            wukv4 = wukv.rearrange("p k (h c) -> p k h c", c=128)
            for k in range(2):
                CP("dve", wvc[:, k, :].rearrange("p (h c) -> p h c", c=64), wukv4[:, k, :, 64:128], reads=["wukv"], writes=["wvc"])
            for blk in range(32):
                b = 6 + blk % 2
                for k in range(2):
                    MM(bk[b][:, :], ckvn[:, k, blk * 128:(blk + 1) * 128],
                       wvc[:, k, :], k == 0, k == 1, reads=["wvc"], writes=["bk%d" % b])
                pv = bk[b][:, :].rearrange("p (q t c) -> p q t c", t=2, c=64)
                ACT(Vm[:, blk, :, 0:64], pv[:, :, 0, :], AF.Copy, reads=["bk%d" % b], writes=["Vm"])
                CP("dve", Vm[:, blk, :, 128:192], pv[:, :, 1, :], reads=["bk%d" % b], writes=["Vm"])

            def prep_head(h):
                Kh, Qh = KhT[h % 2], QhT[h % 2]
                kk, qk_ = KKEY[h % 2], "QhT%d" % (h % 2)
                for tc in range(8):
                    b = 6 + tc % 2
                    for k in range(2):
                        MM(bk[b][0:64, :], wukv[:, k, h * 128:h * 128 + 64], ckvn[:, k, tc * 512:(tc + 1) * 512], k == 0, k == 1,
                           reads=["wukv"], writes=["bk%d" % b])
                    CP("dve", Kh[0:64, tc * 512:(tc + 1) * 512], bk[b][0:64, :], reads=["bk%d" % b], writes=[kk])
                    yield
                for qc in range(4):
                    b = 6 + qc % 2
                    kb_ = "bk%d" % b
                    cs = slice(qc * 512, (qc + 1) * 512)
                    for k in range(3):
                        MM(bk[b][:, :], wuqa[:, k, h, :], cqn[:, k, cs], k == 0, k == 2, reads=["wuqa"], writes=[kb_])
                    CP("dve", Qh[0:64, cs], bk[b][0:64, :], reads=[kb_], writes=[qk_])
                    TT("dve", tq1[64:96, :], bk[b][64:96, :], jq[64:96, cs], ALU.mult, reads=[kb_, "jq"], writes=["tq1"])
                    TT("dve", tq2[64:96, :], bk[b][96:128, :], jq[96:128, cs], ALU.mult, reads=[kb_, "jq"], writes=["tq2"])
                    TT("dve", Qh[64:96, cs], tq1[64:96, :], tq2[64:96, :], ALU.add, reads=["tq1", "tq2"], writes=[qk_])
                    yield
                    yield

            sm_scale = 1.0 / math.sqrt(96.0)

            def mla_parts(h, g, ci):
                cc, bp = h // 2, (h % 2) * 64
                Kh, Qh = KhT[h % 2], QhT[h % 2]
                kk, qk_ = KKEY[h % 2], "QhT%d" % (h % 2)
                lb = 64 - bp
                vl = Vm[:, :, h // 2, bp:bp + 128]
                ob = 4 + ci % 2
                ko = "bk%d" % ob
                units = list(range(8 * g + 7, -1, -1))
                nP = len(units) // 2

                def qk(j):
                    zb = 2 * (j % 2)
                    kz = "zp%d" % (j % 2)
                    lo = chain_cols(g, units[2 * j])[0] * 128
                    for c in range(2):
                        kb = units[2 * j + c]
                        c0, masks = chain_cols(g, kb)
                        assert c0 * 128 == lo
                        MM(bk[zb + c][:, lo:512], Kh[0:96, kb * 128:(kb + 1) * 128], Qh[0:96, g * 512 + lo:(g + 1) * 512],
                           True, len(masks) == 0, reads=[kk, qk_], writes=[kz])
                        for mi, (cb, which) in enumerate(masks):
                            MM(bk[zb + c][:, cb * 128:(cb + 1) * 128], ident, m_mla[which], False, mi == len(masks) - 1, writes=[kz])
                    ACT(p_sb[:, j % 2, :, lo:512], bkpair(zb)[:, :, lo:512], AF.Exp, reads=[kz], writes=["p%d" % (j % 2)],
                        scale=sm_scale)

                def prologue():
                    MM(bk[ob][:, :], zeros_bf, ckvn[:, 0, 0:512], True, True, reads=["zeros"], writes=[ko])
                    qk(0)
                    if nP > 1:
                        qk(1)

                def rounds(stepper, next_prologue=None):
                    for j in range(nP):
                        lo = chain_cols(g, units[2 * j])[0] * 128
                        for c in range(2):
                            kb = units[2 * j + c]
                            MM(bk[ob][:, lo:512], vl[:, kb, :], p_sb[:, j % 2, c, lo:512], False, True,
                               reads=["p%d" % (j % 2), "Vm"], writes=[ko])
                        if j == nP - 1 and next_prologue is not None:
                            next_prologue()
                        if j + 2 < nP:
                            for fi in range(MLA_FILL):
                                MM(bk[2 * (j % 2) + fi % 2][:, :], ident, ckvn[:, 0, 0:512], True, True, writes=["zp%d" % (j % 2)])
                            qk(j + 2)
                        stepper()

                def tail():
                    ACT(lnl[bp:bp + 64, :], bk[ob][lb:lb + 64, :], AF.Ln, reads=[ko], writes=["lnl"])
                    ACT(rinv[bp:bp + 64, :], lnl[bp:bp + 64, :], AF.Exp, reads=["lnl"], writes=["rinv"], scale=-1.0)
                    TT("dve", obT[bp:bp + 64, cc, g * 512:(g + 1) * 512], bk[ob][bp:bp + 64, :], rinv[bp:bp + 64, :], ALU.mult,
                       reads=[ko, "rinv"], writes=["obT"])

                return prologue, rounds, tail

            for bi in range(2):
                for half in range(2):
                    hs_ = slice(half * 2048, (half + 1) * 2048)
                    CP("dve", KhT[bi][64:96, hs_], kpe[64:96, hs_], reads=["wuqa", "wukv"], writes=[KKEY[bi]])
            for _ in prep_head(0):
                pass
            if debug:
                dump("KhT0", KhT[0][0:96, :], [96, S_LEN], reads=["KhT0"])
                dump("QhT0", QhT[0][0:96, :], [96, NQ], reads=["QhT0"])
            mparts = [(h, g, mla_parts(h, g, 4 * h + gi)) for h in range(8) for gi, g in enumerate((3, 2, 1, 0))]
            mparts[0][2][0]()
            prev_tail = None
            prep = None
            for idx, (h, g, (prologue, rounds, tail)) in enumerate(mparts):
                if g == 3:
                    prep = prep_head(h + 1) if h + 1 < 8 else iter(())

                def stepper(prep=prep):
                    next(prep, None)

                nxt = None
                if idx + 1 < len(mparts):
                    nh = mparts[idx + 1][0]
                    npro = mparts[idx + 1][2][0]

                    def nxt(nh=nh, h=h, npro=npro, prep=prep):
                        if nh != h:
                            for _ in prep:
                                pass
                        npro()
                if prev_tail is not None:
                    prev_tail()
                rounds(stepper, nxt)
                prev_tail = tail
            prev_tail()
            dump("obT", obT.rearrange("p a b -> p (a b)"), [128, 4 * NQ], reads=["obT"])
            S.barrier()

        if stop_after is None:
            Dw = Bump(16, 132)
            WG = Dw([128, 8, 3072], BF16)
            WA = Dw([128, 4, D], BF16)
            WB = Dw([128, 4, D], BF16)
            WO = Dw([128, 8, D], BF16)
            gate_bc = Dw([128, D], F32)
            fg_bc = Dw([128, D], F32)
            hTd = Dw([128, 8, 512], BF16)
            merged_off = Dw.o
            mergedT = Dw([128, 8, 512], BF16)
            xnew_off = Dw.o
            xnew = Dw([128, D], F32)
            outt_off = Dw.o
            outt = Dw([128, D], F32)
            sqjD = Dw([128, D], BF16)
            Dt = Bump(164, 204)
            xtD = [Dt([128, D], F32) for _ in range(2)]
            xres = [Dt([128, D], F32) for _ in range(2)]
            xnD = Dt([128, 4, D], BF16)
            og_off = Dt.o
            ogA = Dt([128, 4, 512], BF16)
            ogB = Dt([128, 4, 512], BF16)
            xnew2 = view(og_off, [128, D], F32)
            outt2 = view(og_off + 4096, [128, D], F32)
            sg = [Dt([128, 512], F32) for _ in range(2)]
            tm = [Dt([128, 512], F32) for _ in range(2)]
            bgate_bc = view(merged_off, [128, D], F32)
            cbc = view(merged_off + 4096, [128, 8, 128], F32)
            wstD = [view(xnew_off, [128, D], F32), view(outt_off, [128, D], F32)]
            WSTK = ["xnew", "outt"]

            for k in range(8):
                S.dma("pool", WG[:, k, 0:512], w3[:, k, SBZ:SBZ + 512], writes=["WGz0"])
            for k in range(8):
                S.dma("pool", WG[:, k, 512:1024], w3[:, k, MLAZ:MLAZ + 512], writes=["WGz512"])
            for k in range(8):
                S.dma("pool", WG[:, k, 1024:3072], w3[:, k, GA:GA + 2048], writes=["WGg"])
            for k in range(4):
                S.dma("pool", WA[:, k, :], w_a[k * 128:(k + 1) * 128, :], writes=["WA"])
                S.dma("pool", WB[:, k, :], w_b[k * 128:(k + 1) * 128, :], writes=["WB"])
            for k in range(8):
                S.dma("pool", WO[:, k, :], w_out[k * 128:(k + 1) * 128, :], writes=["WO"])

            xn = xnD
            sqj = sqjD
            norm_stats(xq[0:512, :], xtD)
            S.dma("sp", fg_bc, fg_row.broadcast_to([128, D]), writes=["fg_bc"])
            S.dma("sp", bgate_bc, b_gate.broadcast_to([128, D]), writes=["merged"])
            S.op("dve", lambda e: e.memset(cbc, 1.0), writes=["merged"])
            for k in range(8):
                TS("dve", cbc[:, k, :], cbc[:, k, :], cT[:, k:k + 1], None, ALU.mult, reads=["merged", "small"], writes=["merged"])
            for k in range(8):
                S.dma("sp", wstD[k % 2], w_ada[k * 128:(k + 1) * 128, 2 * D:3 * D], writes=[WSTK[k % 2]])
                for half in range(2):
                    MM(bk[half][:, :], cbc[:, k, :], wstD[k % 2][:, half * 512:(half + 1) * 512], k == 0, k == 7,
                       reads=["merged", WSTK[k % 2]], writes=["bk%d" % half])
            for half in range(2):
                TT("dve", gate_bc[:, half * 512:(half + 1) * 512], bk[half][:, :], bgate_bc[:, half * 512:(half + 1) * 512],
                   ALU.add, reads=["bk%d" % half, "merged"], writes=["gate_bc"])
            hkd = ("hT", id(hTd))
            norm_T(hTd)
            for qc in range(4):
                q0 = qc * 512
                if qc + 1 < 4:
                    queue_stats(xq[q0 + 512:q0 + 1024, :], xtD)
                for (off, oT, og, nm) in ((0, oaT, ogA, "ogA"), (512, obT, ogB, "ogB")):
                    for cc in range(4):
                        b = next_bank()
                        for k in range(8):
                            MM(bk[b][:, :], WG[:, k, off + cc * 128:off + (cc + 1) * 128], hTd[:, k, :], k == 0, k == 7,
                               reads=["WGz%d" % off, hkd + (k,)], writes=["bk%d" % b])
                        si = cc % 2
                        ACT(sg[si], bk[b][:, :], AF.Sigmoid, reads=["bk%d" % b], writes=["sg%d" % si])
                        TT("dve", tm[si], bk[b][:, :], sg[si], ALU.mult, reads=["bk%d" % b, "sg%d" % si], writes=["tm%d" % si])
                        TT("dve", og[:, cc, :], tm[si], oT[:, cc, q0:q0 + 512], ALU.mult, reads=["tm%d" % si], writes=[nm])
                for n in range(8):
                    bga, bgb, bya, byb = next_bank(), next_bank(), next_bank(), next_bank()
                    for k in range(8):
                        MM(bk[bga][:, :], WG[:, k, 1024 + n * 128:1024 + (n + 1) * 128], hTd[:, k, :], k == 0, k == 7,
                           reads=["WGg", hkd + (k,)], writes=["bk%d" % bga])
                    for k in range(8):
                        MM(bk[bgb][:, :], WG[:, k, 2048 + n * 128:2048 + (n + 1) * 128], hTd[:, k, :], k == 0, k == 7,
                           reads=["WGg", hkd + (k,)], writes=["bk%d" % bgb])
                    for k in range(4):
                        MM(bk[bya][:, :], WA[:, k, n * 128:(n + 1) * 128], ogA[:, k, :], k == 0, k == 3,
                           reads=["WA", "ogA"], writes=["bk%d" % bya])
                    for k in range(4):
                        MM(bk[byb][:, :], WB[:, k, n * 128:(n + 1) * 128], ogB[:, k, :], k == 0, k == 3,
                           reads=["WB", "ogB"], writes=["bk%d" % byb])
                    ACT(sg[0], bk[bga][:, :], AF.Sigmoid, reads=["bk%d" % bga], writes=["sg0"])
                    ACT(sg[1], bk[bgb][:, :], AF.Sigmoid, reads=["bk%d" % bgb], writes=["sg1"])
                    TT("dve", tm[0], bk[bya][:, :], sg[0], ALU.mult, reads=["bk%d" % bya, "sg0"], writes=["tm0"])
                    TT("dve", tm[1], bk[byb][:, :], sg[1], ALU.mult, reads=["bk%d" % byb, "sg1"], writes=["tm1"])
                    TT("dve", mergedT[:, n, :], tm[0], tm[1], ALU.add, reads=["tm0", "tm1"], writes=["merged"])
                    if n % 2 == 1:
                        drain_stats(1)
                drain_stats(4)
                if qc + 1 < 4:
                    norm_T(hTd)
                for blk in range(4):
                    rs = blk % 2
                    xk = "xres%d" % rs
                    xnw, xnk = (xnew, "xnew") if blk % 2 == 0 else (xnew2, "ogA")
                    ott, otk = (outt, "outt") if blk % 2 == 0 else (outt2, "ogB")
                    S.dma("sp", xres[rs], xq[q0 + blk * 128:q0 + (blk + 1) * 128, :], writes=[xk])
                    for half in range(2):
                        b = next_bank()
                        for k in range(8):
                            MM(bk[b][:, :], mergedT[:, k, blk * 128:(blk + 1) * 128], WO[:, k, half * 512:(half + 1) * 512],
                               k == 0, k == 7, reads=["merged", "WO"], writes=["bk%d" % b])
                        hs = slice(half * 512, (half + 1) * 512)
                        TT("dve", xnw[:, hs], bk[b][:, :], gate_bc[:, hs], ALU.mult, reads=["bk%d" % b, "gate_bc"], writes=[xnk])
                    TT("dve", xnw, xnw, xres[rs], ALU.add, reads=[xnk, xk], writes=[xnk])
                    c = state["st"]
                    state["st"] = (c + 1) % 8
                    sk = "stat%d" % c
                    ACT(sqjD, xnw, AF.Square, reads=[xnk], writes=["sqjD", sk], accum_out=stat[:, c:c + 1])
                    ACT(stat[:, 8 + c:9 + c], stat[:, c:c + 1], AF.Ln, reads=[sk, "small"], writes=[sk], scale=1.0 / D, bias=epsc)
                    ACT(stat[:, 16 + c:17 + c], stat[:, 8 + c:9 + c], AF.Exp, reads=[sk], writes=[sk], scale=-0.5)
                    S.op("dve", lambda e, c=c, ott=ott, xnw=xnw: e.scalar_tensor_tensor(ott, xnw, stat[:, 16 + c:17 + c], fg_bc, ALU.mult, ALU.mult),
                         reads=[xnk, sk, "fg_bc"], writes=[otk])
                    S.dma("pool", out_d[q0 + blk * 128:q0 + (blk + 1) * 128, :], ott, reads=[otk])

        for q in ("sp", "pool"):
            for i in range(max(0, S.dma_n[q] - ND), S.dma_n[q]):
                S._wait("pool", ("d", (q, i)))
        with nc.Block() as block:
            S.emit(block)
    return nc, dbg_outs


def host_inputs(inputs):
    x = np.asarray(inputs["x"], np.float32)
    c = np.asarray(inputs["c"], np.float32)
    pos = np.asarray(inputs["positions"], np.int32)
    f = lambda k: np.ascontiguousarray(np.asarray(inputs[k], np.float32))
    w_ada = f("w_ada")[0]
    b_ada = f("b_ada")[0]
    tri = np.tril(np.ones((128, 128), np.float32))
    ident = np.eye(128, dtype=np.float32)
    omt = 1.0 - tri
    ss, tt = np.meshgrid(np.arange(128), np.arange(128), indexing="ij")
    strict = np.where(ss < tt, 0.0, MASKV).astype(np.float32)
    causal = np.where(ss <= tt, 0.0, MASKV).astype(np.float32)
    allm = np.full((128, 128), MASKV, np.float32)
    nom = np.zeros((128, 128), np.float32)
    inv_freq = (10000.0 ** (-np.arange(0, 32, 2, dtype=np.float32) / np.float32(32))).astype(np.float32)
    invf = np.tile(np.concatenate([inv_freq, inv_freq]), 4).reshape(128, 1).astype(np.float32)
    common = {
        "w_ada": w_ada,
        "b_adaT": np.ascontiguousarray(b_ada.reshape(24, 128).T),
        "b_gate": np.ascontiguousarray(b_ada[2 * D:3 * D].reshape(1, D)),
        "ngT": np.ascontiguousarray(f("norm_gain")[0].reshape(8, 128).T),
        "w_in": f("w_in")[0],
        "qgT": np.ascontiguousarray(f("q_norm_gain")[0].reshape(3, 128).T),
        "w_uq": f("w_uq")[0],
        "kvgT": np.ascontiguousarray(f("kv_norm_gain")[0].reshape(2, 128).T),
        "w_ukv": f("w_ukv")[0],
        "w_a": f("w_branch_a")[0],
        "w_b": f("w_branch_b")[0],
        "w_out": f("w_out")[0],
        "fg_row": np.ascontiguousarray(f("final_norm_gain").reshape(1, D)),
        "invf": invf,
    }
    maps = []
    for core in range(8):
        b, p = core // 2, core % 2
        blocks = [2 * j + p for j in range(16)]
        xb = x[b].reshape(32, 128, D)
        xq = np.ascontiguousarray(xb[blocks].reshape(NQ, D))
        pq = pos[b].reshape(32, 128)[blocks].reshape(NQ)
        posall = np.ascontiguousarray(np.concatenate([pos[b], pq]).reshape(4, 1536).astype(np.int32))
        if p == 0:
            mats = [ident, tri, omt, allm, strict, allm, causal]
        else:
            mats = [ident, tri, omt, strict, nom, causal, nom]
        cm = np.ascontiguousarray(np.stack(mats, axis=1).astype(np.float32))
        m = dict(common)
        m.update({"xf": np.ascontiguousarray(x[b]), "xq": xq, "posall": posall,
                  "cT": np.ascontiguousarray(c[b].reshape(8, 128).T), "cmats": cm})
        maps.append(m)
    return maps


_CACHE = {}


def kernel(**inputs):
    maps = host_inputs(inputs)
    if "nc" not in _CACHE:
        _CACHE["nc"] = build_program()[0]
    nc = _CACHE["nc"]
    res = run_bass_kernel_spmd(nc, maps, core_ids=list(range(8)))
    out = np.zeros((4, 32, 128, D), np.float32)
    for core in range(8):
        b, p = core // 2, core % 2
        o = np.asarray(res.results[core]["out"], np.float32).reshape(16, 128, D)
        out[b, p::2] = o
    return out.reshape(4, S_LEN, D)
```

```python
import math
from contextlib import ExitStack
import numpy as np
import concourse.bass as bass
import concourse.mybir as mybir
from concourse.bass_utils import run_bass_kernel_spmd

F32 = mybir.dt.float32
BF16 = mybir.dt.bfloat16
I32 = mybir.dt.int32
AF = mybir.ActivationFunctionType
ALU = mybir.AluOpType

ENGS = ["pe", "act", "dve", "pool", "sp"]
PH = 2048
NPH = {"pe": 7, "act": 5, "dve": 4, "pool": 1, "sp": 1}
ND = 16

D = 1024
S_LEN = 4096
NQ = 2048
SBQ, SBK, SBV, SBZ, CQ, CKV, KROT, MLAZ, GA, GB = 0, 512, 1024, 1536, 2048, 2432, 2688, 2720, 3232, 4256
EPS = 1e-6
MASKV = -30000.0
SB_FILL = 4
MLA_FILL = 0
DEBUG = False


class Sched:
    def __init__(self, nc, esem, dsem):
        self.nc = nc
        self.esem = esem
        self.dsem = dsem
        self.ops = {e: [] for e in ENGS}
        self.cnt = {e: 0 for e in ENGS}
        self.waited_e = {e: {} for e in ENGS}
        self.waited_d = {e: {} for e in ENGS}
        self.last_w = {}
        self.readers = {}
        self.dma_i = 0
        self.dma_n = {"sp": 0, "pool": 0}

    def _wait(self, E, ev):
        if ev[0] == "e":
            _, e2, n = ev
            if e2 == E and E == "pe":
                return
            if self.waited_e[E].get(e2, 0) >= n:
                return
            self.waited_e[E][e2] = n
            sem = self.esem[e2][(n - 1) // PH]
            val = (n - 1) % PH + 1
        else:
            q, i = ev[1]
            k = i % ND
            val = 16 * (i // ND + 1)
            if self.waited_d[E].get((q, k), 0) >= val:
                return
            self.waited_d[E][(q, k)] = val
            sem = self.dsem[q][k]
        self.ops[E].append(lambda eng, sem=sem, val=val: eng.wait_ge(sem, val))

    def _deps(self, E, reads, writes, extra=(), dma_accum=False):
        deps = list(extra)
        for b in reads:
            for ev in self.last_w.get(b, ()):
                deps.append(ev)
        for b in writes:
            for ev in self.last_w.get(b, ()):
                if dma_accum and ev[0] == "d":
                    continue
                deps.append(ev)
            r = self.readers.get(b)
            if r:
                for e2, n in r[0].items():
                    deps.append(("e", e2, n))
                for i in r[1]:
                    deps.append(("d", i))
        for ev in deps:
            self._wait(E, ev)

    def _record(self, ev, reads, writes, dma_accum=False):
        for b in reads:
            r = self.readers.setdefault(b, ({}, []))
            if ev[0] == "e":
                r[0][ev[1]] = ev[2]
            else:
                r[1].append(ev[1])
        for b in writes:
            r = self.readers.get(b)
            had_readers = bool(r and (r[0] or r[1]))
            if dma_accum and not had_readers:
                self.last_w[b] = [e for e in self.last_w.get(b, ()) if e[0] == "d"] + [ev]
            else:
                self.last_w[b] = [ev]
            self.readers[b] = ({}, [])

    def op(self, E, fn, reads=(), writes=(), extra=()):
        self._deps(E, reads, writes, extra)
        n = self.cnt[E] + 1
        self.cnt[E] = n
        assert (n - 1) // PH < NPH[E], "too many instrs on %s" % E
        sem = self.esem[E][(n - 1) // PH]
        self.ops[E].append(lambda eng, fn=fn, sem=sem: fn(eng).then_inc(sem, 1))
        ev = ("e", E, n)
        self._record(ev, reads, writes)
        return ev

    def dma(self, Q, out, in_, reads=(), writes=(), extra=()):
        i = self.dma_n[Q]
        self.dma_n[Q] += 1
        self.dma_i += 1
        ex = list(extra)
        if i >= ND:
            ex.append(("d", (Q, i - ND)))
        self._deps(Q, reads, writes, ex, dma_accum=True)
        sem = self.dsem[Q][i % ND]
        self.ops[Q].append(
            lambda eng, out=out, in_=in_, sem=sem: eng.dma_start(out=out, in_=in_).then_inc(sem, 16))
        ev = ("d", (Q, i))
        self._record(ev, reads, writes, dma_accum=True)
        return ev

    def barrier(self):
        snap = dict(self.cnt)
        for E in ENGS:
            for e2 in ENGS:
                if e2 != E and snap[e2] > 0:
                    self._wait(E, ("e", e2, snap[e2]))

    def emit(self, block):
        ops = self.ops

        @block.tensor
        def _(eng):
            for f in ops["pe"]:
                f(eng)

        @block.scalar
        def _(eng):
            for f in ops["act"]:
                f(eng)

        @block.vector
        def _(eng):
            for f in ops["dve"]:
                f(eng)

        @block.gpsimd
        def _(eng):
            for f in ops["pool"]:
                f(eng)

        @block.sync
        def _(eng):
            for f in ops["sp"]:
                f(eng)


def build_program(debug=False, stop_after=None):
    nc = bass.Bass("TRN2", target_bir_lowering=False)

    def din(name, shape, dt=F32):
        return nc.dram_tensor(name, list(shape), dt, kind="ExternalInput").ap()

    xf = din("xf", [S_LEN, D])
    xq = din("xq", [NQ, D])
    posall = din("posall", [4, 1536], I32)
    cT_d = din("cT", [128, 8])
    w_ada = din("w_ada", [D, 3 * D])
    b_adaT = din("b_adaT", [128, 24])
    b_gate = din("b_gate", [1, D])
    ngT = din("ngT", [128, 8])
    w_in = din("w_in", [D, 5280])
    qgT = din("qgT", [128, 3])
    w_uq = din("w_uq", [384, 768])
    kvgT = din("kvgT", [128, 2])
    w_ukv = din("w_ukv", [256, 1024])
    w_a = din("w_a", [512, D])
    w_b = din("w_b", [512, D])
    w_out = din("w_out", [D, D])
    fg_row = din("fg_row", [1, D])
    cmats = din("cmats", [128, 7, 128])
    invf_d = din("invf", [128, 1])
    out_d = nc.dram_tensor("out", [NQ, D], F32, kind="ExternalOutput").ap()
    dbg_outs = {}

    with ExitStack() as es:
        esem = {e: [es.enter_context(nc.semaphore("s_%s_%d" % (e, i))) for i in range(NPH[e])] for e in ENGS}
        dsem = {q: [es.enter_context(nc.semaphore("d_%s_%d" % (q, i))) for i in range(ND)] for q in ("sp", "pool")}
        S = Sched(nc, esem, dsem)

        ARENA_KB = 204
        arena = es.enter_context(nc.sbuf_tensor("arena", [128, ARENA_KB * 512], BF16))
        arena32 = arena.bitcast(F32)
        arenai = arena.bitcast(I32)

        def view(off_bytes, shape, dt):
            n = int(np.prod(shape[1:]))
            esz = 2 if dt == BF16 else 4
            assert off_bytes % 4 == 0
            assert off_bytes + n * esz <= ARENA_KB * 1024, (off_bytes, shape)
            base = {BF16: arena, F32: arena32, I32: arenai}[dt]
            o = off_bytes // esz
            ap = base[:, o:o + n]
            if len(shape) == 3:
                ap = ap.rearrange("p (a b) -> p a b", b=shape[2])
            elif len(shape) == 4:
                ap = ap.rearrange("p (a b c) -> p a b c", b=shape[2], c=shape[3])
            return ap

        class Bump:
            def __init__(self, start_kb, end_kb):
                self.o = start_kb * 1024
                self.end = end_kb * 1024

            def __call__(self, shape, dt):
                n = int(np.prod(shape[1:])) * (2 if dt == BF16 else 4)
                n = (n + 3) // 4 * 4
                v = view(self.o, shape, dt)
                self.o += n
                assert self.o <= self.end, ("bump overflow", self.o, self.end)
                return v

        psum_all = es.enter_context(nc.psum_tensor("psall", [128, 4096], F32))
        psum_bf = psum_all.bitcast(BF16)
        bk = [psum_all[:, i * 512:(i + 1) * 512] for i in range(8)]
        bkb = [psum_bf[:, i * 1024:(i + 1) * 1024] for i in range(8)]

        def bkpair(i):
            return psum_all[:, i * 512:(i + 2) * 512].rearrange("p (c n) -> p c n", n=512)

        def MM(out, lhsT, rhs, start, stop, reads=(), writes=()):
            return S.op("pe", lambda e: e.matmul(out, lhsT, rhs, start=start, stop=stop, skip_group_check=True),
                        reads=reads, writes=writes)

        def ACT(out, in_, func, reads=(), writes=(), **kw):
            return S.op("act", lambda e: e.activation(out, in_, func, **kw), reads=reads, writes=writes)

        def TT(eng, out, in0, in1, op, reads=(), writes=()):
            return S.op(eng, lambda e: e.tensor_tensor(out, in0, in1, op), reads=reads, writes=writes)

        def TS(eng, out, in0, s1, s2, op0, op1=None, reads=(), writes=()):
            if op1 is None:
                return S.op(eng, lambda e: e.tensor_scalar(out, in0, s1, None, op0), reads=reads, writes=writes)
            return S.op(eng, lambda e: e.tensor_scalar(out, in0, s1, s2, op0, op1), reads=reads, writes=writes)

        def CP(eng, out, in_, reads=(), writes=()):
            return S.op(eng, lambda e: e.tensor_copy(out, in_), reads=reads, writes=writes)

        def dump(name, ap, shape, reads=()):
            if not debug:
                return
            t = nc.dram_tensor("dbg_" + name, list(shape), F32, kind="ExternalOutput").ap()
            dbg_outs[name] = t
            S.dma("pool", t, ap, reads=reads)

        P = Bump(0, 16)
        cm = P([128, 7, 128], BF16)
        ident = cm[:, 0, :]
        triI = cm[:, 1, :]
        omt = cm[:, 2, :]
        m_sb = [cm[:, 3, :], cm[:, 4, :]]
        m_mla = [cm[:, 5, :], cm[:, 6, :]]
        zeros_bf = P([128, 128], BF16)
        ones_bf = P([128, 128], BF16)
        mod_sb = P([128, 24], F32)
        gs = P([128, 8], F32)
        small = P([128, 64], F32)
        cT = small[:, 0:8]
        badaT = small[:, 8:32]
        ng = small[:, 32:40]
        qg = small[:, 40:43]
        kvg = small[:, 43:45]
        epsc = small[:, 45:46]
        invf = small[:, 46:47]
        tmp8 = small[:, 48:56]
        stat = P([128, 64], F32)
        tabcos = P([128, 1536], F32)
        tabsin = P([128, 1536], F32)
        shift = mod_sb[:, 0:8]

        L = Bump(16, 52)
        ckvn = L([128, 2, S_LEN], BF16)
        kpe = L([128, S_LEN], BF16)
        cqn = L([128, 3, NQ], BF16)
        SBD = Bump(52, 132)
        KT = SBD([128, 4, S_LEN], BF16)
        Vsb = SBD([128, 32, 512], BF16)
        QT = SBD([128, 4, NQ], BF16)
        OA = Bump(132, 164)
        oaT = OA([128, 4, NQ], BF16)
        obT = OA([128, 4, NQ], BF16)

        WK = view(132 * 1024, [128, 8, 1344], BF16)
        w3 = w_in.rearrange("(k p) n -> p k n", p=128)

        def load_WK():
            for k in range(8):
                S.dma("pool", WK[:, k, 0:1024], w3[:, k, SBK:SBK + 1024], writes=["WK"])
                S.dma("pool", WK[:, k, 1024:1312], w3[:, k, CKV:CKV + 288], writes=["WK"])
                S.dma("pool", WK[:, k, 1312:1328], w3[:, k, KROT + 16:KROT + 32], writes=["WK"])
                S.dma("pool", WK[:, k, 1328:1344], w3[:, k, KROT:KROT + 16], writes=["WK"])
            TS("pool", WK[:, :, 1312:1328], WK[:, :, 1312:1328], -1.0, None, ALU.mult, reads=["WK"], writes=["WK"])

        def load_WQ():
            for k in range(8):
                S.dma("pool", WK[:, k, 0:512], w3[:, k, SBQ:SBQ + 512], writes=["WK"])
                S.dma("pool", WK[:, k, 512:896], w3[:, k, CQ:CQ + 384], writes=["WK"])

        A_ = Bump(52, 132)
        A2_ = Bump(164, 204)
        wst = [A_([128, D], F32) for _ in range(8)] + [A2_([128, D], F32) for _ in range(8)]
        posi = A_([128, 1536], I32)
        ang = A_([128, 1536], F32)
        tt = A_([128, 1536], F32)
        ki = A_([128, 1536], I32)
        kf = A_([128, 1536], F32)
        ff = A_([128, 1536], F32)

        S.dma("pool", cm, cmats, writes=["cm"])
        load_WK()
        S.dma("sp", cT, cT_d, writes=["small"])
        S.dma("sp", badaT, b_adaT, writes=["small"])
        S.dma("sp", ng, ngT, writes=["small"])
        S.dma("sp", qg, qgT, writes=["small"])
        S.dma("sp", kvg, kvgT, writes=["small"])
        S.dma("sp", invf, invf_d, writes=["small"])
        for q in range(4):
            S.dma("sp", posi[q * 32:(q + 1) * 32, :], posall[q:q + 1, :].broadcast_to([32, 1536]), writes=["posi"])
        S.op("dve", lambda e: e.memset(zeros_bf, 0.0), writes=["zeros"])
        S.op("dve", lambda e: e.memset(ones_bf, 1.0), writes=["ones"])
        S.op("dve", lambda e: e.memset(epsc, EPS), reads=["small"], writes=["small"])

        CP("dve", ang, posi, reads=["posi"], writes=["ang"])
        TS("dve", ang, ang, invf, None, ALU.mult, reads=["ang", "small"], writes=["ang"])
        inv2pi = 1.0 / (2.0 * math.pi)
        for (tab, phase, nm) in ((tabsin, 0.0, "sin"), (tabcos, 0.25, "cos")):
            TS("dve", tt, ang, inv2pi, phase, ALU.mult, ALU.add, reads=["ang"], writes=["tt"])
            CP("dve", ki, tt, reads=["tt"], writes=["ki"])
            CP("dve", kf, ki, reads=["ki"], writes=["kf"])
            TT("dve", ff, tt, kf, ALU.subtract, reads=["tt", "kf"], writes=["ff"])
            TS("dve", kf, ff, 0.5, None, ALU.is_gt, reads=["ff"], writes=["kf"])
            TT("dve", ff, ff, kf, ALU.subtract, reads=["ff", "kf"], writes=["ff"])
            TS("dve", kf, ff, -0.5, None, ALU.is_lt, reads=["ff"], writes=["kf"])
            TT("dve", ff, ff, kf, ALU.add, reads=["ff", "kf"], writes=["ff"])
            ACT(tab, ff, AF.Sin, reads=["ff"], writes=["tab" + nm], scale=6.283185)
        for half in range(2):
            for k in range(8):
                S.dma("sp", wst[half * 8 + k], w_ada[k * 128:(k + 1) * 128, half * D:(half + 1) * D], writes=["wst%d" % (half * 8 + k)])
        for half in range(2):
            for jj in range(8):
                j = half * 8 + jj
                for k in range(8):
                    MM(bk[0][:, j:j + 1], wst[half * 8 + k][:, jj * 128:(jj + 1) * 128], cT[:, k:k + 1], k == 0, k == 7,
                       reads=["wst%d" % (half * 8 + k), "small"], writes=["bk0"])
        TT("dve", mod_sb[:, 0:16], bk[0][:, 0:16], badaT[:, 0:16], ALU.add, reads=["bk0", "small"], writes=["mod"])
        TS("dve", tmp8, mod_sb[:, 8:16], 1.0, None, ALU.add, reads=["mod"], writes=["tmp8"])
        TT("dve", gs, tmp8, ng, ALU.mult, reads=["tmp8", "small"], writes=["gs"])

        dump("gs", gs, [128, 8], reads=["gs"])
        dump("mod", mod_sb[:, 0:16], [128, 16], reads=["mod"])
        dump("tabcos", tabcos, [128, 1536], reads=["tabcos"])
        dump("tabsin", tabsin, [128, 1536], reads=["tabsin"])
        S.barrier()

        B_ = Bump(164, 204)
        B2 = Bump(154, 164)
        jit = B2([128, 2, 512], F32)
        sqj = B2([128, D], BF16)
        sq = B2([128, 3, 512], BF16)
        xt = [B_([128, D], F32) for _ in range(2)]
        xn = B_([128, 4, D], BF16)
        hTs = [B_([128, 8, 512], BF16) for _ in range(2)]
        rbc = B_([128, 512], F32)
        t1 = B_([128, 512], F32)
        t2 = B_([128, 512], F32)

        state = {"xt": 0, "st": 0, "bank": 2, "ev": 0, "fill": 0, "sbu": 0, "pend": []}

        def next_bank():
            b = state["bank"]
            state["bank"] = 2 + (b - 2 + 1) % 6
            return b

        def evac_engine():
            state["ev"] ^= 1
            return "act" if state["ev"] else "dve"

        def norm_stats_blk(src, xts, blk):
            slot = state["xt"]
            state["xt"] ^= 1
            xk = "xt%d" % slot
            c = state["st"]
            state["st"] = (c + 1) % 8
            sk = "stat%d" % c
            S.dma("sp", xts[slot], src[blk * 128:(blk + 1) * 128, :], writes=[xk])
            ACT(sqj, xts[slot], AF.Square, reads=[xk], writes=["sqj", sk], accum_out=stat[:, c:c + 1])
            ACT(stat[:, 8 + c:9 + c], stat[:, c:c + 1], AF.Ln, reads=[sk, "small"], writes=[sk],
                scale=1.0 / D, bias=epsc)
            ACT(stat[:, 16 + c:17 + c], stat[:, 8 + c:9 + c], AF.Exp, reads=[sk], writes=[sk], scale=-0.5)
            TS("dve", xn[:, blk, :], xts[slot], stat[:, 16 + c:17 + c], None, ALU.mult,
               reads=[xk, sk], writes=["xn%d" % blk])

        def norm_stats(src, xts):
            for blk in range(4):
                norm_stats_blk(src, xts, blk)

        def queue_stats(src, xts):
            state["pend"] = [(src, xts, blk) for blk in range(4)]

        def drain_stats(n=4):
            for _ in range(n):
                if state["pend"]:
                    a_, b_, c_ = state["pend"].pop(0)
                    norm_stats_blk(a_, b_, c_)

        def norm_T(hT):
            for k in range(8):
                tb = k % 2
                for blk in range(4):
                    S.op("pe", lambda e, k=k, blk=blk, tb=tb, xn_=xn: e.transpose(
                        bkb[tb][:, blk * 128:(blk + 1) * 128], xn_[:, blk, k * 128:(k + 1) * 128], ident),
                        reads=["xn%d" % blk, "cm"], writes=["bk%d" % tb])
                if evac_engine() == "act":
                    ACT(hT[:, k, :], bkb[tb][:, 0:512], AF.Identity, reads=["bk%d" % tb, "gs", "mod"],
                        writes=[("hT", id(hT), k)], scale=gs[:, k:k + 1], bias=shift[:, k:k + 1])
                else:
                    TS("dve", hT[:, k, :], bkb[tb][:, 0:512], gs[:, k:k + 1], shift[:, k:k + 1], ALU.mult, ALU.add,
                       reads=["bk%d" % tb, "gs", "mod"], writes=[("hT", id(hT), k)])

        def rms_sq(banks, nch):
            for c in range(nch):
                ACT(sq[:, c, :], bk[banks[c]][:, :], AF.Square, reads=["bk%d" % banks[c]], writes=["sq%d" % c])

        def rms_fin(banks, nch, dim, dst, t0, gkey):
            sb_ = next_bank()
            for c in range(nch):
                MM(bk[sb_][:, :], ones_bf, sq[:, c, :], c == 0, c == nch - 1, reads=["sq%d" % c, "ones"],
                   writes=["bk%d" % sb_])
            ACT(rbc, bk[sb_][:, :], AF.Ln, reads=["bk%d" % sb_, "small"], writes=["rbc"], scale=1.0 / dim, bias=epsc)
            ACT(rbc, rbc, AF.Exp, reads=["rbc"], writes=["rbc"], scale=-0.5)
            for c in range(nch):
                TT("dve", dst[:, c, t0:t0 + 512], bk[banks[c]][:, :], rbc, ALU.mult,
                   reads=["bk%d" % banks[c], "rbc"], writes=[])

        def proj_K(tc):
            t0 = tc * 512
            hT = hTs[tc % 2]
            hk = ("hT", id(hT))
            for cc in range(4):
                b = next_bank()
                for k in range(8):
                    MM(bk[b][:, :], WK[:, k, cc * 128:(cc + 1) * 128], hT[:, k, :], k == 0, k == 7,
                       reads=["WK", hk + (k,)], writes=["bk%d" % b])
                if evac_engine() == "act":
                    ACT(KT[:, cc, t0:t0 + 512], bk[b][:, :], AF.Copy, reads=["bk%d" % b], writes=[], scale=0.125)
                else:
                    TS("dve", KT[:, cc, t0:t0 + 512], bk[b][:, :], 0.125, None, ALU.mult, reads=["bk%d" % b], writes=[])
                if cc % 2 == 1:
                    drain_stats(1)
            for blk in range(4):
                b = next_bank()
                for k in range(8):
                    MM(bk[b][:, :], hT[:, k, blk * 128:(blk + 1) * 128], WK[:, k, 512:1024], k == 0, k == 7,
                       reads=["WK", hk + (k,)], writes=["bk%d" % b])
                if evac_engine() == "act":
                    ACT(Vsb[:, tc * 4 + blk, :], bk[b][:, :], AF.Copy, reads=["bk%d" % b], writes=[])
                else:
                    CP("dve", Vsb[:, tc * 4 + blk, :], bk[b][:, :], reads=["bk%d" % b], writes=[])
                if blk % 2 == 1:
                    drain_stats(1)
            cb = []
            for c2 in range(2):
                b = next_bank()
                cb.append(b)
                for k in range(8):
                    MM(bk[b][:, :], WK[:, k, 1024 + c2 * 128:1024 + (c2 + 1) * 128], hT[:, k, :], k == 0, k == 7,
                       reads=["WK", hk + (k,)], writes=["bk%d" % b])
            rms_sq(cb, 2)
            bA = next_bank()
            for k in range(8):
                MM(bk[bA][0:96, :], WK[:, k, 1216:1312], hT[:, k, :], k == 0, k == 7, reads=["WK", hk + (k,)], writes=["bk%d" % bA])
            bB = next_bank()
            for k in range(8):
                MM(bk[bB][0:96, :], WK[:, k, 1248:1344], hT[:, k, :], k == 0, k == 7, reads=["WK", hk + (k,)], writes=["bk%d" % bB])
            qd, col = tc // 3, (tc % 3) * 512
            S.dma("sp", jit[64:96, 0, :], tabcos[qd * 32:(qd + 1) * 32, col:col + 512], writes=["jit"])
            S.dma("sp", jit[64:96, 1, :], tabsin[qd * 32:(qd + 1) * 32, col:col + 512], writes=["jit"])
            TT("dve", t1[64:96, :], bk[bA][64:96, :], jit[64:96, 0, :], ALU.mult, reads=["bk%d" % bA, "jit"], writes=["t1"])
            TT("dve", t2[64:96, :], bk[bB][64:96, :], jit[64:96, 1, :], ALU.mult, reads=["bk%d" % bB, "jit"], writes=["t2"])
            TT("dve", kpe[64:96, t0:t0 + 512], t1[64:96, :], t2[64:96, :], ALU.add, reads=["t1", "t2"], writes=[])
            return lambda: rms_fin(cb, 2, 256, ckvn, t0, "ckvn")
        def proj_Q(qc):
            q0 = qc * 512
            hT = hTs[qc % 2]
            hk = ("hT", id(hT))
            for cc in range(4):
                b = next_bank()
                for k in range(8):
                    MM(bk[b][:, :], WK[:, k, cc * 128:(cc + 1) * 128], hT[:, k, :], k == 0, k == 7,
                       reads=["WK", hk + (k,)], writes=["bk%d" % b])
                if evac_engine() == "act":
                    ACT(QT[:, cc, q0:q0 + 512], bk[b][:, :], AF.Copy, reads=["bk%d" % b], writes=[])
                else:
                    CP("dve", QT[:, cc, q0:q0 + 512], bk[b][:, :], reads=["bk%d" % b], writes=[])
                drain_stats(1)
            cb = []
            for c3 in range(3):
                b = next_bank()
                cb.append(b)
                for k in range(8):
                    MM(bk[b][:, :], WK[:, k, 512 + c3 * 128:512 + (c3 + 1) * 128], hT[:, k, :], k == 0, k == 7,
                       reads=["WK", hk + (k,)], writes=["bk%d" % b])
            rms_sq(cb, 3)
            return lambda: rms_fin(cb, 3, 384, cqn, q0, "cqn")
        chunks = [("K", i) for i in range(8)] + [("Q", i) for i in range(4)]

        def src_of(ch):
            return (xf if ch[0] == "K" else xq)[ch[1] * 512:(ch[1] + 1) * 512, :]

        norm_stats(src_of(chunks[0]), xt)
        norm_T(hTs[0])
        for i, ch in enumerate(chunks):
            if i + 1 < len(chunks):
                queue_stats(src_of(chunks[i + 1]), xt)
            if ch == ("Q", 0):
                load_WQ()
            fin = (proj_K if ch[0] == "K" else proj_Q)(ch[1])
            drain_stats(4)
            if i + 1 < len(chunks):
                norm_T(hTs[(i + 1) % 2])
            fin()
        if debug:
            S.barrier()
        dump("KT", KT.rearrange("p a b -> p (a b)"), [128, 4 * S_LEN], reads=["KT"])
        dump("QT", QT.rearrange("p a b -> p (a b)"), [128, 4 * NQ], reads=["QT"])
        dump("Vsb", Vsb.rearrange("p a b -> p (a b)"), [128, 32 * 512], reads=["Vsb"])
        dump("ckvn", ckvn.rearrange("p a b -> p (a b)"), [128, 2 * S_LEN], reads=["ckvn"])
        dump("cqn", cqn.rearrange("p a b -> p (a b)"), [128, 3 * NQ], reads=["cqn"])
        dump("kpe", kpe[64:96, :], [32, S_LEN], reads=["kpe"])
        S.barrier()

        def chain_cols(g, kb):
            c0 = max(0, kb // 2 - 4 * g)
            masks = []
            for c in range(c0, 4):
                j = 4 * g + c
                if kb == 2 * j + 1:
                    masks.append((c, 0))
                elif kb == 2 * j:
                    masks.append((c, 1))
            return c0, masks

        def run_chains(chains):
            live = list(chains)
            while live:
                nxt = []
                for gen in live:
                    try:
                        next(gen)
                        nxt.append(gen)
                    except StopIteration:
                        pass
                live = nxt

        def run_slots(slots):
            slots = [list(x) for x in slots]
            cur = [sl.pop(0) if sl else None for sl in slots]
            while any(c is not None for c in cur):
                for i in range(len(cur)):
                    while cur[i] is not None:
                        try:
                            next(cur[i])
                            break
                        except StopIteration:
                            cur[i] = slots[i].pop(0) if slots[i] else None

        if stop_after != "B":
            C1 = Bump(164, 204)
            e_sb = C1([128, 3, 2, 512], F32)
            sp_sb = C1([128, 2, 2, 512], BF16)
            g_sb = C1([128, 2, 2, 512], F32)
            w_sb = C1([128, 2, 2, 512], BF16)
            zp, Ap = bkpair(0), bkpair(2)

            def sb_pair(hp, g):
                cc = hp
                units = list(range(8 * g + 7, -1, -1))
                nU = len(units)
                ub = state["sbu"]
                state["sbu"] += nU
                def fill(n, lo=0):
                    for _ in range(n):
                        fb = 6 + (state["fill"] % 2)
                        state["fill"] += 1
                        MM(bk[fb][:, lo:512], ident, QT[:, 0, lo:512], True, True)

                def lo_of(i):
                    return chain_cols(g, units[i])[0] * 128

                def qk(i):
                    kb = units[i]
                    c0, masks = chain_cols(g, kb)
                    lo = c0 * 128
                    for c in range(2):
                        bp = 64 * c
                        MM(bk[c][:, lo:512], KT[bp:bp + 64, cc, kb * 128:(kb + 1) * 128],
                           QT[bp:bp + 64, cc, g * 512 + lo:(g + 1) * 512], True, len(masks) == 0, writes=["z"])
                        for mi, (cb, which) in enumerate(masks):
                            MM(bk[c][:, cb * 128:(cb + 1) * 128], ident, m_sb[which], False, mi == len(masks) - 1, writes=["z"])

                def tri(i):
                    lo = lo_of(i)
                    for c in range(2):
                        MM(bk[2 + c][:, lo:512], triI, sp_sb[:, (ub + i) % 2, c, lo:512], False, True, reads=["sp%d" % ((ub + i) % 2)], writes=["A"])

                def omt_(i):
                    lo = lo_of(i)
                    for c in range(2):
                        MM(bk[2 + c][:, lo:512], omt, sp_sb[:, (ub + i) % 2, c, lo:512], False, True, reads=["sp%d" % ((ub + i) % 2)], writes=["A"])

                def pv(i):
                    lo = lo_of(i)
                    kb = units[i]
                    for c in range(2):
                        h = 2 * hp + c
                        MM(bk[4 + c][0:64, lo:512], Vsb[:, kb, h * 64:(h + 1) * 64], w_sb[:, (ub + i) % 2, c, lo:512], False, True,
                           reads=["w%d" % ((ub + i) % 2)], writes=["o"])

                def act_e(i):
                    lo = lo_of(i)
                    ACT(e_sb[:, (ub + i) % 3, :, lo:512], zp[:, :, lo:512], AF.Exp, reads=["z"], writes=["e%d" % ((ub + i) % 3)])

                def act_sp(i):
                    lo = lo_of(i)
                    ACT(sp_sb[:, (ub + i) % 2, :, lo:512], e_sb[:, (ub + i) % 3, :, lo:512], AF.Ln, reads=["e%d" % ((ub + i) % 3)],
                        writes=["sp%d" % ((ub + i) % 2)], bias=1.0)

                def act_g(i):
                    lo = lo_of(i)
                    ACT(g_sb[:, (ub + i) % 2, :, lo:512], Ap[:, :, lo:512], AF.Exp, reads=["A"], writes=["g%d" % ((ub + i) % 2)], scale=-1.0)

                def dve_w(i):
                    lo = lo_of(i)
                    TT("dve", w_sb[:, (ub + i) % 2, :, lo:512], e_sb[:, (ub + i) % 3, :, lo:512], g_sb[:, (ub + i) % 2, :, lo:512], ALU.mult,
                       reads=["e%d" % ((ub + i) % 3), "g%d" % ((ub + i) % 2)], writes=["w%d" % ((ub + i) % 2)])

                def prologue():
                    qk(0)
                    act_e(0)
                    act_sp(0)
                    if nU > 1:
                        qk(1)

                def main(next_prologue):
                    for c in range(2):
                        MM(bk[2 + c][:, :], zeros_bf, QT[:, 0, 0:512], True, True, reads=["zeros"], writes=["A"])
                        MM(bk[4 + c][0:64, :], zeros_bf[:, 0:64], QT[:, 0, 0:512], True, True, reads=["zeros"], writes=["o"])
                    for i in range(nU):
                        if i == nU - 1 and next_prologue is not None:
                            next_prologue()
                        tri(i)
                        fill(SB_FILL, lo_of(i))
                        if i + 1 < nU:
                            act_e(i + 1)
                        act_g(i)
                        if i + 1 < nU:
                            act_sp(i + 1)
                        if i >= 1:
                            pv(i - 1)
                        if i + 2 < nU:
                            qk(i + 2)
                        if i + 1 < nU:
                            omt_(i)
                        dve_w(i)
                    pv(nU - 1)
                    for c in range(2):
                        CP("dve", oaT[64 * c:64 * c + 64, cc, g * 512:(g + 1) * 512], bk[4 + c][0:64, :], reads=["o"], writes=["oaT"])

                return prologue, main

            sb_chains = [sb_pair(hp, g) for g in range(4) for hp in range(4)]
            sb_chains[0][0]()
            for ci_, (pro, main) in enumerate(sb_chains):
                main(sb_chains[ci_ + 1][0] if ci_ + 1 < len(sb_chains) else None)
            dump("oaT", oaT.rearrange("p a b -> p (a b)"), [128, 4 * NQ], reads=["oaT"])
            S.barrier()

        if stop_after not in ("B", "C1"):
            C2 = Bump(52, 132)
            Vm = C2([128, 32, 4, 192], BF16)
            KhT0_ = C2([128, S_LEN], BF16)
            QhT = [C2([128, NQ], BF16) for _ in range(2)]
            wukv = C2([128, 2, 1024], BF16)
            wuqa = C2([128, 3, 8, 128], BF16)
            wvc = C2([128, 2, 512], BF16)
            C2b = Bump(164, 204)
            wst2 = C2b([128, 3, 1024], F32)
            KhT = [KhT0_, view(164 * 1024, [128, S_LEN], BF16)]
            KKEY = ["KhT0", "wst2"]
            jq = C2b([128, NQ], F32)
            p_sb = C2b([128, 2, 2, 512], BF16)
            lnl = C2b([128, 512], F32)
            rinv = C2b([128, 512], F32)
            tq1 = C2b([128, 512], F32)
            tq2 = C2b([128, 512], F32)

            S.dma("sp", wst2[:, 0:2, :], w_ukv.rearrange("(k p) n -> p k n", p=128), writes=["wst2"])
            for k in range(2):
                TS("dve", wukv[:, k, :], wst2[:, k, :], kvg[:, k:k + 1], None, ALU.mult, reads=["wst2", "small"], writes=["wukv"])
            S.dma("sp", wst2[:, :, 0:768], w_uq.rearrange("(k p) n -> p k n", p=128), reads=[], writes=["wst2"])
            wst2q = wst2[:, :, 0:768].rearrange("p k (h c) -> p k h c", c=96)
            for k in range(3):
                TS("dve", wuqa[:, k, :, 0:96], wst2q[:, k, :, :], qg[:, k:k + 1], None, ALU.mult, reads=["wst2", "small"], writes=["wuqa"])
                TS("dve", wuqa[:, k, :, 96:112], wst2q[:, k, :, 80:96], qg[:, k:k + 1], -1.0, ALU.mult, ALU.mult,
                   reads=["wst2", "small"], writes=["wuqa"])
                TS("dve", wuqa[:, k, :, 112:128], wst2q[:, k, :, 64:80], qg[:, k:k + 1], None, ALU.mult,
                   reads=["wst2", "small"], writes=["wuqa"])
            for qc in range(4):
                ci = 8 + qc
                qd, col = ci // 3, (ci % 3) * 512
                S.dma("sp", jq[64:96, qc * 512:(qc + 1) * 512], tabcos[qd * 32:(qd + 1) * 32, col:col + 512], writes=["jq"])
                S.dma("sp", jq[96:128, qc * 512:(qc + 1) * 512], tabsin[qd * 32:(qd + 1) * 32, col:col + 512], writes=["jq"])
            for blk in range(32):
                S.op("pool", lambda e, blk=blk: e.memset(Vm[:, blk, :, 64:128], 1.0), writes=["Vm1"])
            wukv4 = wukv.rearrange("p k (h c) -> p k h c", c=128)
            for k in range(2):
                CP("dve", wvc[:, k, :].rearrange("p (h c) -> p h c", c=64), wukv4[:, k, :, 64:128], reads=["wukv"], writes=["wvc"])
            for blk in range(32):
                b = 6 + blk % 2
                for k in range(2):
                    MM(bk[b][:, :], ckvn[:, k, blk * 128:(blk + 1) * 128],
                       wvc[:, k, :], k == 0, k == 1, reads=["wvc"], writes=["bk%d" % b])
                pv = bk[b][:, :].rearrange("p (q t c) -> p q t c", t=2, c=64)
                ACT(Vm[:, blk, :, 0:64], pv[:, :, 0, :], AF.Copy, reads=["bk%d" % b], writes=[("Vme", blk)])
                CP("dve", Vm[:, blk, :, 128:192], pv[:, :, 1, :], reads=["bk%d" % b, ("Vme", blk)], writes=[("Vmo", blk)])

            def prep_head(h):
                Kh, Qh = KhT[h % 2], QhT[h % 2]
                kk, qk_ = KKEY[h % 2], "QhT%d" % (h % 2)
                for tc in range(8):
                    b = 6 + tc % 2
                    for k in range(2):
                        MM(bk[b][0:64, :], wukv[:, k, h * 128:h * 128 + 64], ckvn[:, k, tc * 512:(tc + 1) * 512], k == 0, k == 1,
                           reads=["wukv"], writes=["bk%d" % b])
                    CP("dve", Kh[0:64, tc * 512:(tc + 1) * 512], bk[b][0:64, :], reads=["bk%d" % b], writes=[kk])
                    yield
                for qc in range(4):
                    b = 6 + qc % 2
                    kb_ = "bk%d" % b
                    cs = slice(qc * 512, (qc + 1) * 512)
                    for k in range(3):
                        MM(bk[b][:, :], wuqa[:, k, h, :], cqn[:, k, cs], k == 0, k == 2, reads=["wuqa"], writes=[kb_])
                    CP("dve", Qh[0:64, cs], bk[b][0:64, :], reads=[kb_], writes=[qk_])
                    TT("dve", tq1[64:96, :], bk[b][64:96, :], jq[64:96, cs], ALU.mult, reads=[kb_, "jq"], writes=["tq1"])
                    TT("dve", tq2[64:96, :], bk[b][96:128, :], jq[96:128, cs], ALU.mult, reads=[kb_, "jq"], writes=["tq2"])
                    TT("dve", Qh[64:96, cs], tq1[64:96, :], tq2[64:96, :], ALU.add, reads=["tq1", "tq2"], writes=[qk_])
                    yield
                    yield

            sm_scale = 1.0 / math.sqrt(96.0)

            def mla_parts(h, g, ci):
                cc, bp = h // 2, (h % 2) * 64
                Kh, Qh = KhT[h % 2], QhT[h % 2]
                kk, qk_ = KKEY[h % 2], "QhT%d" % (h % 2)
                lb = 64 - bp
                vl = Vm[:, :, h // 2, bp:bp + 128]
                ob = 4 + ci % 2
                ko = "bk%d" % ob
                units = list(range(8 * g + 7, -1, -1))
                nP = len(units) // 2

                def qk(j):
                    zb = 2 * (j % 2)
                    kz = "zp%d" % (j % 2)
                    lo = chain_cols(g, units[2 * j])[0] * 128
                    for c in range(2):
                        kb = units[2 * j + c]
                        c0, masks = chain_cols(g, kb)
                        assert c0 * 128 == lo
                        MM(bk[zb + c][:, lo:512], Kh[0:96, kb * 128:(kb + 1) * 128], Qh[0:96, g * 512 + lo:(g + 1) * 512],
                           True, len(masks) == 0, reads=[kk, qk_], writes=[kz])
                        for mi, (cb, which) in enumerate(masks):
                            MM(bk[zb + c][:, cb * 128:(cb + 1) * 128], ident, m_mla[which], False, mi == len(masks) - 1, writes=[kz])
                    ACT(p_sb[:, j % 2, :, lo:512], bkpair(zb)[:, :, lo:512], AF.Exp, reads=[kz], writes=["p%d" % (j % 2)],
                        scale=sm_scale)

                def prologue():
                    MM(bk[ob][:, :], zeros_bf, ckvn[:, 0, 0:512], True, True, reads=["zeros"], writes=[ko])
                    qk(0)
                    if nP > 1:
                        qk(1)

                def rounds(stepper, next_prologue=None):
                    for j in range(nP):
                        lo = chain_cols(g, units[2 * j])[0] * 128
                        for c in range(2):
                            kb = units[2 * j + c]
                            MM(bk[ob][:, lo:512], vl[:, kb, :], p_sb[:, j % 2, c, lo:512], False, True,
                               reads=["p%d" % (j % 2), "Vm1", ("Vme", kb), ("Vmo", kb)], writes=[ko])
                        if j == nP - 1 and next_prologue is not None:
                            next_prologue()
                        if j + 2 < nP:
                            for fi in range(MLA_FILL):
                                MM(bk[2 * (j % 2) + fi % 2][:, :], ident, ckvn[:, 0, 0:512], True, True, writes=["zp%d" % (j % 2)])
                            qk(j + 2)
                        stepper()

                def tail():
                    ACT(lnl[bp:bp + 64, :], bk[ob][lb:lb + 64, :], AF.Ln, reads=[ko], writes=["lnl"])
                    ACT(rinv[bp:bp + 64, :], lnl[bp:bp + 64, :], AF.Exp, reads=["lnl"], writes=["rinv"], scale=-1.0)
                    TT("dve", obT[bp:bp + 64, cc, g * 512:(g + 1) * 512], bk[ob][bp:bp + 64, :], rinv[bp:bp + 64, :], ALU.mult,
                       reads=[ko, "rinv"], writes=["obT"])

                return prologue, rounds, tail

            for bi in range(2):
                for half in range(2):
                    hs_ = slice(half * 2048, (half + 1) * 2048)
                    CP("dve", KhT[bi][64:96, hs_], kpe[64:96, hs_], reads=["wuqa", "wukv"], writes=[KKEY[bi]])
            for _ in prep_head(0):
                pass
            if debug:
                dump("KhT0", KhT[0][0:96, :], [96, S_LEN], reads=["KhT0"])
                dump("QhT0", QhT[0][0:96, :], [96, NQ], reads=["QhT0"])
            mparts = [(h, g, mla_parts(h, g, 4 * h + gi)) for h in range(8) for gi, g in enumerate((3, 2, 1, 0))]
            mparts[0][2][0]()
            prev_tail = None
            prep = None
            for idx, (h, g, (prologue, rounds, tail)) in enumerate(mparts):
                if g == 3:
                    prep = prep_head(h + 1) if h + 1 < 8 else iter(())

                def stepper(prep=prep):
                    next(prep, None)

                nxt = None
                if idx + 1 < len(mparts):
                    nh = mparts[idx + 1][0]
                    npro = mparts[idx + 1][2][0]

                    def nxt(nh=nh, h=h, npro=npro, prep=prep):
                        if nh != h:
                            for _ in prep:
                                pass
                        npro()
                if prev_tail is not None:
                    prev_tail()
                rounds(stepper, nxt)
                prev_tail = tail
            prev_tail()
            dump("obT", obT.rearrange("p a b -> p (a b)"), [128, 4 * NQ], reads=["obT"])
            S.barrier()

        if stop_after is None:
            Dw = Bump(16, 132)
            WG = Dw([128, 8, 3072], BF16)
            WA = Dw([128, 4, D], BF16)
            WB = Dw([128, 4, D], BF16)
            WO = Dw([128, 8, D], BF16)
            gate_bc = Dw([128, D], F32)
            fg_bc = Dw([128, D], F32)
            hTd = Dw([128, 8, 512], BF16)
            merged_off = Dw.o
            mergedT = Dw([128, 8, 512], BF16)
            xnew_off = Dw.o
            xnew = Dw([128, D], F32)
            outt_off = Dw.o
            outt = Dw([128, D], F32)
            sqjD = Dw([128, D], BF16)
            Dt = Bump(164, 204)
            xtD = [Dt([128, D], F32) for _ in range(2)]
            xres = [Dt([128, D], F32) for _ in range(2)]
            xnD = Dt([128, 4, D], BF16)
            og_off = Dt.o
            ogA = Dt([128, 4, 512], BF16)
            ogB = Dt([128, 4, 512], BF16)
            xnew2 = view(og_off, [128, D], F32)
            outt2 = view(og_off + 4096, [128, D], F32)
            sg = [Dt([128, 512], F32) for _ in range(2)]
            tm = [Dt([128, 512], F32) for _ in range(2)]
            bgate_bc = view(merged_off, [128, D], F32)
            cbc = view(merged_off + 4096, [128, 8, 128], F32)
            wstD = [view(xnew_off, [128, D], F32), view(outt_off, [128, D], F32)]
            WSTK = ["xnew", "outt"]

            for k in range(8):
                S.dma("pool", WG[:, k, 0:512], w3[:, k, SBZ:SBZ + 512], writes=["WGz0"])
            for k in range(8):
                S.dma("pool", WG[:, k, 512:1024], w3[:, k, MLAZ:MLAZ + 512], writes=["WGz512"])
            for k in range(8):
                S.dma("pool", WG[:, k, 1024:3072], w3[:, k, GA:GA + 2048], writes=["WGg"])
            for k in range(4):
                S.dma("pool", WA[:, k, :], w_a[k * 128:(k + 1) * 128, :], writes=["WA"])
                S.dma("pool", WB[:, k, :], w_b[k * 128:(k + 1) * 128, :], writes=["WB"])
            for k in range(8):
                S.dma("pool", WO[:, k, :], w_out[k * 128:(k + 1) * 128, :], writes=["WO"])

            xn = xnD
            sqj = sqjD
            norm_stats(xq[0:512, :], xtD)
            S.dma("sp", fg_bc, fg_row.broadcast_to([128, D]), writes=["fg_bc"])
            S.dma("sp", bgate_bc, b_gate.broadcast_to([128, D]), writes=["merged"])
            S.op("dve", lambda e: e.memset(cbc, 1.0), writes=["merged"])
            for k in range(8):
                TS("dve", cbc[:, k, :], cbc[:, k, :], cT[:, k:k + 1], None, ALU.mult, reads=["merged", "small"], writes=["merged"])
            for k in range(8):
                S.dma("sp", wstD[k % 2], w_ada[k * 128:(k + 1) * 128, 2 * D:3 * D], writes=[WSTK[k % 2]])
                for half in range(2):
                    MM(bk[half][:, :], cbc[:, k, :], wstD[k % 2][:, half * 512:(half + 1) * 512], k == 0, k == 7,
                       reads=["merged", WSTK[k % 2]], writes=["bk%d" % half])
            for half in range(2):
                TT("dve", gate_bc[:, half * 512:(half + 1) * 512], bk[half][:, :], bgate_bc[:, half * 512:(half + 1) * 512],
                   ALU.add, reads=["bk%d" % half, "merged"], writes=["gate_bc"])
            hkd = ("hT", id(hTd))
            norm_T(hTd)
            for qc in range(4):
                q0 = qc * 512
                if qc + 1 < 4:
                    queue_stats(xq[q0 + 512:q0 + 1024, :], xtD)
                for (off, oT, og, nm) in ((0, oaT, ogA, "ogA"), (512, obT, ogB, "ogB")):
                    for cc in range(4):
                        b = next_bank()
                        for k in range(8):
                            MM(bk[b][:, :], WG[:, k, off + cc * 128:off + (cc + 1) * 128], hTd[:, k, :], k == 0, k == 7,
                               reads=["WGz%d" % off, hkd + (k,)], writes=["bk%d" % b])
                        si = cc % 2
                        ACT(sg[si], bk[b][:, :], AF.Sigmoid, reads=["bk%d" % b], writes=["sg%d" % si])
                        TT("dve", tm[si], bk[b][:, :], sg[si], ALU.mult, reads=["bk%d" % b, "sg%d" % si], writes=["tm%d" % si])
                        TT("dve", og[:, cc, :], tm[si], oT[:, cc, q0:q0 + 512], ALU.mult, reads=["tm%d" % si], writes=[nm])
                for n in range(8):
                    bga, bgb, bya, byb = next_bank(), next_bank(), next_bank(), next_bank()
                    for k in range(8):
                        MM(bk[bga][:, :], WG[:, k, 1024 + n * 128:1024 + (n + 1) * 128], hTd[:, k, :], k == 0, k == 7,
                           reads=["WGg", hkd + (k,)], writes=["bk%d" % bga])
                    for k in range(8):
                        MM(bk[bgb][:, :], WG[:, k, 2048 + n * 128:2048 + (n + 1) * 128], hTd[:, k, :], k == 0, k == 7,
                           reads=["WGg", hkd + (k,)], writes=["bk%d" % bgb])
                    for k in range(4):
                        MM(bk[bya][:, :], WA[:, k, n * 128:(n + 1) * 128], ogA[:, k, :], k == 0, k == 3,
                           reads=["WA", "ogA"], writes=["bk%d" % bya])
                    for k in range(4):
                        MM(bk[byb][:, :], WB[:, k, n * 128:(n + 1) * 128], ogB[:, k, :], k == 0, k == 3,
                           reads=["WB", "ogB"], writes=["bk%d" % byb])
                    ACT(sg[0], bk[bga][:, :], AF.Sigmoid, reads=["bk%d" % bga], writes=["sg0"])
                    ACT(sg[1], bk[bgb][:, :], AF.Sigmoid, reads=["bk%d" % bgb], writes=["sg1"])
                    TT("dve", tm[0], bk[bya][:, :], sg[0], ALU.mult, reads=["bk%d" % bya, "sg0"], writes=["tm0"])
                    TT("dve", tm[1], bk[byb][:, :], sg[1], ALU.mult, reads=["bk%d" % byb, "sg1"], writes=["tm1"])
                    TT("dve", mergedT[:, n, :], tm[0], tm[1], ALU.add, reads=["tm0", "tm1"], writes=["merged"])
                    if n % 2 == 1:
                        drain_stats(1)
                drain_stats(4)
                if qc + 1 < 4:
                    norm_T(hTd)
                for blk in range(4):
                    rs = blk % 2
                    xk = "xres%d" % rs
                    xnw, xnk = (xnew, "xnew") if blk % 2 == 0 else (xnew2, "ogA")
                    ott, otk = (outt, "outt") if blk % 2 == 0 else (outt2, "ogB")
                    S.dma("sp", xres[rs], xq[q0 + blk * 128:q0 + (blk + 1) * 128, :], writes=[xk])
                    for half in range(2):
                        b = next_bank()
                        for k in range(8):
                            MM(bk[b][:, :], mergedT[:, k, blk * 128:(blk + 1) * 128], WO[:, k, half * 512:(half + 1) * 512],
                               k == 0, k == 7, reads=["merged", "WO"], writes=["bk%d" % b])
                        hs = slice(half * 512, (half + 1) * 512)
                        TT("dve", xnw[:, hs], bk[b][:, :], gate_bc[:, hs], ALU.mult, reads=["bk%d" % b, "gate_bc"], writes=[xnk])
                    TT("dve", xnw, xnw, xres[rs], ALU.add, reads=[xnk, xk], writes=[xnk])
                    c = state["st"]
                    state["st"] = (c + 1) % 8
                    sk = "stat%d" % c
                    ACT(sqjD, xnw, AF.Square, reads=[xnk], writes=["sqjD", sk], accum_out=stat[:, c:c + 1])
                    ACT(stat[:, 8 + c:9 + c], stat[:, c:c + 1], AF.Ln, reads=[sk, "small"], writes=[sk], scale=1.0 / D, bias=epsc)
                    ACT(stat[:, 16 + c:17 + c], stat[:, 8 + c:9 + c], AF.Exp, reads=[sk], writes=[sk], scale=-0.5)
                    S.op("dve", lambda e, c=c, ott=ott, xnw=xnw: e.scalar_tensor_tensor(ott, xnw, stat[:, 16 + c:17 + c], fg_bc, ALU.mult, ALU.mult),
                         reads=[xnk, sk, "fg_bc"], writes=[otk])
                    S.dma("pool", out_d[q0 + blk * 128:q0 + (blk + 1) * 128, :], ott, reads=[otk])

        for q in ("sp", "pool"):
            for i in range(max(0, S.dma_n[q] - ND), S.dma_n[q]):
                S._wait("pool", ("d", (q, i)))
        with nc.Block() as block:
            S.emit(block)
    return nc, dbg_outs


def host_inputs(inputs):
    x = np.asarray(inputs["x"], np.float32)
    c = np.asarray(inputs["c"], np.float32)
    pos = np.asarray(inputs["positions"], np.int32)
    f = lambda k: np.ascontiguousarray(np.asarray(inputs[k], np.float32))
    w_ada = f("w_ada")[0]
    b_ada = f("b_ada")[0]
    tri = np.tril(np.ones((128, 128), np.float32))
    ident = np.eye(128, dtype=np.float32)
    omt = 1.0 - tri
    ss, tt = np.meshgrid(np.arange(128), np.arange(128), indexing="ij")
    strict = np.where(ss < tt, 0.0, MASKV).astype(np.float32)
    causal = np.where(ss <= tt, 0.0, MASKV).astype(np.float32)
    allm = np.full((128, 128), MASKV, np.float32)
    nom = np.zeros((128, 128), np.float32)
    inv_freq = (10000.0 ** (-np.arange(0, 32, 2, dtype=np.float32) / np.float32(32))).astype(np.float32)
    invf = np.tile(np.concatenate([inv_freq, inv_freq]), 4).reshape(128, 1).astype(np.float32)
    common = {
        "w_ada": w_ada,
        "b_adaT": np.ascontiguousarray(b_ada.reshape(24, 128).T),
        "b_gate": np.ascontiguousarray(b_ada[2 * D:3 * D].reshape(1, D)),
        "ngT": np.ascontiguousarray(f("norm_gain")[0].reshape(8, 128).T),
        "w_in": f("w_in")[0],
        "qgT": np.ascontiguousarray(f("q_norm_gain")[0].reshape(3, 128).T),
        "w_uq": f("w_uq")[0],
        "kvgT": np.ascontiguousarray(f("kv_norm_gain")[0].reshape(2, 128).T),
        "w_ukv": f("w_ukv")[0],
        "w_a": f("w_branch_a")[0],
        "w_b": f("w_branch_b")[0],
        "w_out": f("w_out")[0],
        "fg_row": np.ascontiguousarray(f("final_norm_gain").reshape(1, D)),
        "invf": invf,
    }
    maps = []
    for core in range(8):
        b, p = core // 2, core % 2
        blocks = [2 * j + p for j in range(16)]
        xb = x[b].reshape(32, 128, D)
        xq = np.ascontiguousarray(xb[blocks].reshape(NQ, D))
        pq = pos[b].reshape(32, 128)[blocks].reshape(NQ)
        posall = np.ascontiguousarray(np.concatenate([pos[b], pq]).reshape(4, 1536).astype(np.int32))
        if p == 0:
            mats = [ident, tri, omt, allm, strict, allm, causal]
        else:
            mats = [ident, tri, omt, strict, nom, causal, nom]
        cm = np.ascontiguousarray(np.stack(mats, axis=1).astype(np.float32))
        m = dict(common)
        m.update({"xf": np.ascontiguousarray(x[b]), "xq": xq, "posall": posall,
                  "cT": np.ascontiguousarray(c[b].reshape(8, 128).T), "cmats": cm})
        maps.append(m)
    return maps


_CACHE = {}


def kernel(**inputs):
    maps = host_inputs(inputs)
    if "nc" not in _CACHE:
        _CACHE["nc"] = build_program()[0]
    nc = _CACHE["nc"]
    res = run_bass_kernel_spmd(nc, maps, core_ids=list(range(8)))
    out = np.zeros((4, 32, 128, D), np.float32)
    for core in range(8):
        b, p = core // 2, core % 2
        o = np.asarray(res.results[core]["out"], np.float32).reshape(16, 128, D)
        out[b, p::2] = o
    return out.reshape(4, S_LEN, D)
```

```python
import math
from contextlib import ExitStack
import numpy as np
import concourse.bass as bass
import concourse.mybir as mybir
from concourse.bass_utils import run_bass_kernel_spmd

F32 = mybir.dt.float32
BF16 = mybir.dt.bfloat16
I32 = mybir.dt.int32
AF = mybir.ActivationFunctionType
ALU = mybir.AluOpType

ENGS = ["pe", "act", "dve", "pool", "sp"]
PH = 2048
NPH = {"pe": 7, "act": 5, "dve": 4, "pool": 1, "sp": 1}
ND = 16

D = 1024
S_LEN = 4096
NQ = 2048
SBQ, SBK, SBV, SBZ, CQ, CKV, KROT, MLAZ, GA, GB = 0, 512, 1024, 1536, 2048, 2432, 2688, 2720, 3232, 4256
EPS = 1e-6
MASKV = -30000.0
SB_FILL = 4
MLA_FILL = 0
DEBUG = False


class Sched:
    def __init__(self, nc, esem, dsem):
        self.nc = nc
        self.esem = esem
        self.dsem = dsem
        self.ops = {e: [] for e in ENGS}
        self.cnt = {e: 0 for e in ENGS}
        self.waited_e = {e: {} for e in ENGS}
        self.waited_d = {e: {} for e in ENGS}
        self.last_w = {}
        self.readers = {}
        self.dma_i = 0
        self.dma_n = {"sp": 0, "pool": 0}

    def _wait(self, E, ev):
        if ev[0] == "e":
            _, e2, n = ev
            if e2 == E and E == "pe":
                return
            if self.waited_e[E].get(e2, 0) >= n:
                return
            self.waited_e[E][e2] = n
            sem = self.esem[e2][(n - 1) // PH]
            val = (n - 1) % PH + 1
        else:
            q, i = ev[1]
            k = i % ND
            val = 16 * (i // ND + 1)
            if self.waited_d[E].get((q, k), 0) >= val:
                return
            self.waited_d[E][(q, k)] = val
            sem = self.dsem[q][k]
        self.ops[E].append(lambda eng, sem=sem, val=val: eng.wait_ge(sem, val))

    def _deps(self, E, reads, writes, extra=(), dma_accum=False):
        deps = list(extra)
        for b in reads:
            for ev in self.last_w.get(b, ()):
                deps.append(ev)
        for b in writes:
            for ev in self.last_w.get(b, ()):
                if dma_accum and ev[0] == "d":
                    continue
                deps.append(ev)
            r = self.readers.get(b)
            if r:
                for e2, n in r[0].items():
                    deps.append(("e", e2, n))
                for i in r[1]:
                    deps.append(("d", i))
        for ev in deps:
            self._wait(E, ev)

    def _record(self, ev, reads, writes, dma_accum=False):
        for b in reads:
            r = self.readers.setdefault(b, ({}, []))
            if ev[0] == "e":
                r[0][ev[1]] = ev[2]
            else:
                r[1].append(ev[1])
        for b in writes:
            r = self.readers.get(b)
            had_readers = bool(r and (r[0] or r[1]))
            if dma_accum and not had_readers:
                self.last_w[b] = [e for e in self.last_w.get(b, ()) if e[0] == "d"] + [ev]
            else:
                self.last_w[b] = [ev]
            self.readers[b] = ({}, [])

    def op(self, E, fn, reads=(), writes=(), extra=()):
        self._deps(E, reads, writes, extra)
        n = self.cnt[E] + 1
        self.cnt[E] = n
        assert (n - 1) // PH < NPH[E], "too many instrs on %s" % E
        sem = self.esem[E][(n - 1) // PH]
        self.ops[E].append(lambda eng, fn=fn, sem=sem: fn(eng).then_inc(sem, 1))
        ev = ("e", E, n)
        self._record(ev, reads, writes)
        return ev

    def dma(self, Q, out, in_, reads=(), writes=(), extra=()):
        i = self.dma_n[Q]
        self.dma_n[Q] += 1
        self.dma_i += 1
        ex = list(extra)
        if i >= ND:
            ex.append(("d", (Q, i - ND)))
        self._deps(Q, reads, writes, ex, dma_accum=True)
        sem = self.dsem[Q][i % ND]
        self.ops[Q].append(
            lambda eng, out=out, in_=in_, sem=sem: eng.dma_start(out=out, in_=in_).then_inc(sem, 16))
        ev = ("d", (Q, i))
        self._record(ev, reads, writes, dma_accum=True)
        return ev

    def barrier(self):
        snap = dict(self.cnt)
        for E in ENGS:
            for e2 in ENGS:
                if e2 != E and snap[e2] > 0:
                    self._wait(E, ("e", e2, snap[e2]))

    def emit(self, block):
        ops = self.ops

        @block.tensor
        def _(eng):
            for f in ops["pe"]:
                f(eng)

        @block.scalar
        def _(eng):
            for f in ops["act"]:
                f(eng)

        @block.vector
        def _(eng):
            for f in ops["dve"]:
                f(eng)

        @block.gpsimd
        def _(eng):
            for f in ops["pool"]:
                f(eng)

        @block.sync
        def _(eng):
            for f in ops["sp"]:
                f(eng)


def build_program(debug=False, stop_after=None):
    nc = bass.Bass("TRN2", target_bir_lowering=False)

    def din(name, shape, dt=F32):
        return nc.dram_tensor(name, list(shape), dt, kind="ExternalInput").ap()

    xf = din("xf", [S_LEN, D])
    xq = din("xq", [NQ, D])
    posall = din("posall", [4, 1536], I32)
    cT_d = din("cT", [128, 8])
    w_ada = din("w_ada", [D, 3 * D])
    b_adaT = din("b_adaT", [128, 24])
    b_gate = din("b_gate", [1, D])
    ngT = din("ngT", [128, 8])
    w_in = din("w_in", [D, 5280])
    qgT = din("qgT", [128, 3])
    w_uq = din("w_uq", [384, 768])
    kvgT = din("kvgT", [128, 2])
    w_ukv = din("w_ukv", [256, 1024])
    w_a = din("w_a", [512, D])
    w_b = din("w_b", [512, D])
    w_out = din("w_out", [D, D])
    fg_row = din("fg_row", [1, D])
    cmats = din("cmats", [128, 7, 128])
    invf_d = din("invf", [128, 1])
    out_d = nc.dram_tensor("out", [NQ, D], F32, kind="ExternalOutput").ap()
    dbg_outs = {}

    with ExitStack() as es:
        esem = {e: [es.enter_context(nc.semaphore("s_%s_%d" % (e, i))) for i in range(NPH[e])] for e in ENGS}
        dsem = {q: [es.enter_context(nc.semaphore("d_%s_%d" % (q, i))) for i in range(ND)] for q in ("sp", "pool")}
        S = Sched(nc, esem, dsem)

        ARENA_KB = 204
        arena = es.enter_context(nc.sbuf_tensor("arena", [128, ARENA_KB * 512], BF16))
        arena32 = arena.bitcast(F32)
        arenai = arena.bitcast(I32)

        def view(off_bytes, shape, dt):
            n = int(np.prod(shape[1:]))
            esz = 2 if dt == BF16 else 4
            assert off_bytes % 4 == 0
            assert off_bytes + n * esz <= ARENA_KB * 1024, (off_bytes, shape)
            base = {BF16: arena, F32: arena32, I32: arenai}[dt]
            o = off_bytes // esz
            ap = base[:, o:o + n]
            if len(shape) == 3:
                ap = ap.rearrange("p (a b) -> p a b", b=shape[2])
            elif len(shape) == 4:
                ap = ap.rearrange("p (a b c) -> p a b c", b=shape[2], c=shape[3])
            return ap

        class Bump:
            def __init__(self, start_kb, end_kb):
                self.o = start_kb * 1024
                self.end = end_kb * 1024

            def __call__(self, shape, dt):
                n = int(np.prod(shape[1:])) * (2 if dt == BF16 else 4)
                n = (n + 3) // 4 * 4
                v = view(self.o, shape, dt)
                self.o += n
                assert self.o <= self.end, ("bump overflow", self.o, self.end)
                return v

        psum_all = es.enter_context(nc.psum_tensor("psall", [128, 4096], F32))
        psum_bf = psum_all.bitcast(BF16)
        bk = [psum_all[:, i * 512:(i + 1) * 512] for i in range(8)]
        bkb = [psum_bf[:, i * 1024:(i + 1) * 1024] for i in range(8)]

        def bkpair(i):
            return psum_all[:, i * 512:(i + 2) * 512].rearrange("p (c n) -> p c n", n=512)

        def MM(out, lhsT, rhs, start, stop, reads=(), writes=()):
            return S.op("pe", lambda e: e.matmul(out, lhsT, rhs, start=start, stop=stop, skip_group_check=True),
                        reads=reads, writes=writes)

        def ACT(out, in_, func, reads=(), writes=(), **kw):
            return S.op("act", lambda e: e.activation(out, in_, func, **kw), reads=reads, writes=writes)

        def TT(eng, out, in0, in1, op, reads=(), writes=()):
            return S.op(eng, lambda e: e.tensor_tensor(out, in0, in1, op), reads=reads, writes=writes)

        def TS(eng, out, in0, s1, s2, op0, op1=None, reads=(), writes=()):
            if op1 is None:
                return S.op(eng, lambda e: e.tensor_scalar(out, in0, s1, None, op0), reads=reads, writes=writes)
            return S.op(eng, lambda e: e.tensor_scalar(out, in0, s1, s2, op0, op1), reads=reads, writes=writes)

        def CP(eng, out, in_, reads=(), writes=()):
            return S.op(eng, lambda e: e.tensor_copy(out, in_), reads=reads, writes=writes)

        def dump(name, ap, shape, reads=()):
            if not debug:
                return
            t = nc.dram_tensor("dbg_" + name, list(shape), F32, kind="ExternalOutput").ap()
            dbg_outs[name] = t
            S.dma("pool", t, ap, reads=reads)

        P = Bump(0, 16)
        cm = P([128, 7, 128], BF16)
        ident = cm[:, 0, :]
        triI = cm[:, 1, :]
        omt = cm[:, 2, :]
        m_sb = [cm[:, 3, :], cm[:, 4, :]]
        m_mla = [cm[:, 5, :], cm[:, 6, :]]
        zeros_bf = P([128, 128], BF16)
        ones_bf = P([128, 128], BF16)
        mod_sb = P([128, 24], F32)
        gs = P([128, 8], F32)
        small = P([128, 64], F32)
        cT = small[:, 0:8]
        badaT = small[:, 8:32]
        ng = small[:, 32:40]
        qg = small[:, 40:43]
        kvg = small[:, 43:45]
        epsc = small[:, 45:46]
        invf = small[:, 46:47]
        tmp8 = small[:, 48:56]
        stat = P([128, 64], F32)
        tabcos = P([128, 1536], F32)
        tabsin = P([128, 1536], F32)
        shift = mod_sb[:, 0:8]

        L = Bump(16, 52)
        ckvn = L([128, 2, S_LEN], BF16)
        kpe = L([128, S_LEN], BF16)
        cqn = L([128, 3, NQ], BF16)
        SBD = Bump(52, 132)
        KT = SBD([128, 4, S_LEN], BF16)
        Vsb = SBD([128, 32, 512], BF16)
        QT = SBD([128, 4, NQ], BF16)
        OA = Bump(132, 164)
        oaT = OA([128, 4, NQ], BF16)
        obT = OA([128, 4, NQ], BF16)

        WK = view(132 * 1024, [128, 8, 1344], BF16)
        w3 = w_in.rearrange("(k p) n -> p k n", p=128)

        def load_WK():
            for k in range(8):
                S.dma("pool", WK[:, k, 0:1024], w3[:, k, SBK:SBK + 1024], writes=["WK"])
                S.dma("pool", WK[:, k, 1024:1312], w3[:, k, CKV:CKV + 288], writes=["WK"])
                S.dma("pool", WK[:, k, 1312:1328], w3[:, k, KROT + 16:KROT + 32], writes=["WK"])
                S.dma("pool", WK[:, k, 1328:1344], w3[:, k, KROT:KROT + 16], writes=["WK"])
            TS("pool", WK[:, :, 1312:1328], WK[:, :, 1312:1328], -1.0, None, ALU.mult, reads=["WK"], writes=["WK"])

        def load_WQ():
            for k in range(8):
                S.dma("pool", WK[:, k, 0:512], w3[:, k, SBQ:SBQ + 512], writes=["WK"])
                S.dma("pool", WK[:, k, 512:896], w3[:, k, CQ:CQ + 384], writes=["WK"])

        A_ = Bump(52, 132)
        A2_ = Bump(164, 204)
        wst = [A_([128, D], F32) for _ in range(8)] + [A2_([128, D], F32) for _ in range(8)]
        posi = A_([128, 1536], I32)
        ang = A_([128, 1536], F32)
        tt = A_([128, 1536], F32)
        ki = A_([128, 1536], I32)
        kf = A_([128, 1536], F32)
        ff = A_([128, 1536], F32)

        S.dma("pool", cm, cmats, writes=["cm"])
        load_WK()
        S.dma("sp", cT, cT_d, writes=["small"])
        S.dma("sp", badaT, b_adaT, writes=["small"])
        S.dma("sp", ng, ngT, writes=["small"])
        S.dma("sp", qg, qgT, writes=["small"])
        S.dma("sp", kvg, kvgT, writes=["small"])
        S.dma("sp", invf, invf_d, writes=["small"])
        for q in range(4):
            S.dma("sp", posi[q * 32:(q + 1) * 32, :], posall[q:q + 1, :].broadcast_to([32, 1536]), writes=["posi"])
        S.op("dve", lambda e: e.memset(zeros_bf, 0.0), writes=["zeros"])
        S.op("dve", lambda e: e.memset(ones_bf, 1.0), writes=["ones"])
        S.op("dve", lambda e: e.memset(epsc, EPS), reads=["small"], writes=["small"])

        CP("dve", ang, posi, reads=["posi"], writes=["ang"])
        TS("dve", ang, ang, invf, None, ALU.mult, reads=["ang", "small"], writes=["ang"])
        inv2pi = 1.0 / (2.0 * math.pi)
        for (tab, phase, nm) in ((tabsin, 0.0, "sin"), (tabcos, 0.25, "cos")):
            TS("dve", tt, ang, inv2pi, phase, ALU.mult, ALU.add, reads=["ang"], writes=["tt"])
            CP("dve", ki, tt, reads=["tt"], writes=["ki"])
            CP("dve", kf, ki, reads=["ki"], writes=["kf"])
            TT("dve", ff, tt, kf, ALU.subtract, reads=["tt", "kf"], writes=["ff"])
            TS("dve", kf, ff, 0.5, None, ALU.is_gt, reads=["ff"], writes=["kf"])
            TT("dve", ff, ff, kf, ALU.subtract, reads=["ff", "kf"], writes=["ff"])
            TS("dve", kf, ff, -0.5, None, ALU.is_lt, reads=["ff"], writes=["kf"])
            TT("dve", ff, ff, kf, ALU.add, reads=["ff", "kf"], writes=["ff"])
            ACT(tab, ff, AF.Sin, reads=["ff"], writes=["tab" + nm], scale=6.283185)
        for half in range(2):
            for k in range(8):
                S.dma("sp", wst[half * 8 + k], w_ada[k * 128:(k + 1) * 128, half * D:(half + 1) * D], writes=["wst%d" % (half * 8 + k)])
        for half in range(2):
            for jj in range(8):
                j = half * 8 + jj
                for k in range(8):
                    MM(bk[0][:, j:j + 1], wst[half * 8 + k][:, jj * 128:(jj + 1) * 128], cT[:, k:k + 1], k == 0, k == 7,
                       reads=["wst%d" % (half * 8 + k), "small"], writes=["bk0"])
        TT("dve", mod_sb[:, 0:16], bk[0][:, 0:16], badaT[:, 0:16], ALU.add, reads=["bk0", "small"], writes=["mod"])
        TS("dve", tmp8, mod_sb[:, 8:16], 1.0, None, ALU.add, reads=["mod"], writes=["tmp8"])
        TT("dve", gs, tmp8, ng, ALU.mult, reads=["tmp8", "small"], writes=["gs"])

        dump("gs", gs, [128, 8], reads=["gs"])
        dump("mod", mod_sb[:, 0:16], [128, 16], reads=["mod"])
        dump("tabcos", tabcos, [128, 1536], reads=["tabcos"])
        dump("tabsin", tabsin, [128, 1536], reads=["tabsin"])
        S.barrier()

        B_ = Bump(164, 204)
        B2 = Bump(154, 164)
        jit = B2([128, 2, 512], F32)
        sqj = B2([128, D], BF16)
        sq = B2([128, 3, 512], BF16)
        xt = [B_([128, D], F32) for _ in range(2)]
        xn = B_([128, 4, D], BF16)
        hTs = [B_([128, 8, 512], BF16) for _ in range(2)]
        rbc = B_([128, 512], F32)
        t1 = B_([128, 512], F32)
        t2 = B_([128, 512], F32)

        state = {"xt": 0, "st": 0, "bank": 2, "ev": 0, "fill": 0, "sbu": 0, "pend": []}

        def next_bank():
            b = state["bank"]
            state["bank"] = 2 + (b - 2 + 1) % 6
            return b

        def evac_engine():
            state["ev"] ^= 1
            return "act" if state["ev"] else "dve"

        def norm_stats_blk(src, xts, blk):
            slot = state["xt"]
            state["xt"] ^= 1
            xk = "xt%d" % slot
            c = state["st"]
            state["st"] = (c + 1) % 8
            sk = "stat%d" % c
            S.dma("sp", xts[slot], src[blk * 128:(blk + 1) * 128, :], writes=[xk])
            ACT(sqj, xts[slot], AF.Square, reads=[xk], writes=["sqj", sk], accum_out=stat[:, c:c + 1])
            ACT(stat[:, 8 + c:9 + c], stat[:, c:c + 1], AF.Ln, reads=[sk, "small"], writes=[sk],
                scale=1.0 / D, bias=epsc)
            ACT(stat[:, 16 + c:17 + c], stat[:, 8 + c:9 + c], AF.Exp, reads=[sk], writes=[sk], scale=-0.5)
            TS("dve", xn[:, blk, :], xts[slot], stat[:, 16 + c:17 + c], None, ALU.mult,
               reads=[xk, sk], writes=["xn%d" % blk])

        def norm_stats(src, xts):
            for blk in range(4):
                norm_stats_blk(src, xts, blk)

        def queue_stats(src, xts):
            state["pend"] = [(src, xts, blk) for blk in range(4)]

        def drain_stats(n=4):
            for _ in range(n):
                if state["pend"]:
                    a_, b_, c_ = state["pend"].pop(0)
                    norm_stats_blk(a_, b_, c_)

        def norm_T(hT):
            for k in range(8):
                tb = k % 2
                for blk in range(4):
                    S.op("pe", lambda e, k=k, blk=blk, tb=tb, xn_=xn: e.transpose(
                        bkb[tb][:, blk * 128:(blk + 1) * 128], xn_[:, blk, k * 128:(k + 1) * 128], ident),
                        reads=["xn%d" % blk, "cm"], writes=["bk%d" % tb])
                if evac_engine() == "act":
                    ACT(hT[:, k, :], bkb[tb][:, 0:512], AF.Identity, reads=["bk%d" % tb, "gs", "mod"],
                        writes=[("hT", id(hT), k)], scale=gs[:, k:k + 1], bias=shift[:, k:k + 1])
                else:
                    TS("dve", hT[:, k, :], bkb[tb][:, 0:512], gs[:, k:k + 1], shift[:, k:k + 1], ALU.mult, ALU.add,
                       reads=["bk%d" % tb, "gs", "mod"], writes=[("hT", id(hT), k)])

        def rms_sq(banks, nch):
            for c in range(nch):
                ACT(sq[:, c, :], bk[banks[c]][:, :], AF.Square, reads=["bk%d" % banks[c]], writes=["sq%d" % c])

        def rms_fin(banks, nch, dim, dst, t0, gkey):
            sb_ = next_bank()
            for c in range(nch):
                MM(bk[sb_][:, :], ones_bf, sq[:, c, :], c == 0, c == nch - 1, reads=["sq%d" % c, "ones"],
                   writes=["bk%d" % sb_])
            ACT(rbc, bk[sb_][:, :], AF.Ln, reads=["bk%d" % sb_, "small"], writes=["rbc"], scale=1.0 / dim, bias=epsc)
            ACT(rbc, rbc, AF.Exp, reads=["rbc"], writes=["rbc"], scale=-0.5)
            for c in range(nch):
                TT("dve", dst[:, c, t0:t0 + 512], bk[banks[c]][:, :], rbc, ALU.mult,
                   reads=["bk%d" % banks[c], "rbc"], writes=[])

        def proj_K(tc):
            t0 = tc * 512
            hT = hTs[tc % 2]
            hk = ("hT", id(hT))
            for cc in range(4):
                b = next_bank()
                for k in range(8):
                    MM(bk[b][:, :], WK[:, k, cc * 128:(cc + 1) * 128], hT[:, k, :], k == 0, k == 7,
                       reads=["WK", hk + (k,)], writes=["bk%d" % b])
                if evac_engine() == "act":
                    ACT(KT[:, cc, t0:t0 + 512], bk[b][:, :], AF.Copy, reads=["bk%d" % b], writes=[], scale=0.125)
                else:
                    TS("dve", KT[:, cc, t0:t0 + 512], bk[b][:, :], 0.125, None, ALU.mult, reads=["bk%d" % b], writes=[])
                if cc % 2 == 1:
                    drain_stats(1)
            for blk in range(4):
                b = next_bank()
                for k in range(8):
                    MM(bk[b][:, :], hT[:, k, blk * 128:(blk + 1) * 128], WK[:, k, 512:1024], k == 0, k == 7,
                       reads=["WK", hk + (k,)], writes=["bk%d" % b])
                if evac_engine() == "act":
                    ACT(Vsb[:, tc * 4 + blk, :], bk[b][:, :], AF.Copy, reads=["bk%d" % b], writes=[])
                else:
                    CP("dve", Vsb[:, tc * 4 + blk, :], bk[b][:, :], reads=["bk%d" % b], writes=[])
                if blk % 2 == 1:
                    drain_stats(1)
            cb = []
            for c2 in range(2):
                b = next_bank()
                cb.append(b)
                for k in range(8):
                    MM(bk[b][:, :], WK[:, k, 1024 + c2 * 128:1024 + (c2 + 1) * 128], hT[:, k, :], k == 0, k == 7,
                       reads=["WK", hk + (k,)], writes=["bk%d" % b])
            rms_sq(cb, 2)
            bA = next_bank()
            for k in range(8):
                MM(bk[bA][0:96, :], WK[:, k, 1216:1312], hT[:, k, :], k == 0, k == 7, reads=["WK", hk + (k,)], writes=["bk%d" % bA])
            bB = next_bank()
            for k in range(8):
                MM(bk[bB][0:96, :], WK[:, k, 1248:1344], hT[:, k, :], k == 0, k == 7, reads=["WK", hk + (k,)], writes=["bk%d" % bB])
            qd, col = tc // 3, (tc % 3) * 512
            S.dma("sp", jit[64:96, 0, :], tabcos[qd * 32:(qd + 1) * 32, col:col + 512], writes=["jit"])
            S.dma("sp", jit[64:96, 1, :], tabsin[qd * 32:(qd + 1) * 32, col:col + 512], writes=["jit"])
            TT("dve", t1[64:96, :], bk[bA][64:96, :], jit[64:96, 0, :], ALU.mult, reads=["bk%d" % bA, "jit"], writes=["t1"])
            TT("dve", t2[64:96, :], bk[bB][64:96, :], jit[64:96, 1, :], ALU.mult, reads=["bk%d" % bB, "jit"], writes=["t2"])
            TT("dve", kpe[64:96, t0:t0 + 512], t1[64:96, :], t2[64:96, :], ALU.add, reads=["t1", "t2"], writes=[])
            return lambda: rms_fin(cb, 2, 256, ckvn, t0, "ckvn")
        def proj_Q(qc):
            q0 = qc * 512
            hT = hTs[qc % 2]
            hk = ("hT", id(hT))
            for cc in range(4):
                b = next_bank()
                for k in range(8):
                    MM(bk[b][:, :], WK[:, k, cc * 128:(cc + 1) * 128], hT[:, k, :], k == 0, k == 7,
                       reads=["WK", hk + (k,)], writes=["bk%d" % b])
                if evac_engine() == "act":
                    ACT(QT[:, cc, q0:q0 + 512], bk[b][:, :], AF.Copy, reads=["bk%d" % b], writes=[])
                else:
                    CP("dve", QT[:, cc, q0:q0 + 512], bk[b][:, :], reads=["bk%d" % b], writes=[])
                drain_stats(1)
            cb = []
            for c3 in range(3):
                b = next_bank()
                cb.append(b)
                for k in range(8):
                    MM(bk[b][:, :], WK[:, k, 512 + c3 * 128:512 + (c3 + 1) * 128], hT[:, k, :], k == 0, k == 7,
                       reads=["WK", hk + (k,)], writes=["bk%d" % b])
            rms_sq(cb, 3)
            return lambda: rms_fin(cb, 3, 384, cqn, q0, "cqn")
        chunks = [("K", i) for i in range(8)] + [("Q", i) for i in range(4)]

        def src_of(ch):
            return (xf if ch[0] == "K" else xq)[ch[1] * 512:(ch[1] + 1) * 512, :]

        norm_stats(src_of(chunks[0]), xt)
        norm_T(hTs[0])
        for i, ch in enumerate(chunks):
            if i + 1 < len(chunks):
                queue_stats(src_of(chunks[i + 1]), xt)
            if ch == ("Q", 0):
                load_WQ()
            fin = (proj_K if ch[0] == "K" else proj_Q)(ch[1])
            drain_stats(4)
            if i + 1 < len(chunks):
                norm_T(hTs[(i + 1) % 2])
            fin()
        if debug:
            S.barrier()
        dump("KT", KT.rearrange("p a b -> p (a b)"), [128, 4 * S_LEN], reads=["KT"])
        dump("QT", QT.rearrange("p a b -> p (a b)"), [128, 4 * NQ], reads=["QT"])
        dump("Vsb", Vsb.rearrange("p a b -> p (a b)"), [128, 32 * 512], reads=["Vsb"])
        dump("ckvn", ckvn.rearrange("p a b -> p (a b)"), [128, 2 * S_LEN], reads=["ckvn"])
        dump("cqn", cqn.rearrange("p a b -> p (a b)"), [128, 3 * NQ], reads=["cqn"])
        dump("kpe", kpe[64:96, :], [32, S_LEN], reads=["kpe"])
        S.barrier()

        def chain_cols(g, kb):
            c0 = max(0, kb // 2 - 4 * g)
            masks = []
            for c in range(c0, 4):
                j = 4 * g + c
                if kb == 2 * j + 1:
                    masks.append((c, 0))
                elif kb == 2 * j:
                    masks.append((c, 1))
            return c0, masks

        def run_chains(chains):
            live = list(chains)
            while live:
                nxt = []
                for gen in live:
                    try:
                        next(gen)
                        nxt.append(gen)
                    except StopIteration:
                        pass
                live = nxt

        def run_slots(slots):
            slots = [list(x) for x in slots]
            cur = [sl.pop(0) if sl else None for sl in slots]
            while any(c is not None for c in cur):
                for i in range(len(cur)):
                    while cur[i] is not None:
                        try:
                            next(cur[i])
                            break
                        except StopIteration:
                            cur[i] = slots[i].pop(0) if slots[i] else None

        if stop_after != "B":
            C1 = Bump(164, 204)
            e_sb = C1([128, 3, 2, 512], F32)
            sp_sb = C1([128, 2, 2, 512], BF16)
            g_sb = C1([128, 2, 2, 512], F32)
            w_sb = C1([128, 2, 2, 512], BF16)
            zp, Ap = bkpair(0), bkpair(2)

            def sb_pair(hp, g):
                cc = hp
                units = list(range(8 * g + 7, -1, -1))
                nU = len(units)
                ub = state["sbu"]
                state["sbu"] += nU
                def fill(n, lo=0):
                    for _ in range(n):
                        fb = 6 + (state["fill"] % 2)
                        state["fill"] += 1
                        MM(bk[fb][:, lo:512], ident, QT[:, 0, lo:512], True, True)

                def lo_of(i):
                    return chain_cols(g, units[i])[0] * 128

                def qk(i):
                    kb = units[i]
                    c0, masks = chain_cols(g, kb)
                    lo = c0 * 128
                    for c in range(2):
                        bp = 64 * c
                        MM(bk[c][:, lo:512], KT[bp:bp + 64, cc, kb * 128:(kb + 1) * 128],
                           QT[bp:bp + 64, cc, g * 512 + lo:(g + 1) * 512], True, len(masks) == 0, writes=["z"])
                        for mi, (cb, which) in enumerate(masks):
                            MM(bk[c][:, cb * 128:(cb + 1) * 128], ident, m_sb[which], False, mi == len(masks) - 1, writes=["z"])

                def tri(i):
                    lo = lo_of(i)
                    for c in range(2):
                        MM(bk[2 + c][:, lo:512], triI, sp_sb[:, (ub + i) % 2, c, lo:512], False, True, reads=["sp%d" % ((ub + i) % 2)], writes=["A"])

                def omt_(i):
                    lo = lo_of(i)
                    for c in range(2):
                        MM(bk[2 + c][:, lo:512], omt, sp_sb[:, (ub + i) % 2, c, lo:512], False, True, reads=["sp%d" % ((ub + i) % 2)], writes=["A"])

                def pv(i):
                    lo = lo_of(i)
                    kb = units[i]
                    for c in range(2):
                        h = 2 * hp + c
                        MM(bk[4 + c][0:64, lo:512], Vsb[:, kb, h * 64:(h + 1) * 64], w_sb[:, (ub + i) % 2, c, lo:512], False, True,
                           reads=["w%d" % ((ub + i) % 2)], writes=["o"])

                def act_e(i):
                    lo = lo_of(i)
                    ACT(e_sb[:, (ub + i) % 3, :, lo:512], zp[:, :, lo:512], AF.Exp, reads=["z"], writes=["e%d" % ((ub + i) % 3)])

                def act_sp(i):
                    lo = lo_of(i)
                    ACT(sp_sb[:, (ub + i) % 2, :, lo:512], e_sb[:, (ub + i) % 3, :, lo:512], AF.Ln, reads=["e%d" % ((ub + i) % 3)],
                        writes=["sp%d" % ((ub + i) % 2)], bias=1.0)

                def act_g(i):
                    lo = lo_of(i)
                    ACT(g_sb[:, (ub + i) % 2, :, lo:512], Ap[:, :, lo:512], AF.Exp, reads=["A"], writes=["g%d" % ((ub + i) % 2)], scale=-1.0)

                def dve_w(i):
                    lo = lo_of(i)
                    TT("dve", w_sb[:, (ub + i) % 2, :, lo:512], e_sb[:, (ub + i) % 3, :, lo:512], g_sb[:, (ub + i) % 2, :, lo:512], ALU.mult,
                       reads=["e%d" % ((ub + i) % 3), "g%d" % ((ub + i) % 2)], writes=["w%d" % ((ub + i) % 2)])

                def prologue():
                    qk(0)
                    act_e(0)
                    act_sp(0)
                    if nU > 1:
                        qk(1)

                def main(next_prologue):
                    for c in range(2):
                        MM(bk[2 + c][:, :], zeros_bf, QT[:, 0, 0:512], True, True, reads=["zeros"], writes=["A"])
                        MM(bk[4 + c][0:64, :], zeros_bf[:, 0:64], QT[:, 0, 0:512], True, True, reads=["zeros"], writes=["o"])
                    for i in range(nU):
                        if i == nU - 1 and next_prologue is not None:
                            next_prologue()
                        tri(i)
                        fill(SB_FILL, lo_of(i))
                        if i + 1 < nU:
                            act_e(i + 1)
                        act_g(i)
                        if i + 1 < nU:
                            act_sp(i + 1)
                        if i >= 1:
                            pv(i - 1)
                        if i + 2 < nU:
                            qk(i + 2)
                        if i + 1 < nU:
                            omt_(i)
                        dve_w(i)
                    pv(nU - 1)
                    for c in range(2):
                        CP("dve", oaT[64 * c:64 * c + 64, cc, g * 512:(g + 1) * 512], bk[4 + c][0:64, :], reads=["o"], writes=["oaT"])

                return prologue, main

            sb_chains = [sb_pair(hp, g) for g in range(4) for hp in range(4)]
            sb_chains[0][0]()
            for ci_, (pro, main) in enumerate(sb_chains):
                main(sb_chains[ci_ + 1][0] if ci_ + 1 < len(sb_chains) else None)
            dump("oaT", oaT.rearrange("p a b -> p (a b)"), [128, 4 * NQ], reads=["oaT"])
            S.barrier()

        if stop_after not in ("B", "C1"):
            C2 = Bump(52, 132)
            Vm = C2([128, 32, 4, 192], BF16)
            KhT0_ = C2([128, S_LEN], BF16)
            QhT = [C2([128, NQ], BF16) for _ in range(2)]
            wukv = C2([128, 2, 1024], BF16)
            wuqa = C2([128, 3, 8, 128], BF16)
            wvc = C2([128, 2, 512], BF16)
            C2b = Bump(164, 204)
            wst2 = C2b([128, 3, 1024], F32)
            KhT = [KhT0_, view(164 * 1024, [128, S_LEN], BF16)]
            KKEY = ["KhT0", "wst2"]
            jq = C2b([128, NQ], F32)
            p_sb = C2b([128, 2, 2, 512], BF16)
            lnl = C2b([128, 512], F32)
            rinv = C2b([128, 512], F32)
            tq1 = C2b([128, 512], F32)
            tq2 = C2b([128, 512], F32)

            S.dma("sp", wst2[:, 0:2, :], w_ukv.rearrange("(k p) n -> p k n", p=128), writes=["wst2"])
            for k in range(2):
                TS("dve", wukv[:, k, :], wst2[:, k, :], kvg[:, k:k + 1], None, ALU.mult, reads=["wst2", "small"], writes=["wukv"])
            S.dma("sp", wst2[:, :, 0:768], w_uq.rearrange("(k p) n -> p k n", p=128), reads=[], writes=["wst2"])
            wst2q = wst2[:, :, 0:768].rearrange("p k (h c) -> p k h c", c=96)
            for k in range(3):
                TS("dve", wuqa[:, k, :, 0:96], wst2q[:, k, :, :], qg[:, k:k + 1], None, ALU.mult, reads=["wst2", "small"], writes=["wuqa"])
                TS("dve", wuqa[:, k, :, 96:112], wst2q[:, k, :, 80:96], qg[:, k:k + 1], -1.0, ALU.mult, ALU.mult,
                   reads=["wst2", "small"], writes=["wuqa"])
                TS("dve", wuqa[:, k, :, 112:128], wst2q[:, k, :, 64:80], qg[:, k:k + 1], None, ALU.mult,
                   reads=["wst2", "small"], writes=["wuqa"])
            for qc in range(4):
                ci = 8 + qc
                qd, col = ci // 3, (ci % 3) * 512
                S.dma("sp", jq[64:96, qc * 512:(qc + 1) * 512], tabcos[qd * 32:(qd + 1) * 32, col:col + 512], writes=["jq"])
                S.dma("sp", jq[96:128, qc * 512:(qc + 1) * 512], tabsin[qd * 32:(qd + 1) * 32, col:col + 512], writes=["jq"])
            for blk in range(32):
                S.op("pool", lambda e, blk=blk: e.memset(Vm[:, blk, :, 64:128], 1.0), writes=["Vm1"])
            wukv4 = wukv.rearrange("p k (h c) -> p k h c", c=128)
            for k in range(2):
                CP("dve", wvc[:, k, :].rearrange("p (h c) -> p h c", c=64), wukv4[:, k, :, 64:128], reads=["wukv"], writes=["wvc"])
            for blk in range(32):
                b = 6 + blk % 2
                for k in range(2):
                    MM(bk[b][:, :], ckvn[:, k, blk * 128:(blk + 1) * 128],
                       wvc[:, k, :], k == 0, k == 1, reads=["wvc"], writes=["bk%d" % b])
                pv = bk[b][:, :].rearrange("p (q t c) -> p q t c", t=2, c=64)
                if blk % 2 == 0:
                    ACT(Vm[:, blk, :, 0:64], pv[:, :, 0, :], AF.Copy, reads=["bk%d" % b], writes=[("Vme", blk)])
                    ACT(Vm[:, blk, :, 128:192], pv[:, :, 1, :], AF.Copy, reads=["bk%d" % b], writes=[("Vmo", blk)])
                else:
                    CP("dve", Vm[:, blk, :, 0:64], pv[:, :, 0, :], reads=["bk%d" % b], writes=[("Vme", blk)])
                    CP("dve", Vm[:, blk, :, 128:192], pv[:, :, 1, :], reads=["bk%d" % b], writes=[("Vmo", blk)])

            def prep_head(h):
                Kh, Qh = KhT[h % 2], QhT[h % 2]
                kk, qk_ = KKEY[h % 2], "QhT%d" % (h % 2)
                for tc in range(8):
                    b = 6 + tc % 2
                    for k in range(2):
                        MM(bk[b][0:64, :], wukv[:, k, h * 128:h * 128 + 64], ckvn[:, k, tc * 512:(tc + 1) * 512], k == 0, k == 1,
                           reads=["wukv"], writes=["bk%d" % b])
                    CP("dve", Kh[0:64, tc * 512:(tc + 1) * 512], bk[b][0:64, :], reads=["bk%d" % b], writes=[kk])
                    yield
                for qc in range(4):
                    b = 6 + qc % 2
                    kb_ = "bk%d" % b
                    cs = slice(qc * 512, (qc + 1) * 512)
                    for k in range(3):
                        MM(bk[b][:, :], wuqa[:, k, h, :], cqn[:, k, cs], k == 0, k == 2, reads=["wuqa"], writes=[kb_])
                    CP("dve", Qh[0:64, cs], bk[b][0:64, :], reads=[kb_], writes=[qk_])
                    TT("dve", tq1[64:96, :], bk[b][64:96, :], jq[64:96, cs], ALU.mult, reads=[kb_, "jq"], writes=["tq1"])
                    TT("dve", tq2[64:96, :], bk[b][96:128, :], jq[96:128, cs], ALU.mult, reads=[kb_, "jq"], writes=["tq2"])
                    TT("dve", Qh[64:96, cs], tq1[64:96, :], tq2[64:96, :], ALU.add, reads=["tq1", "tq2"], writes=[qk_])
                    yield
                    yield

            sm_scale = 1.0 / math.sqrt(96.0)

            def mla_parts(h, g, ci):
                cc, bp = h // 2, (h % 2) * 64
                Kh, Qh = KhT[h % 2], QhT[h % 2]
                kk, qk_ = KKEY[h % 2], "QhT%d" % (h % 2)
                lb = 64 - bp
                vl = Vm[:, :, h // 2, bp:bp + 128]
                ob = 4 + ci % 2
                ko = "bk%d" % ob
                units = list(range(8 * g + 7, -1, -1))
                nP = len(units) // 2

                def qk(j):
                    zb = 2 * (j % 2)
                    kz = "zp%d" % (j % 2)
                    lo = chain_cols(g, units[2 * j])[0] * 128
                    for c in range(2):
                        kb = units[2 * j + c]
                        c0, masks = chain_cols(g, kb)
                        assert c0 * 128 == lo
                        MM(bk[zb + c][:, lo:512], Kh[0:96, kb * 128:(kb + 1) * 128], Qh[0:96, g * 512 + lo:(g + 1) * 512],
                           True, len(masks) == 0, reads=[kk, qk_], writes=[kz])
                        for mi, (cb, which) in enumerate(masks):
                            MM(bk[zb + c][:, cb * 128:(cb + 1) * 128], ident, m_mla[which], False, mi == len(masks) - 1, writes=[kz])
                    ACT(p_sb[:, j % 2, :, lo:512], bkpair(zb)[:, :, lo:512], AF.Exp, reads=[kz], writes=["p%d" % (j % 2)],
                        scale=sm_scale)

                def prologue():
                    MM(bk[ob][:, :], zeros_bf, ckvn[:, 0, 0:512], True, True, reads=["zeros"], writes=[ko])
                    qk(0)
                    if nP > 1:
                        qk(1)

                def rounds(stepper, next_prologue=None):
                    for j in range(nP):
                        lo = chain_cols(g, units[2 * j])[0] * 128
                        for c in range(2):
                            kb = units[2 * j + c]
                            MM(bk[ob][:, lo:512], vl[:, kb, :], p_sb[:, j % 2, c, lo:512], False, True,
                               reads=["p%d" % (j % 2), "Vm1", ("Vme", kb), ("Vmo", kb)], writes=[ko])
                        if j == nP - 1 and next_prologue is not None:
                            next_prologue()
                        if j + 2 < nP:
                            for fi in range(MLA_FILL):
                                MM(bk[2 * (j % 2) + fi % 2][:, :], ident, ckvn[:, 0, 0:512], True, True, writes=["zp%d" % (j % 2)])
                            qk(j + 2)
                        stepper()

                def tail():
                    ACT(lnl[bp:bp + 64, :], bk[ob][lb:lb + 64, :], AF.Ln, reads=[ko], writes=["lnl"])
                    ACT(rinv[bp:bp + 64, :], lnl[bp:bp + 64, :], AF.Exp, reads=["lnl"], writes=["rinv"], scale=-1.0)
                    TT("dve", obT[bp:bp + 64, cc, g * 512:(g + 1) * 512], bk[ob][bp:bp + 64, :], rinv[bp:bp + 64, :], ALU.mult,
                       reads=[ko, "rinv"], writes=["obT"])

                return prologue, rounds, tail

            for bi in range(2):
                for half in range(2):
                    hs_ = slice(half * 2048, (half + 1) * 2048)
                    CP("dve", KhT[bi][64:96, hs_], kpe[64:96, hs_], reads=["wuqa", "wukv"], writes=[KKEY[bi]])
            for _ in prep_head(0):
                pass
            if debug:
                dump("KhT0", KhT[0][0:96, :], [96, S_LEN], reads=["KhT0"])
                dump("QhT0", QhT[0][0:96, :], [96, NQ], reads=["QhT0"])
            mparts = [(h, g, mla_parts(h, g, 4 * h + gi)) for h in range(8) for gi, g in enumerate((3, 2, 1, 0))]
            mparts[0][2][0]()
            prev_tail = None
            prep = None
            for idx, (h, g, (prologue, rounds, tail)) in enumerate(mparts):
                if g == 3:
                    prep = prep_head(h + 1) if h + 1 < 8 else iter(())

                def stepper(prep=prep):
                    next(prep, None)

                nxt = None
                if idx + 1 < len(mparts):
                    nh = mparts[idx + 1][0]
                    npro = mparts[idx + 1][2][0]

                    def nxt(nh=nh, h=h, npro=npro, prep=prep):
                        if nh != h:
                            for _ in prep:
                                pass
                        npro()
                if prev_tail is not None:
                    prev_tail()
                rounds(stepper, nxt)
                prev_tail = tail
            prev_tail()
            dump("obT", obT.rearrange("p a b -> p (a b)"), [128, 4 * NQ], reads=["obT"])
            S.barrier()

        if stop_after is None:
            Dw = Bump(16, 132)
            WG = Dw([128, 8, 3072], BF16)
            WA = Dw([128, 4, D], BF16)
            WB = Dw([128, 4, D], BF16)
            WO = Dw([128, 8, D], BF16)
            gate_bc = Dw([128, D], F32)
            fg_bc = Dw([128, D], F32)
            hTd = Dw([128, 8, 512], BF16)
            merged_off = Dw.o
            mergedT = Dw([128, 8, 512], BF16)
            xnew_off = Dw.o
            xnew = Dw([128, D], F32)
            outt_off = Dw.o
            outt = Dw([128, D], F32)
            sqjD = Dw([128, D], BF16)
            Dt = Bump(164, 204)
            xtD = [Dt([128, D], F32) for _ in range(2)]
            xres = [Dt([128, D], F32) for _ in range(2)]
            xnD = Dt([128, 4, D], BF16)
            og_off = Dt.o
            ogA = Dt([128, 4, 512], BF16)
            ogB = Dt([128, 4, 512], BF16)
            xnew2 = view(og_off, [128, D], F32)
            outt2 = view(og_off + 4096, [128, D], F32)
            sg = [Dt([128, 512], F32) for _ in range(2)]
            tm = [Dt([128, 512], F32) for _ in range(2)]
            bgate_bc = view(merged_off, [128, D], F32)
            cbc = view(merged_off + 4096, [128, 8, 128], F32)
            wstD = [view(xnew_off, [128, D], F32), view(outt_off, [128, D], F32)]
            WSTK = ["xnew", "outt"]

            for k in range(8):
                S.dma("pool", WG[:, k, 0:512], w3[:, k, SBZ:SBZ + 512], writes=["WGz0"])
            for k in range(8):
                S.dma("pool", WG[:, k, 512:1024], w3[:, k, MLAZ:MLAZ + 512], writes=["WGz512"])
            for k in range(8):
                S.dma("pool", WG[:, k, 1024:3072], w3[:, k, GA:GA + 2048], writes=["WGg"])
            for k in range(4):
                S.dma("pool", WA[:, k, :], w_a[k * 128:(k + 1) * 128, :], writes=["WA"])
                S.dma("pool", WB[:, k, :], w_b[k * 128:(k + 1) * 128, :], writes=["WB"])
            for k in range(8):
                S.dma("pool", WO[:, k, :], w_out[k * 128:(k + 1) * 128, :], writes=["WO"])

            xn = xnD
            sqj = sqjD
            norm_stats(xq[0:512, :], xtD)
            S.dma("sp", fg_bc, fg_row.broadcast_to([128, D]), writes=["fg_bc"])
            S.dma("sp", bgate_bc, b_gate.broadcast_to([128, D]), writes=["merged"])
            S.op("dve", lambda e: e.memset(cbc, 1.0), writes=["merged"])
            for k in range(8):
                TS("dve", cbc[:, k, :], cbc[:, k, :], cT[:, k:k + 1], None, ALU.mult, reads=["merged", "small"], writes=["merged"])
            for k in range(8):
                S.dma("sp", wstD[k % 2], w_ada[k * 128:(k + 1) * 128, 2 * D:3 * D], writes=[WSTK[k % 2]])
                for half in range(2):
                    MM(bk[half][:, :], cbc[:, k, :], wstD[k % 2][:, half * 512:(half + 1) * 512], k == 0, k == 7,
                       reads=["merged", WSTK[k % 2]], writes=["bk%d" % half])
            for half in range(2):
                TT("dve", gate_bc[:, half * 512:(half + 1) * 512], bk[half][:, :], bgate_bc[:, half * 512:(half + 1) * 512],
                   ALU.add, reads=["bk%d" % half, "merged"], writes=["gate_bc"])
            hkd = ("hT", id(hTd))
            norm_T(hTd)
            for qc in range(4):
                q0 = qc * 512
                if qc + 1 < 4:
                    queue_stats(xq[q0 + 512:q0 + 1024, :], xtD)
                for (off, oT, og, nm) in ((0, oaT, ogA, "ogA"), (512, obT, ogB, "ogB")):
                    for cc in range(4):
                        b = next_bank()
                        for k in range(8):
                            MM(bk[b][:, :], WG[:, k, off + cc * 128:off + (cc + 1) * 128], hTd[:, k, :], k == 0, k == 7,
                               reads=["WGz%d" % off, hkd + (k,)], writes=["bk%d" % b])
                        si = cc % 2
                        ACT(sg[si], bk[b][:, :], AF.Sigmoid, reads=["bk%d" % b], writes=["sg%d" % si])
                        TT("dve", tm[si], bk[b][:, :], sg[si], ALU.mult, reads=["bk%d" % b, "sg%d" % si], writes=["tm%d" % si])
                        TT("dve", og[:, cc, :], tm[si], oT[:, cc, q0:q0 + 512], ALU.mult, reads=["tm%d" % si], writes=[nm])
                for n in range(8):
                    bga, bgb, bya, byb = next_bank(), next_bank(), next_bank(), next_bank()
                    for k in range(8):
                        MM(bk[bga][:, :], WG[:, k, 1024 + n * 128:1024 + (n + 1) * 128], hTd[:, k, :], k == 0, k == 7,
                           reads=["WGg", hkd + (k,)], writes=["bk%d" % bga])
                    for k in range(8):
                        MM(bk[bgb][:, :], WG[:, k, 2048 + n * 128:2048 + (n + 1) * 128], hTd[:, k, :], k == 0, k == 7,
                           reads=["WGg", hkd + (k,)], writes=["bk%d" % bgb])
                    for k in range(4):
                        MM(bk[bya][:, :], WA[:, k, n * 128:(n + 1) * 128], ogA[:, k, :], k == 0, k == 3,
                           reads=["WA", "ogA"], writes=["bk%d" % bya])
                    for k in range(4):
                        MM(bk[byb][:, :], WB[:, k, n * 128:(n + 1) * 128], ogB[:, k, :], k == 0, k == 3,
                           reads=["WB", "ogB"], writes=["bk%d" % byb])
                    ACT(sg[0], bk[bga][:, :], AF.Sigmoid, reads=["bk%d" % bga], writes=["sg0"])
                    ACT(sg[1], bk[bgb][:, :], AF.Sigmoid, reads=["bk%d" % bgb], writes=["sg1"])
                    TT("dve", tm[0], bk[bya][:, :], sg[0], ALU.mult, reads=["bk%d" % bya, "sg0"], writes=["tm0"])
                    TT("dve", tm[1], bk[byb][:, :], sg[1], ALU.mult, reads=["bk%d" % byb, "sg1"], writes=["tm1"])
                    TT("dve", mergedT[:, n, :], tm[0], tm[1], ALU.add, reads=["tm0", "tm1"], writes=["merged"])
                    if n % 2 == 1:
                        drain_stats(1)
                drain_stats(4)
                if qc + 1 < 4:
                    norm_T(hTd)
                for blk in range(4):
                    rs = blk % 2
                    xk = "xres%d" % rs
                    xnw, xnk = (xnew, "xnew") if blk % 2 == 0 else (xnew2, "ogA")
                    ott, otk = (outt, "outt") if blk % 2 == 0 else (outt2, "ogB")
                    S.dma("sp", xres[rs], xq[q0 + blk * 128:q0 + (blk + 1) * 128, :], writes=[xk])
                    for half in range(2):
                        b = next_bank()
                        for k in range(8):
                            MM(bk[b][:, :], mergedT[:, k, blk * 128:(blk + 1) * 128], WO[:, k, half * 512:(half + 1) * 512],
                               k == 0, k == 7, reads=["merged", "WO"], writes=["bk%d" % b])
                        hs = slice(half * 512, (half + 1) * 512)
                        TT("dve", xnw[:, hs], bk[b][:, :], gate_bc[:, hs], ALU.mult, reads=["bk%d" % b, "gate_bc"], writes=[xnk])
                    TT("dve", xnw, xnw, xres[rs], ALU.add, reads=[xnk, xk], writes=[xnk])
                    c = state["st"]
                    state["st"] = (c + 1) % 8
                    sk = "stat%d" % c
                    ACT(sqjD, xnw, AF.Square, reads=[xnk], writes=["sqjD", sk], accum_out=stat[:, c:c + 1])
                    ACT(stat[:, 8 + c:9 + c], stat[:, c:c + 1], AF.Ln, reads=[sk, "small"], writes=[sk], scale=1.0 / D, bias=epsc)
                    ACT(stat[:, 16 + c:17 + c], stat[:, 8 + c:9 + c], AF.Exp, reads=[sk], writes=[sk], scale=-0.5)
                    S.op("dve", lambda e, c=c, ott=ott, xnw=xnw: e.scalar_tensor_tensor(ott, xnw, stat[:, 16 + c:17 + c], fg_bc, ALU.mult, ALU.mult),
                         reads=[xnk, sk, "fg_bc"], writes=[otk])
                    S.dma("pool", out_d[q0 + blk * 128:q0 + (blk + 1) * 128, :], ott, reads=[otk])

        for q in ("sp", "pool"):
            for i in range(max(0, S.dma_n[q] - ND), S.dma_n[q]):
                S._wait("pool", ("d", (q, i)))
        with nc.Block() as block:
            S.emit(block)
    return nc, dbg_outs


def host_inputs(inputs):
    x = np.asarray(inputs["x"], np.float32)
    c = np.asarray(inputs["c"], np.float32)
    pos = np.asarray(inputs["positions"], np.int32)
    f = lambda k: np.ascontiguousarray(np.asarray(inputs[k], np.float32))
    w_ada = f("w_ada")[0]
    b_ada = f("b_ada")[0]
    tri = np.tril(np.ones((128, 128), np.float32))
    ident = np.eye(128, dtype=np.float32)
    omt = 1.0 - tri
    ss, tt = np.meshgrid(np.arange(128), np.arange(128), indexing="ij")
    strict = np.where(ss < tt, 0.0, MASKV).astype(np.float32)
    causal = np.where(ss <= tt, 0.0, MASKV).astype(np.float32)
    allm = np.full((128, 128), MASKV, np.float32)
    nom = np.zeros((128, 128), np.float32)
    inv_freq = (10000.0 ** (-np.arange(0, 32, 2, dtype=np.float32) / np.float32(32))).astype(np.float32)
    invf = np.tile(np.concatenate([inv_freq, inv_freq]), 4).reshape(128, 1).astype(np.float32)
    common = {
        "w_ada": w_ada,
        "b_adaT": np.ascontiguousarray(b_ada.reshape(24, 128).T),
        "b_gate": np.ascontiguousarray(b_ada[2 * D:3 * D].reshape(1, D)),
        "ngT": np.ascontiguousarray(f("norm_gain")[0].reshape(8, 128).T),
        "w_in": f("w_in")[0],
        "qgT": np.ascontiguousarray(f("q_norm_gain")[0].reshape(3, 128).T),
        "w_uq": f("w_uq")[0],
        "kvgT": np.ascontiguousarray(f("kv_norm_gain")[0].reshape(2, 128).T),
        "w_ukv": f("w_ukv")[0],
        "w_a": f("w_branch_a")[0],
        "w_b": f("w_branch_b")[0],
        "w_out": f("w_out")[0],
        "fg_row": np.ascontiguousarray(f("final_norm_gain").reshape(1, D)),
        "invf": invf,
    }
    maps = []
    for core in range(8):
        b, p = core // 2, core % 2
        blocks = [2 * j + p for j in range(16)]
        xb = x[b].reshape(32, 128, D)
        xq = np.ascontiguousarray(xb[blocks].reshape(NQ, D))
        pq = pos[b].reshape(32, 128)[blocks].reshape(NQ)
        posall = np.ascontiguousarray(np.concatenate([pos[b], pq]).reshape(4, 1536).astype(np.int32))
        if p == 0:
            mats = [ident, tri, omt, allm, strict, allm, causal]
        else:
            mats = [ident, tri, omt, strict, nom, causal, nom]
        cm = np.ascontiguousarray(np.stack(mats, axis=1).astype(np.float32))
        m = dict(common)
        m.update({"xf": np.ascontiguousarray(x[b]), "xq": xq, "posall": posall,
                  "cT": np.ascontiguousarray(c[b].reshape(8, 128).T), "cmats": cm})
        maps.append(m)
    return maps


_CACHE = {}


def kernel(**inputs):
    maps = host_inputs(inputs)
    if "nc" not in _CACHE:
        _CACHE["nc"] = build_program()[0]
    nc = _CACHE["nc"]
    res = run_bass_kernel_spmd(nc, maps, core_ids=list(range(8)))
    out = np.zeros((4, 32, 128, D), np.float32)
    for core in range(8):
        b, p = core // 2, core % 2
        o = np.asarray(res.results[core]["out"], np.float32).reshape(16, 128, D)
        out[b, p::2] = o
    return out.reshape(4, S_LEN, D)
```
